# Optimizing a Trainium2 kernel written in Bass

```python
import math
import jax, jax.numpy as jnp
from jax import lax
import numpy as np

D_MODEL = 2048
BATCH = 4
SEQ = 2048
DEPTH = 1
DEC_BATCH = 8
DEC_SEQ = 8
PAST_LEN = 16384
PAGE_SIZE = 128

GDN_QK_HEADS = 16
GDN_V_HEADS = 32
GDN_HEAD_DIM = 128
GDN_CONV = 4
GDN_CHUNK = 64
GDN_QK_W = GDN_QK_HEADS * GDN_HEAD_DIM
GDN_V_W = GDN_V_HEADS * GDN_HEAD_DIM
GDN_QKV_W = 2 * GDN_QK_W + GDN_V_W

NSA_HEADS = 16
NSA_KV_HEADS = 4
NSA_GROUP = NSA_HEADS // NSA_KV_HEADS
NSA_HEAD_DIM = 128
NSA_Q_W = NSA_HEADS * NSA_HEAD_DIM
NSA_KV_W = NSA_KV_HEADS * NSA_HEAD_DIM
CMP_LEN = 32
CMP_STRIDE = 16
SEL_BLOCK = 64
SEL_TOP = 16
WINDOW = 512
SEL_QBLK = 64
WIN_QBLK = 128

REL_BUCKETS = 32
REL_MAX_DIST = 128

EPS = 1e-6
NEG = -1e30
BIG = 1e9

IN_SPLIT_SIZES = (GDN_QKV_W, GDN_V_W, GDN_V_HEADS, GDN_V_HEADS, NSA_Q_W) + (NSA_KV_W,) * 6 + (3 * NSA_HEADS, NSA_Q_W, D_MODEL, D_MODEL)
IN_W = sum(IN_SPLIT_SIZES)

kernel_name = 'hybrid_gdn_nsa_decoder_step'


def rmsnorm(x, g):
    xf = x.astype(jnp.float32)
    return xf * lax.rsqrt(jnp.mean(xf * xf, axis=-1, keepdims=True) + EPS) * g.astype(jnp.float32)


def l2norm(x):
    return x * lax.rsqrt(jnp.sum(x * x, axis=-1, keepdims=True) + EPS)


def t5_bucket(rel):
    n = jnp.maximum(rel, 0)
    exact = REL_BUCKETS // 2
    nf = jnp.maximum(n, 1).astype(jnp.float32)
    large = exact + (jnp.log(nf / exact) / math.log(REL_MAX_DIST / exact) * (REL_BUCKETS - exact)).astype(jnp.int32)
    return jnp.where(n < exact, n, jnp.minimum(large, REL_BUCKETS - 1))


def input_projection(x, norm_g, w_in):
    h = rmsnorm(x, norm_g).astype(x.dtype)
    u = jnp.einsum('btd,de->bte', h, w_in)
    points = np.cumsum(IN_SPLIT_SIZES)[:-1].tolist()
    return jnp.split(u, points, axis=-1)


def gated_delta_chunked(q, k, v, beta, g, s0):
    B, T, H, _ = q.shape
    C = GDN_CHUNK
    n = -(-T // C)
    pad = n * C - T

    def prep(a):
        a = jnp.pad(a, [(0, 0), (0, pad)] + [(0, 0)] * (a.ndim - 2))
        a = a.reshape(B, n, C, *a.shape[2:])
        return jnp.swapaxes(a, 2, 3)

    q, k, v, beta, g = prep(q), prep(k), prep(v), prep(beta), prep(g)
    gc = jnp.cumsum(g, axis=-1)
    ii = jnp.arange(C)
    tril = ii[:, None] >= ii[None, :]
    strict = ii[:, None] > ii[None, :]
    gamma = jnp.exp(jnp.where(tril, gc[..., :, None] - gc[..., None, :], -jnp.inf))
    kb = k * beta[..., None]
    a_mat = jnp.eye(C, dtype=q.dtype) + jnp.where(strict, jnp.einsum('bnhid,bnhjd->bnhij', kb, k) * gamma, 0.0)
    u = lax.linalg.triangular_solve(a_mat, v * beta[..., None], left_side=True, lower=True, unit_diagonal=True)
    w = lax.linalg.triangular_solve(a_mat, kb * jnp.exp(gc)[..., None], left_side=True, lower=True, unit_diagonal=True)
    attn = jnp.einsum('bnhid,bnhjd->bnhij', q, k) * gamma
    qg = q * jnp.exp(gc)[..., None]
    kdec = k * jnp.exp(gc[..., -1:] - gc)[..., None]
    glast = jnp.exp(gc[..., -1])

    def step(s, xs):
        u_c, w_c, attn_c, qg_c, kdec_c, gl_c = xs
        v_new = u_c - jnp.einsum('bhcd,bhde->bhce', w_c, s)
        o = jnp.einsum('bhcd,bhde->bhce', qg_c, s) + jnp.einsum('bhij,bhje->bhie', attn_c, v_new)
        s = s * gl_c[..., None, None] + jnp.einsum('bhcd,bhce->bhde', kdec_c, v_new)
        return s, o

    xs = tuple(jnp.moveaxis(a, 1, 0) for a in (u, w, attn, qg, kdec, glast))
    s_fin, o = lax.scan(step, s0, xs)
    o = o.transpose(1, 0, 3, 2, 4).reshape(B, n * C, H, v.shape[-1])[:, :T]
    return o, s_fin


def gdn_branch(qkv, z, b_raw, a_raw, conv_state, s0, conv_w, a_log, dt_bias, norm_g):
    B, T, _ = qkv.shape
    xp = jnp.concatenate([conv_state.astype(qkv.dtype), qkv], axis=1)
    conv = xp[:, 0:T] * conv_w[0]
    for j in range(1, GDN_CONV):
        conv = conv + xp[:, j:j + T] * conv_w[j]
    new_conv = xp[:, T:]
    act = jax.nn.silu(conv.astype(jnp.float32))
    q, k, v = jnp.split(act, [GDN_QK_W, 2 * GDN_QK_W], axis=-1)
    rep = GDN_V_HEADS // GDN_QK_HEADS
    q = jnp.repeat(l2norm(q.reshape(B, T, GDN_QK_HEADS, GDN_HEAD_DIM)), rep, axis=2) * GDN_HEAD_DIM ** -0.5
    k = jnp.repeat(l2norm(k.reshape(B, T, GDN_QK_HEADS, GDN_HEAD_DIM)), rep, axis=2)
    v = v.reshape(B, T, GDN_V_HEADS, GDN_HEAD_DIM)
    beta = jax.nn.sigmoid(b_raw.astype(jnp.float32))
    g = -jnp.exp(a_log.astype(jnp.float32)) * jax.nn.softplus(a_raw.astype(jnp.float32) + dt_bias.astype(jnp.float32))
    o, s_fin = gated_delta_chunked(q, k, v, beta, g, s0.astype(jnp.float32))
    o = rmsnorm(o, norm_g) * jax.nn.silu(z.astype(jnp.float32).reshape(B, T, GDN_V_HEADS, GDN_HEAD_DIM))
    return o.reshape(B, T, GDN_V_W), new_conv, s_fin.astype(s0.dtype)


def nsa_inputs(q_raw, kc, vc, ks, vs, kw, vw, q_norm_g, k_norm_g):
    B, T, _ = q_raw.shape
    heads = lambda a: a.reshape(B, T, NSA_KV_HEADS, NSA_HEAD_DIM)
    q = rmsnorm(q_raw.reshape(B, T, NSA_KV_HEADS, NSA_GROUP, NSA_HEAD_DIM), q_norm_g)
    ks_n = rmsnorm(heads(ks), k_norm_g[1]).astype(ks.dtype)
    kw_n = rmsnorm(heads(kw), k_norm_g[2]).astype(kw.dtype)
    return q, heads(kc), heads(vc), ks_n, heads(vs), kw_n, heads(vw)


def compress(kv, pe, w, proj):
    B, L, G, D = kv.shape
    nc = (L - CMP_LEN) // CMP_STRIDE + 1
    r = CMP_LEN // CMP_STRIDE
    chunks = kv[:, :(nc + r - 1) * CMP_STRIDE].astype(jnp.float32).reshape(B, nc + r - 1, CMP_STRIDE, G, D)
    pe = pe.astype(jnp.float32)
    w = w.astype(jnp.float32)
    pooled = None
    for s in range(r):
        sl = slice(s * CMP_STRIDE, (s + 1) * CMP_STRIDE)
        term = jnp.einsum('bnpgd,p->bngd', chunks[:, s:s + nc] + pe[sl, None, :], w[sl])
        pooled = term if pooled is None else pooled + term
    return jnp.einsum('bngd,de->bnge', pooled, proj.astype(jnp.float32))


def dense_attend(q, k, v, rel, mask, rel_bias):
    G, Hpg, D = q.shape[2:]
    T, K = rel.shape
    s = jnp.einsum('btghd,bkgd->btghk', q, k.astype(jnp.float32)) * D ** -0.5
    bias = rel_bias.astype(jnp.float32)[t5_bucket(rel)].reshape(T, K, G, Hpg).transpose(0, 2, 3, 1)
    m = mask[:, None, None, :]
    p = jax.nn.softmax(jnp.where(m, s + bias, NEG), axis=-1) * m
    o = jnp.einsum('btghk,bkgd->btghd', p, v.astype(jnp.float32))
    return o, p


def sparse_attend(q, q_pos, kg, vg, key_pos, rel_bias):
    G, Hpg, D = q.shape[2:]
    s = jnp.einsum('btghd,btgkd->btghk', q, kg.astype(jnp.float32)) * D ** -0.5
    rel = q_pos[None, :, None, None] - key_pos
    table = rel_bias.astype(jnp.float32).reshape(REL_BUCKETS, G, Hpg)
    bias = jnp.moveaxis(table[t5_bucket(rel), jnp.arange(G)[None, None, :, None]], -1, 3)
    m = (rel >= 0)[:, :, :, None, :]
    p = jax.nn.softmax(jnp.where(m, s + bias, NEG), axis=-1) * m
    return jnp.einsum('btghk,btgkd->btghd', p, vg.astype(jnp.float32))


def block_positions(idx):
    pos = idx[..., None] * SEL_BLOCK + jnp.arange(SEL_BLOCK, dtype=idx.dtype)
    return pos.reshape(*idx.shape[:-1], idx.shape[-1] * SEL_BLOCK)


def cmp_select(q, q_pos, kc_full, vc_full, pe_k, w_k, proj_k, pe_v, w_v, proj_v, k_gain, rel_bias):
    L = kc_full.shape[1]
    kc = rmsnorm(compress(kc_full, pe_k, w_k, proj_k), k_gain)
    vc = compress(vc_full, pe_v, w_v, proj_v)
    nc = kc.shape[1]
    ends = jnp.arange(nc, dtype=jnp.int32) * CMP_STRIDE + (CMP_LEN - 1)
    rel = q_pos[:, None] - ends[None, :]
    o, p = dense_attend(q, kc, vc, rel, rel >= 0, rel_bias)
    imp = p.sum(axis=3)
    nb = -(-L // SEL_BLOCK)
    j = np.arange(nb)
    lo = np.clip((SEL_BLOCK * j - CMP_LEN) // CMP_STRIDE + 1, 0, nc)
    hi = np.clip(-(-(SEL_BLOCK * (j + 1)) // CMP_STRIDE), 0, nc)
    csum = jnp.concatenate([jnp.zeros_like(imp[..., :1]), jnp.cumsum(imp, axis=-1)], axis=-1)
    score = csum[..., hi] - csum[..., lo]
    qblk = (q_pos // SEL_BLOCK)[:, None]
    jj = jnp.arange(nb, dtype=jnp.int32)[None, :]
    forced = (jj == 0) | (jj == qblk) | (jj == qblk - 1)
    score = jnp.where(forced[None, :, None, :], BIG, score)
    score = jnp.where((jj <= qblk)[None, :, None, :], score, -BIG)
    _, idx = lax.top_k(score, min(SEL_TOP, nb))
    return o, idx


def select_prompt(q, q_pos, idx, ks, vs, rel_bias):
    B, T, G, Hpg, D = q.shape
    nb = T // SEL_BLOCK
    src_k = ks.reshape(B, nb, SEL_BLOCK, G, D)
    src_v = vs.reshape(B, nb, SEL_BLOCK, G, D)
    nq = T // SEL_QBLK
    bi = jnp.arange(B)[:, None, None, None]
    gi = jnp.arange(G)[None, None, :, None]

    def chunk(args):
        qc, idxc, posc = args
        kg = src_k[bi, idxc, :, gi].reshape(B, SEL_QBLK, G, -1, D)
        vg = src_v[bi, idxc, :, gi].reshape(B, SEL_QBLK, G, -1, D)
        return sparse_attend(qc, posc, kg, vg, block_positions(idxc), rel_bias)

    split = lambda a: jnp.swapaxes(a.reshape(B, nq, SEL_QBLK, *a.shape[2:]), 0, 1)
    o = lax.map(chunk, (split(q), split(idx), q_pos.reshape(nq, SEL_QBLK)))
    return jnp.swapaxes(o, 0, 1).reshape(B, T, G, Hpg, D)


def select_sample(q, q_pos, idx, cache_k, cache_v, page_table, k_new, v_new, rel_bias):
    B, T, G, Hpg, D = q.shape
    bpp = PAGE_SIZE // SEL_BLOCK
    nb_past = page_table.shape[1] * bpp
    n_new = -(-T // SEL_BLOCK)
    pad = n_new * SEL_BLOCK - T
    pool_k = cache_k.reshape(-1, SEL_BLOCK, G, D)
    pool_v = cache_v.reshape(-1, SEL_BLOCK, G, D)
    new_k = jnp.pad(k_new, ((0, 0), (0, pad), (0, 0), (0, 0))).reshape(B, n_new, SEL_BLOCK, G, D)
    new_v = jnp.pad(v_new, ((0, 0), (0, pad), (0, 0), (0, 0))).reshape(B, n_new, SEL_BLOCK, G, D)
    bi = jnp.arange(B)[:, None, None, None]
    gi = jnp.arange(G)[None, None, :, None]
    jp = jnp.minimum(idx, nb_past - 1)
    phys = page_table[bi, jp // bpp] * bpp + jp % bpp
    jn = jnp.clip(idx - nb_past, 0, n_new - 1)
    is_past = (idx < nb_past)[..., None, None]

    def gather(pool, new):
        g_past = pool[phys, :, gi].astype(jnp.float32)
        g_new = new[bi, jn, :, gi].astype(jnp.float32)
        return jnp.where(is_past, g_past, g_new).reshape(B, T, G, -1, D)

    return sparse_attend(q, q_pos, gather(pool_k, new_k), gather(pool_v, new_v), block_positions(idx), rel_bias)


def window_prompt(q, kw, vw, rel_bias):
    B, T, G, Hpg, D = q.shape
    nq = T // WIN_QBLK
    kwin = WIN_QBLK + WINDOW
    kp = jnp.concatenate([jnp.zeros((B, WINDOW, G, D), kw.dtype), kw], axis=1)
    vp = jnp.concatenate([jnp.zeros((B, WINDOW, G, D), vw.dtype), vw], axis=1)
    rel = WINDOW + jnp.arange(WIN_QBLK, dtype=jnp.int32)[:, None] - jnp.arange(kwin, dtype=jnp.int32)[None, :]
    band = (rel >= 0) & (rel < WINDOW)

    def blk(i):
        start = i * WIN_QBLK
        qs = lax.dynamic_slice_in_dim(q, start, WIN_QBLK, axis=1)
        ks = lax.dynamic_slice_in_dim(kp, start, kwin, axis=1)
        vs = lax.dynamic_slice_in_dim(vp, start, kwin, axis=1)
        mask = band & ((start - WINDOW + jnp.arange(kwin, dtype=jnp.int32)) >= 0)[None, :]
        return dense_attend(qs, ks, vs, rel, mask, rel_bias)[0]

    o = lax.map(blk, jnp.arange(nq, dtype=jnp.int32))
    return jnp.moveaxis(o, 0, 1).reshape(B, T, G, Hpg, D)


def window_sample(q, q_pos, cache_k_win, cache_v_win, k_new, v_new, rel_bias):
    T = q.shape[1]
    wb = cache_k_win.shape[1]
    k = jnp.concatenate([cache_k_win.astype(jnp.float32), k_new.astype(jnp.float32)], axis=1)
    v = jnp.concatenate([cache_v_win.astype(jnp.float32), v_new.astype(jnp.float32)], axis=1)
    key_pos = PAST_LEN - wb + jnp.arange(wb + T, dtype=jnp.int32)
    rel = q_pos[:, None] - key_pos[None, :]
    o, _ = dense_attend(q, k, v, rel, (rel >= 0) & (rel < WINDOW), rel_bias)
    return o, k[:, -wb:].astype(cache_k_win.dtype), v[:, -wb:].astype(cache_v_win.dtype)


def nsa_merge(o_cmp, o_sel, o_win, gates_raw, z):
    B, T = z.shape[:2]
    g = jax.nn.sigmoid(gates_raw.astype(jnp.float32)).reshape(B, T, 3, NSA_KV_HEADS, NSA_GROUP, 1)
    o = g[:, :, 0] * o_cmp + g[:, :, 1] * o_sel + g[:, :, 2] * o_win
    return o.reshape(B, T, NSA_Q_W) * jax.nn.silu(z.astype(jnp.float32))


def output_merge(x, o_a, o_b, m_a, m_b, w_branch_a, w_branch_b, w_out):
    dt = x.dtype
    p_a = jnp.einsum('bte,ed->btd', o_a.astype(dt), w_branch_a)
    p_b = jnp.einsum('bte,ed->btd', o_b.astype(dt), w_branch_b)
    m = jax.nn.sigmoid(m_a) * p_a + jax.nn.sigmoid(m_b) * p_b
    return x + jnp.einsum('btd,de->bte', m, w_out)


def setup_inputs(seed: int = 0) -> dict:
    key = jax.random.key(seed)
    ks = jax.random.split(key, 32)
    f32 = jnp.float32
    n_pages = PAST_LEN // PAGE_SIZE
    n_pool = (5 * DEC_BATCH * n_pages + 3) // 4
    wb = min(WINDOW, PAST_LEN)

    def nrm(k, shape, scale=1.0):
        return scale * jax.random.normal(k, shape, f32)

    kv_page = (n_pool, PAGE_SIZE, NSA_KV_HEADS, NSA_HEAD_DIM)
    kv_win = (DEC_BATCH, wb, NSA_KV_HEADS, NSA_HEAD_DIM)
    dt = jnp.exp(jax.random.uniform(ks[14], (GDN_V_HEADS,), f32, math.log(1e-3), math.log(1e-1)))
    page_table = jax.random.permutation(ks[10], n_pool)[:DEC_BATCH * n_pages].reshape(DEC_BATCH, n_pages).astype(jnp.int32)
    return {
        'x_prompt': nrm(ks[0], (BATCH, SEQ, D_MODEL)),
        'x_sample': nrm(ks[1], (DEC_BATCH, DEC_SEQ, D_MODEL)),
        'cache_k_cmp': nrm(ks[2], kv_page),
        'cache_v_cmp': nrm(ks[3], kv_page),
        'cache_k_sel': nrm(ks[4], kv_page),
        'cache_v_sel': nrm(ks[5], kv_page),
        'cache_k_win': nrm(ks[6], kv_win),
        'cache_v_win': nrm(ks[7], kv_win),
        'state_conv': nrm(ks[8], (DEC_BATCH, GDN_CONV - 1, GDN_QKV_W)),
        'state_gdn': nrm(ks[9], (DEC_BATCH, GDN_V_HEADS, GDN_HEAD_DIM, GDN_HEAD_DIM), 0.05),
        'page_table': page_table,
        'norm_g': 1.0 + nrm(ks[11], (D_MODEL,), 0.02),
        'w_in': nrm(ks[12], (D_MODEL, IN_W), D_MODEL ** -0.5),
        'gdn_conv_w': nrm(ks[13], (GDN_CONV, GDN_QKV_W), GDN_CONV ** -0.5),
        'gdn_a_log': jnp.log(jax.random.uniform(ks[15], (GDN_V_HEADS,), f32, 1.0, 16.0)),
        'gdn_dt_bias': dt + jnp.log(-jnp.expm1(-dt)),
        'gdn_norm_g': 1.0 + nrm(ks[16], (GDN_HEAD_DIM,), 0.02),
        'q_norm_g': 1.0 + nrm(ks[17], (NSA_HEAD_DIM,), 0.02),
        'k_norm_g': 1.0 + nrm(ks[18], (3, NSA_HEAD_DIM), 0.02),
        'cmp_pe_k': nrm(ks[19], (CMP_LEN, NSA_HEAD_DIM), 0.1),
        'cmp_w_k': (1.0 + nrm(ks[20], (CMP_LEN,), 0.1)) / CMP_LEN,
        'cmp_proj_k': nrm(ks[21], (NSA_HEAD_DIM, NSA_HEAD_DIM), NSA_HEAD_DIM ** -0.5),
        'cmp_pe_v': nrm(ks[22], (CMP_LEN, NSA_HEAD_DIM), 0.1),
        'cmp_w_v': (1.0 + nrm(ks[23], (CMP_LEN,), 0.1)) / CMP_LEN,
        'cmp_proj_v': nrm(ks[24], (NSA_HEAD_DIM, NSA_HEAD_DIM), NSA_HEAD_DIM ** -0.5),
        'rel_bias': nrm(ks[25], (REL_BUCKETS, NSA_HEADS), 0.2),
        'w_branch_a': nrm(ks[26], (GDN_V_W, D_MODEL), GDN_V_W ** -0.5),
        'w_branch_b': nrm(ks[27], (NSA_Q_W, D_MODEL), NSA_Q_W ** -0.5),
        'w_out': nrm(ks[28], (D_MODEL, D_MODEL), D_MODEL ** -0.5),
    }


def reference(x_prompt, x_sample, cache_k_cmp, cache_v_cmp, cache_k_sel, cache_v_sel, cache_k_win, cache_v_win,
              state_conv, state_gdn, page_table, norm_g, w_in, gdn_conv_w, gdn_a_log, gdn_dt_bias, gdn_norm_g,
              q_norm_g, k_norm_g, cmp_pe_k, cmp_w_k, cmp_proj_k, cmp_pe_v, cmp_w_v, cmp_proj_v, rel_bias,
              w_branch_a, w_branch_b, w_out):
    f32 = jnp.float32
    G, Dh = NSA_KV_HEADS, NSA_HEAD_DIM
    wb = cache_k_win.shape[1]
    gdn_p = (gdn_conv_w, gdn_a_log, gdn_dt_bias, gdn_norm_g)
    cmp_p = (cmp_pe_k, cmp_w_k, cmp_proj_k, cmp_pe_v, cmp_w_v, cmp_proj_v)

    bp, tp = x_prompt.shape[:2]
    (qkv_a, z_a, b_a, a_a, q_b, kc_b, vc_b, ks_b, vs_b, kw_b, vw_b, g_b, z_b, m_a, m_b) = input_projection(x_prompt, norm_g, w_in)
    conv0 = jnp.zeros((bp, GDN_CONV - 1, GDN_QKV_W), x_prompt.dtype)
    s0 = jnp.zeros((bp, GDN_V_HEADS, GDN_HEAD_DIM, GDN_HEAD_DIM), f32)
    o_a, p_conv, p_gdn = gdn_branch(qkv_a, z_a, b_a, a_a, conv0, s0, *gdn_p)
    q, p_kc, p_vc, p_ks, p_vs, kw, vw = nsa_inputs(q_b, kc_b, vc_b, ks_b, vs_b, kw_b, vw_b, q_norm_g, k_norm_g)
    pos = jnp.arange(tp, dtype=jnp.int32)
    o_cmp, idx = cmp_select(q, pos, p_kc, p_vc, *cmp_p, k_norm_g[0], rel_bias)
    o_sel = select_prompt(q, pos, idx, p_ks, p_vs, rel_bias)
    o_win = window_prompt(q, kw, vw, rel_bias)
    o_b = nsa_merge(o_cmp, o_sel, o_win, g_b, z_b)
    y_prompt = output_merge(x_prompt, o_a, o_b, m_a, m_b, w_branch_a, w_branch_b, w_out)
    p_kw = jnp.concatenate([jnp.zeros((bp, wb, G, Dh), kw.dtype), kw], axis=1)[:, -wb:]
    p_vw = jnp.concatenate([jnp.zeros((bp, wb, G, Dh), vw.dtype), vw], axis=1)[:, -wb:]

    bs, ts = x_sample.shape[:2]
    (qkv_a, z_a, b_a, a_a, q_b, kc_b, vc_b, ks_b, vs_b, kw_b, vw_b, g_b, z_b, m_a, m_b) = input_projection(x_sample, norm_g, w_in)
    o_a_s, s_conv, s_gdn = gdn_branch(qkv_a, z_a, b_a, a_a, state_conv, state_gdn, *gdn_p)
    q_s, s_kc, s_vc, s_ks, s_vs, kw_s, vw_s = nsa_inputs(q_b, kc_b, vc_b, ks_b, vs_b, kw_b, vw_b, q_norm_g, k_norm_g)
    pos_s = PAST_LEN + jnp.arange(ts, dtype=jnp.int32)
    n_pages = PAST_LEN // PAGE_SIZE
    past = lambda c: c[page_table].reshape(bs, n_pages * PAGE_SIZE, G, Dh).astype(f32)
    kc_full = jnp.concatenate([past(cache_k_cmp), s_kc.astype(f32)], axis=1)
    vc_full = jnp.concatenate([past(cache_v_cmp), s_vc.astype(f32)], axis=1)
    o_cmp_s, idx_s = cmp_select(q_s, pos_s, kc_full, vc_full, *cmp_p, k_norm_g[0], rel_bias)
    o_sel_s = select_sample(q_s, pos_s, idx_s, cache_k_sel, cache_v_sel, page_table, s_ks, s_vs, rel_bias)
    o_win_s, s_kw, s_vw = window_sample(q_s, pos_s, cache_k_win, cache_v_win, kw_s, vw_s, rel_bias)
    o_b_s = nsa_merge(o_cmp_s, o_sel_s, o_win_s, g_b, z_b)
    y_sample = output_merge(x_sample, o_a_s, o_b_s, m_a, m_b, w_branch_a, w_branch_b, w_out)

    return (y_prompt, y_sample, p_kc, p_vc, p_ks, p_vs, p_kw, p_vw, p_conv, p_gdn,
            s_kc, s_vc, s_ks, s_vs, s_kw, s_vw, s_conv, s_gdn)
```

```python
import os
import math
from contextlib import ExitStack
import numpy as np
import concourse.bass as bass
import concourse.mybir as mybir
from concourse.bass_utils import run_bass_kernel_spmd

F32 = mybir.dt.float32
BF16 = mybir.dt.bfloat16
I32 = mybir.dt.int32
ALU = mybir.AluOpType
AF = mybir.ActivationFunctionType
AX = mybir.AxisListType

EPOCH = 20000

D_MODEL = 2048
SEQ = 2048
DEC_SEQ = 8
NTOK = SEQ + DEC_SEQ
IN_W = 23664
C_Q, C_K, C_V, C_Z = 0, 2048, 4096, 8192
C_B, C_A = 12288, 12320
C_QB = 12352
C_KC, C_VC, C_KS, C_VS, C_KW, C_VW = 14400, 14912, 15424, 15936, 16448, 16960
C_GB = 17472
C_ZB = 17520
C_MA, C_MB = 19568, 21616
EPS = 1e-6


class Prog:
    ENGS = ("sync", "scalar", "vector", "gpsimd", "tensor")

    def __init__(self, nc):
        self.nc = nc
        self.stack = ExitStack()
        self.streams = {e: [] for e in self.ENGS}
        self.count = {e: 0 for e in self.ENGS}
        self.waited = {e: {} for e in self.ENGS}
        self.res = {}
        self.dma_sem = {}
        self.semkeys = set()
        self.n_ops = 0
        self.limit = int(os.environ.get("PROG_LIMIT", "1000000000"))
        self.alias = {}
        self.free_phys = []
        self.n_phys = 0
        self.log = []

    def _r(self, key):
        if key not in self.res:
            self.res[key] = [[], []]
        return self.res[key]

    def _deps(self, eng, reads, writes, skip_same=False):
        ev = {}

        def add(e):
            k, v = e
            if skip_same and k[0] == "eng" and k[1] == eng:
                return
            if ev.get(k, 0) < v:
                ev[k] = v
        for r in reads:
            for e in self._r(r)[0]:
                add(e)
        for w in writes:
            st = self._r(w)
            for e in st[0]:
                add(e)
            for e in st[1]:
                add(e)
        out = []
        wd = self.waited[eng]
        for k, v in ev.items():
            if wd.get(k, 0) < v:
                wd[k] = v
                out.append((k, v))
        return out

    def _commit(self, event, reads, writes):
        for r in reads:
            if r in writes:
                continue
            st = self._r(r)
            st[1] = [e for e in st[1] if e[0] != event[0]] + [event]
        for w in writes:
            st = self._r(w)
            st[0] = [e for e in st[0] if e[0] != event[0]] + [event]
            st[1] = []

    def op(self, eng, fn, reads=(), writes=(), skip_same=False):
        if self.n_ops >= self.limit:
            return
        self.log.append((eng, "op", tuple(reads), tuple(writes)))
        waits = self._deps(eng, reads, writes, skip_same)
        self.count[eng] += 1
        n = self.count[eng]
        ep = (n - 1) // EPOCH
        key = ("eng", eng, ep)
        val = n - ep * EPOCH
        self.semkeys.add(key)
        for k, _ in waits:
            self.semkeys.add(k)
        self.streams[eng].append((waits, fn, key, 1))
        self._commit((key, val), reads, writes)
        self.n_ops += 1

    def _phys(self, name):
        if name not in self.alias:
            if self.free_phys:
                self.free_phys.sort(key=lambda p: self.dma_sem.get(p, 0))
                self.alias[name] = self.free_phys.pop(0)
            else:
                self.alias[name] = "phys%d" % self.n_phys
                self.n_phys += 1
        return self.alias[name]

    def dma(self, q, out, in_, reads=(), writes=(), group=None, **kw):
        if self.n_ops >= self.limit:
            return
        self.log.append((q, "dma", tuple(reads), tuple(writes)))
        waits = self._deps(q, reads, writes)
        g = self._phys(group if group is not None else writes[0])
        key = ("dma", g)
        self.dma_sem[g] = self.dma_sem.get(g, 0) + 16
        val = self.dma_sem[g]
        self.semkeys.add(key)
        for k, _ in waits:
            self.semkeys.add(k)
        fn = (lambda e, out=out, in_=in_, kw=kw: e.dma_start(out=out, in_=in_, **kw))
        self.streams[q].append((waits, fn, key, 16))
        self._commit((key, val), reads, writes)
        self.n_ops += 1

    def dma_fn(self, q, fn, reads=(), writes=(), group=None):
        if self.n_ops >= self.limit:
            return
        self.log.append((q, "dmafn", tuple(reads), tuple(writes)))
        waits = self._deps(q, reads, writes)
        g = self._phys(group if group is not None else writes[0])
        key = ("dma", g)
        self.dma_sem[g] = self.dma_sem.get(g, 0) + 16
        val = self.dma_sem[g]
        self.semkeys.add(key)
        for k, _ in waits:
            self.semkeys.add(k)
        self.streams[q].append((waits, fn, key, 16))
        self._commit((key, val), reads, writes)
        self.n_ops += 1

    def barrier(self):
        if os.environ.get("PROG_VERBOSE"):
            print("barrier at op", self.n_ops, flush=True)
        evs = []
        for e in self.ENGS:
            n = self.count[e]
            if n > 0:
                ep = (n - 1) // EPOCH
                evs.append((("eng", e, ep), n - ep * EPOCH))
        for g, v in self.dma_sem.items():
            evs.append((("dma", g), v))
        for e in self.ENGS:
            waits = []
            for k, v in evs:
                if k[0] == "eng" and k[1] == e:
                    continue
                if self.waited[e].get(k, 0) < v:
                    self.waited[e][k] = v
                    waits.append((k, v))
                    self.semkeys.add(k)
            self.streams[e].append((waits, None, None, 0))
        self.free_phys = sorted(set(self.free_phys) | set(self.alias.values()))
        self.alias = {}

    def emit(self):
        nc = self.nc
        sems = {}
        for i, k in enumerate(sorted(self.semkeys, key=str)):
            sems[k] = self.stack.enter_context(nc.semaphore("s%d" % i))
        self.nsems = len(sems)
        streams = self.streams

        def run(e, lst):
            for waits, fn, key, inc in lst:
                for k, v in waits:
                    e.wait_ge(sems[k], v)
                if fn is not None:
                    fn(e).then_inc(sems[key], inc)

        with nc.Block() as block:
            @block.sync
            def _(e):
                run(e, streams["sync"])

            @block.scalar
            def _(e):
                run(e, streams["scalar"])

            @block.vector
            def _(e):
                run(e, streams["vector"])

            @block.gpsimd
            def _(e):
                run(e, streams["gpsimd"])

            @block.tensor
            def _(e):
                run(e, streams["tensor"])
        self.stack.close()


class Carver:
    def __init__(self, big, nwords):
        self.big = big
        self.n = nwords
        self.off = 0

    def reset(self, off=0):
        self.off = off

    def f32(self, nwords, shape=None):
        a = self.big[:, self.off:self.off + nwords]
        self.off += (nwords + 7) // 8 * 8
        assert self.off <= self.n, ("SBUF overflow", self.off, self.n)
        return a

    def bf16(self, nelem):
        assert nelem % 2 == 0
        return self.f32(nelem // 2).bitcast(BF16)


def make_consts():
    c = np.zeros((128, 6, 128), np.float32)
    p = np.arange(128)[:, None]
    f = np.arange(128)[None, :]
    c[:, 0, :] = (p <= f)
    c[:, 1, :] = np.where(p > f, 0.0, -1e30)
    c[:, 2, :] = np.where(f > p, 0.0, -1e30)
    c[:, 3, :] = np.where(f >= p, 0.0, -1e30)
    c[:, 4, :] = 1.0
    c[:, 5, :] = (p == f)
    return c


def stage_gdn(P, nc, cv, banks, U, OA, consts_d, conv_w, a_log, dt_bias, gnorm_g, sconv, sgdn,
              o_p_gdn, o_s_gdn, idb, epsb, hq_list, n_ptiles):
    from concourse.ap import AP
    SEQ_ = n_ptiles * 128
    NT = n_ptiles + 1
    IW = U.shape[1]
    V = lambda fn, r, w: P.op("vector", fn, r, w)
    S = lambda fn, r, w: P.op("scalar", fn, r, w)
    G = lambda fn, r, w: P.op("gpsimd", fn, r, w)
    T = lambda fn, r, w: P.op("tensor", fn, r, w, skip_same=True)

    cst = cv.f32(6 * 128).rearrange("p (a b) -> p a b", a=6)
    tri, maskL, maskU, maskUE, ones_f, idf = (cst[:, k, :] for k in range(6))
    oneb = cv.f32(1)
    ba = cv.f32(NT * 64).rearrange("p (i c) -> p i c", i=NT)
    def gate_buf():
        return cv.f32(NT * 32).rearrange("p (i c) -> p i c", i=NT)
    lnb, beta, g_all, gc_all, a_all, ea_all, ngc_all, eg_all = (gate_buf() for _ in range(8))
    nA = cv.f32(32)
    dtb = cv.f32(32)
    gng = cv.f32(128)
    Sp = cv.f32(32 * 128).rearrange("p (h e) -> p h e", h=32)
    Ss = cv.f32(32 * 128).rearrange("p (h e) -> p h e", h=32)
    Sp_bf = cv.bf16(32 * 128).rearrange("p (h e) -> p h e", h=32)
    Ss_bf = cv.bf16(32 * 128).rearrange("p (h e) -> p h e", h=32)
    Wc = cv.f32(4 * 512).rearrange("p (j c) -> p j c", j=4)
    X = [cv.f32(4 * 512).rearrange("p (j c) -> p j c", j=4) for _ in range(2)]
    Z = [cv.f32(256) for _ in range(2)]
    prod = cv.f32(4 * 512).rearrange("p (j c) -> p j c", j=4)
    conv = cv.f32(512)
    A = cv.f32(512)
    zs = cv.f32(256)
    sq2 = cv.f32(256)
    ss2 = cv.f32(2)
    rq = cv.f32(2)
    qn = cv.f32(128)
    kn = cv.f32(128)
    qn_bf = cv.bf16(128)
    kn_bf = cv.bf16(128)
    kT = cv.bf16(128)
    qT = cv.bf16(128)
    Dg = cv.f32(128)
    Da = cv.f32(128)
    arg = cv.f32(128)
    argT = cv.f32(128)
    argG = cv.f32(128)
    E1 = cv.f32(128)
    E1T = cv.f32(128)
    GT = cv.f32(128)
    XX = [cv.f32(256).rearrange("p (a b) -> p a b", a=2) for _ in range(2)]
    R = [cv.f32(256) for _ in range(2)]
    Rf = cv.f32(256)
    w_bf = cv.bf16(128)
    wT = cv.bf16(128)
    vnew_bf = cv.bf16(128)
    ekd = cv.f32(1)
    gl = cv.f32(1)
    kdec_bf = cv.bf16(128)
    attnT = cv.bf16(128)
    qg_bf = cv.bf16(128)
    qgT = cv.bf16(128)
    o_sb = cv.f32(128)
    osq = cv.f32(128)
    oss = cv.f32(1)
    ors = cv.f32(1)
    ot = cv.f32(128)
    oa = [cv.f32(256) for _ in range(2)]

    bT, bK, bM, bX, bR, bV, bS, bO = banks
    bTb = bT[:].bitcast(BF16)

    P.dma("sync", cst, consts_d, writes=["cst"])
    V(lambda e: e.memset(oneb, 1.0), [], ["oneb"])
    G(lambda e: e.memset(ba, 0.0), [], ["ba"])
    P.dma("sync", ba[:, 0:n_ptiles, :], U[0:SEQ_, C_B:C_B + 64].rearrange("(i p) c -> p i c", p=128),
          reads=["U"], writes=["ba"])
    P.dma("sync", ba[:DEC_SEQ, n_ptiles, :], U[SEQ_:SEQ_ + DEC_SEQ, C_B:C_B + 64], reads=["U"], writes=["ba"])
    P.dma("sync", nA, a_log.broadcast_to([128, 32]), writes=["nA"])
    P.dma("sync", dtb, dt_bias.broadcast_to([128, 32]), writes=["dtb"])
    P.dma("sync", gng, gnorm_g.broadcast_to([128, 128]), writes=["gng"])
    for q4 in range(4):
        P.dma("sync", Ss[:, q4 * 8:(q4 + 1) * 8, :], sgdn[q4 * 8:(q4 + 1) * 8].rearrange("h d e -> d h e"), writes=["Ss"])
    V(lambda e: e.memset(Sp, 0.0), [], ["Sp"])
    V(lambda e: e.memset(Sp_bf, 0.0), [], ["Sp_bf"])
    S(lambda e: e.copy(out=Ss_bf, in_=Ss), ["Ss"], ["Ss_bf"])
    S(lambda e: e.activation(out=nA, in_=nA, func=AF.Exp), ["nA"], ["nA"])
    V(lambda e: e.tensor_scalar(out=nA, in0=nA, scalar1=-1.0, scalar2=None, op0=ALU.mult), ["nA"], ["nA"])
    S(lambda e: e.activation(out=lnb, in_=ba[:, :, 0:32], func=AF.Exp, scale=-1.0), ["ba"], ["lnb"])
    S(lambda e: e.activation(out=lnb, in_=lnb, func=AF.Ln, bias=oneb), ["lnb", "oneb"], ["lnb"])
    V(lambda e: e.tensor_scalar(out=lnb, in0=lnb, scalar1=-1.0, scalar2=None, op0=ALU.mult), ["lnb"], ["lnb"])
    S(lambda e: e.activation(out=beta, in_=lnb, func=AF.Exp), ["lnb"], ["beta"])
    for i in range(NT):
        V(lambda e, i=i: e.tensor_tensor(out=g_all[:, i, :], in0=ba[:, i, 32:64], in1=dtb, op=ALU.add),
          ["ba", "dtb"], ["g_all"])
    t1, t2, t3, t4 = a_all, ea_all, ngc_all, eg_all
    V(lambda e: e.tensor_scalar(out=t2, in0=g_all, scalar1=-1.0, scalar2=None, op0=ALU.mult), ["g_all"], ["ea_all"])
    V(lambda e: e.tensor_tensor(out=t1, in0=g_all, in1=t2, op=ALU.max), ["g_all", "ea_all"], ["a_all"])
    S(lambda e: e.activation(out=t1, in_=t1, func=AF.Exp, scale=-1.0), ["a_all"], ["a_all"])
    V(lambda e: e.tensor_scalar(out=t2, in0=t1, scalar1=2.0, scalar2=None, op0=ALU.add), ["a_all"], ["ea_all"])
    V(lambda e: e.reciprocal(out=t2, in_=t2), ["ea_all"], ["ea_all"])
    V(lambda e: e.tensor_tensor(out=t2, in0=t2, in1=t1, op=ALU.mult), ["ea_all", "a_all"], ["ea_all"])
    V(lambda e: e.tensor_tensor(out=t3, in0=t2, in1=t2, op=ALU.mult), ["ea_all"], ["ngc_all"])
    V(lambda e: e.tensor_scalar(out=t4, in0=t3, scalar1=1.0 / 13, scalar2=None, op0=ALU.mult), ["ngc_all"], ["eg_all"])
    for cc in (1.0 / 11, 1.0 / 9, 1.0 / 7, 1.0 / 5, 1.0 / 3):
        V(lambda e, cc=cc: e.scalar_tensor_tensor(out=t4, in0=t4, scalar=cc, in1=t3, op0=ALU.add, op1=ALU.mult),
          ["eg_all", "ngc_all"], ["eg_all"])
    V(lambda e: e.scalar_tensor_tensor(out=t4, in0=t4, scalar=1.0, in1=t2, op0=ALU.add, op1=ALU.mult),
      ["eg_all", "ea_all"], ["eg_all"])
    V(lambda e: e.tensor_scalar(out=g_all, in0=g_all, scalar1=0.0, scalar2=None, op0=ALU.max), ["g_all"], ["g_all"])
    V(lambda e: e.scalar_tensor_tensor(out=g_all, in0=t4, scalar=2.0, in1=g_all, op0=ALU.mult, op1=ALU.add),
      ["eg_all", "g_all"], ["g_all"])
    for i in range(NT):
        V(lambda e, i=i: e.tensor_tensor(out=g_all[:, i, :], in0=g_all[:, i, :], in1=nA, op=ALU.mult),
          ["g_all", "nA"], ["g_all"])
    for i in range(NT):
        r = 128 if i < n_ptiles else DEC_SEQ
        T(lambda e, i=i, r=r: e.matmul(bM[:r, 0:32], lhsT=tri[:r, :r], rhs=g_all[:r, i, :], start=True, stop=True),
          ["cst", "g_all"], ["bM"])
        V(lambda e, i=i, r=r: e.tensor_copy(out=gc_all[:r, i, :], in_=bM[:r, 0:32]), ["bM"], ["gc_all"])
    for i in range(NT):
        r = 128 if i < n_ptiles else DEC_SEQ
        V(lambda e, i=i, r=r: e.tensor_tensor(out=a_all[:r, i, :], in0=lnb[:r, i, :], in1=gc_all[:r, i, :], op=ALU.add),
          ["lnb", "gc_all"], ["a_all"])
        V(lambda e, i=i, r=r: e.tensor_scalar(out=ngc_all[:r, i, :], in0=gc_all[:r, i, :], scalar1=-1.0, scalar2=None,
                                                op0=ALU.mult), ["gc_all"], ["ngc_all"])
        S(lambda e, i=i, r=r: e.activation(out=ea_all[:r, i, :], in_=a_all[:r, i, :], func=AF.Exp), ["a_all"], ["ea_all"])
        S(lambda e, i=i, r=r: e.activation(out=eg_all[:r, i, :], in_=gc_all[:r, i, :], func=AF.Exp), ["gc_all"], ["eg_all"])

    cnt = 0
    for hq in hq_list:
        segs = ((C_Q + hq * 128, 0, 128), (C_K + hq * 128, 128, 128), (C_V + hq * 256, 256, 256))
        for col, s0, w in segs:
            P.dma("sync", Wc[:, :, s0:s0 + w], AP(conv_w.tensor, col, [[0, 128], [8192, 4], [1, w]]), writes=["Wc"])
        for i in range(NT):
            samp = i == n_ptiles
            r = DEC_SEQ if samp else 128
            t0 = i * 128
            Xi = X[cnt % 2]
            Xk = "X%d" % (cnt % 2)
            Zi = Z[cnt % 2]
            Zk = "Z%d" % (cnt % 2)
            oai = oa[cnt % 2]
            oak = "oa%d" % (cnt % 2)
            cnt += 1
            Sx, Sx_bf, Sk, Sbk = (Ss, Ss_bf, "Ss", "Ss_bf") if samp else (Sp, Sp_bf, "Sp", "Sp_bf")
            if i == 0:
                G(lambda e, Xi=Xi: e.memset(Xi, 0.0), [], [Xk])
                for col, s0, w in segs:
                    for j in range(4):
                        sh = 3 - j
                        P.dma("sync", Xi[sh:128, j, s0:s0 + w], U[0:128 - sh, col:col + w], reads=["U"], writes=[Xk])
            elif samp:
                for col, s0, w in segs:
                    for j in range(4):
                        sh = 3 - j
                        if sh > 0:
                            P.dma("sync", Xi[0:sh, j, s0:s0 + w], sconv[j:3, col:col + w], writes=[Xk])
                        P.dma("sync", Xi[sh:DEC_SEQ, j, s0:s0 + w], U[t0:t0 + DEC_SEQ - sh, col:col + w],
                              reads=["U"], writes=[Xk])
            else:
                for col, s0, w in segs:
                    P.dma("sync", Xi[:, :, s0:s0 + w],
                          AP(U.tensor, (t0 - 3) * IW + col, [[IW, 128], [IW, 4], [1, w]]), reads=["U"], writes=[Xk])
            P.dma("sync", Zi[:r, :], U[t0:t0 + r, C_Z + hq * 256:C_Z + hq * 256 + 256], reads=["U"], writes=[Zk])
            G(lambda e, Xi=Xi, r=r: e.tensor_tensor(out=prod[:r], in0=Xi[:r], in1=Wc[:r], op=ALU.mult), [Xk, "Wc"], ["prod"])
            V(lambda e, r=r: e.tensor_reduce(out=conv[:r, :], in_=prod[:r].rearrange("p j c -> p c j"), axis=AX.X, op=ALU.add),
              ["prod"], ["conv"])
            S(lambda e, r=r: e.activation(out=A[:r, :], in_=conv[:r, :], func=AF.Silu), ["conv"], ["A"])
            S(lambda e, r=r, Zi=Zi: e.activation(out=zs[:r, :], in_=Zi[:r, :], func=AF.Silu), [Zk], ["zs"])
            V(lambda e, r=r: e.tensor_tensor(out=sq2[:r, :], in0=A[:r, 0:256], in1=A[:r, 0:256], op=ALU.mult), ["A"], ["sq2"])
            V(lambda e, r=r: e.tensor_reduce(out=ss2[:r, :], in_=sq2[:r, :].rearrange("p (a b) -> p a b", a=2), axis=AX.X,
                                              op=ALU.add), ["sq2"], ["ss2"])
            S(lambda e, r=r: e.activation(out=rq[:r, :], in_=ss2[:r, :], func=AF.Ln, bias=epsb[:r, :]), ["ss2", "epsb"], ["rq"])
            S(lambda e, r=r: e.activation(out=rq[:r, :], in_=rq[:r, :], func=AF.Exp, scale=-0.5), ["rq"], ["rq"])
            G(lambda e, r=r: e.tensor_scalar(out=qn[:r, :], in0=A[:r, 0:128], scalar1=rq[:r, 0:1], scalar2=128 ** -0.5,
                                              op0=ALU.mult, op1=ALU.mult), ["A", "rq"], ["qn"])
            G(lambda e, r=r: e.tensor_scalar(out=kn[:r, :], in0=A[:r, 128:256], scalar1=rq[:r, 1:2], scalar2=None,
                                              op0=ALU.mult), ["A", "rq"], ["kn"])
            G(lambda e, r=r: e.tensor_copy(out=qn_bf[:r, :], in_=qn[:r, :]), ["qn"], ["qn_bf"])
            G(lambda e, r=r: e.tensor_copy(out=kn_bf[:r, :], in_=kn[:r, :]), ["kn"], ["kn_bf"])
            T(lambda e, r=r: e.transpose(out=bTb[:, 0:r], in_=kn_bf[:r, :], identity=idb[:r, :r]), ["kn_bf", "idb"], ["bT"])
            T(lambda e, r=r: e.transpose(out=bTb[:, 128:128 + r], in_=qn_bf[:r, :], identity=idb[:r, :r]), ["qn_bf", "idb"], ["bT"])
            V(lambda e, r=r: e.tensor_copy(out=kT[:, :r], in_=bTb[:, 0:r]), ["bT"], ["kT"])
            V(lambda e, r=r: e.tensor_copy(out=qT[:, :r], in_=bTb[:, 128:128 + r]), ["bT"], ["qT"])
            T(lambda e, r=r: e.matmul(bK[:r, 0:r], lhsT=kT[:, :r], rhs=kT[:, :r], start=True, stop=True), ["kT"], ["bK"])
            T(lambda e, r=r: e.matmul(bK[:r, 128:128 + r], lhsT=kT[:, :r], rhs=qT[:, :r], start=True, stop=True),
              ["kT", "qT"], ["bK"])
            for hv in range(2):
                h = 2 * hq + hv
                gcc, ac, eac, ngc, bec, egc = (t[:r, i, h:h + 1] for t in (gc_all, a_all, ea_all, ngc_all, beta, eg_all))
                gk = ["gc_all", "a_all", "ea_all", "ngc_all", "beta", "eg_all"]
                G(lambda e, r=r, gcc=gcc: e.tensor_scalar(out=Dg[:r, :r], in0=idf[:r, :r], scalar1=gcc, scalar2=None,
                                                           op0=ALU.mult), ["cst"] + gk, ["Dg"])
                G(lambda e, r=r, ac=ac: e.tensor_scalar(out=Da[:r, :r], in0=idf[:r, :r], scalar1=ac, scalar2=None,
                                                         op0=ALU.mult), ["cst"] + gk, ["Da"])
                T(lambda e, r=r: e.matmul(bM[:, 0:r], lhsT=ones_f[:r, :], rhs=Dg[:r, :r], start=True, stop=True),
                  ["cst", "Dg"], ["bM"])
                T(lambda e, r=r: e.matmul(bM[:, 128:128 + r], lhsT=ones_f[:r, :], rhs=Da[:r, :r], start=True, stop=True),
                  ["cst", "Da"], ["bM"])
                V(lambda e, r=r: e.scalar_tensor_tensor(out=arg[:r, :r], in0=bM[:r, 0:r], scalar=-1.0, in1=maskL[:r, :r],
                                                         op0=ALU.mult, op1=ALU.add), ["bM", "cst"], ["arg"])
                S(lambda e, r=r, ac=ac: e.activation(out=E1[:r, :r], in_=arg[:r, :r], func=AF.Exp, bias=ac),
                  ["arg"] + gk, ["E1"])
                V(lambda e, r=r: e.scalar_tensor_tensor(out=XX[0][:r, 0, :r], in0=E1[:r, :r], scalar=-1.0, in1=bK[:r, 0:r],
                                                         op0=ALU.mult, op1=ALU.mult), ["E1", "bK"], ["XX0"])
                V(lambda e, r=r: e.tensor_tensor(out=argT[:r, :r], in0=bM[:r, 128:128 + r], in1=maskU[:r, :r], op=ALU.add),
                  ["bM", "cst"], ["argT"])
                S(lambda e, r=r, ngc=ngc: e.activation(out=E1T[:r, :r], in_=argT[:r, :r], func=AF.Exp, bias=ngc),
                  ["argT"] + gk, ["E1T"])
                V(lambda e, r=r: e.scalar_tensor_tensor(out=XX[0][:r, 1, :r], in0=E1T[:r, :r], scalar=-1.0, in1=bK[:r, 0:r],
                                                         op0=ALU.mult, op1=ALU.mult), ["E1T", "bK"], ["XX0"])
                V(lambda e, r=r: e.tensor_tensor(out=argG[:r, :r], in0=bM[:r, 0:r], in1=maskUE[:r, :r], op=ALU.add),
                  ["bM", "cst"], ["argG"])
                S(lambda e, r=r, ngc=ngc: e.activation(out=GT[:r, :r], in_=argG[:r, :r], func=AF.Exp, bias=ngc),
                  ["argG"] + gk, ["GT"])
                V(lambda e, r=r: e.tensor_tensor(out=attnT[:r, :r], in0=GT[:r, :r], in1=bK[:r, 128:128 + r], op=ALU.mult),
                  ["GT", "bK"], ["attnT"])
                G(lambda e, r=r, hv=hv, bec=bec: e.tensor_scalar(out=R[0][:r, 0:128], in0=A[:r, 256 + hv * 128:384 + hv * 128],
                                                                  scalar1=bec, scalar2=None, op0=ALU.mult),
                  ["A"] + gk, ["R0"])
                G(lambda e, r=r, eac=eac: e.tensor_scalar(out=R[0][:r, 128:256], in0=kn[:r, :], scalar1=eac, scalar2=None,
                                                           op0=ALU.mult), ["kn"] + gk, ["R0"])
                S(lambda e, r=r, ngc=ngc: e.activation(out=ekd[:r, :], in_=bM[:r, r - 1:r], func=AF.Exp, bias=ngc),
                  ["bM"] + gk, ["ekd"])
                S(lambda e, r=r: e.activation(out=gl, in_=bM[:, r - 1:r], func=AF.Exp), ["bM"], ["gl"])
                G(lambda e, r=r: e.tensor_scalar(out=kdec_bf[:r, :], in0=kn[:r, :], scalar1=ekd[:r, 0:1], scalar2=None,
                                                  op0=ALU.mult), ["kn", "ekd"], ["kdec_bf"])
                G(lambda e, r=r, egc=egc: e.tensor_scalar(out=qg_bf[:r, :], in0=qn[:r, :], scalar1=egc, scalar2=None,
                                                           op0=ALU.mult), ["qn"] + gk, ["qg_bf"])
                T(lambda e, r=r: e.transpose(out=bTb[:, 256:256 + r], in_=qg_bf[:r, :], identity=idb[:r, :r]),
                  ["qg_bf", "idb"], ["bT"])
                V(lambda e, r=r: e.tensor_copy(out=qgT[:, :r], in_=bTb[:, 256:256 + r]), ["bT"], ["qgT"])
                for k in range(7):
                    cur, nxt = k % 2, (k + 1) % 2
                    T(lambda e, r=r, cur=cur: e.matmul(bR[:r, 0:256], lhsT=idf[:r, :r], rhs=R[cur][:r, :], start=True, stop=False),
                      ["cst", "R%d" % cur], ["bR"])
                    T(lambda e, r=r, cur=cur: e.matmul(bR[:r, 0:256], lhsT=XX[cur][:r, 1, :r], rhs=R[cur][:r, :], start=False, stop=True),
                      ["XX%d" % cur, "R%d" % cur], ["bR"])
                    if k < 6:
                        S(lambda e, r=r, nxt=nxt: e.copy(out=R[nxt][:r, :], in_=bR[:r, 0:256]), ["bR"], ["R%d" % nxt])
                        T(lambda e, r=r, cur=cur: e.matmul(bX[:r, 0:r], lhsT=XX[cur][:r, 1, :r], rhs=XX[cur][:r, 0, :r], start=True, stop=True),
                          ["XX%d" % cur], ["bX"])
                        T(lambda e, r=r, cur=cur: e.matmul(bX[:r, 128:128 + r], lhsT=XX[cur][:r, 0, :r], rhs=XX[cur][:r, 1, :r], start=True, stop=True),
                          ["XX%d" % cur], ["bX"])
                        V(lambda e, r=r, nxt=nxt: e.tensor_copy(out=XX[nxt][:r, :, :r],
                                                                 in_=bX[:r, 0:256].rearrange("p (a b) -> p a b", a=2)[:, :, 0:r]),
                          ["bX"], ["XX%d" % nxt])
                    else:
                        V(lambda e, r=r: e.tensor_copy(out=Rf[:r, :], in_=bR[:r, 0:256]), ["bR"], ["Rf"])
                        V(lambda e, r=r: e.tensor_copy(out=w_bf[:r, :], in_=bR[:r, 128:256]), ["bR"], ["w_bf"])
                T(lambda e, r=r: e.transpose(out=bTb[:, 384:384 + r], in_=w_bf[:r, :], identity=idb[:r, :r]), ["w_bf", "idb"], ["bT"])
                V(lambda e, r=r: e.tensor_copy(out=wT[:, :r], in_=bTb[:, 384:384 + r]), ["bT"], ["wT"])
                T(lambda e, r=r, h=h, Sx_bf=Sx_bf: e.matmul(bV[:r, 0:128], lhsT=wT[:, :r], rhs=Sx_bf[:, h, :], start=True, stop=True),
                  ["wT", Sbk], ["bV"])
                V(lambda e, r=r: e.tensor_tensor(out=vnew_bf[:r, :], in0=Rf[:r, 0:128], in1=bV[:r, 0:128], op=ALU.subtract),
                  ["Rf", "bV"], ["vnew_bf"])
                T(lambda e, r=r, h=h, Sx_bf=Sx_bf: e.matmul(bO[:r, 0:128], lhsT=qgT[:, :r], rhs=Sx_bf[:, h, :], start=True, stop=False),
                  ["qgT", Sbk], ["bO"])
                T(lambda e, r=r: e.matmul(bO[:r, 0:128], lhsT=attnT[:r, :r], rhs=vnew_bf[:r, :], start=False, stop=True),
                  ["attnT", "vnew_bf"], ["bO"])
                T(lambda e, r=r: e.matmul(bS[:, 0:128], lhsT=kdec_bf[:r, :], rhs=vnew_bf[:r, :], start=True, stop=True),
                  ["kdec_bf", "vnew_bf"], ["bS"])
                V(lambda e, h=h, Sx=Sx: e.scalar_tensor_tensor(out=Sx[:, h, :], in0=Sx[:, h, :], scalar=gl[:, 0:1], in1=bS[:, 0:128],
                                                               op0=ALU.mult, op1=ALU.add), [Sk, "gl", "bS"], [Sk])
                S(lambda e, h=h, Sx=Sx, Sx_bf=Sx_bf: e.copy(out=Sx_bf[:, h, :], in_=Sx[:, h, :]), [Sk], [Sbk])
                S(lambda e, r=r: e.copy(out=o_sb[:r, :], in_=bO[:r, 0:128]), ["bO"], ["o_sb"])
                V(lambda e, r=r: e.tensor_tensor(out=osq[:r, :], in0=o_sb[:r, :], in1=o_sb[:r, :], op=ALU.mult), ["o_sb"], ["osq"])
                V(lambda e, r=r: e.tensor_reduce(out=oss[:r, :], in_=osq[:r, :], axis=AX.X, op=ALU.add), ["osq"], ["oss"])
                S(lambda e, r=r: e.activation(out=ors[:r, :], in_=oss[:r, :], func=AF.Ln, scale=1.0 / 128, bias=epsb[:r, :]),
                  ["oss", "epsb"], ["ors"])
                S(lambda e, r=r: e.activation(out=ors[:r, :], in_=ors[:r, :], func=AF.Exp, scale=-0.5), ["ors"], ["ors"])
                V(lambda e, r=r: e.scalar_tensor_tensor(out=ot[:r, :], in0=o_sb[:r, :], scalar=ors[:r, 0:1], in1=gng[:r, :],
                                                         op0=ALU.mult, op1=ALU.mult), ["o_sb", "ors", "gng"], ["ot"])
                G(lambda e, r=r, hv=hv, oai=oai: e.tensor_tensor(out=oai[:r, hv * 128:(hv + 1) * 128], in0=ot[:r, :],
                                                                  in1=zs[:r, hv * 128:(hv + 1) * 128], op=ALU.mult),
                  ["ot", "zs"], [oak])
            P.dma("scalar", OA[t0:t0 + r, hq * 256:(hq + 1) * 256], oai[:r, :], reads=[oak], writes=["OA"], group=oak)
    for q4 in range(4):
        P.dma("scalar", o_p_gdn[q4 * 1024:(q4 + 1) * 1024, :].rearrange("(h d) e -> d h e", h=8), Sp[:, q4 * 8:(q4 + 1) * 8, :],
              reads=["Sp"], writes=["o_p_gdn"])
        P.dma("scalar", o_s_gdn[q4 * 1024:(q4 + 1) * 1024, :].rearrange("(h d) e -> d h e", h=8), Ss[:, q4 * 8:(q4 + 1) * 8, :],
              reads=["Ss"], writes=["o_s_gdn"])


def stage_linear(P, cv, banks, idb, src, srckey, K, W, N, ntok, groups, CW, epi, nm):
    KC = K // 128
    rows = lambda i: min(128, ntok - i * 128)
    gmax = max((len(g) - 1) * 128 + (rows(g[-1]) + 7) // 8 * 8 for g in groups)
    hT = cv.bf16(KC * gmax).rearrange("p (k t) -> p k t", k=KC)
    wf = [cv.f32(KC * CW).rearrange("p (k n) -> p k n", k=KC) for _ in range(2)]
    wb = [cv.bf16(KC * CW).rearrange("p (k n) -> p k n", k=KC) for _ in range(2)]
    hb = cv.bf16(K)
    xt = [w.rearrange("p k n -> p (k n)")[:, 0:K] for w in wf]
    k_hT, k_hb = nm + "hT", nm + "hb"
    k_wf = [nm + "wf0", nm + "wf1"]
    k_wb = [nm + "wb0", nm + "wb1"]
    NCB = N // CW
    ev = 0
    ld = 0
    for g in groups:
        for li, i in enumerate(g):
            r = rows(i)
            xi, xk = xt[ld % 2], k_wf[ld % 2]
            ld += 1
            P.dma("sync", xi[:r, :], src[i * 128:i * 128 + r, 0:K], reads=[srckey], writes=[xk])
            P.op("gpsimd", lambda e, xi=xi, r=r: e.tensor_copy(out=hb[:r, :], in_=xi[:r, :]), reads=[xk], writes=[k_hb])
            for k4 in range(KC // 4):
                bk = banks[k4 % 2]
                bkey = "bank%d" % (k4 % 2)
                pT = bk[:].bitcast(BF16)
                for j in range(4):
                    k = k4 * 4 + j
                    P.op("tensor", lambda e, k=k, j=j, r=r, pT=pT: e.transpose(
                        out=pT[:, j * 128:j * 128 + r], in_=hb[:r, k * 128:(k + 1) * 128], identity=idb[:r, :r]),
                        reads=[k_hb, "idb"], writes=[bkey], skip_same=True)
                P.op("vector", lambda e, k4=k4, r=r, pT=pT, li=li: e.tensor_copy(
                    out=hT[:, k4 * 4:(k4 + 1) * 4, li * 128:li * 128 + r],
                    in_=pT[:, 0:512].rearrange("p (j t) -> p j t", j=4)[:, :, 0:r]),
                    reads=[bkey], writes=[k_hT])
        for cb in range(NCB):
            c0 = cb * CW
            wfi, wbi = wf[ld % 2], wb[ld % 2]
            wfk, wbk = k_wf[ld % 2], k_wb[ld % 2]
            ld += 1
            wsrc = W[:, c0:c0 + CW].rearrange("(k p) n -> p k n", p=128)
            for k8 in range(KC // 8):
                P.dma("sync", wfi[:, k8 * 8:(k8 + 1) * 8, :], wsrc[:, k8 * 8:(k8 + 1) * 8, :], writes=[wfk])
            P.op("gpsimd", lambda e, wfi=wfi, wbi=wbi: e.tensor_copy(out=wbi, in_=wfi), reads=[wfk], writes=[wbk])
            for li, i in enumerate(g):
                r = rows(i)
                bi = 2 + (ev % 4)
                bk, bkey = banks[bi], "bank%d" % bi
                for k in range(KC):
                    P.op("tensor", lambda e, k=k, li=li, r=r, bk=bk, wbi=wbi: e.matmul(
                        bk[:r, 0:CW], lhsT=hT[:, k, li * 128:li * 128 + r], rhs=wbi[:, k, :],
                        start=(k == 0), stop=(k == KC - 1)),
                        reads=[k_hT, wbk], writes=[bkey], skip_same=True)
                epi(i, r, c0, CW, bk, bkey, ev)
                ev += 1


def make_epi(P, cv, nm, CW, dstf, gatef=None, addf=None):
    ost = [cv.f32(CW) for _ in range(4)]
    gt = [cv.f32(CW) for _ in range(4)] if gatef else None
    at = [cv.f32(CW) for _ in range(4)] if addf else None

    def epi(i, r, c0, cw, bk, bkey, ev):
        s = ev % 4
        ok = "%sost%d" % (nm, s)
        if gatef:
            gap, gkey = gatef(i, r, c0, cw)
            gk = "%sgt%d" % (nm, s)
            P.dma("sync", gt[s][:r, :cw], gap, reads=[gkey], writes=[gk])
            P.op("scalar", lambda e, s=s, r=r, cw=cw: e.activation(out=gt[s][:r, :cw], in_=gt[s][:r, :cw], func=AF.Sigmoid),
                 reads=[gk], writes=[gk])
        if addf:
            aap, akey = addf(i, r, c0, cw)
            ak = "%sat%d" % (nm, s)
            P.dma("sync", at[s][:r, :cw], aap, reads=[akey], writes=[ak])
        if gatef:
            P.op("vector", lambda e, s=s, r=r, cw=cw, bk=bk: e.tensor_tensor(out=ost[s][:r, :cw], in0=bk[:r, :cw], in1=gt[s][:r, :cw],
                                                                             op=ALU.mult), reads=[bkey, gk], writes=[ok])
            if addf:
                P.op("gpsimd", lambda e, s=s, r=r, cw=cw: e.tensor_tensor(out=ost[s][:r, :cw], in0=ost[s][:r, :cw], in1=at[s][:r, :cw],
                                                                          op=ALU.add), reads=[ok, ak], writes=[ok])
        else:
            P.op("vector", lambda e, s=s, r=r, cw=cw, bk=bk: e.tensor_tensor(out=ost[s][:r, :cw], in0=bk[:r, :cw], in1=at[s][:r, :cw],
                                                                             op=ALU.add), reads=[bkey, ak], writes=[ok])
        dap, dkey = dstf(i, r, c0, cw)
        P.dma("scalar" if ev % 2 == 0 else "gpsimd", dap, ost[s][:r, :cw], reads=[ok], writes=[dkey], group=ok)
    return epi


def stage_merge(P, cv, banks, idb, base0, U, OA, OB, M1, M2, w_a, w_b, w_o, xsrc, ydst, ntok):
    nt = (ntok + 127) // 128
    alltiles = list(range(nt))
    half = (nt + 1) // 2
    sl = lambda A, key: (lambda i, r, c0, cw: (A[i * 128:i * 128 + r, c0:c0 + cw], key))
    P.barrier()
    cv.reset(base0)
    epi = make_epi(P, cv, "m", 256, sl(M1, "M1"), gatef=lambda i, r, c0, cw: (U[i * 128:i * 128 + r, C_MA + c0:C_MA + c0 + cw], "U"))
    stage_linear(P, cv, banks, idb, OA, "OA", 4096, w_a, D_MODEL, ntok, [alltiles[:half], alltiles[half:]], 256, epi, "m")
    P.barrier()
    cv.reset(base0)
    epi = make_epi(P, cv, "m", 512, sl(M2, "M2"), gatef=lambda i, r, c0, cw: (U[i * 128:i * 128 + r, C_MB + c0:C_MB + c0 + cw], "U"),
                   addf=sl(M1, "M1"))
    stage_linear(P, cv, banks, idb, OB, "OB", 2048, w_b, D_MODEL, ntok, [alltiles], 512, epi, "m")
    P.barrier()
    cv.reset(base0)
    epi = make_epi(P, cv, "m", 512, ydst, addf=xsrc)
    stage_linear(P, cv, banks, idb, M2, "M2", 2048, w_o, D_MODEL, ntok, [alltiles], 512, epi, "m")


NSA_L = 4352
NSA_R0 = 2176
NSAC_W = 3504
NEGM = -30000.0


def t5_bucket_np(rel):
    rel = np.asarray(rel, np.int64)
    n = np.maximum(rel, 0)
    nf = np.maximum(n, 1).astype(np.float32)
    large = 16 + (np.log(nf / np.float32(16)) / np.float32(math.log(8.0)) * np.float32(16)).astype(np.int32)
    return np.where(n < 16, n, np.minimum(large, 31))


def make_nsa_consts(n_ptiles):
    L, R0 = NSA_L, NSA_R0
    seq = n_ptiles * 128
    ncb = seq // 16 - 1
    oh = np.zeros((33, L), np.float32)
    rel = np.arange(L) - R0
    b = np.where(rel < 0, 32, t5_bucket_np(rel))
    oh[b, np.arange(L)] = 1.0
    c = np.zeros((128, NSAC_W), np.float32)
    j = np.arange(32)
    lo = np.clip((64 * j - 32) // 16 + 1, 0, ncb)
    hi = np.clip(-(-(64 * (j + 1)) // 16), 0, ncb)
    n = np.arange(128)[:, None]
    c[:, 0:32] = ((n >= lo[None, :]) & (n < hi[None, :]) & (n < ncb)).astype(np.float32)
    p = np.arange(128)[:, None, None]
    ii = np.arange(16)[None, :, None]
    jj = np.arange(32)[None, None, :]
    qblk = (128 * ii + p) // 64
    valid = jj <= qblk
    forced = (jj == 0) | (jj == qblk) | (jj == qblk - 1)
    km = (valid & ~forced).astype(np.float32)
    fm = np.where(~valid, -1e9, np.where(forced, 1e9, 0.0)).astype(np.float32)
    c[:, 32:544] = km.reshape(128, 512)
    c[:, 544:1056] = fm.reshape(128, 512)
    k = np.arange(128)[:, None]
    q = np.arange(128)[None, :]
    c[:, 1056:1184] = np.where(q < k, 0.0, NEGM)
    t = np.arange(128)[:, None]
    cc = np.arange(8)[None, :]
    mab = (t // 16 == cc).astype(np.float32)
    c[:, 1184:1192] = mab
    c[:, 1192:1200] = mab
    kk = np.arange(2048)[None, :]
    c[0:32, 1200:3248] = (np.arange(32)[:, None] == kk // 64).astype(np.float32)
    pp = np.arange(32)[:, None]
    tt = np.arange(128)[None, :]
    c[0:32, 3248:3376] = (pp == tt % 16).astype(np.float32)
    c[0:32, 3376:3504] = (pp == 16 + tt % 16).astype(np.float32)
    return oh, c


def stage_nsa_prompt(P, nc, cv, banks, U, KSN, ksn_key, KWN, OB, nsac_d, oh_d, TVd, Gd, rel_bias, q_norm_g, k_norm_g,
                     pe_k, w_k, proj_k, pe_v, w_v, proj_v, idf, idb, epsb, n_ptiles):
    from concourse.ap import AP
    NTq = n_ptiles
    SEQ_ = NTq * 128
    NCB_ = SEQ_ // 16 - 1
    L, R0 = NSA_L, NSA_R0
    V = lambda fn, r, w: P.op("vector", fn, r, w)
    S = lambda fn, r, w: P.op("scalar", fn, r, w)
    G = lambda fn, r, w: P.op("gpsimd", fn, r, w)
    T = lambda fn, r, w: P.op("tensor", fn, r, w, skip_same=True)
    b0, b1, bS0, bS1, bOa, bOb, bX, bY = banks
    bS = [bS0, bS1]
    bO = [bOa, bOb]
    bSk = ["bank2", "bank3"]
    bOk = ["bank4", "bank5"]
    bO4 = [banks[4], banks[5], banks[6], banks[7]]
    bO4k = ["bank4", "bank5", "bank6", "bank7"]
    b0bf = b0[:].bitcast(BF16)
    b1bf = b1[:].bitcast(BF16)

    nsac = cv.f32(NSAC_W)
    Mc = nsac[:, 0:32]
    KM = nsac[:, 32:544].rearrange("p (i j) -> p i j", i=16)
    FM = nsac[:, 544:1056].rearrange("p (i j) -> p i j", i=16)
    LT = nsac[:, 1056:1184]
    MAB = nsac[:, 1184:1200]
    Ekf = nsac[:, 1200:3248]
    SelA = nsac[:, 3248:3376]
    SelB = nsac[:, 3376:3504]
    Ek = cv.bf16(2048)
    Bd = cv.bf16(2048)
    Bo = cv.bf16(2048)
    Bw = cv.bf16(2048)
    tb31b = cv.bf16(16)
    tb31f = cv.f32(16)
    tbrow = cv.bf16(2048)
    ones1 = cv.bf16(128)
    onesf = cv.f32(128)
    qgain = cv.f32(128)
    kgain0 = cv.f32(128)
    ksT = cv.bf16(4 * SEQ_).rearrange("p (g t) -> p g t", g=4)
    kwT = cv.bf16(4 * SEQ_).rearrange("p (g t) -> p g t", g=4)
    vse = cv.bf16(NTq * 4 * 132).rearrange("p (i g e) -> p i g e", i=NTq, g=4)
    vwe = cv.bf16(NTq * 4 * 132).rearrange("p (i g e) -> p i g e", i=NTq, g=4)
    kcT = cv.bf16(512).rearrange("p (g n) -> p g n", g=4)
    vce = cv.bf16(4 * 132).rearrange("p (g e) -> p g e", g=4)
    mark = cv.off

    tab = cv.f32(16)
    oh = cv.f32(L)
    tvb = cv.bf16(L)
    stg = [[cv.f32(512) for _ in range(2)] for _ in range(6)]
    sbf = [[cv.bf16(512) for _ in range(2)] for _ in range(4)]
    wk32 = cv.f32(1)
    wv32 = cv.f32(1)
    pek = cv.f32(128)
    pev = cv.f32(128)
    wrep = cv.f32(4)
    pec = cv.f32(2)
    WAB = cv.bf16(32)
    pjf = cv.f32(256)
    pjb = cv.bf16(256)
    ATs = cv.f32(2 * 512).rearrange("p (s g n) -> p s g n", s=2, g=4)
    BTs = cv.f32(2 * 512).rearrange("p (s g n) -> p s g n", s=2, g=4)
    pooled = cv.bf16(2 * 512).rearrange("p (s g n) -> p s g n", s=2, g=4)
    ksq = cv.f32(512)
    kss = cv.f32(4)
    krs = cv.f32(4)
    kcn = cv.f32(512)
    kcnb = cv.bf16(512)

    P.dma("sync", nsac, nsac_d, writes=["nsac"])
    G(lambda e: e.tensor_copy(out=Ek[:32, :], in_=Ekf[:32, :]), ["nsac"], ["Ek"])
    V(lambda e: e.memset(ones1, 1.0), [], ["ones1"])
    V(lambda e: e.memset(onesf, 1.0), [], ["onesf"])
    P.dma("sync", qgain, q_norm_g.broadcast_to([128, 128]), writes=["qgain"])
    P.dma("sync", kgain0, k_norm_g[0:1, :].broadcast_to([128, 128]), writes=["kgain0"])

    V(lambda e: e.memset(tab[:33, :], NEGM), [], ["tab"])
    P.dma("sync", tab[:32, :], rel_bias, writes=["tab"])
    P.dma("sync", oh[:33, :], oh_d, writes=["oh"])
    nch = (L + 511) // 512
    for c in range(nch):
        w = min(512, L - c * 512)
        T(lambda e, c=c, w=w: e.matmul(bX[:16, 0:w], lhsT=tab[:33, 0:16], rhs=oh[:33, c * 512:c * 512 + w], start=True, stop=True),
          ["tab", "oh"], ["bank6"])
        V(lambda e, c=c, w=w: e.tensor_copy(out=tvb[:16, c * 512:c * 512 + w], in_=bX[:16, 0:w]), ["bank6"], ["tvb"])
    P.dma("sync", TVd, tvb[:16, :], reads=["tvb"], writes=["TV"])
    for h in range(16):
        P.dma("sync", Gd[h], TVd[h:h + 1, :].broadcast_to([128, L]), reads=["TV"], writes=["G"])
    for hh in range(2):
        P.dma("sync", Bd[:, hh * 1024:(hh + 1) * 1024].rearrange("p (h q) -> p h q", h=8),
              AP(Gd.tensor, hh * 8 * 128 * L + R0, [[L - 1, 128], [128 * L, 8], [1, 128]]), reads=["G"], writes=["Bd"])
        P.dma("sync", Bo[:, hh * 1024:(hh + 1) * 1024].rearrange("p (h q) -> p h q", h=8),
              AP(Gd.tensor, hh * 8 * 128 * L + R0 + 128, [[L - 1, 128], [128 * L, 8], [1, 128]]), reads=["G"], writes=["Bo"])
    P.dma("sync", tb31b, AP(TVd.tensor, R0 + 200, [[0, 128], [L, 16]]), reads=["TV"], writes=["tb31b"],
          allow_slow_non_contiguous=True)
    V(lambda e: e.tensor_copy(out=tb31f, in_=tb31b), ["tb31b"], ["tb31f"])
    for h in range(16):
        V(lambda e, h=h: e.tensor_scalar(out=Bw[:, h * 128:(h + 1) * 128], in0=LT, scalar1=tb31f[:, h:h + 1], scalar2=None,
                                         op0=ALU.add), ["nsac", "tb31f"], ["Bw"])
    V(lambda e: e.tensor_copy(out=tbrow[0:1, :].rearrange("p (h q) -> p h q", h=16),
                              in_=tb31b[0:1, :].unsqueeze(2).broadcast_to([1, 16, 128])), ["tb31b"], ["tbrow"])

    P.dma("sync", wk32[:32, :], w_k, writes=["wk32"])
    P.dma("sync", wv32[:32, :], w_v, writes=["wv32"])
    P.dma("sync", pek[:32, :], pe_k, writes=["pek"])
    P.dma("sync", pev[:32, :], pe_v, writes=["pev"])
    for s_, (w32, wkey) in enumerate(((wk32, "wk32"), (wv32, "wv32"))):
        T(lambda e, s_=s_, w32=w32: e.matmul(bX[:, 2 * s_:2 * s_ + 1], lhsT=SelA[:32, :], rhs=w32[:32, :], start=True, stop=True),
          ["nsac", wkey], ["bank6"])
        T(lambda e, s_=s_, w32=w32: e.matmul(bX[:, 2 * s_ + 1:2 * s_ + 2], lhsT=SelB[:32, :], rhs=w32[:32, :], start=True, stop=True),
          ["nsac", wkey], ["bank6"])
    T(lambda e: e.matmul(bX[:, 8:9], lhsT=pek[:32, :], rhs=wk32[:32, :], start=True, stop=True), ["pek", "wk32"], ["bank6"])
    T(lambda e: e.matmul(bX[:, 9:10], lhsT=pev[:32, :], rhs=wv32[:32, :], start=True, stop=True), ["pev", "wv32"], ["bank6"])
    V(lambda e: e.tensor_copy(out=wrep, in_=bX[:, 0:4]), ["bank6"], ["wrep"])
    V(lambda e: e.tensor_copy(out=pec, in_=bX[:, 8:10]), ["bank6"], ["pec"])
    for s_ in range(2):
        for ab in range(2):
            V(lambda e, s_=s_, ab=ab: e.tensor_scalar(out=WAB[:, s_ * 16 + ab * 8:s_ * 16 + ab * 8 + 8], in0=MAB[:, ab * 8:ab * 8 + 8],
                                                      scalar1=wrep[:, 2 * s_ + ab:2 * s_ + ab + 1], scalar2=None, op0=ALU.mult),
              ["nsac", "wrep"], ["WAB"])
    P.dma("sync", pjf[:, 0:128], proj_k, writes=["pjf"])
    P.dma("sync", pjf[:, 128:256], proj_v, writes=["pjf"])
    V(lambda e: e.tensor_copy(out=pjb, in_=pjf), ["pjf"], ["pjb"])
    V(lambda e: e.memset(vse, 1.0), [], ["vse"])
    V(lambda e: e.memset(vwe, 1.0), [], ["vwe"])
    V(lambda e: e.memset(vce, 1.0), [], ["vce"])

    srcs = ((KSN, 0, ksn_key), (KWN, 0, "KWN"), (U, C_VS, "U"), (U, C_VW, "U"), (U, C_KC, "U"), (U, C_VC, "U"))
    for i in range(NTq):
        t0 = i * 128
        d = i % 2
        for s_, (src, col, key) in enumerate(srcs):
            P.dma("sync", stg[s_][d], src[t0:t0 + 128, col:col + 512], reads=[key], writes=["stg%d%d" % (s_, d)])
        for s_, (dstT, bbf, bkey, dk) in enumerate(((ksT, b0bf, "bank0", "ksT"), (kwT, b1bf, "bank1", "kwT"))):
            G(lambda e, s_=s_, d=d: e.tensor_copy(out=sbf[s_][d], in_=stg[s_][d]), ["stg%d%d" % (s_, d)], ["sbf%d%d" % (s_, d)])
            for g in range(4):
                T(lambda e, s_=s_, d=d, g=g, bbf=bbf: e.transpose(out=bbf[:, g * 128:(g + 1) * 128], in_=sbf[s_][d][:, g * 128:(g + 1) * 128],
                                                                 identity=idb), ["sbf%d%d" % (s_, d), "idb"], [bkey])
            V(lambda e, dstT=dstT, bbf=bbf, t0=t0: e.tensor_copy(out=dstT[:, :, t0:t0 + 128],
                                                                 in_=bbf[:, 0:512].rearrange("p (g t) -> p g t", g=4)), [bkey], [dk])
        for s_, (dstV, dk) in ((2, (vse, "vse")), (3, (vwe, "vwe"))):
            G(lambda e, s_=s_, d=d, dstV=dstV, i=i: e.tensor_copy(out=dstV[:, i, :, 0:128],
                                                                  in_=stg[s_][d].rearrange("p (g e) -> p g e", g=4)),
              ["stg%d%d" % (s_, d)], [dk])
        for s_ in range(2):
            sb = sbf[2 + s_][d]
            sk = "sbf%d%d" % (2 + s_, d)
            V(lambda e, s_=s_, d=d, sb=sb: e.tensor_copy(out=sb, in_=stg[4 + s_][d]), ["stg%d%d" % (4 + s_, d)], [sk])
            for g in range(4):
                bk = (bS if s_ == 0 else bO)[g // 2]
                bkey = (bSk if s_ == 0 else bOk)[g // 2]
                c0 = (g % 2) * 256 + i * 16
                T(lambda e, s_=s_, g=g, sb=sb, bk=bk, c0=c0: e.matmul(bk[:, c0:c0 + 16], lhsT=sb[:, g * 128:(g + 1) * 128],
                                                                      rhs=WAB[:, s_ * 16:s_ * 16 + 16], start=True, stop=True),
                  [sk, "WAB"], [bkey])
    for s_ in range(2):
        for gg in range(2):
            bk = (bS if s_ == 0 else bO)[gg]
            bkey = (bSk if s_ == 0 else bOk)[gg]
            vw_ = bk[:, 0:512].rearrange("p (g i ab c) -> p g i ab c", g=2, i=16, ab=2)
            V(lambda e, s_=s_, gg=gg, vw_=vw_: e.tensor_copy(
                out=ATs[:, s_, 2 * gg:2 * gg + 2, 0:NTq * 8].rearrange("p g (i c) -> p g i c", c=8), in_=vw_[:, :, 0:NTq, 0, :]),
              [bkey], ["ATs"])
            V(lambda e, s_=s_, gg=gg, vw_=vw_: e.tensor_copy(
                out=BTs[:, s_, 2 * gg:2 * gg + 2, 0:NTq * 8].rearrange("p g (i c) -> p g i c", c=8), in_=vw_[:, :, 0:NTq, 1, :]),
              [bkey], ["BTs"])
        V(lambda e, s_=s_: e.scalar_tensor_tensor(out=pooled[:, s_, :, 0:NCB_], in0=ATs[:, s_, :, 0:NCB_], scalar=pec[:, s_:s_ + 1],
                                                  in1=BTs[:, s_, :, 1:NCB_ + 1], op0=ALU.add, op1=ALU.add),
          ["ATs", "BTs", "pec"], ["pooled"])
        bk, bkey = (bX, "bank6") if s_ == 0 else (bY, "bank7")
        for g in range(4):
            T(lambda e, s_=s_, g=g, bk=bk: e.matmul(bk[:NCB_, g * 128:(g + 1) * 128], lhsT=pooled[:, s_, g, 0:NCB_],
                                                    rhs=pjb[:, s_ * 128:(s_ + 1) * 128], start=True, stop=True),
              ["pooled", "pjb"], [bkey])
    S(lambda e: e.activation(out=ksq[:NCB_, :], in_=bX[:NCB_, 0:512], func=AF.Square), ["bank6"], ["ksq"])
    V(lambda e: e.tensor_reduce(out=kss[:NCB_, :], in_=ksq[:NCB_, :].rearrange("p (g d) -> p g d", g=4), axis=AX.X, op=ALU.add),
      ["ksq"], ["kss"])
    S(lambda e: e.activation(out=krs[:NCB_, :], in_=kss[:NCB_, :], func=AF.Ln, scale=1.0 / 128, bias=epsb[:NCB_, :]),
      ["kss", "epsb"], ["krs"])
    S(lambda e: e.activation(out=krs[:NCB_, :], in_=krs[:NCB_, :], func=AF.Exp, scale=-0.5), ["krs"], ["krs"])
    V(lambda e: e.tensor_tensor(out=kcn[:NCB_, :].rearrange("p (g d) -> p g d", g=4), in0=bX[:NCB_, 0:512].rearrange("p (g d) -> p g d", g=4),
                                in1=krs[:NCB_, :].unsqueeze(2).broadcast_to([NCB_, 4, 128]), op=ALU.mult), ["bank6", "krs"], ["kcn"])
    V(lambda e: e.tensor_tensor(out=kcnb[:NCB_, :].rearrange("p (g d) -> p g d", g=4), in0=kcn[:NCB_, :].rearrange("p (g d) -> p g d", g=4),
                                in1=kgain0[:NCB_, :].unsqueeze(1).broadcast_to([NCB_, 4, 128]), op=ALU.mult), ["kcn", "kgain0"], ["kcnb"])
    for g in range(4):
        T(lambda e, g=g: e.transpose(out=b0bf[:, g * 128:g * 128 + NCB_], in_=kcnb[:NCB_, g * 128:(g + 1) * 128],
                                     identity=idb[:NCB_, :NCB_]), ["kcnb", "idb"], ["bank0"])
    V(lambda e: e.tensor_copy(out=kcT[:, :, 0:NCB_], in_=b0bf[:, 0:512].rearrange("p (g n) -> p g n", g=4)[:, :, 0:NCB_]),
      ["bank0"], ["kcT"])
    V(lambda e: e.tensor_copy(out=vce[:NCB_, :, 0:128], in_=bY[:NCB_, 0:512].rearrange("p (g e) -> p g e", g=4)), ["bank7"], ["vce"])

    P.barrier()
    cv.reset(mark)
    qf = cv.f32(2048)
    sq = cv.f32(2048)
    qnb = cv.bf16(2048)
    qT = cv.bf16(2048)
    zf = cv.f32(2048)
    gtf = cv.f32(48)
    cb = cv.bf16(2048)
    ss16 = cv.f32(16)
    rs16 = cv.f32(16)
    Ef = cv.f32(512)
    Ec = cv.bf16(512)
    rdb = cv.f32(512)
    impT = cv.f32(128)
    sc = cv.f32(32)
    cmp3 = cv.f32(1024)
    cnt = cv.f32(32)
    sneg = cv.bf16(32)
    sn4 = cv.bf16(512)
    E = [cv.bf16(512) for _ in range(2)]
    acc = cv.f32(2048)
    rd2 = cv.f32(2)
    cf2 = cv.f32(2)
    rsq = 128 ** -0.5

    def finalize(br, g, first):
        for h4 in range(4):
            h = 4 * g + h4
            c0 = br * 16 + h
            V(lambda e, h4=h4: e.tensor_scalar(out=rd2[:, 0:1], in0=bO4[h4][:, 128:129], scalar1=1e-30, scalar2=None, op0=ALU.add),
              [bO4k[h4]], ["rd2"])
            V(lambda e: e.reciprocal(out=rd2[:, 0:1], in_=rd2[:, 0:1]), ["rd2"], ["rd2"])
            V(lambda e, c0=c0: e.tensor_tensor(out=cf2[:, 0:1], in0=rd2[:, 0:1], in1=gtf[:, c0:c0 + 1], op=ALU.mult), ["rd2", "gtf"], ["cf2"])
            if first:
                V(lambda e, h4=h4, h=h: e.tensor_scalar(out=acc[:, h * 128:(h + 1) * 128], in0=bO4[h4][:, 0:128],
                                                        scalar1=cf2[:, 0:1], scalar2=None, op0=ALU.mult),
                  [bO4k[h4], "cf2"], ["acc"])
            else:
                V(lambda e, h4=h4, h=h: e.scalar_tensor_tensor(out=acc[:, h * 128:(h + 1) * 128], in0=bO4[h4][:, 0:128],
                                                               scalar=cf2[:, 0:1], in1=acc[:, h * 128:(h + 1) * 128],
                                                               op0=ALU.mult, op1=ALU.add),
                  [bO4k[h4], "cf2", "acc"], ["acc"])

    ecnt = 0
    for i in range(NTq):
        t0 = i * 128
        P.dma("sync", qf, U[t0:t0 + 128, C_QB:C_QB + 2048], reads=["U"], writes=["qf"])
        P.dma("sync", zf, U[t0:t0 + 128, C_ZB:C_ZB + 2048], reads=["U"], writes=["zf"])
        P.dma("sync", gtf, U[t0:t0 + 128, C_GB:C_GB + 48], reads=["U"], writes=["gtf"])
        for hh in range(2):
            P.dma("sync", cb[:NCB_, hh * 1024:(hh + 1) * 1024].rearrange("p (h q) -> p h q", h=8),
                  AP(Gd.tensor, hh * 8 * 128 * L + (128 * i - 31 + R0), [[L - 16, NCB_], [128 * L, 8], [1, 128]]),
                  reads=["G"], writes=["cb"])
        S(lambda e: e.activation(out=sq, in_=qf, func=AF.Square), ["qf"], ["sq"])
        V(lambda e: e.tensor_reduce(out=ss16, in_=sq.rearrange("p (h d) -> p h d", h=16), axis=AX.X, op=ALU.add), ["sq"], ["ss16"])
        S(lambda e: e.activation(out=rs16, in_=ss16, func=AF.Ln, scale=1.0 / 128, bias=epsb), ["ss16", "epsb"], ["rs16"])
        S(lambda e: e.activation(out=rs16, in_=rs16, func=AF.Exp, scale=-0.5), ["rs16"], ["rs16"])
        V(lambda e: e.tensor_tensor(out=sq.rearrange("p (h d) -> p h d", h=16), in0=qf.rearrange("p (h d) -> p h d", h=16),
                                    in1=rs16.unsqueeze(2).broadcast_to([128, 16, 128]), op=ALU.mult), ["qf", "rs16", "sq"], ["sq"])
        V(lambda e: e.scalar_tensor_tensor(out=qnb.rearrange("p (h d) -> p h d", h=16), in0=sq.rearrange("p (h d) -> p h d", h=16),
                                           scalar=rsq, in1=qgain.unsqueeze(1).broadcast_to([128, 16, 128]),
                                           op0=ALU.mult, op1=ALU.mult), ["sq", "qgain"], ["qnb"])
        for hb, (bbf, bkey) in enumerate(((b0bf, "bank0"), (b1bf, "bank1"))):
            for j in range(8):
                T(lambda e, hb=hb, j=j, bbf=bbf: e.transpose(out=bbf[:, j * 128:(j + 1) * 128],
                                                             in_=qnb[:, (hb * 8 + j) * 128:(hb * 8 + j + 1) * 128], identity=idb),
                  ["qnb", "idb"], [bkey])
            V(lambda e, hb=hb, bbf=bbf: e.tensor_copy(out=qT[:, hb * 1024:(hb + 1) * 1024], in_=bbf[:, 0:1024]), [bkey], ["qT"])
        S(lambda e: e.activation(out=zf, in_=zf, func=AF.Silu), ["zf"], ["zf"])
        S(lambda e: e.activation(out=gtf, in_=gtf, func=AF.Sigmoid), ["gtf"], ["gtf"])

        for g in range(4):
            q4 = qT[:, g * 512:(g + 1) * 512]
            T(lambda e, g=g, q4=q4: e.matmul(bS0[:NCB_, 0:512], lhsT=kcT[:, g, 0:NCB_], rhs=q4, start=True, stop=False),
              ["kcT", "qT"], ["bank2"])
            T(lambda e, g=g: e.matmul(bS0[:NCB_, 0:512], lhsT=idb[:NCB_, :NCB_], rhs=cb[:NCB_, g * 512:(g + 1) * 512], start=False, stop=True),
              ["idb", "cb"], ["bank2"])
            S(lambda e: e.activation(out=Ef[:NCB_, :], in_=bS0[:NCB_, 0:512], func=AF.Exp), ["bank2"], ["Ef"])
            G(lambda e: e.tensor_copy(out=Ec[:NCB_, :], in_=Ef[:NCB_, :]), ["Ef"], ["Ec"])
            for h in range(4):
                T(lambda e, g=g, h=h: e.matmul(bO4[h][:, 0:129], lhsT=Ec[:NCB_, h * 128:(h + 1) * 128],
                                               rhs=vce[:NCB_, g, 0:129], start=True, stop=True), ["Ec", "vce"], [bO4k[h]])
            T(lambda e: e.matmul(b1[:NCB_, 0:512], lhsT=onesf[:NCB_, :NCB_], rhs=Ef[:NCB_, :], start=True, stop=True),
              ["onesf", "Ef"], ["bank1"])
            V(lambda e: e.tensor_scalar(out=rdb[:NCB_, :], in0=b1[:NCB_, 0:512], scalar1=1e-30, scalar2=None, op0=ALU.add), ["bank1"], ["rdb"])
            V(lambda e: e.reciprocal(out=rdb[:NCB_, :], in_=rdb[:NCB_, :]), ["rdb"], ["rdb"])
            V(lambda e: e.tensor_tensor(out=Ef[:NCB_, :], in0=Ef[:NCB_, :], in1=rdb[:NCB_, :], op=ALU.mult), ["Ef", "rdb"], ["Ef"])
            V(lambda e: e.tensor_reduce(out=impT[:NCB_, :], in_=Ef[:NCB_, :].rearrange("p (h q) -> p q h", h=4), axis=AX.X, op=ALU.add),
              ["Ef"], ["impT"])
            T(lambda e: e.matmul(b0[:, 0:32], lhsT=impT[:NCB_, :], rhs=Mc[:NCB_, :], start=True, stop=True), ["impT", "nsac"], ["bank0"])
            V(lambda e, i=i: e.tensor_tensor(out=sc, in0=b0[:, 0:32], in1=KM[:, i, :], op=ALU.mult), ["bank0", "nsac"], ["sc"])
            V(lambda e, i=i: e.tensor_tensor(out=sc, in0=sc, in1=FM[:, i, :], op=ALU.add), ["sc", "nsac"], ["sc"])
            V(lambda e: e.tensor_tensor(out=cmp3.rearrange("p (a b) -> p a b", a=32), in0=sc.unsqueeze(1).broadcast_to([128, 32, 32]),
                                        in1=sc.unsqueeze(2).broadcast_to([128, 32, 32]), op=ALU.is_gt), ["sc"], ["cmp3"])
            V(lambda e: e.tensor_reduce(out=cnt, in_=cmp3.rearrange("p (a b) -> p a b", a=32), axis=AX.X, op=ALU.add), ["cmp3"], ["cnt"])
            V(lambda e: e.tensor_scalar(out=sneg, in0=cnt, scalar1=15.5, scalar2=NEGM, op0=ALU.is_gt, op1=ALU.mult), ["cnt"], ["sneg"])
            T(lambda e: e.transpose(out=b0bf[:32, 0:128], in_=sneg[:, 0:32], identity=idb), ["sneg", "idb"], ["bank0"])
            V(lambda e: e.tensor_copy(out=sn4[:32, :].rearrange("p (h q) -> p h q", h=4),
                                      in_=b0bf[:32, 0:128].unsqueeze(1).broadcast_to([32, 4, 128])), ["bank0"], ["sn4"])
            finalize(0, g, True)
            for br, kT_, ve, kts in ((1, ksT, vse, list(range(0, i + 1))), (2, kwT, vwe, list(range(max(0, i - 4), i + 1)))):
                for kt in kts:
                    d = ecnt % 2
                    ecnt += 1
                    bSx, bSxk = bS[d], bSk[d]
                    T(lambda e, g=g, kt=kt, kT_=kT_, bSx=bSx, q4=q4: e.matmul(bSx[:, 0:512], lhsT=kT_[:, g, kt * 128:(kt + 1) * 128], rhs=q4,
                                                                            start=True, stop=False),
                      ["ksT", "kwT", "qT"], [bSxk])
                    if br == 1:
                        T(lambda e, kt=kt, bSx=bSx: e.matmul(bSx[:, 0:512], lhsT=Ek[:32, kt * 128:(kt + 1) * 128], rhs=sn4[:32, :],
                                                             start=False, stop=False), ["Ek", "sn4"], [bSxk])
                    if kt == i:
                        lh, rh, rk = idb, Bd[:, g * 512:(g + 1) * 512], "Bd"
                    elif kt == i - 1:
                        lh, rh, rk = idb, Bo[:, g * 512:(g + 1) * 512], "Bo"
                    elif br == 2 and kt == i - 4:
                        lh, rh, rk = idb, Bw[:, g * 512:(g + 1) * 512], "Bw"
                    else:
                        lh, rh, rk = ones1[0:1, :], tbrow[0:1, g * 512:(g + 1) * 512], "tbrow"
                    T(lambda e, bSx=bSx, lh=lh, rh=rh: e.matmul(bSx[:, 0:512], lhsT=lh, rhs=rh, start=False, stop=True),
                      ["idb", "ones1", rk], [bSxk])
                    S(lambda e, d=d, bSx=bSx: e.activation(out=E[d], in_=bSx[:, 0:512], func=AF.Exp), [bSxk], ["E%d" % d])
                    for h in range(4):
                        T(lambda e, g=g, kt=kt, h=h, d=d, ve=ve, kts=kts: e.matmul(
                            bO4[h][:, 0:129], lhsT=E[d][:, h * 128:(h + 1) * 128],
                            rhs=ve[:, kt, g, 0:129], start=(kt == kts[0]), stop=(kt == kts[-1])),
                          ["E%d" % d, "vse", "vwe"], [bO4k[h]])
                finalize(br, g, False)
        V(lambda e: e.tensor_tensor(out=acc, in0=acc, in1=zf, op=ALU.mult), ["acc", "zf"], ["acc"])
        P.dma("scalar", OB[t0:t0 + 128, :], acc, reads=["acc"], writes=["OB"], group="acc")


NSAS_W = 2712


def make_nsa_sample_consts():
    ncs, nb = 1023, 257
    c = np.zeros((128, NSAS_W), np.float32)
    j = np.arange(nb)
    lo = np.clip((64 * j - 32) // 16 + 1, 0, ncs)
    hi = np.clip(-(-(64 * (j + 1)) // 16), 0, ncs)
    n = np.arange(1024)[:, None]
    m = ((n >= lo[None]) & (n < hi[None]) & (n < ncs)).astype(np.float32)
    c[:, 0:2056] = m.reshape(8, 128, nb).transpose(1, 0, 2).reshape(128, 2056)
    forced = (j == 0) | (j == 256) | (j == 255)
    c[0:8, 2056:2313] = (~forced).astype(np.float32)
    c[0:8, 2313:2570] = np.where(forced, 1e9, 0.0)
    kp = np.arange(128)[:, None]
    t = np.arange(8)[None, :]
    c[:, 2570:2578] = np.where(kp <= t, NEGM, 0.0)
    c[0, 2578:2578 + 64] = 1.0
    c[1, 2578 + 64:2578 + 128] = 1.0
    c[:, 2706] = np.arange(128)
    return c


def stage_nsa_sample(P, nc, cv, banks, U, KSNs, ksn_key, KWN, OB, nsac_d, nsas_d, TVd, Gd, q_norm_g, k_norm_g,
                     pe_k, w_k, proj_k, pe_v, w_v, proj_v, pool_kc, pool_vc, pool_ks, pool_vs, ckw, cvw, ptab,
                     idf, idb, epsb):
    from concourse.ap import AP
    L, R0 = NSA_L, NSA_R0
    T0 = SEQ
    V = lambda fn, r, w: P.op("vector", fn, r, w)
    S = lambda fn, r, w: P.op("scalar", fn, r, w)
    G = lambda fn, r, w: P.op("gpsimd", fn, r, w)
    T = lambda fn, r, w: P.op("tensor", fn, r, w, skip_same=True)
    b0, b1, bS0, bS1 = banks[0:4]
    bS = [bS0, bS1]
    bSk = ["bank2", "bank3"]
    bO4 = [banks[4], banks[5], banks[6], banks[7]]
    bO4k = ["bank4", "bank5", "bank6", "bank7"]
    b0bf = b0[:].bitcast(BF16)
    b1bf = b1[:].bitcast(BF16)
    rsq = 128 ** -0.5

    nsas = cv.f32(NSAS_W)
    Ms = nsas[:, 0:2056].rearrange("p (a j) -> p a j", a=8)
    KMs = nsas[:, 2056:2313]
    FMs = nsas[:, 2313:2570]
    LTs = nsas[:, 2570:2578]
    E2f = nsas[:, 2578:2706]
    iot = nsas[:, 2706:2707]
    nsm = cv.f32(512)
    P.dma("sync", nsas, nsas_d, writes=["nsas"])
    P.dma("sync", nsm[:, 0:256], nsac_d[:, 3248:3504], writes=["nsm"])
    P.dma("sync", nsm[:, 256:272], nsac_d[:, 1184:1200], writes=["nsm"])
    SelA, SelB, MAB = nsm[:, 0:128], nsm[:, 128:256], nsm[:, 256:272]
    E2 = cv.bf16(128)
    V(lambda e: e.tensor_copy(out=E2[:2, :], in_=E2f[:2, :]), ["nsas"], ["E2"])
    ones1 = cv.bf16(128)
    onesf = cv.f32(128)
    V(lambda e: e.memset(ones1, 1.0), [], ["ones1"])
    V(lambda e: e.memset(onesf, 1.0), [], ["onesf"])
    qgain = cv.f32(128)
    kgain0 = cv.f32(128)
    P.dma("sync", qgain[:8, :], q_norm_g.broadcast_to([8, 128]), writes=["qgain"])
    P.dma("sync", kgain0, k_norm_g[0:1, :].broadcast_to([128, 128]), writes=["kgain0"])

    pti = cv.f32(128).bitcast(I32)
    ptf = cv.f32(128)
    idx = cv.f32(128).bitcast(I32)
    P.dma("sync", pti, ptab.broadcast_to([128, 128]), writes=["pti"])
    V(lambda e: e.tensor_copy(out=ptf, in_=pti), ["pti"], ["ptf"])
    V(lambda e: e.tensor_scalar(out=ptf, in0=ptf, scalar1=128.0, scalar2=iot, op0=ALU.mult, op1=ALU.add), ["ptf", "nsas"], ["ptf"])
    V(lambda e: e.tensor_copy(out=idx, in_=ptf), ["ptf"], ["idx"])

    def gather(dst, dkey, pool, pg):
        fn = lambda e, dst=dst, pool=pool, pg=pg: e.indirect_dma_start(
            out=dst, out_offset=None, in_=pool, in_offset=bass.IndirectOffsetOnAxis(ap=idx[:, pg:pg + 1], axis=0))
        P.dma_fn("gpsimd", fn, reads=["idx"], writes=[dkey])

    tb31b = cv.bf16(16)
    tb31f = cv.f32(16)
    tbrow = cv.bf16(128)
    Bc7 = cv.bf16(128)
    B127 = cv.bf16(128)
    Bnew = cv.bf16(128)
    Bw0 = cv.bf16(128)
    P.dma("sync", tb31b, AP(TVd.tensor, R0 + 200, [[0, 128], [L, 16]]), reads=["TV"], writes=["tb31b"], allow_slow_non_contiguous=True)
    V(lambda e: e.tensor_copy(out=tb31f, in_=tb31b), ["tb31b"], ["tb31f"])
    V(lambda e: e.tensor_copy(out=tbrow[0:1, :].rearrange("p (h q) -> p h q", h=16),
                              in_=tb31b[0:1, :].unsqueeze(2).broadcast_to([1, 16, 8])), ["tb31b"], ["tbrow"])
    for hh in range(2):
        P.dma("sync", Bc7[:, hh * 64:(hh + 1) * 64].rearrange("p (h q) -> p h q", h=8),
              AP(Gd.tensor, hh * 8 * 128 * L + R0 + 2017, [[L - 16, 128], [128 * L, 8], [1, 8]]), reads=["G"], writes=["Bc7"])
        P.dma("sync", B127[:, hh * 64:(hh + 1) * 64].rearrange("p (h q) -> p h q", h=8),
              AP(Gd.tensor, hh * 8 * 128 * L + R0 + 128, [[L - 1, 128], [128 * L, 8], [1, 8]]), reads=["G"], writes=["B127"])
    P.dma("sync", Bnew[:8, :].rearrange("p (h q) -> p h q", h=16),
          AP(Gd.tensor, R0, [[L - 1, 8], [128 * L, 16], [1, 8]]), reads=["G"], writes=["Bnew"])
    for h in range(16):
        V(lambda e, h=h: e.tensor_scalar(out=Bw0[:, h * 8:(h + 1) * 8], in0=LTs, scalar1=tb31f[:, h:h + 1], scalar2=None, op0=ALU.add),
          ["nsas", "tb31f"], ["Bw0"])

    qf = cv.f32(2048)
    sq = cv.f32(2048)
    qnb = cv.bf16(2048)
    ss16 = cv.f32(16)
    rs16 = cv.f32(16)
    qTs = cv.bf16(128)
    gT = cv.f32(12)
    zT = cv.f32(512)
    acc = cv.f32(512)
    rd = cv.f32(1)
    cf = cv.f32(1)
    P.dma("sync", qf[:8, :], U[T0:T0 + 8, C_QB:C_QB + 2048], reads=["U"], writes=["qf"])
    for h4 in range(4):
        P.dma("sync", gT[h4 * 8:(h4 + 1) * 8, :].rearrange("p (b g) -> p b g", b=3),
              AP(U.tensor, T0 * IN_W + C_GB + h4, [[IN_W, 8], [16, 3], [4, 4]]), reads=["U"], writes=["gT"], allow_slow_non_contiguous=True)
        P.dma("sync", zT[h4 * 8:(h4 + 1) * 8, :].rearrange("p (g d) -> p g d", g=4),
              AP(U.tensor, T0 * IN_W + C_ZB + h4 * 128, [[IN_W, 8], [512, 4], [1, 128]]), reads=["U"], writes=["zT"])
    S(lambda e: e.activation(out=sq[:8, :], in_=qf[:8, :], func=AF.Square), ["qf"], ["sq"])
    V(lambda e: e.tensor_reduce(out=ss16[:8, :], in_=sq[:8, :].rearrange("p (h d) -> p h d", h=16), axis=AX.X, op=ALU.add), ["sq"], ["ss16"])
    S(lambda e: e.activation(out=rs16[:8, :], in_=ss16[:8, :], func=AF.Ln, scale=1.0 / 128, bias=epsb[:8, :]), ["ss16", "epsb"], ["rs16"])
    S(lambda e: e.activation(out=rs16[:8, :], in_=rs16[:8, :], func=AF.Exp, scale=-0.5), ["rs16"], ["rs16"])
    V(lambda e: e.tensor_tensor(out=sq[:8, :].rearrange("p (h d) -> p h d", h=16), in0=qf[:8, :].rearrange("p (h d) -> p h d", h=16),
                                in1=rs16[:8, :].unsqueeze(2).broadcast_to([8, 16, 128]), op=ALU.mult), ["qf", "rs16", "sq"], ["sq"])
    V(lambda e: e.scalar_tensor_tensor(out=qnb[:8, :].rearrange("p (h d) -> p h d", h=16), in0=sq[:8, :].rearrange("p (h d) -> p h d", h=16),
                                       scalar=rsq, in1=qgain[:8, :].unsqueeze(1).broadcast_to([8, 16, 128]),
                                       op0=ALU.mult, op1=ALU.mult), ["sq", "qgain"], ["qnb"])
    for h in range(16):
        T(lambda e, h=h: e.transpose(out=b0bf[:, h * 8:(h + 1) * 8], in_=qnb[:8, h * 128:(h + 1) * 128], identity=idb[:8, :8]),
          ["qnb", "idb"], ["bank0"])
    V(lambda e: e.tensor_copy(out=qTs, in_=b0bf[:, 0:128]), ["bank0"], ["qTs"])
    S(lambda e: e.activation(out=zT[:32, :], in_=zT[:32, :], func=AF.Silu), ["zT"], ["zT"])
    S(lambda e: e.activation(out=gT[:32, :], in_=gT[:32, :], func=AF.Sigmoid), ["gT"], ["gT"])

    def finalize(br, first):
        for g in range(4):
            V(lambda e, g=g: e.tensor_scalar(out=rd[:32, :], in0=bO4[g][:32, 128:129], scalar1=1e-30, scalar2=None, op0=ALU.add),
              [bO4k[g]], ["rd"])
            V(lambda e: e.reciprocal(out=rd[:32, :], in_=rd[:32, :]), ["rd"], ["rd"])
            V(lambda e, g=g: e.tensor_tensor(out=cf[:32, :], in0=rd[:32, :], in1=gT[:32, br * 4 + g:br * 4 + g + 1], op=ALU.mult),
              ["rd", "gT"], ["cf"])
            if first:
                V(lambda e, g=g: e.tensor_scalar(out=acc[:32, g * 128:(g + 1) * 128], in0=bO4[g][:32, 0:128], scalar1=cf[:32, 0:1],
                                                 scalar2=None, op0=ALU.mult), [bO4k[g], "cf"], ["acc"])
            else:
                V(lambda e, g=g: e.scalar_tensor_tensor(out=acc[:32, g * 128:(g + 1) * 128], in0=bO4[g][:32, 0:128], scalar=cf[:32, 0:1],
                                                        in1=acc[:32, g * 128:(g + 1) * 128], op0=ALU.mult, op1=ALU.add),
                  [bO4k[g], "cf", "acc"], ["acc"])

    wk32 = cv.f32(1)
    wv32 = cv.f32(1)
    pek = cv.f32(128)
    pev = cv.f32(128)
    wrep = cv.f32(4)
    pec = cv.f32(2)
    WAB = cv.bf16(32)
    pjf = cv.f32(256)
    pjb = cv.bf16(256)
    P.dma("sync", wk32[:32, :], w_k, writes=["wk32"])
    P.dma("sync", wv32[:32, :], w_v, writes=["wv32"])
    P.dma("sync", pek[:32, :], pe_k, writes=["pek"])
    P.dma("sync", pev[:32, :], pe_v, writes=["pev"])
    for s_, (w32, wkey) in enumerate(((wk32, "wk32"), (wv32, "wv32"))):
        T(lambda e, s_=s_, w32=w32: e.matmul(b1[:, 2 * s_:2 * s_ + 1], lhsT=SelA[:32, :], rhs=w32[:32, :], start=True, stop=True),
          ["nsm", wkey], ["bank1"])
        T(lambda e, s_=s_, w32=w32: e.matmul(b1[:, 2 * s_ + 1:2 * s_ + 2], lhsT=SelB[:32, :], rhs=w32[:32, :], start=True, stop=True),
          ["nsm", wkey], ["bank1"])
    T(lambda e: e.matmul(b1[:, 8:9], lhsT=pek[:32, :], rhs=wk32[:32, :], start=True, stop=True), ["pek", "wk32"], ["bank1"])
    T(lambda e: e.matmul(b1[:, 9:10], lhsT=pev[:32, :], rhs=wv32[:32, :], start=True, stop=True), ["pev", "wv32"], ["bank1"])
    V(lambda e: e.tensor_copy(out=wrep, in_=b1[:, 0:4]), ["bank1"], ["wrep"])
    V(lambda e: e.tensor_copy(out=pec, in_=b1[:, 8:10]), ["bank1"], ["pec"])
    for s_ in range(2):
        for ab in range(2):
            V(lambda e, s_=s_, ab=ab: e.tensor_scalar(out=WAB[:, s_ * 16 + ab * 8:s_ * 16 + ab * 8 + 8], in0=MAB[:, ab * 8:ab * 8 + 8],
                                                      scalar1=wrep[:, 2 * s_ + ab:2 * s_ + ab + 1], scalar2=None, op0=ALU.mult),
              ["nsm", "wrep"], ["WAB"])
    P.dma("sync", pjf[:, 0:128], proj_k, writes=["pjf"])
    P.dma("sync", pjf[:, 128:256], proj_v, writes=["pjf"])
    V(lambda e: e.tensor_copy(out=pjb, in_=pjf), ["pjf"], ["pjb"])

    ATs = cv.f32(4096).rearrange("p (g n) -> p g n", g=4)
    BTs = cv.f32(4096).rearrange("p (g n) -> p g n", g=4)
    pooled = cv.bf16(4096).rearrange("p (g n) -> p g n", g=4)
    kcTs = cv.bf16(4096).rearrange("p (g n) -> p g n", g=4)
    vces = cv.bf16(8 * 4 * 132).rearrange("p (a g e) -> p a g e", a=8, g=4)
    stg = [cv.f32(512) for _ in range(4)]
    sbf = [cv.bf16(512) for _ in range(4)]
    ksq = cv.f32(512)
    kss = cv.f32(4)
    krs = cv.f32(4)
    kcn = cv.f32(512)
    kcnb = cv.bf16(512)
    V(lambda e: e.memset(vces, 1.0), [], ["vces"])
    for s_, pool in enumerate((pool_kc, pool_vc)):
        V(lambda e: e.memset(pooled, 0.0), [], ["pooled"])
        for pg in range(128):
            d = pg % 2
            sk, bk_ = "sg%d" % d, "sb%d" % d
            gather(stg[d], sk, pool, pg)
            V(lambda e, d=d: e.tensor_copy(out=sbf[d], in_=stg[d]), [sk], [bk_])
            for g in range(4):
                c0 = (g % 2) * 256 + (pg % 16) * 16
                T(lambda e, s_=s_, g=g, d=d, c0=c0: e.matmul(bS[g // 2][:, c0:c0 + 16], lhsT=sbf[d][:, g * 128:(g + 1) * 128],
                                                             rhs=WAB[:, s_ * 16:s_ * 16 + 16], start=True, stop=True),
                  [bk_, "WAB"], [bSk[g // 2]])
            if pg % 16 == 15:
                blk = pg // 16
                for gg in range(2):
                    vw_ = bS[gg][:, 0:512].rearrange("p (g i ab c) -> p g i ab c", g=2, i=16, ab=2)
                    V(lambda e, gg=gg, vw_=vw_, blk=blk: e.tensor_copy(
                        out=ATs[:, 2 * gg:2 * gg + 2, blk * 128:(blk + 1) * 128].rearrange("p g (i c) -> p g i c", c=8),
                        in_=vw_[:, :, :, 0, :]), [bSk[gg]], ["ATs"])
                    V(lambda e, gg=gg, vw_=vw_, blk=blk: e.tensor_copy(
                        out=BTs[:, 2 * gg:2 * gg + 2, blk * 128:(blk + 1) * 128].rearrange("p g (i c) -> p g i c", c=8),
                        in_=vw_[:, :, :, 1, :]), [bSk[gg]], ["BTs"])
        V(lambda e, s_=s_: e.scalar_tensor_tensor(out=pooled[:, :, 0:1023], in0=ATs[:, :, 0:1023], scalar=pec[:, s_:s_ + 1],
                                                  in1=BTs[:, :, 1:1024], op0=ALU.add, op1=ALU.add), ["ATs", "BTs", "pec"], ["pooled"])
        for nt in range(8):
            for g in range(4):
                T(lambda e, s_=s_, g=g, nt=nt: e.matmul(b1[:, g * 128:(g + 1) * 128], lhsT=pooled[:, g, nt * 128:(nt + 1) * 128],
                                                        rhs=pjb[:, s_ * 128:(s_ + 1) * 128], start=True, stop=True),
                  ["pooled", "pjb"], ["bank1"])
            if s_ == 0:
                S(lambda e: e.activation(out=ksq, in_=b1[:, 0:512], func=AF.Square), ["bank1"], ["ksq"])
                V(lambda e: e.tensor_reduce(out=kss, in_=ksq.rearrange("p (g d) -> p g d", g=4), axis=AX.X, op=ALU.add), ["ksq"], ["kss"])
                S(lambda e: e.activation(out=krs, in_=kss, func=AF.Ln, scale=1.0 / 128, bias=epsb), ["kss", "epsb"], ["krs"])
                S(lambda e: e.activation(out=krs, in_=krs, func=AF.Exp, scale=-0.5), ["krs"], ["krs"])
                V(lambda e: e.tensor_tensor(out=kcn.rearrange("p (g d) -> p g d", g=4), in0=b1[:, 0:512].rearrange("p (g d) -> p g d", g=4),
                                            in1=krs.unsqueeze(2).broadcast_to([128, 4, 128]), op=ALU.mult), ["bank1", "krs"], ["kcn"])
                V(lambda e: e.tensor_tensor(out=kcnb.rearrange("p (g d) -> p g d", g=4), in0=kcn.rearrange("p (g d) -> p g d", g=4),
                                            in1=kgain0.unsqueeze(1).broadcast_to([128, 4, 128]), op=ALU.mult), ["kcn", "kgain0"], ["kcnb"])
                for g in range(4):
                    T(lambda e, g=g: e.transpose(out=b0bf[:, g * 128:(g + 1) * 128], in_=kcnb[:, g * 128:(g + 1) * 128], identity=idb),
                      ["kcnb", "idb"], ["bank0"])
                V(lambda e, nt=nt: e.tensor_copy(out=kcTs[:, :, nt * 128:(nt + 1) * 128], in_=b0bf[:, 0:512].rearrange("p (g n) -> p g n", g=4)),
                  ["bank0"], ["kcTs"])
            else:
                V(lambda e, nt=nt: e.tensor_copy(out=vces[:, nt, :, 0:128], in_=b1[:, 0:512].rearrange("p (g e) -> p g e", g=4)),
                  ["bank1"], ["vces"])

    Ef = cv.f32(256)
    Ec = cv.bf16(256)
    rdb = cv.f32(32)
    impT = cv.f32(64)
    sc = cv.f32(264)
    cmpc = cv.f32(16 * 257)
    cnt = cv.f32(264)
    sneg = cv.bf16(264)
    snT = cv.f32(16)
    snP = cv.bf16(128 * 8)
    snP4 = [cv.bf16(128 * 32) for _ in range(4)]
    SNd = nc.dram_tensor("SNd", [4, 264, 8], BF16, kind="Internal").ap()
    for g in range(4):
        q4 = qTs[:, g * 32:(g + 1) * 32]
        for nt in range(8):
            T(lambda e, g=g, nt=nt, q4=q4: e.matmul(bS0[:, nt * 32:(nt + 1) * 32], lhsT=kcTs[:, g, nt * 128:(nt + 1) * 128], rhs=q4,
                                                    start=True, stop=False), ["kcTs", "qTs"], ["bank2"])
            if nt == 7:
                T(lambda e, g=g, nt=nt: e.matmul(bS0[:, nt * 32:(nt + 1) * 32], lhsT=idb, rhs=Bc7[:, g * 32:(g + 1) * 32], start=False, stop=True),
                  ["idb", "Bc7"], ["bank2"])
            else:
                T(lambda e, g=g, nt=nt: e.matmul(bS0[:, nt * 32:(nt + 1) * 32], lhsT=ones1[0:1, :], rhs=tbrow[0:1, g * 32:(g + 1) * 32],
                                                 start=False, stop=True), ["ones1", "tbrow"], ["bank2"])
        S(lambda e: e.activation(out=Ef, in_=bS0[:, 0:256], func=AF.Exp), ["bank2"], ["Ef"])
        G(lambda e: e.tensor_copy(out=Ec, in_=Ef), ["Ef"], ["Ec"])
        for nt in range(8):
            T(lambda e, g=g, nt=nt: e.matmul(bO4[g][:32, 0:129], lhsT=Ec[:, nt * 32:(nt + 1) * 32], rhs=vces[:, nt, g, 0:129],
                                             start=(nt == 0), stop=(nt == 7)), ["Ec", "vces"], [bO4k[g]])
        for nt in range(8):
            T(lambda e, nt=nt: e.matmul(b1[:, 0:32], lhsT=onesf, rhs=Ef[:, nt * 32:(nt + 1) * 32], start=(nt == 0), stop=(nt == 7)),
              ["onesf", "Ef"], ["bank1"])
        V(lambda e: e.tensor_scalar(out=rdb, in0=b1[:, 0:32], scalar1=1e-30, scalar2=None, op0=ALU.add), ["bank1"], ["rdb"])
        V(lambda e: e.reciprocal(out=rdb, in_=rdb), ["rdb"], ["rdb"])
        V(lambda e: e.tensor_tensor(out=Ef.rearrange("p (a c) -> p a c", a=8), in0=Ef.rearrange("p (a c) -> p a c", a=8),
                                    in1=rdb.unsqueeze(1).broadcast_to([128, 8, 32]), op=ALU.mult), ["Ef", "rdb"], ["Ef"])
        V(lambda e: e.tensor_reduce(out=impT.rearrange("p (a t) -> p a t", a=8), in_=Ef.rearrange("p (a h t) -> p a t h", a=8, h=4),
                                    axis=AX.X, op=ALU.add), ["Ef"], ["impT"])
        for nt in range(8):
            T(lambda e, nt=nt: e.matmul(b0[:8, 0:257], lhsT=impT[:, nt * 8:(nt + 1) * 8], rhs=Ms[:, nt, :], start=(nt == 0), stop=(nt == 7)),
              ["impT", "nsas"], ["bank0"])
        V(lambda e: e.tensor_tensor(out=sc[:8, 0:257], in0=b0[:8, 0:257], in1=KMs[:8, :], op=ALU.mult), ["bank0", "nsas"], ["sc"])
        V(lambda e: e.tensor_tensor(out=sc[:8, 0:257], in0=sc[:8, 0:257], in1=FMs[:8, :], op=ALU.add), ["sc", "nsas"], ["sc"])
        for j0 in range(0, 257, 16):
            jw = min(16, 257 - j0)
            V(lambda e, j0=j0, jw=jw: e.tensor_tensor(out=cmpc[:8, 0:jw * 257].rearrange("p (a b) -> p a b", a=jw),
                                                      in0=sc[:8, 0:257].unsqueeze(1).broadcast_to([8, jw, 257]),
                                                      in1=sc[:8, j0:j0 + jw].unsqueeze(2).broadcast_to([8, jw, 257]), op=ALU.is_gt),
              ["sc"], ["cmpc"])
            V(lambda e, j0=j0, jw=jw: e.tensor_reduce(out=cnt[:8, j0:j0 + jw], in_=cmpc[:8, 0:jw * 257].rearrange("p (a b) -> p a b", a=jw),
                                                      axis=AX.X, op=ALU.add), ["cmpc"], ["cnt"])
        V(lambda e: e.tensor_scalar(out=sneg[:8, 0:257], in0=cnt[:8, 0:257], scalar1=15.5, scalar2=NEGM, op0=ALU.is_gt, op1=ALU.mult),
          ["cnt"], ["sneg"])
        for jt, (j0, jw) in enumerate(((0, 128), (128, 128), (256, 1))):
            T(lambda e, j0=j0, jw=jw: e.transpose(out=b0bf[:jw, 512:520], in_=sneg[:8, j0:j0 + jw], identity=idb[:8, :8]),
              ["sneg", "idb"], ["bank0"])
            V(lambda e, jw=jw: e.tensor_copy(out=snT.bitcast(BF16)[:jw, 0:8], in_=b0bf[:jw, 512:520]), ["bank0"], ["snT"])
            P.dma("sync", SNd[g, j0:j0 + jw, :], snT.bitcast(BF16)[:jw, 0:8], reads=["snT"], writes=["SNd"])
        P.dma("sync", snP[:2, :].rearrange("p (a t) -> p a t", a=128), AP(SNd.tensor, g * 264 * 8, [[8, 2], [16, 128], [1, 8]]),
              reads=["SNd"], writes=["snP"])
        V(lambda e, g=g: e.tensor_copy(out=snP4[g][:2, :].rearrange("p (a h t) -> p a h t", a=128, h=4),
                                       in_=snP[:2, :].rearrange("p (a t) -> p a t", a=128).unsqueeze(2).broadcast_to([2, 128, 4, 8])),
          ["snP"], ["snP4%d" % g])
    finalize(0, True)

    kTp = [cv.bf16(512) for _ in range(2)]
    vpe = [cv.bf16(4 * 132).rearrange("p (g e) -> p g e", g=4) for _ in range(2)]
    E4 = [cv.bf16(128) for _ in range(2)]
    for d in range(2):
        V(lambda e, d=d: e.memset(vpe[d], 1.0), [], ["vpe%d" % d])

    def attend_tile(br, it, first, last, kload, vload, rows, bias_of):
        d = it % 2
        r = rows
        sk, sv = "sg%d" % (2 + d), "sg%d" % d
        kload(stg[2 + d], sk)
        vload(stg[d], sv)
        G(lambda e, d=d, r=r: e.tensor_copy(out=sbf[d][:r, :], in_=stg[2 + d][:r, :]), [sk], ["sb%d" % d])
        V(lambda e, d=d, r=r: e.tensor_copy(out=vpe[d][:r, :, 0:128], in_=stg[d][:r, :].rearrange("p (g e) -> p g e", g=4)), [sv], ["vpe%d" % d])
        bT, bTk = (b0bf, "bank0") if d == 0 else (b1bf, "bank1")
        for g in range(4):
            T(lambda e, d=d, g=g, r=r, bT=bT: e.transpose(out=bT[:, g * 128:g * 128 + r], in_=sbf[d][:r, g * 128:(g + 1) * 128],
                                                         identity=idb[:r, :r]), ["sb%d" % d, "idb"], [bTk])
        V(lambda e, d=d, r=r, bT=bT: e.tensor_copy(out=kTp[d].rearrange("p (g k) -> p g k", g=4)[:, :, 0:r],
                                                   in_=bT[:, 0:512].rearrange("p (g k) -> p g k", g=4)[:, :, 0:r]), [bTk], ["kTp%d" % d])
        for g in range(4):
            T(lambda e, d=d, g=g, r=r: e.matmul(bS[d][:r, g * 32:(g + 1) * 32], lhsT=kTp[d][:, g * 128:g * 128 + r],
                                                rhs=qTs[:, g * 32:(g + 1) * 32], start=True, stop=False), ["kTp%d" % d, "qTs"], [bSk[d]])
            if br == 1 and it < 128:
                T(lambda e, d=d, g=g, it=it: e.matmul(bS[d][:, g * 32:(g + 1) * 32], lhsT=E2[:2, :], rhs=snP4[g][:2, it * 32:(it + 1) * 32],
                                                      start=False, stop=False), ["E2", "snP4%d" % g], [bSk[d]])
            lh, rh, rk = bias_of(g)
            T(lambda e, d=d, g=g, r=r, lh=lh, rh=rh: e.matmul(bS[d][:r, g * 32:(g + 1) * 32], lhsT=lh, rhs=rh, start=False, stop=True),
              ["idb", "ones1", rk], [bSk[d]])
        S(lambda e, d=d, r=r: e.activation(out=E4[d][:r, :], in_=bS[d][:r, 0:128], func=AF.Exp), [bSk[d]], ["E4%d" % d])
        for g in range(4):
            T(lambda e, d=d, g=g, r=r: e.matmul(bO4[g][:32, 0:129], lhsT=E4[d][:r, g * 32:(g + 1) * 32], rhs=vpe[d][:r, g, 0:129],
                                                start=first, stop=last), ["E4%d" % d, "vpe%d" % d], [bO4k[g]])

    cbias = lambda g: (ones1[0:1, :], tbrow[0:1, g * 32:(g + 1) * 32], "tbrow")
    for pg in range(128):
        bias_of = (lambda g: (idb, B127[:, g * 32:(g + 1) * 32], "B127")) if pg == 127 else cbias
        attend_tile(1, pg, pg == 0, False,
                    lambda dst, key, pg=pg: gather(dst, key, pool_ks, pg),
                    lambda dst, key, pg=pg: gather(dst, key, pool_vs, pg), 128, bias_of)
    newbias = lambda g: (idb[:8, :8], Bnew[:8, g * 32:(g + 1) * 32], "Bnew")
    attend_tile(1, 128, False, True,
                lambda dst, key: P.dma("sync", dst[:8, :], KSNs, reads=[ksn_key], writes=[key]),
                lambda dst, key: P.dma("sync", dst[:8, :], U[T0:T0 + 8, C_VS:C_VS + 512], reads=["U"], writes=[key]), 8, newbias)
    finalize(1, False)

    for kt in range(4):
        if kt == 0:
            bias_of = lambda g: (idb, Bw0[:, g * 32:(g + 1) * 32], "Bw0")
        elif kt == 3:
            bias_of = lambda g: (idb, B127[:, g * 32:(g + 1) * 32], "B127")
        else:
            bias_of = cbias
        attend_tile(2, 200 + kt, kt == 0, False,
                    lambda dst, key, kt=kt: P.dma("sync", dst, ckw[kt * 128:(kt + 1) * 128, :], writes=[key]),
                    lambda dst, key, kt=kt: P.dma("sync", dst, cvw[kt * 128:(kt + 1) * 128, :], writes=[key]), 128, bias_of)
    attend_tile(2, 204, False, True,
                lambda dst, key: P.dma("sync", dst[:8, :], KWN[T0:T0 + 8, :], reads=["KWN"], writes=[key]),
                lambda dst, key: P.dma("sync", dst[:8, :], U[T0:T0 + 8, C_VW:C_VW + 512], reads=["U"], writes=[key]), 8, newbias)
    finalize(2, False)

    V(lambda e: e.tensor_tensor(out=acc[:32, :], in0=acc[:32, :], in1=zT[:32, :], op=ALU.mult), ["acc", "zT"], ["acc"])
    for h4 in range(4):
        P.dma("scalar", AP(OB.tensor, T0 * 2048 + h4 * 128, [[2048, 8], [512, 4], [1, 128]]),
              acc[h4 * 8:(h4 + 1) * 8, :].rearrange("p (g d) -> p g d", g=4), reads=["acc"], writes=["OB"], group="acc")


def build_program():
    nc = bass.Bass("TRN2", target_bir_lowering=False)
    P = Prog(nc)

    def din(name, shape, dt=F32):
        return nc.dram_tensor(name, list(shape), dt, kind="ExternalInput").ap()

    def dout(name, shape, dt=F32):
        return nc.dram_tensor(name, list(shape), dt, kind="ExternalOutput").ap()

    def dscr(name, shape, dt=F32):
        return nc.dram_tensor(name, list(shape), dt, kind="Internal").ap()

    xp = din("xp", [SEQ, D_MODEL])
    xs = din("xs", [DEC_SEQ, D_MODEL])
    w_in = din("w_in", [D_MODEL, IN_W])
    norm_g = din("norm_g", [1, D_MODEL])
    k_norm_g = din("k_norm_g", [3, 128])
    ident = din("ident", [128, 128])
    ckw = din("ckw", [512, 512])
    cvw = din("cvw", [512, 512])
    consts_d = din("consts", [128, 6, 128])
    conv_w = din("conv_w", [4, 8192])
    a_log = din("a_log", [1, 32])
    dt_bias = din("dt_bias", [1, 32])
    gnorm_g = din("gnorm_g", [1, 128])
    sconv = din("sconv", [3, 8192])
    sgdn = din("sgdn", [32, 128, 128])
    nsac_d = din("nsac", [128, NSAC_W])
    nsas_d = din("nsas", [128, NSAS_W])
    pool_kc = din("pool_kc", [1280 * 128, 512])
    pool_vc = din("pool_vc", [1280 * 128, 512])
    pool_ks = din("pool_ks", [1280 * 128, 512])
    pool_vs = din("pool_vs", [1280 * 128, 512])
    ptab = din("ptab", [1, 128], I32)
    oh_d = din("oh", [33, NSA_L])
    rel_bias = din("rel_bias", [32, 16])
    q_norm_g = din("q_norm_g", [1, 128])
    pe_k = din("pe_k", [32, 128])
    w_k = din("w_k", [32, 1])
    proj_k = din("proj_k", [128, 128])
    pe_v = din("pe_v", [32, 128])
    w_v = din("w_v", [32, 1])
    proj_v = din("proj_v", [128, 128])
    w_a = din("w_a", [4096, D_MODEL])
    w_b = din("w_b", [D_MODEL, D_MODEL])
    w_o = din("w_o", [D_MODEL, D_MODEL])

    o_y_p = dout("y_p", [SEQ, D_MODEL])
    o_y_s = dout("y_s", [DEC_SEQ, D_MODEL])
    o_p_kc = dout("p_kc", [SEQ, 512])
    o_p_vc = dout("p_vc", [SEQ, 512])
    o_p_ks = dout("p_ks", [SEQ, 512])
    o_p_vs = dout("p_vs", [SEQ, 512])
    o_p_kw = dout("p_kw", [512, 512])
    o_p_vw = dout("p_vw", [512, 512])
    o_p_conv = dout("p_conv", [3, 8192])
    o_p_gdn = dout("p_gdn", [32 * 128, 128])
    o_s_kc = dout("s_kc", [DEC_SEQ, 512])
    o_s_vc = dout("s_vc", [DEC_SEQ, 512])
    o_s_ks = dout("s_ks", [DEC_SEQ, 512])
    o_s_vs = dout("s_vs", [DEC_SEQ, 512])
    o_s_kw = dout("s_kw", [512, 512])
    o_s_vw = dout("s_vw", [512, 512])
    o_s_conv = dout("s_conv", [3, 8192])
    o_s_gdn = dout("s_gdn", [32 * 128, 128])

    U = dscr("U", [NTOK, IN_W])
    OA = dscr("OA", [NTOK, 4096])
    OB = dscr("OB", [NTOK, 2048])
    KWN = dscr("KWN", [NTOK, 512])
    TVd = nc.dram_tensor("TV", [16, NSA_L], BF16, kind="Internal").ap()
    Gd = nc.dram_tensor("G", [16, 128, NSA_L], BF16, kind="Internal").ap()
    M1 = dscr("M1", [NTOK, D_MODEL])
    M2 = dscr("M2", [NTOK, D_MODEL])

    NW = 48640
    big = P.stack.enter_context(nc.sbuf_tensor("big", [128, NW], F32))
    cv = Carver(big, NW)
    banks = [P.stack.enter_context(nc.psum_tensor("bank%d" % i, [128, 512], F32)) for i in range(8)]

    idf = cv.f32(128)
    idb = cv.bf16(128)
    epsb = cv.f32(1)
    base0 = cv.off

    P.dma("sync", idf, ident, writes=["idf"])
    P.op("vector", lambda e: e.tensor_copy(out=idb, in_=idf), reads=["idf"], writes=["idb"])
    P.op("vector", lambda e: e.memset(epsb, EPS), writes=["epsb"])

    NT = 17
    KC = 16

    def rows(i):
        return 128 if i < 16 else DEC_SEQ

    cv.reset(base0)
    hT = cv.bf16(KC * NTOK).rearrange("p (k t) -> p k t", k=KC)
    gb = cv.f32(D_MODEL)
    wf = [cv.f32(KC * 512).rearrange("p (k n) -> p k n", k=KC) for _ in range(2)]
    wb = [cv.bf16(KC * 512).rearrange("p (k n) -> p k n", k=KC) for _ in range(2)]
    hb = cv.bf16(D_MODEL)
    ss = cv.f32(1)
    rstd = cv.f32(1)
    ost = [cv.f32(512) for _ in range(4)]
    xt = [wf[i].rearrange("p k n -> p (k n)")[:, 0:D_MODEL] for i in range(2)]
    sq = cv.f32(D_MODEL)

    P.dma("sync", gb, norm_g.broadcast_to([128, D_MODEL]), writes=["gb"])

    for i in range(NT):
        r = rows(i)
        xi = xt[i % 2]
        xk = "wf%d" % (i % 2)
        src = xp[i * 128:(i + 1) * 128, :] if i < 16 else xs
        P.dma("sync", xi[:r, :], src, writes=[xk])
        P.op("scalar", lambda e, xi=xi, r=r: e.activation(out=sq[:r, :], in_=xi[:r, :], func=AF.Square,
                                                            accum_out=ss[:r, :]),
             reads=[xk], writes=["sq", "ss"])
        P.op("scalar", lambda e, r=r: e.activation(out=rstd[:r, :], in_=ss[:r, :], func=AF.Ln,
                                                    scale=1.0 / D_MODEL, bias=epsb[:r, :]),
             reads=["ss", "epsb"], writes=["rstd"])
        P.op("scalar", lambda e, r=r: e.activation(out=rstd[:r, :], in_=rstd[:r, :], func=AF.Exp, scale=-0.5),
             reads=["rstd"], writes=["rstd"])
        P.op("vector", lambda e, xi=xi, r=r: e.scalar_tensor_tensor(out=hb[:r, :], in0=xi[:r, :], scalar=rstd[:r, 0:1],
                                                                     in1=gb[:r, :], op0=ALU.mult, op1=ALU.mult),
             reads=[xk, "rstd", "gb"], writes=["hb"])
        for k4 in range(4):
            bk = banks[k4 % 2]
            bkey = "bank%d" % (k4 % 2)
            pT = bk[:].bitcast(BF16)
            for j in range(4):
                k = k4 * 4 + j
                P.op("tensor", lambda e, k=k, j=j, r=r, pT=pT: e.transpose(
                    out=pT[:, j * 128:j * 128 + r], in_=hb[:r, k * 128:(k + 1) * 128], identity=idb[:r, :r]),
                    reads=["hb", "idb"], writes=[bkey], skip_same=True)
            eng = "vector" if k4 % 2 == 0 else "gpsimd"
            eng = "vector"
            P.op(eng, lambda e, k4=k4, r=r, pT=pT, i=i: e.tensor_copy(
                out=hT[:, k4 * 4:(k4 + 1) * 4, i * 128:i * 128 + r],
                in_=pT[:, 0:512].rearrange("p (j t) -> p j t", j=4)[:, :, 0:r]),
                reads=[bkey], writes=["hT"])

    NCB = (IN_W + 511) // 512
    ev = 0
    for cb in range(NCB):
        c0 = cb * 512
        cw = min(512, IN_W - c0)
        wfi, wbi = wf[cb % 2], wb[cb % 2]
        wfk, wbk = "wf%d" % (cb % 2), "wb%d" % (cb % 2)
        wsrc = w_in[:, c0:c0 + cw].rearrange("(k p) n -> p k n", p=128)
        P.dma("sync", wfi[:, 0:8, 0:cw], wsrc[:, 0:8, :], writes=[wfk])
        P.dma("sync", wfi[:, 8:16, 0:cw], wsrc[:, 8:16, :], writes=[wfk])
        P.op("gpsimd", lambda e, wfi=wfi, wbi=wbi, cw=cw: e.tensor_copy(out=wbi[:, :, 0:cw], in_=wfi[:, :, 0:cw]),
             reads=[wfk], writes=[wbk])
        for i in range(NT):
            r = rows(i)
            bi = 2 + (ev % 4)
            bk, bkey = banks[bi], "bank%d" % bi
            for k in range(KC):
                P.op("tensor", lambda e, k=k, i=i, r=r, bk=bk, wbi=wbi, cw=cw: e.matmul(
                    bk[:r, 0:cw], lhsT=hT[:, k, i * 128:i * 128 + r], rhs=wbi[:, k, 0:cw],
                    start=(k == 0), stop=(k == KC - 1)),
                    reads=["hT", wbk], writes=[bkey], skip_same=True)
            o = ost[ev % 4]
            okey = "ost%d" % (ev % 4)
            if ev % 2 == 0:
                P.op("scalar", lambda e, o=o, bk=bk, r=r, cw=cw: e.copy(out=o[:r, 0:cw], in_=bk[:r, 0:cw]),
                     reads=[bkey], writes=[okey])
            else:
                P.op("vector", lambda e, o=o, bk=bk, r=r, cw=cw: e.tensor_copy(out=o[:r, 0:cw], in_=bk[:r, 0:cw]),
                     reads=[bkey], writes=[okey])
            P.dma("scalar" if ev % 2 == 0 else "gpsimd", U[i * 128:i * 128 + r, c0:c0 + cw], o[:r, 0:cw],
                  reads=[okey], writes=["U"], group=okey)
            ev += 1

    P.barrier()

    def cp(dst, src, key):
        P.dma("sync", dst, src, reads=["U"], writes=[key])

    cp(o_p_kc, U[0:SEQ, C_KC:C_KC + 512], "o_p_kc")
    cp(o_p_vc, U[0:SEQ, C_VC:C_VC + 512], "o_p_vc")
    cp(o_p_vs, U[0:SEQ, C_VS:C_VS + 512], "o_p_vs")
    cp(o_p_vw, U[SEQ - 512:SEQ, C_VW:C_VW + 512], "o_p_vw")
    cp(o_p_conv, U[SEQ - 3:SEQ, 0:8192], "o_p_conv")
    cp(o_s_kc, U[SEQ:NTOK, C_KC:C_KC + 512], "o_s_kc")
    cp(o_s_vc, U[SEQ:NTOK, C_VC:C_VC + 512], "o_s_vc")
    cp(o_s_vs, U[SEQ:NTOK, C_VS:C_VS + 512], "o_s_vs")
    cp(o_s_conv, U[NTOK - 3:NTOK, 0:8192], "o_s_conv")
    cp(o_s_vw[0:504, :], cvw[8:512, :], "o_s_vw")
    cp(o_s_vw[504:512, :], U[SEQ:NTOK, C_VW:C_VW + 512], "o_s_vw")
    cp(o_s_kw[0:504, :], ckw[8:512, :], "o_s_kw")

    cv.reset(base0)
    kg = cv.f32(2 * 512)
    kt = [cv.f32(512) for _ in range(2)]
    ksq = cv.f32(512)
    kss = cv.f32(4)
    krs = cv.f32(4)
    kn = [cv.f32(512) for _ in range(2)]
    for wi, row in enumerate((1, 2)):
        for g in range(4):
            P.dma("sync", kg[:, wi * 512 + g * 128: wi * 512 + (g + 1) * 128],
                  k_norm_g[row:row + 1, :].broadcast_to([128, 128]), writes=["kg"])
    cnt = 0
    for wi, col in enumerate((C_KS, C_KW)):
        for i in range(NT):
            r = rows(i)
            t = kt[cnt % 2]
            tk = "kt%d" % (cnt % 2)
            o = kn[cnt % 2]
            ok = "kn%d" % (cnt % 2)
            P.dma("sync", t[:r, :], U[i * 128:i * 128 + r, col:col + 512], reads=["U"], writes=[tk])
            P.op("vector", lambda e, t=t, r=r: e.tensor_tensor(out=ksq[:r, :], in0=t[:r, :], in1=t[:r, :], op=ALU.mult),
                 reads=[tk], writes=["ksq"])
            P.op("vector", lambda e, r=r: e.tensor_reduce(out=kss[:r, :], in_=ksq[:r, :].rearrange("p (g d) -> p g d", g=4),
                                                           axis=AX.X, op=ALU.add),
                 reads=["ksq"], writes=["kss"])
            P.op("scalar", lambda e, r=r: e.activation(out=krs[:r, :], in_=kss[:r, :], func=AF.Ln, scale=1.0 / 128,
                                                        bias=epsb[:r, :]),
                 reads=["kss", "epsb"], writes=["krs"])
            P.op("scalar", lambda e, r=r: e.activation(out=krs[:r, :], in_=krs[:r, :], func=AF.Exp, scale=-0.5),
                 reads=["krs"], writes=["krs"])
            for g in range(4):
                P.op("vector", lambda e, t=t, o=o, r=r, g=g, wi=wi: e.scalar_tensor_tensor(
                    out=o[:r, g * 128:(g + 1) * 128], in0=t[:r, g * 128:(g + 1) * 128], scalar=krs[:r, g:g + 1],
                    in1=kg[:r, wi * 512 + g * 128: wi * 512 + (g + 1) * 128], op0=ALU.mult, op1=ALU.mult),
                    reads=[tk, "krs", "kg"], writes=[ok])
            if wi == 0:
                dst = o_p_ks[i * 128:(i + 1) * 128, :] if i < 16 else o_s_ks
                dk = "o_p_ks" if i < 16 else "o_s_ks"
            else:
                P.dma("scalar", KWN[i * 128:i * 128 + r, :], o[:r, :], reads=[ok], writes=["KWN"], group=ok)
                if i < 12:
                    cnt += 1
                    continue
                dst = o_p_kw[(i - 12) * 128:(i - 11) * 128, :] if i < 16 else o_s_kw[504:512, :]
                dk = "o_p_kw" if i < 16 else "o_s_kw"
            P.dma("scalar", dst, o[:r, :], reads=[ok], writes=[dk], group=ok)
            cnt += 1

    P.barrier()
    cv.reset(base0)
    stage_gdn(P, nc, cv, banks, U, OA, consts_d, conv_w, a_log, dt_bias, gnorm_g, sconv, sgdn,
              o_p_gdn, o_s_gdn, idb, epsb, list(range(16)), 16)

    P.barrier()
    cv.reset(base0)
    stage_nsa_prompt(P, nc, cv, banks, U, o_p_ks, "o_p_ks", KWN, OB, nsac_d, oh_d, TVd, Gd, rel_bias, q_norm_g, k_norm_g,
                     pe_k, w_k, proj_k, pe_v, w_v, proj_v, idf, idb, epsb, 16)

    P.barrier()
    cv.reset(base0)
    stage_nsa_sample(P, nc, cv, banks, U, o_s_ks, "o_s_ks", KWN, OB, nsac_d, nsas_d, TVd, Gd, q_norm_g, k_norm_g,
                     pe_k, w_k, proj_k, pe_v, w_v, proj_v, pool_kc, pool_vc, pool_ks, pool_vs, ckw, cvw, ptab,
                     idf, idb, epsb)

    def xsrc(i, r, c0, cw):
        return (xp[i * 128:i * 128 + r, c0:c0 + cw], "xp") if i < 16 else (xs[0:r, c0:c0 + cw], "xs")

    def ydst(i, r, c0, cw):
        return (o_y_p[i * 128:i * 128 + r, c0:c0 + cw], "o_y_p") if i < 16 else (o_y_s[0:r, c0:c0 + cw], "o_y_s")

    stage_merge(P, cv, banks, idb, base0, U, OA, OB, M1, M2, w_a, w_b, w_o, xsrc, ydst, NTOK)

    P.barrier()
    P.emit()
    return nc


_NC_CACHE = {}


def kernel(**inputs):
    f = lambda a: np.ascontiguousarray(np.asarray(a, dtype=np.float32))
    x_prompt = f(inputs["x_prompt"])
    x_sample = f(inputs["x_sample"])
    w_in = f(inputs["w_in"])
    norm_g = f(inputs["norm_g"]).reshape(1, D_MODEL)
    k_norm_g = f(inputs["k_norm_g"])
    ckw = f(inputs["cache_k_win"])
    cvw = f(inputs["cache_v_win"])
    ident = np.eye(128, dtype=np.float32)
    consts = make_consts()
    conv_w = f(inputs["gdn_conv_w"])
    a_log = f(inputs["gdn_a_log"]).reshape(1, 32)
    dt_bias = f(inputs["gdn_dt_bias"]).reshape(1, 32)
    gnorm_g = f(inputs["gdn_norm_g"]).reshape(1, 128)
    sconv = f(inputs["state_conv"])
    sgdn = f(inputs["state_gdn"])
    oh, nsac = make_nsa_consts(16)
    nsas = make_nsa_sample_consts()
    pool_kc = f(inputs["cache_k_cmp"]).reshape(1280 * 128, 512)
    pool_vc = f(inputs["cache_v_cmp"]).reshape(1280 * 128, 512)
    pool_ks = f(inputs["cache_k_sel"]).reshape(1280 * 128, 512)
    pool_vs = f(inputs["cache_v_sel"]).reshape(1280 * 128, 512)
    ptab = np.ascontiguousarray(np.asarray(inputs["page_table"], dtype=np.int32))
    rel_bias = f(inputs["rel_bias"])
    q_norm_g = f(inputs["q_norm_g"]).reshape(1, 128)
    pe_k = f(inputs["cmp_pe_k"])
    w_k = f(inputs["cmp_w_k"]).reshape(32, 1)
    proj_k = f(inputs["cmp_proj_k"])
    pe_v = f(inputs["cmp_pe_v"])
    w_v = f(inputs["cmp_w_v"]).reshape(32, 1)
    proj_v = f(inputs["cmp_proj_v"])
    w_a = f(inputs["w_branch_a"])
    w_b = f(inputs["w_branch_b"])
    w_o = f(inputs["w_out"])

    if "nc" not in _NC_CACHE:
        _NC_CACHE["nc"] = build_program()
    nc = _NC_CACHE["nc"]

    in_maps = []
    for c in range(8):
        b = c // 2
        in_maps.append({
            "xp": x_prompt[b],
            "xs": x_sample[c],
            "w_in": w_in,
            "norm_g": norm_g,
            "k_norm_g": k_norm_g,
            "ident": ident,
            "ckw": ckw[c].reshape(512, 512),
            "cvw": cvw[c].reshape(512, 512),
            "consts": consts,
            "conv_w": conv_w,
            "a_log": a_log,
            "dt_bias": dt_bias,
            "gnorm_g": gnorm_g,
            "sconv": sconv[c],
            "sgdn": sgdn[c],
            "nsac": nsac,
            "nsas": nsas,
            "pool_kc": pool_kc, "pool_vc": pool_vc, "pool_ks": pool_ks, "pool_vs": pool_vs,
            "ptab": ptab[c:c + 1],
            "oh": oh,
            "rel_bias": rel_bias,
            "q_norm_g": q_norm_g,
            "pe_k": pe_k, "w_k": w_k, "proj_k": proj_k,
            "pe_v": pe_v, "w_v": w_v, "proj_v": proj_v,
            "w_a": w_a,
            "w_b": w_b,
            "w_o": w_o,
        })
    res = run_bass_kernel_spmd(nc, in_maps, core_ids=list(range(8)))
    R = res.results

    def pst(name, shape):
        return np.stack([np.asarray(R[2 * b][name], dtype=np.float32).reshape(shape) for b in range(4)])

    def sst(name, shape):
        return np.stack([np.asarray(R[c][name], dtype=np.float32).reshape(shape) for c in range(8)])

    outs = (
        pst("y_p", (SEQ, D_MODEL)), sst("y_s", (DEC_SEQ, D_MODEL)),
        pst("p_kc", (SEQ, 4, 128)), pst("p_vc", (SEQ, 4, 128)), pst("p_ks", (SEQ, 4, 128)), pst("p_vs", (SEQ, 4, 128)),
        pst("p_kw", (512, 4, 128)), pst("p_vw", (512, 4, 128)),
        pst("p_conv", (3, 8192)), pst("p_gdn", (32, 128, 128)),
        sst("s_kc", (DEC_SEQ, 4, 128)), sst("s_vc", (DEC_SEQ, 4, 128)), sst("s_ks", (DEC_SEQ, 4, 128)),
        sst("s_vs", (DEC_SEQ, 4, 128)), sst("s_kw", (512, 4, 128)), sst("s_vw", (512, 4, 128)),
        sst("s_conv", (3, 8192)), sst("s_gdn", (32, 128, 128)),
    )
    return outs
```

```python
import os
import math
from contextlib import ExitStack
import numpy as np
import concourse.bass as bass
import concourse.mybir as mybir
from concourse.bass_utils import run_bass_kernel_spmd

F32 = mybir.dt.float32
BF16 = mybir.dt.bfloat16
I32 = mybir.dt.int32
ALU = mybir.AluOpType
AF = mybir.ActivationFunctionType
AX = mybir.AxisListType

EPOCH = 20000

D_MODEL = 2048
SEQ = 2048
DEC_SEQ = 8
NTOK = SEQ + DEC_SEQ
IN_W = 23664
C_Q, C_K, C_V, C_Z = 0, 2048, 4096, 8192
C_B, C_A = 12288, 12320
C_QB = 12352
C_KC, C_VC, C_KS, C_VS, C_KW, C_VW = 14400, 14912, 15424, 15936, 16448, 16960
C_GB = 17472
C_ZB = 17520
C_MA, C_MB = 19568, 21616
EPS = 1e-6


class Prog:
    ENGS = ("sync", "scalar", "vector", "gpsimd", "tensor")

    def __init__(self, nc):
        self.nc = nc
        self.stack = ExitStack()
        self.streams = {e: [] for e in self.ENGS}
        self.count = {e: 0 for e in self.ENGS}
        self.waited = {e: {} for e in self.ENGS}
        self.res = {}
        self.dma_sem = {}
        self.semkeys = set()
        self.n_ops = 0
        self.limit = int(os.environ.get("PROG_LIMIT", "1000000000"))
        self.alias = {}
        self.free_phys = []
        self.n_phys = 0
        self.log = []

    def _r(self, key):
        if key not in self.res:
            self.res[key] = [[], []]
        return self.res[key]

    def _deps(self, eng, reads, writes, skip_same=False):
        ev = {}

        def add(e):
            k, v = e
            if skip_same and k[0] == "eng" and k[1] == eng:
                return
            if ev.get(k, 0) < v:
                ev[k] = v
        for r in reads:
            for e in self._r(r)[0]:
                add(e)
        for w in writes:
            st = self._r(w)
            for e in st[0]:
                add(e)
            for e in st[1]:
                add(e)
        out = []
        wd = self.waited[eng]
        for k, v in ev.items():
            if wd.get(k, 0) < v:
                wd[k] = v
                out.append((k, v))
        return out

    def _commit(self, event, reads, writes):
        for r in reads:
            if r in writes:
                continue
            st = self._r(r)
            st[1] = [e for e in st[1] if e[0] != event[0]] + [event]
        for w in writes:
            st = self._r(w)
            st[0] = [e for e in st[0] if e[0] != event[0]] + [event]
            st[1] = []

    def op(self, eng, fn, reads=(), writes=(), skip_same=False):
        if self.n_ops >= self.limit:
            return
        self.log.append((eng, "op", tuple(reads), tuple(writes)))
        waits = self._deps(eng, reads, writes, skip_same)
        self.count[eng] += 1
        n = self.count[eng]
        ep = (n - 1) // EPOCH
        key = ("eng", eng, ep)
        val = n - ep * EPOCH
        self.semkeys.add(key)
        for k, _ in waits:
            self.semkeys.add(k)
        self.streams[eng].append((waits, fn, key, 1))
        self._commit((key, val), reads, writes)
        self.n_ops += 1

    def _phys(self, name):
        if name not in self.alias:
            if self.free_phys:
                self.free_phys.sort(key=lambda p: self.dma_sem.get(p, 0))
                self.alias[name] = self.free_phys.pop(0)
            else:
                self.alias[name] = "phys%d" % self.n_phys
                self.n_phys += 1
        return self.alias[name]

    def dma(self, q, out, in_, reads=(), writes=(), group=None, **kw):
        if self.n_ops >= self.limit:
            return
        self.log.append((q, "dma", tuple(reads), tuple(writes)))
        waits = self._deps(q, reads, writes)
        g = self._phys(group if group is not None else writes[0])
        key = ("dma", g)
        self.dma_sem[g] = self.dma_sem.get(g, 0) + 16
        val = self.dma_sem[g]
        self.semkeys.add(key)
        for k, _ in waits:
            self.semkeys.add(k)
        fn = (lambda e, out=out, in_=in_, kw=kw: e.dma_start(out=out, in_=in_, **kw))
        self.streams[q].append((waits, fn, key, 16))
        self._commit((key, val), reads, writes)
        self.n_ops += 1

    def dma_fn(self, q, fn, reads=(), writes=(), group=None):
        if self.n_ops >= self.limit:
            return
        self.log.append((q, "dmafn", tuple(reads), tuple(writes)))
        waits = self._deps(q, reads, writes)
        g = self._phys(group if group is not None else writes[0])
        key = ("dma", g)
        self.dma_sem[g] = self.dma_sem.get(g, 0) + 16
        val = self.dma_sem[g]
        self.semkeys.add(key)
        for k, _ in waits:
            self.semkeys.add(k)
        self.streams[q].append((waits, fn, key, 16))
        self._commit((key, val), reads, writes)
        self.n_ops += 1

    def barrier(self):
        if os.environ.get("PROG_VERBOSE"):
            print("barrier at op", self.n_ops, flush=True)
        evs = []
        for e in self.ENGS:
            n = self.count[e]
            if n > 0:
                ep = (n - 1) // EPOCH
                evs.append((("eng", e, ep), n - ep * EPOCH))
        for g, v in self.dma_sem.items():
            evs.append((("dma", g), v))
        for e in self.ENGS:
            waits = []
            for k, v in evs:
                if k[0] == "eng" and k[1] == e:
                    continue
                if self.waited[e].get(k, 0) < v:
                    self.waited[e][k] = v
                    waits.append((k, v))
                    self.semkeys.add(k)
            self.streams[e].append((waits, None, None, 0))
        self.free_phys = sorted(set(self.free_phys) | set(self.alias.values()))
        self.alias = {}

    def emit(self):
        nc = self.nc
        sems = {}
        for i, k in enumerate(sorted(self.semkeys, key=str)):
            sems[k] = self.stack.enter_context(nc.semaphore("s%d" % i))
        self.nsems = len(sems)
        streams = self.streams

        def run(e, lst):
            for waits, fn, key, inc in lst:
                for k, v in waits:
                    e.wait_ge(sems[k], v)
                if fn is not None:
                    fn(e).then_inc(sems[key], inc)

        with nc.Block() as block:
            @block.sync
            def _(e):
                run(e, streams["sync"])

            @block.scalar
            def _(e):
                run(e, streams["scalar"])

            @block.vector
            def _(e):
                run(e, streams["vector"])

            @block.gpsimd
            def _(e):
                run(e, streams["gpsimd"])

            @block.tensor
            def _(e):
                run(e, streams["tensor"])
        self.stack.close()


class Carver:
    def __init__(self, big, nwords):
        self.big = big
        self.n = nwords
        self.off = 0

    def reset(self, off=0):
        self.off = off

    def f32(self, nwords, shape=None):
        a = self.big[:, self.off:self.off + nwords]
        self.off += (nwords + 7) // 8 * 8
        assert self.off <= self.n, ("SBUF overflow", self.off, self.n)
        return a

    def bf16(self, nelem):
        assert nelem % 2 == 0
        return self.f32(nelem // 2).bitcast(BF16)


def make_consts():
    c = np.zeros((128, 6, 128), np.float32)
    p = np.arange(128)[:, None]
    f = np.arange(128)[None, :]
    c[:, 0, :] = (p <= f)
    c[:, 1, :] = np.where(p > f, 0.0, -1e30)
    c[:, 2, :] = np.where(f > p, 0.0, -1e30)
    c[:, 3, :] = np.where(f >= p, 0.0, -1e30)
    c[:, 4, :] = 1.0
    c[:, 5, :] = (p == f)
    return c


def stage_gdn(P, nc, cv, banks, U, OA, consts_d, conv_w, a_log, dt_bias, gnorm_g, sconv, sgdn,
              o_p_gdn, o_s_gdn, idb, epsb, hq_list, n_ptiles):
    from concourse.ap import AP
    SEQ_ = n_ptiles * 128
    NT = n_ptiles + 1
    IW = U.shape[1]
    V = lambda fn, r, w: P.op("vector", fn, r, w)
    S = lambda fn, r, w: P.op("scalar", fn, r, w)
    G = lambda fn, r, w: P.op("gpsimd", fn, r, w)
    T = lambda fn, r, w: P.op("tensor", fn, r, w, skip_same=True)

    cst = cv.f32(6 * 128).rearrange("p (a b) -> p a b", a=6)
    tri, maskL, maskU, maskUE, ones_f, idf = (cst[:, k, :] for k in range(6))
    oneb = cv.f32(1)
    ba = cv.f32(NT * 64).rearrange("p (i c) -> p i c", i=NT)
    def gate_buf():
        return cv.f32(NT * 32).rearrange("p (i c) -> p i c", i=NT)
    lnb, beta, g_all, gc_all, a_all, ea_all, ngc_all, eg_all = (gate_buf() for _ in range(8))
    nA = cv.f32(32)
    dtb = cv.f32(32)
    gng = cv.f32(128)
    Sp = cv.f32(32 * 128).rearrange("p (h e) -> p h e", h=32)
    Ss = cv.f32(32 * 128).rearrange("p (h e) -> p h e", h=32)
    Sp_bf = cv.bf16(32 * 128).rearrange("p (h e) -> p h e", h=32)
    Ss_bf = cv.bf16(32 * 128).rearrange("p (h e) -> p h e", h=32)
    bT, bK, bM, bX, bR, bV, bS, bO = banks
    bTb = bT[:].bitcast(BF16)

    P.dma("sync", cst, consts_d, writes=["cst"])
    V(lambda e: e.memset(oneb, 1.0), [], ["oneb"])
    G(lambda e: e.memset(ba, 0.0), [], ["ba"])
    P.dma("sync", ba[:, 0:n_ptiles, :], U[0:SEQ_, C_B:C_B + 64].rearrange("(i p) c -> p i c", p=128),
          reads=["U"], writes=["ba"])
    P.dma("sync", ba[:DEC_SEQ, n_ptiles, :], U[SEQ_:SEQ_ + DEC_SEQ, C_B:C_B + 64], reads=["U"], writes=["ba"])
    P.dma("sync", nA, a_log.broadcast_to([128, 32]), writes=["nA"])
    P.dma("sync", dtb, dt_bias.broadcast_to([128, 32]), writes=["dtb"])
    P.dma("sync", gng, gnorm_g.broadcast_to([128, 128]), writes=["gng"])
    for q4 in range(4):
        P.dma("sync", Ss[:, q4 * 8:(q4 + 1) * 8, :], sgdn[q4 * 8:(q4 + 1) * 8].rearrange("h d e -> d h e"), writes=["Ss"])
    V(lambda e: e.memset(Sp, 0.0), [], ["Sp"])
    V(lambda e: e.memset(Sp_bf, 0.0), [], ["Sp_bf"])
    S(lambda e: e.copy(out=Ss_bf, in_=Ss), ["Ss"], ["Ss_bf"])
    S(lambda e: e.activation(out=nA, in_=nA, func=AF.Exp), ["nA"], ["nA"])
    V(lambda e: e.tensor_scalar(out=nA, in0=nA, scalar1=-1.0, scalar2=None, op0=ALU.mult), ["nA"], ["nA"])
    S(lambda e: e.activation(out=lnb, in_=ba[:, :, 0:32], func=AF.Exp, scale=-1.0), ["ba"], ["lnb"])
    S(lambda e: e.activation(out=lnb, in_=lnb, func=AF.Ln, bias=oneb), ["lnb", "oneb"], ["lnb"])
    V(lambda e: e.tensor_scalar(out=lnb, in0=lnb, scalar1=-1.0, scalar2=None, op0=ALU.mult), ["lnb"], ["lnb"])
    S(lambda e: e.activation(out=beta, in_=lnb, func=AF.Exp), ["lnb"], ["beta"])
    for i in range(NT):
        V(lambda e, i=i: e.tensor_tensor(out=g_all[:, i, :], in0=ba[:, i, 32:64], in1=dtb, op=ALU.add),
          ["ba", "dtb"], ["g_all"])
    t1, t2, t3, t4 = a_all, ea_all, ngc_all, eg_all
    V(lambda e: e.tensor_scalar(out=t2, in0=g_all, scalar1=-1.0, scalar2=None, op0=ALU.mult), ["g_all"], ["ea_all"])
    V(lambda e: e.tensor_tensor(out=t1, in0=g_all, in1=t2, op=ALU.max), ["g_all", "ea_all"], ["a_all"])
    S(lambda e: e.activation(out=t1, in_=t1, func=AF.Exp, scale=-1.0), ["a_all"], ["a_all"])
    V(lambda e: e.tensor_scalar(out=t2, in0=t1, scalar1=2.0, scalar2=None, op0=ALU.add), ["a_all"], ["ea_all"])
    V(lambda e: e.reciprocal(out=t2, in_=t2), ["ea_all"], ["ea_all"])
    V(lambda e: e.tensor_tensor(out=t2, in0=t2, in1=t1, op=ALU.mult), ["ea_all", "a_all"], ["ea_all"])
    V(lambda e: e.tensor_tensor(out=t3, in0=t2, in1=t2, op=ALU.mult), ["ea_all"], ["ngc_all"])
    V(lambda e: e.tensor_scalar(out=t4, in0=t3, scalar1=1.0 / 13, scalar2=None, op0=ALU.mult), ["ngc_all"], ["eg_all"])
    for cc in (1.0 / 11, 1.0 / 9, 1.0 / 7, 1.0 / 5, 1.0 / 3):
        V(lambda e, cc=cc: e.scalar_tensor_tensor(out=t4, in0=t4, scalar=cc, in1=t3, op0=ALU.add, op1=ALU.mult),
          ["eg_all", "ngc_all"], ["eg_all"])
    V(lambda e: e.scalar_tensor_tensor(out=t4, in0=t4, scalar=1.0, in1=t2, op0=ALU.add, op1=ALU.mult),
      ["eg_all", "ea_all"], ["eg_all"])
    V(lambda e: e.tensor_scalar(out=g_all, in0=g_all, scalar1=0.0, scalar2=None, op0=ALU.max), ["g_all"], ["g_all"])
    V(lambda e: e.scalar_tensor_tensor(out=g_all, in0=t4, scalar=2.0, in1=g_all, op0=ALU.mult, op1=ALU.add),
      ["eg_all", "g_all"], ["g_all"])
    for i in range(NT):
        V(lambda e, i=i: e.tensor_tensor(out=g_all[:, i, :], in0=g_all[:, i, :], in1=nA, op=ALU.mult),
          ["g_all", "nA"], ["g_all"])
    for i in range(NT):
        r = 128 if i < n_ptiles else DEC_SEQ
        T(lambda e, i=i, r=r: e.matmul(bM[:r, 0:32], lhsT=tri[:r, :r], rhs=g_all[:r, i, :], start=True, stop=True),
          ["cst", "g_all"], ["bM"])
        V(lambda e, i=i, r=r: e.tensor_copy(out=gc_all[:r, i, :], in_=bM[:r, 0:32]), ["bM"], ["gc_all"])
    for i in range(NT):
        r = 128 if i < n_ptiles else DEC_SEQ
        V(lambda e, i=i, r=r: e.tensor_tensor(out=a_all[:r, i, :], in0=lnb[:r, i, :], in1=gc_all[:r, i, :], op=ALU.add),
          ["lnb", "gc_all"], ["a_all"])
        V(lambda e, i=i, r=r: e.tensor_scalar(out=ngc_all[:r, i, :], in0=gc_all[:r, i, :], scalar1=-1.0, scalar2=None,
                                                op0=ALU.mult), ["gc_all"], ["ngc_all"])
        S(lambda e, i=i, r=r: e.activation(out=ea_all[:r, i, :], in_=a_all[:r, i, :], func=AF.Exp), ["a_all"], ["ea_all"])
        S(lambda e, i=i, r=r: e.activation(out=eg_all[:r, i, :], in_=gc_all[:r, i, :], func=AF.Exp), ["gc_all"], ["eg_all"])

    P.barrier()
    SHARED = {"cst", "U", "idb", "epsb", "gc_all", "a_all", "ea_all", "ngc_all", "beta", "eg_all", "gng", "oneb"}

    def build_chain(c, hqs):
        rec = []
        K = lambda names: [n if n in SHARED else "%s_%d" % (n, c) for n in names]
        V = lambda fn, r, w: rec.append(("op", "vector", fn, K(r), K(w), False))
        S = lambda fn, r, w: rec.append(("op", "scalar", fn, K(r), K(w), False))
        G = lambda fn, r, w: rec.append(("op", "gpsimd", fn, K(r), K(w), False))
        T = lambda fn, r, w: rec.append(("op", "tensor", fn, K(r), K(w), True))

        def D(q, out, in_, reads=(), writes=(), group=None):
            rec.append(("dma", q, out, in_, K(reads), K(writes), K([group])[0] if group else None))
        bA, bB, bC, bD = banks[4 * c:4 * c + 4]
        bTb = bA[:].bitcast(BF16)
        bK = bA[:, 256:512]
        bM = bB[:, 0:256]
        bX = bB[:, 256:512]
        bR = bC[:, 0:256]
        bV = bC[:, 256:384]
        bS = bC[:, 384:512]
        bO = bD[:, 0:128]
        Wc = cv.f32(4 * 512).rearrange("p (j c) -> p j c", j=4)
        X = [cv.f32(4 * 512).rearrange("p (j c) -> p j c", j=4) for _ in range(2)]
        Z = [cv.f32(256) for _ in range(2)]
        prod = cv.f32(4 * 512).rearrange("p (j c) -> p j c", j=4)
        conv = cv.f32(512)
        A = cv.f32(512)
        zs = cv.f32(256)
        sq2 = cv.f32(256)
        ss2 = cv.f32(2)
        rq = cv.f32(2)
        qn = cv.f32(128)
        kn = cv.f32(128)
        qn_bf = cv.bf16(128)
        kn_bf = cv.bf16(128)
        kT = cv.bf16(128)
        qT = cv.bf16(128)
        Dg = cv.f32(128)
        Da = cv.f32(128)
        arg = cv.f32(128)
        argT = cv.f32(128)
        argG = cv.f32(128)
        E1 = cv.f32(128)
        E1T = cv.f32(128)
        GT = cv.f32(128)
        XX = [cv.f32(256).rearrange("p (a b) -> p a b", a=2) for _ in range(2)]
        R = [cv.f32(256) for _ in range(2)]
        Rf = cv.f32(256)
        w_bf = cv.bf16(128)
        wT = cv.bf16(128)
        vnew_bf = cv.bf16(128)
        ekd = cv.f32(1)
        gl = cv.f32(1)
        kdec_bf = cv.bf16(128)
        attnT = cv.bf16(128)
        qg_bf = cv.bf16(128)
        qgT = cv.bf16(128)
        o_sb = cv.f32(128)
        osq = cv.f32(128)
        oss = cv.f32(1)
        ors = cv.f32(1)
        ot = cv.f32(128)
        oa = [cv.f32(256) for _ in range(2)]

        cnt = 0
        for hq in hqs:
            segs = ((C_Q + hq * 128, 0, 128), (C_K + hq * 128, 128, 128), (C_V + hq * 256, 256, 256))
            for col, s0, w in segs:
                D("sync", Wc[:, :, s0:s0 + w], AP(conv_w.tensor, col, [[0, 128], [8192, 4], [1, w]]), writes=["Wc"])
            for i in range(NT):
                samp = i == n_ptiles
                r = DEC_SEQ if samp else 128
                t0 = i * 128
                Xi = X[cnt % 2]
                Xk = "X%d" % (cnt % 2)
                Zi = Z[cnt % 2]
                Zk = "Z%d" % (cnt % 2)
                oai = oa[cnt % 2]
                oak = "oa%d" % (cnt % 2)
                cnt += 1
                Sx, Sx_bf, Sk, Sbk = (Ss, Ss_bf, "Ss", "Ss_bf") if samp else (Sp, Sp_bf, "Sp", "Sp_bf")
                if i == 0:
                    G(lambda e, Xi=Xi: e.memset(Xi, 0.0), [], [Xk])
                    for col, s0, w in segs:
                        for j in range(4):
                            sh = 3 - j
                            D("sync", Xi[sh:128, j, s0:s0 + w], U[0:128 - sh, col:col + w], reads=["U"], writes=[Xk])
                elif samp:
                    for col, s0, w in segs:
                        for j in range(4):
                            sh = 3 - j
                            if sh > 0:
                                D("sync", Xi[0:sh, j, s0:s0 + w], sconv[j:3, col:col + w], writes=[Xk])
                            D("sync", Xi[sh:DEC_SEQ, j, s0:s0 + w], U[t0:t0 + DEC_SEQ - sh, col:col + w],
                                  reads=["U"], writes=[Xk])
                else:
                    for col, s0, w in segs:
                        D("sync", Xi[:, :, s0:s0 + w],
                              AP(U.tensor, (t0 - 3) * IW + col, [[IW, 128], [IW, 4], [1, w]]), reads=["U"], writes=[Xk])
                D("sync", Zi[:r, :], U[t0:t0 + r, C_Z + hq * 256:C_Z + hq * 256 + 256], reads=["U"], writes=[Zk])
                G(lambda e, Xi=Xi, r=r: e.tensor_tensor(out=prod[:r], in0=Xi[:r], in1=Wc[:r], op=ALU.mult), [Xk, "Wc"], ["prod"])
                V(lambda e, r=r: e.tensor_reduce(out=conv[:r, :], in_=prod[:r].rearrange("p j c -> p c j"), axis=AX.X, op=ALU.add),
                  ["prod"], ["conv"])
                S(lambda e, r=r: e.activation(out=A[:r, :], in_=conv[:r, :], func=AF.Silu), ["conv"], ["A"])
                S(lambda e, r=r, Zi=Zi: e.activation(out=zs[:r, :], in_=Zi[:r, :], func=AF.Silu), [Zk], ["zs"])
                V(lambda e, r=r: e.tensor_tensor(out=sq2[:r, :], in0=A[:r, 0:256], in1=A[:r, 0:256], op=ALU.mult), ["A"], ["sq2"])
                V(lambda e, r=r: e.tensor_reduce(out=ss2[:r, :], in_=sq2[:r, :].rearrange("p (a b) -> p a b", a=2), axis=AX.X,
                                                  op=ALU.add), ["sq2"], ["ss2"])
                S(lambda e, r=r: e.activation(out=rq[:r, :], in_=ss2[:r, :], func=AF.Ln, bias=epsb[:r, :]), ["ss2", "epsb"], ["rq"])
                S(lambda e, r=r: e.activation(out=rq[:r, :], in_=rq[:r, :], func=AF.Exp, scale=-0.5), ["rq"], ["rq"])
                G(lambda e, r=r: e.tensor_scalar(out=qn[:r, :], in0=A[:r, 0:128], scalar1=rq[:r, 0:1], scalar2=128 ** -0.5,
                                                  op0=ALU.mult, op1=ALU.mult), ["A", "rq"], ["qn"])
                G(lambda e, r=r: e.tensor_scalar(out=kn[:r, :], in0=A[:r, 128:256], scalar1=rq[:r, 1:2], scalar2=None,
                                                  op0=ALU.mult), ["A", "rq"], ["kn"])
                G(lambda e, r=r: e.tensor_copy(out=qn_bf[:r, :], in_=qn[:r, :]), ["qn"], ["qn_bf"])
                G(lambda e, r=r: e.tensor_copy(out=kn_bf[:r, :], in_=kn[:r, :]), ["kn"], ["kn_bf"])
                T(lambda e, r=r: e.transpose(out=bTb[:, 0:r], in_=kn_bf[:r, :], identity=idb[:r, :r]), ["kn_bf", "idb"], ["bT"])
                T(lambda e, r=r: e.transpose(out=bTb[:, 128:128 + r], in_=qn_bf[:r, :], identity=idb[:r, :r]), ["qn_bf", "idb"], ["bT"])
                V(lambda e, r=r: e.tensor_copy(out=kT[:, :r], in_=bTb[:, 0:r]), ["bT"], ["kT"])
                V(lambda e, r=r: e.tensor_copy(out=qT[:, :r], in_=bTb[:, 128:128 + r]), ["bT"], ["qT"])
                T(lambda e, r=r: e.matmul(bK[:r, 0:r], lhsT=kT[:, :r], rhs=kT[:, :r], start=True, stop=True), ["kT"], ["bK"])
                T(lambda e, r=r: e.matmul(bK[:r, 128:128 + r], lhsT=kT[:, :r], rhs=qT[:, :r], start=True, stop=True),
                  ["kT", "qT"], ["bK"])
                for hv in range(2):
                    h = 2 * hq + hv
                    gcc, ac, eac, ngc, bec, egc = (t[:r, i, h:h + 1] for t in (gc_all, a_all, ea_all, ngc_all, beta, eg_all))
                    gk = ["gc_all", "a_all", "ea_all", "ngc_all", "beta", "eg_all"]
                    G(lambda e, r=r, gcc=gcc: e.tensor_scalar(out=Dg[:r, :r], in0=idf[:r, :r], scalar1=gcc, scalar2=None,
                                                               op0=ALU.mult), ["cst"] + gk, ["Dg"])
                    G(lambda e, r=r, ac=ac: e.tensor_scalar(out=Da[:r, :r], in0=idf[:r, :r], scalar1=ac, scalar2=None,
                                                             op0=ALU.mult), ["cst"] + gk, ["Da"])
                    T(lambda e, r=r: e.matmul(bM[:, 0:r], lhsT=ones_f[:r, :], rhs=Dg[:r, :r], start=True, stop=True),
                      ["cst", "Dg"], ["bM"])
                    T(lambda e, r=r: e.matmul(bM[:, 128:128 + r], lhsT=ones_f[:r, :], rhs=Da[:r, :r], start=True, stop=True),
                      ["cst", "Da"], ["bM"])
                    V(lambda e, r=r: e.scalar_tensor_tensor(out=arg[:r, :r], in0=bM[:r, 0:r], scalar=-1.0, in1=maskL[:r, :r],
                                                             op0=ALU.mult, op1=ALU.add), ["bM", "cst"], ["arg"])
                    S(lambda e, r=r, ac=ac: e.activation(out=E1[:r, :r], in_=arg[:r, :r], func=AF.Exp, bias=ac),
                      ["arg"] + gk, ["E1"])
                    V(lambda e, r=r: e.scalar_tensor_tensor(out=XX[0][:r, 0, :r], in0=E1[:r, :r], scalar=-1.0, in1=bK[:r, 0:r],
                                                             op0=ALU.mult, op1=ALU.mult), ["E1", "bK"], ["XX0"])
                    V(lambda e, r=r: e.tensor_tensor(out=argT[:r, :r], in0=bM[:r, 128:128 + r], in1=maskU[:r, :r], op=ALU.add),
                      ["bM", "cst"], ["argT"])
                    S(lambda e, r=r, ngc=ngc: e.activation(out=E1T[:r, :r], in_=argT[:r, :r], func=AF.Exp, bias=ngc),
                      ["argT"] + gk, ["E1T"])
                    V(lambda e, r=r: e.scalar_tensor_tensor(out=XX[0][:r, 1, :r], in0=E1T[:r, :r], scalar=-1.0, in1=bK[:r, 0:r],
                                                             op0=ALU.mult, op1=ALU.mult), ["E1T", "bK"], ["XX0"])
                    V(lambda e, r=r: e.tensor_tensor(out=argG[:r, :r], in0=bM[:r, 0:r], in1=maskUE[:r, :r], op=ALU.add),
                      ["bM", "cst"], ["argG"])
                    S(lambda e, r=r, ngc=ngc: e.activation(out=GT[:r, :r], in_=argG[:r, :r], func=AF.Exp, bias=ngc),
                      ["argG"] + gk, ["GT"])
                    V(lambda e, r=r: e.tensor_tensor(out=attnT[:r, :r], in0=GT[:r, :r], in1=bK[:r, 128:128 + r], op=ALU.mult),
                      ["GT", "bK"], ["attnT"])
                    G(lambda e, r=r, hv=hv, bec=bec: e.tensor_scalar(out=R[0][:r, 0:128], in0=A[:r, 256 + hv * 128:384 + hv * 128],
                                                                      scalar1=bec, scalar2=None, op0=ALU.mult),
                      ["A"] + gk, ["R0"])
                    G(lambda e, r=r, eac=eac: e.tensor_scalar(out=R[0][:r, 128:256], in0=kn[:r, :], scalar1=eac, scalar2=None,
                                                               op0=ALU.mult), ["kn"] + gk, ["R0"])
                    S(lambda e, r=r, ngc=ngc: e.activation(out=ekd[:r, :], in_=bM[:r, r - 1:r], func=AF.Exp, bias=ngc),
                      ["bM"] + gk, ["ekd"])
                    S(lambda e, r=r: e.activation(out=gl, in_=bM[:, r - 1:r], func=AF.Exp), ["bM"], ["gl"])
                    G(lambda e, r=r: e.tensor_scalar(out=kdec_bf[:r, :], in0=kn[:r, :], scalar1=ekd[:r, 0:1], scalar2=None,
                                                      op0=ALU.mult), ["kn", "ekd"], ["kdec_bf"])
                    G(lambda e, r=r, egc=egc: e.tensor_scalar(out=qg_bf[:r, :], in0=qn[:r, :], scalar1=egc, scalar2=None,
                                                               op0=ALU.mult), ["qn"] + gk, ["qg_bf"])
                    T(lambda e, r=r: e.transpose(out=bTb[:, 256:256 + r], in_=qg_bf[:r, :], identity=idb[:r, :r]),
                      ["qg_bf", "idb"], ["bT"])
                    V(lambda e, r=r: e.tensor_copy(out=qgT[:, :r], in_=bTb[:, 256:256 + r]), ["bT"], ["qgT"])
                    for k in range(7):
                        cur, nxt = k % 2, (k + 1) % 2
                        T(lambda e, r=r, cur=cur: e.matmul(bR[:r, 0:256], lhsT=idf[:r, :r], rhs=R[cur][:r, :], start=True, stop=False),
                          ["cst", "R%d" % cur], ["bR"])
                        T(lambda e, r=r, cur=cur: e.matmul(bR[:r, 0:256], lhsT=XX[cur][:r, 1, :r], rhs=R[cur][:r, :], start=False, stop=True),
                          ["XX%d" % cur, "R%d" % cur], ["bR"])
                        if k < 6:
                            S(lambda e, r=r, nxt=nxt: e.copy(out=R[nxt][:r, :], in_=bR[:r, 0:256]), ["bR"], ["R%d" % nxt])
                            T(lambda e, r=r, cur=cur: e.matmul(bX[:r, 0:r], lhsT=XX[cur][:r, 1, :r], rhs=XX[cur][:r, 0, :r], start=True, stop=True),
                              ["XX%d" % cur], ["bX"])
                            T(lambda e, r=r, cur=cur: e.matmul(bX[:r, 128:128 + r], lhsT=XX[cur][:r, 0, :r], rhs=XX[cur][:r, 1, :r], start=True, stop=True),
                              ["XX%d" % cur], ["bX"])
                            V(lambda e, r=r, nxt=nxt: e.tensor_copy(out=XX[nxt][:r, :, :r],
                                                                     in_=bX[:r, 0:256].rearrange("p (a b) -> p a b", a=2)[:, :, 0:r]),
                              ["bX"], ["XX%d" % nxt])
                        else:
                            V(lambda e, r=r: e.tensor_copy(out=Rf[:r, :], in_=bR[:r, 0:256]), ["bR"], ["Rf"])
                            V(lambda e, r=r: e.tensor_copy(out=w_bf[:r, :], in_=bR[:r, 128:256]), ["bR"], ["w_bf"])
                    T(lambda e, r=r: e.transpose(out=bTb[:, 384:384 + r], in_=w_bf[:r, :], identity=idb[:r, :r]), ["w_bf", "idb"], ["bT"])
                    V(lambda e, r=r: e.tensor_copy(out=wT[:, :r], in_=bTb[:, 384:384 + r]), ["bT"], ["wT"])
                    T(lambda e, r=r, h=h, Sx_bf=Sx_bf: e.matmul(bV[:r, 0:128], lhsT=wT[:, :r], rhs=Sx_bf[:, h, :], start=True, stop=True),
                      ["wT", Sbk], ["bV"])
                    V(lambda e, r=r: e.tensor_tensor(out=vnew_bf[:r, :], in0=Rf[:r, 0:128], in1=bV[:r, 0:128], op=ALU.subtract),
                      ["Rf", "bV"], ["vnew_bf"])
                    T(lambda e, r=r, h=h, Sx_bf=Sx_bf: e.matmul(bO[:r, 0:128], lhsT=qgT[:, :r], rhs=Sx_bf[:, h, :], start=True, stop=False),
                      ["qgT", Sbk], ["bO"])
                    T(lambda e, r=r: e.matmul(bO[:r, 0:128], lhsT=attnT[:r, :r], rhs=vnew_bf[:r, :], start=False, stop=True),
                      ["attnT", "vnew_bf"], ["bO"])
                    T(lambda e, r=r: e.matmul(bS[:, 0:128], lhsT=kdec_bf[:r, :], rhs=vnew_bf[:r, :], start=True, stop=True),
                      ["kdec_bf", "vnew_bf"], ["bS"])
                    V(lambda e, h=h, Sx=Sx: e.scalar_tensor_tensor(out=Sx[:, h, :], in0=Sx[:, h, :], scalar=gl[:, 0:1], in1=bS[:, 0:128],
                                                                   op0=ALU.mult, op1=ALU.add), [Sk, "gl", "bS"], [Sk])
                    S(lambda e, h=h, Sx=Sx, Sx_bf=Sx_bf: e.copy(out=Sx_bf[:, h, :], in_=Sx[:, h, :]), [Sk], [Sbk])
                    S(lambda e, r=r: e.copy(out=o_sb[:r, :], in_=bO[:r, 0:128]), ["bO"], ["o_sb"])
                    V(lambda e, r=r: e.tensor_tensor(out=osq[:r, :], in0=o_sb[:r, :], in1=o_sb[:r, :], op=ALU.mult), ["o_sb"], ["osq"])
                    V(lambda e, r=r: e.tensor_reduce(out=oss[:r, :], in_=osq[:r, :], axis=AX.X, op=ALU.add), ["osq"], ["oss"])
                    S(lambda e, r=r: e.activation(out=ors[:r, :], in_=oss[:r, :], func=AF.Ln, scale=1.0 / 128, bias=epsb[:r, :]),
                      ["oss", "epsb"], ["ors"])
                    S(lambda e, r=r: e.activation(out=ors[:r, :], in_=ors[:r, :], func=AF.Exp, scale=-0.5), ["ors"], ["ors"])
                    V(lambda e, r=r: e.scalar_tensor_tensor(out=ot[:r, :], in0=o_sb[:r, :], scalar=ors[:r, 0:1], in1=gng[:r, :],
                                                             op0=ALU.mult, op1=ALU.mult), ["o_sb", "ors", "gng"], ["ot"])
                    G(lambda e, r=r, hv=hv, oai=oai: e.tensor_tensor(out=oai[:r, hv * 128:(hv + 1) * 128], in0=ot[:r, :],
                                                                      in1=zs[:r, hv * 128:(hv + 1) * 128], op=ALU.mult),
                      ["ot", "zs"], [oak])
                D("scalar", OA[t0:t0 + r, hq * 256:(hq + 1) * 256], oai[:r, :], reads=[oak], writes=["OA"], group=oak)
        return rec

    recs = [build_chain(0, hq_list[0::2]), build_chain(1, hq_list[1::2])]
    for k in range(max(len(r_) for r_ in recs)):
        for r_ in recs:
            if k < len(r_):
                it = r_[k]
                if it[0] == "op":
                    P.op(it[1], it[2], it[3], it[4], skip_same=it[5])
                else:
                    P.dma(it[1], it[2], it[3], reads=it[4], writes=it[5], group=it[6])
    for q4 in range(4):
        P.dma("scalar", o_p_gdn[q4 * 1024:(q4 + 1) * 1024, :].rearrange("(h d) e -> d h e", h=8), Sp[:, q4 * 8:(q4 + 1) * 8, :],
              reads=["Sp_0", "Sp_1"], writes=["o_p_gdn"])
        P.dma("scalar", o_s_gdn[q4 * 1024:(q4 + 1) * 1024, :].rearrange("(h d) e -> d h e", h=8), Ss[:, q4 * 8:(q4 + 1) * 8, :],
              reads=["Ss_0", "Ss_1"], writes=["o_s_gdn"])


def stage_linear(P, cv, banks, idb, src, srckey, K, W, N, ntok, groups, CW, epi, nm):
    KC = K // 128
    rows = lambda i: min(128, ntok - i * 128)
    gmax = max((len(g) - 1) * 128 + (rows(g[-1]) + 7) // 8 * 8 for g in groups)
    hT = cv.bf16(KC * gmax).rearrange("p (k t) -> p k t", k=KC)
    wf = [cv.f32(KC * CW).rearrange("p (k n) -> p k n", k=KC) for _ in range(2)]
    wb = [cv.bf16(KC * CW).rearrange("p (k n) -> p k n", k=KC) for _ in range(2)]
    hb = cv.bf16(K)
    xt = [w.rearrange("p k n -> p (k n)")[:, 0:K] for w in wf]
    k_hT, k_hb = nm + "hT", nm + "hb"
    k_wf = [nm + "wf0", nm + "wf1"]
    k_wb = [nm + "wb0", nm + "wb1"]
    NCB = N // CW
    ev = 0
    ld = 0
    for g in groups:
        for li, i in enumerate(g):
            r = rows(i)
            xi, xk = xt[ld % 2], k_wf[ld % 2]
            ld += 1
            P.dma("sync", xi[:r, :], src[i * 128:i * 128 + r, 0:K], reads=[srckey], writes=[xk])
            P.op("gpsimd", lambda e, xi=xi, r=r: e.tensor_copy(out=hb[:r, :], in_=xi[:r, :]), reads=[xk], writes=[k_hb])
            for k4 in range(KC // 4):
                bk = banks[k4 % 2]
                bkey = "bank%d" % (k4 % 2)
                pT = bk[:].bitcast(BF16)
                for j in range(4):
                    k = k4 * 4 + j
                    P.op("tensor", lambda e, k=k, j=j, r=r, pT=pT: e.transpose(
                        out=pT[:, j * 128:j * 128 + r], in_=hb[:r, k * 128:(k + 1) * 128], identity=idb[:r, :r]),
                        reads=[k_hb, "idb"], writes=[bkey], skip_same=True)
                P.op("vector", lambda e, k4=k4, r=r, pT=pT, li=li: e.tensor_copy(
                    out=hT[:, k4 * 4:(k4 + 1) * 4, li * 128:li * 128 + r],
                    in_=pT[:, 0:512].rearrange("p (j t) -> p j t", j=4)[:, :, 0:r]),
                    reads=[bkey], writes=[k_hT])
        for cb in range(NCB):
            c0 = cb * CW
            wfi, wbi = wf[ld % 2], wb[ld % 2]
            wfk, wbk = k_wf[ld % 2], k_wb[ld % 2]
            ld += 1
            wsrc = W[:, c0:c0 + CW].rearrange("(k p) n -> p k n", p=128)
            for k8 in range(KC // 8):
                P.dma("sync", wfi[:, k8 * 8:(k8 + 1) * 8, :], wsrc[:, k8 * 8:(k8 + 1) * 8, :], writes=[wfk])
            P.op("gpsimd", lambda e, wfi=wfi, wbi=wbi: e.tensor_copy(out=wbi, in_=wfi), reads=[wfk], writes=[wbk])
            for li, i in enumerate(g):
                r = rows(i)
                bi = 2 + (ev % 4)
                bk, bkey = banks[bi], "bank%d" % bi
                for k in range(KC):
                    P.op("tensor", lambda e, k=k, li=li, r=r, bk=bk, wbi=wbi: e.matmul(
                        bk[:r, 0:CW], lhsT=hT[:, k, li * 128:li * 128 + r], rhs=wbi[:, k, :],
                        start=(k == 0), stop=(k == KC - 1)),
                        reads=[k_hT, wbk], writes=[bkey], skip_same=True)
                epi(i, r, c0, CW, bk, bkey, ev)
                ev += 1


def make_epi(P, cv, nm, CW, dstf, gatef=None, addf=None):
    ost = [cv.f32(CW) for _ in range(4)]
    gt = [cv.f32(CW) for _ in range(4)] if gatef else None
    at = [cv.f32(CW) for _ in range(4)] if addf else None

    def epi(i, r, c0, cw, bk, bkey, ev):
        s = ev % 4
        ok = "%sost%d" % (nm, s)
        if gatef:
            gap, gkey = gatef(i, r, c0, cw)
            gk = "%sgt%d" % (nm, s)
            P.dma("sync", gt[s][:r, :cw], gap, reads=[gkey], writes=[gk])
            P.op("scalar", lambda e, s=s, r=r, cw=cw: e.activation(out=gt[s][:r, :cw], in_=gt[s][:r, :cw], func=AF.Sigmoid),
                 reads=[gk], writes=[gk])
        if addf:
            aap, akey = addf(i, r, c0, cw)
            ak = "%sat%d" % (nm, s)
            P.dma("sync", at[s][:r, :cw], aap, reads=[akey], writes=[ak])
        if gatef:
            P.op("vector", lambda e, s=s, r=r, cw=cw, bk=bk: e.tensor_tensor(out=ost[s][:r, :cw], in0=bk[:r, :cw], in1=gt[s][:r, :cw],
                                                                             op=ALU.mult), reads=[bkey, gk], writes=[ok])
            if addf:
                P.op("gpsimd", lambda e, s=s, r=r, cw=cw: e.tensor_tensor(out=ost[s][:r, :cw], in0=ost[s][:r, :cw], in1=at[s][:r, :cw],
                                                                          op=ALU.add), reads=[ok, ak], writes=[ok])
        else:
            P.op("vector", lambda e, s=s, r=r, cw=cw, bk=bk: e.tensor_tensor(out=ost[s][:r, :cw], in0=bk[:r, :cw], in1=at[s][:r, :cw],
                                                                             op=ALU.add), reads=[bkey, ak], writes=[ok])
        dap, dkey = dstf(i, r, c0, cw)
        P.dma("scalar" if ev % 2 == 0 else "gpsimd", dap, ost[s][:r, :cw], reads=[ok], writes=[dkey], group=ok)
    return epi


def stage_merge(P, cv, banks, idb, base0, U, OA, OB, M1, M2, w_a, w_b, w_o, xsrc, ydst, ntok):
    nt = (ntok + 127) // 128
    alltiles = list(range(nt))
    half = (nt + 1) // 2
    sl = lambda A, key: (lambda i, r, c0, cw: (A[i * 128:i * 128 + r, c0:c0 + cw], key))
    P.barrier()
    cv.reset(base0)
    epi = make_epi(P, cv, "m", 256, sl(M1, "M1"), gatef=lambda i, r, c0, cw: (U[i * 128:i * 128 + r, C_MA + c0:C_MA + c0 + cw], "U"))
    stage_linear(P, cv, banks, idb, OA, "OA", 4096, w_a, D_MODEL, ntok, [alltiles[:half], alltiles[half:]], 256, epi, "m")
    P.barrier()
    cv.reset(base0)
    epi = make_epi(P, cv, "m", 512, sl(M2, "M2"), gatef=lambda i, r, c0, cw: (U[i * 128:i * 128 + r, C_MB + c0:C_MB + c0 + cw], "U"),
                   addf=sl(M1, "M1"))
    stage_linear(P, cv, banks, idb, OB, "OB", 2048, w_b, D_MODEL, ntok, [alltiles], 512, epi, "m")
    P.barrier()
    cv.reset(base0)
    epi = make_epi(P, cv, "m", 512, ydst, addf=xsrc)
    stage_linear(P, cv, banks, idb, M2, "M2", 2048, w_o, D_MODEL, ntok, [alltiles], 512, epi, "m")


NSA_L = 4352
NSA_R0 = 2176
NSAC_W = 3504
NEGM = -30000.0


def t5_bucket_np(rel):
    rel = np.asarray(rel, np.int64)
    n = np.maximum(rel, 0)
    nf = np.maximum(n, 1).astype(np.float32)
    large = 16 + (np.log(nf / np.float32(16)) / np.float32(math.log(8.0)) * np.float32(16)).astype(np.int32)
    return np.where(n < 16, n, np.minimum(large, 31))


def make_nsa_consts(n_ptiles):
    L, R0 = NSA_L, NSA_R0
    seq = n_ptiles * 128
    ncb = seq // 16 - 1
    oh = np.zeros((33, L), np.float32)
    rel = np.arange(L) - R0
    b = np.where(rel < 0, 32, t5_bucket_np(rel))
    oh[b, np.arange(L)] = 1.0
    c = np.zeros((128, NSAC_W), np.float32)
    j = np.arange(32)
    lo = np.clip((64 * j - 32) // 16 + 1, 0, ncb)
    hi = np.clip(-(-(64 * (j + 1)) // 16), 0, ncb)
    n = np.arange(128)[:, None]
    c[:, 0:32] = ((n >= lo[None, :]) & (n < hi[None, :]) & (n < ncb)).astype(np.float32)
    p = np.arange(128)[:, None, None]
    ii = np.arange(16)[None, :, None]
    jj = np.arange(32)[None, None, :]
    qblk = (128 * ii + p) // 64
    valid = jj <= qblk
    forced = (jj == 0) | (jj == qblk) | (jj == qblk - 1)
    km = (valid & ~forced).astype(np.float32)
    fm = np.where(~valid, -1e9, np.where(forced, 1e9, 0.0)).astype(np.float32)
    c[:, 32:544] = km.reshape(128, 512)
    c[:, 544:1056] = fm.reshape(128, 512)
    k = np.arange(128)[:, None]
    q = np.arange(128)[None, :]
    c[:, 1056:1184] = np.where(q < k, 0.0, NEGM)
    t = np.arange(128)[:, None]
    cc = np.arange(8)[None, :]
    mab = (t // 16 == cc).astype(np.float32)
    c[:, 1184:1192] = mab
    c[:, 1192:1200] = mab
    kk = np.arange(2048)[None, :]
    c[0:32, 1200:3248] = (np.arange(32)[:, None] == kk // 64).astype(np.float32)
    pp = np.arange(32)[:, None]
    tt = np.arange(128)[None, :]
    c[0:32, 3248:3376] = (pp == tt % 16).astype(np.float32)
    c[0:32, 3376:3504] = (pp == 16 + tt % 16).astype(np.float32)
    return oh, c


def stage_nsa_prompt(P, nc, cv, banks, U, KSN, ksn_key, KWN, OB, nsac_d, oh_d, TVd, Gd, rel_bias, q_norm_g, k_norm_g,
                     pe_k, w_k, proj_k, pe_v, w_v, proj_v, idf, idb, epsb, n_ptiles):
    from concourse.ap import AP
    NTq = n_ptiles
    SEQ_ = NTq * 128
    NCB_ = SEQ_ // 16 - 1
    L, R0 = NSA_L, NSA_R0
    V = lambda fn, r, w: P.op("vector", fn, r, w)
    S = lambda fn, r, w: P.op("scalar", fn, r, w)
    G = lambda fn, r, w: P.op("gpsimd", fn, r, w)
    T = lambda fn, r, w: P.op("tensor", fn, r, w, skip_same=True)
    b0, b1, bS0, bS1, bOa, bOb, bX, bY = banks
    bS = [bS0, bS1]
    bO = [bOa, bOb]
    bSk = ["bank2", "bank3"]
    bOk = ["bank4", "bank5"]
    bO4 = [banks[4], banks[5], banks[6], banks[7]]
    bO4k = ["bank4", "bank5", "bank6", "bank7"]
    b0bf = b0[:].bitcast(BF16)
    b1bf = b1[:].bitcast(BF16)

    nsac = cv.f32(NSAC_W)
    Mc = nsac[:, 0:32]
    KM = nsac[:, 32:544].rearrange("p (i j) -> p i j", i=16)
    FM = nsac[:, 544:1056].rearrange("p (i j) -> p i j", i=16)
    LT = nsac[:, 1056:1184]
    MAB = nsac[:, 1184:1200]
    Ekf = nsac[:, 1200:3248]
    SelA = nsac[:, 3248:3376]
    SelB = nsac[:, 3376:3504]
    Ek = cv.bf16(2048)
    Bd = cv.bf16(2048)
    Bo = cv.bf16(2048)
    Bw = cv.bf16(2048)
    tb31b = cv.bf16(16)
    tb31f = cv.f32(16)
    tbrow = cv.bf16(2048)
    ones1 = cv.bf16(128)
    onesf = cv.f32(128)
    qgain = cv.f32(128)
    kgain0 = cv.f32(128)
    ksT = cv.bf16(4 * SEQ_).rearrange("p (g t) -> p g t", g=4)
    kwT = cv.bf16(4 * SEQ_).rearrange("p (g t) -> p g t", g=4)
    vse = cv.bf16(NTq * 4 * 132).rearrange("p (i g e) -> p i g e", i=NTq, g=4)
    vwe = cv.bf16(NTq * 4 * 132).rearrange("p (i g e) -> p i g e", i=NTq, g=4)
    kcT = cv.bf16(512).rearrange("p (g n) -> p g n", g=4)
    vce = cv.bf16(4 * 132).rearrange("p (g e) -> p g e", g=4)
    mark = cv.off

    tab = cv.f32(16)
    oh = cv.f32(L)
    tvb = cv.bf16(L)
    stg = [[cv.f32(512) for _ in range(2)] for _ in range(6)]
    sbf = [[cv.bf16(512) for _ in range(2)] for _ in range(4)]
    wk32 = cv.f32(1)
    wv32 = cv.f32(1)
    pek = cv.f32(128)
    pev = cv.f32(128)
    wrep = cv.f32(4)
    pec = cv.f32(2)
    WAB = cv.bf16(32)
    pjf = cv.f32(256)
    pjb = cv.bf16(256)
    ATs = cv.f32(2 * 512).rearrange("p (s g n) -> p s g n", s=2, g=4)
    BTs = cv.f32(2 * 512).rearrange("p (s g n) -> p s g n", s=2, g=4)
    pooled = cv.bf16(2 * 512).rearrange("p (s g n) -> p s g n", s=2, g=4)
    ksq = cv.f32(512)
    kss = cv.f32(4)
    krs = cv.f32(4)
    kcn = cv.f32(512)
    kcnb = cv.bf16(512)

    P.dma("sync", nsac, nsac_d, writes=["nsac"])
    G(lambda e: e.tensor_copy(out=Ek[:32, :], in_=Ekf[:32, :]), ["nsac"], ["Ek"])
    V(lambda e: e.memset(ones1, 1.0), [], ["ones1"])
    V(lambda e: e.memset(onesf, 1.0), [], ["onesf"])
    P.dma("sync", qgain, q_norm_g.broadcast_to([128, 128]), writes=["qgain"])
    P.dma("sync", kgain0, k_norm_g[0:1, :].broadcast_to([128, 128]), writes=["kgain0"])

    V(lambda e: e.memset(tab[:33, :], NEGM), [], ["tab"])
    P.dma("sync", tab[:32, :], rel_bias, writes=["tab"])
    P.dma("sync", oh[:33, :], oh_d, writes=["oh"])
    nch = (L + 511) // 512
    for c in range(nch):
        w = min(512, L - c * 512)
        T(lambda e, c=c, w=w: e.matmul(bX[:16, 0:w], lhsT=tab[:33, 0:16], rhs=oh[:33, c * 512:c * 512 + w], start=True, stop=True),
          ["tab", "oh"], ["bank6"])
        V(lambda e, c=c, w=w: e.tensor_copy(out=tvb[:16, c * 512:c * 512 + w], in_=bX[:16, 0:w]), ["bank6"], ["tvb"])
    P.dma("sync", TVd, tvb[:16, :], reads=["tvb"], writes=["TV"])
    for h in range(16):
        P.dma("sync", Gd[h], TVd[h:h + 1, :].broadcast_to([128, L]), reads=["TV"], writes=["G"])
    for hh in range(2):
        P.dma("sync", Bd[:, hh * 1024:(hh + 1) * 1024].rearrange("p (h q) -> p h q", h=8),
              AP(Gd.tensor, hh * 8 * 128 * L + R0, [[L - 1, 128], [128 * L, 8], [1, 128]]), reads=["G"], writes=["Bd"])
        P.dma("sync", Bo[:, hh * 1024:(hh + 1) * 1024].rearrange("p (h q) -> p h q", h=8),
              AP(Gd.tensor, hh * 8 * 128 * L + R0 + 128, [[L - 1, 128], [128 * L, 8], [1, 128]]), reads=["G"], writes=["Bo"])
    P.dma("sync", tb31b, AP(TVd.tensor, R0 + 200, [[0, 128], [L, 16]]), reads=["TV"], writes=["tb31b"],
          allow_slow_non_contiguous=True)
    V(lambda e: e.tensor_copy(out=tb31f, in_=tb31b), ["tb31b"], ["tb31f"])
    for h in range(16):
        V(lambda e, h=h: e.tensor_scalar(out=Bw[:, h * 128:(h + 1) * 128], in0=LT, scalar1=tb31f[:, h:h + 1], scalar2=None,
                                         op0=ALU.add), ["nsac", "tb31f"], ["Bw"])
    V(lambda e: e.tensor_copy(out=tbrow[0:1, :].rearrange("p (h q) -> p h q", h=16),
                              in_=tb31b[0:1, :].unsqueeze(2).broadcast_to([1, 16, 128])), ["tb31b"], ["tbrow"])

    P.dma("sync", wk32[:32, :], w_k, writes=["wk32"])
    P.dma("sync", wv32[:32, :], w_v, writes=["wv32"])
    P.dma("sync", pek[:32, :], pe_k, writes=["pek"])
    P.dma("sync", pev[:32, :], pe_v, writes=["pev"])
    for s_, (w32, wkey) in enumerate(((wk32, "wk32"), (wv32, "wv32"))):
        T(lambda e, s_=s_, w32=w32: e.matmul(bX[:, 2 * s_:2 * s_ + 1], lhsT=SelA[:32, :], rhs=w32[:32, :], start=True, stop=True),
          ["nsac", wkey], ["bank6"])
        T(lambda e, s_=s_, w32=w32: e.matmul(bX[:, 2 * s_ + 1:2 * s_ + 2], lhsT=SelB[:32, :], rhs=w32[:32, :], start=True, stop=True),
          ["nsac", wkey], ["bank6"])
    T(lambda e: e.matmul(bX[:, 8:9], lhsT=pek[:32, :], rhs=wk32[:32, :], start=True, stop=True), ["pek", "wk32"], ["bank6"])
    T(lambda e: e.matmul(bX[:, 9:10], lhsT=pev[:32, :], rhs=wv32[:32, :], start=True, stop=True), ["pev", "wv32"], ["bank6"])
    V(lambda e: e.tensor_copy(out=wrep, in_=bX[:, 0:4]), ["bank6"], ["wrep"])
    V(lambda e: e.tensor_copy(out=pec, in_=bX[:, 8:10]), ["bank6"], ["pec"])
    for s_ in range(2):
        for ab in range(2):
            V(lambda e, s_=s_, ab=ab: e.tensor_scalar(out=WAB[:, s_ * 16 + ab * 8:s_ * 16 + ab * 8 + 8], in0=MAB[:, ab * 8:ab * 8 + 8],
                                                      scalar1=wrep[:, 2 * s_ + ab:2 * s_ + ab + 1], scalar2=None, op0=ALU.mult),
              ["nsac", "wrep"], ["WAB"])
    P.dma("sync", pjf[:, 0:128], proj_k, writes=["pjf"])
    P.dma("sync", pjf[:, 128:256], proj_v, writes=["pjf"])
    V(lambda e: e.tensor_copy(out=pjb, in_=pjf), ["pjf"], ["pjb"])
    V(lambda e: e.memset(vse, 1.0), [], ["vse"])
    V(lambda e: e.memset(vwe, 1.0), [], ["vwe"])
    V(lambda e: e.memset(vce, 1.0), [], ["vce"])

    srcs = ((KSN, 0, ksn_key), (KWN, 0, "KWN"), (U, C_VS, "U"), (U, C_VW, "U"), (U, C_KC, "U"), (U, C_VC, "U"))
    for i in range(NTq):
        t0 = i * 128
        d = i % 2
        for s_, (src, col, key) in enumerate(srcs):
            P.dma("sync", stg[s_][d], src[t0:t0 + 128, col:col + 512], reads=[key], writes=["stg%d%d" % (s_, d)])
        for s_, (dstT, bbf, bkey, dk) in enumerate(((ksT, b0bf, "bank0", "ksT"), (kwT, b1bf, "bank1", "kwT"))):
            G(lambda e, s_=s_, d=d: e.tensor_copy(out=sbf[s_][d], in_=stg[s_][d]), ["stg%d%d" % (s_, d)], ["sbf%d%d" % (s_, d)])
            for g in range(4):
                T(lambda e, s_=s_, d=d, g=g, bbf=bbf: e.transpose(out=bbf[:, g * 128:(g + 1) * 128], in_=sbf[s_][d][:, g * 128:(g + 1) * 128],
                                                                 identity=idb), ["sbf%d%d" % (s_, d), "idb"], [bkey])
            V(lambda e, dstT=dstT, bbf=bbf, t0=t0: e.tensor_copy(out=dstT[:, :, t0:t0 + 128],
                                                                 in_=bbf[:, 0:512].rearrange("p (g t) -> p g t", g=4)), [bkey], [dk])
        for s_, (dstV, dk) in ((2, (vse, "vse")), (3, (vwe, "vwe"))):
            G(lambda e, s_=s_, d=d, dstV=dstV, i=i: e.tensor_copy(out=dstV[:, i, :, 0:128],
                                                                  in_=stg[s_][d].rearrange("p (g e) -> p g e", g=4)),
              ["stg%d%d" % (s_, d)], [dk])
        for s_ in range(2):
            sb = sbf[2 + s_][d]
            sk = "sbf%d%d" % (2 + s_, d)
            V(lambda e, s_=s_, d=d, sb=sb: e.tensor_copy(out=sb, in_=stg[4 + s_][d]), ["stg%d%d" % (4 + s_, d)], [sk])
            for g in range(4):
                bk = (bS if s_ == 0 else bO)[g // 2]
                bkey = (bSk if s_ == 0 else bOk)[g // 2]
                c0 = (g % 2) * 256 + i * 16
                T(lambda e, s_=s_, g=g, sb=sb, bk=bk, c0=c0: e.matmul(bk[:, c0:c0 + 16], lhsT=sb[:, g * 128:(g + 1) * 128],
                                                                      rhs=WAB[:, s_ * 16:s_ * 16 + 16], start=True, stop=True),
                  [sk, "WAB"], [bkey])
    for s_ in range(2):
        for gg in range(2):
            bk = (bS if s_ == 0 else bO)[gg]
            bkey = (bSk if s_ == 0 else bOk)[gg]
            vw_ = bk[:, 0:512].rearrange("p (g i ab c) -> p g i ab c", g=2, i=16, ab=2)
            V(lambda e, s_=s_, gg=gg, vw_=vw_: e.tensor_copy(
                out=ATs[:, s_, 2 * gg:2 * gg + 2, 0:NTq * 8].rearrange("p g (i c) -> p g i c", c=8), in_=vw_[:, :, 0:NTq, 0, :]),
              [bkey], ["ATs"])
            V(lambda e, s_=s_, gg=gg, vw_=vw_: e.tensor_copy(
                out=BTs[:, s_, 2 * gg:2 * gg + 2, 0:NTq * 8].rearrange("p g (i c) -> p g i c", c=8), in_=vw_[:, :, 0:NTq, 1, :]),
              [bkey], ["BTs"])
        V(lambda e, s_=s_: e.scalar_tensor_tensor(out=pooled[:, s_, :, 0:NCB_], in0=ATs[:, s_, :, 0:NCB_], scalar=pec[:, s_:s_ + 1],
                                                  in1=BTs[:, s_, :, 1:NCB_ + 1], op0=ALU.add, op1=ALU.add),
          ["ATs", "BTs", "pec"], ["pooled"])
        bk, bkey = (bX, "bank6") if s_ == 0 else (bY, "bank7")
        for g in range(4):
            T(lambda e, s_=s_, g=g, bk=bk: e.matmul(bk[:NCB_, g * 128:(g + 1) * 128], lhsT=pooled[:, s_, g, 0:NCB_],
                                                    rhs=pjb[:, s_ * 128:(s_ + 1) * 128], start=True, stop=True),
              ["pooled", "pjb"], [bkey])
    S(lambda e: e.activation(out=ksq[:NCB_, :], in_=bX[:NCB_, 0:512], func=AF.Square), ["bank6"], ["ksq"])
    V(lambda e: e.tensor_reduce(out=kss[:NCB_, :], in_=ksq[:NCB_, :].rearrange("p (g d) -> p g d", g=4), axis=AX.X, op=ALU.add),
      ["ksq"], ["kss"])
    S(lambda e: e.activation(out=krs[:NCB_, :], in_=kss[:NCB_, :], func=AF.Ln, scale=1.0 / 128, bias=epsb[:NCB_, :]),
      ["kss", "epsb"], ["krs"])
    S(lambda e: e.activation(out=krs[:NCB_, :], in_=krs[:NCB_, :], func=AF.Exp, scale=-0.5), ["krs"], ["krs"])
    V(lambda e: e.tensor_tensor(out=kcn[:NCB_, :].rearrange("p (g d) -> p g d", g=4), in0=bX[:NCB_, 0:512].rearrange("p (g d) -> p g d", g=4),
                                in1=krs[:NCB_, :].unsqueeze(2).broadcast_to([NCB_, 4, 128]), op=ALU.mult), ["bank6", "krs"], ["kcn"])
    V(lambda e: e.tensor_tensor(out=kcnb[:NCB_, :].rearrange("p (g d) -> p g d", g=4), in0=kcn[:NCB_, :].rearrange("p (g d) -> p g d", g=4),
                                in1=kgain0[:NCB_, :].unsqueeze(1).broadcast_to([NCB_, 4, 128]), op=ALU.mult), ["kcn", "kgain0"], ["kcnb"])
    for g in range(4):
        T(lambda e, g=g: e.transpose(out=b0bf[:, g * 128:g * 128 + NCB_], in_=kcnb[:NCB_, g * 128:(g + 1) * 128],
                                     identity=idb[:NCB_, :NCB_]), ["kcnb", "idb"], ["bank0"])
    V(lambda e: e.tensor_copy(out=kcT[:, :, 0:NCB_], in_=b0bf[:, 0:512].rearrange("p (g n) -> p g n", g=4)[:, :, 0:NCB_]),
      ["bank0"], ["kcT"])
    V(lambda e: e.tensor_copy(out=vce[:NCB_, :, 0:128], in_=bY[:NCB_, 0:512].rearrange("p (g e) -> p g e", g=4)), ["bank7"], ["vce"])

    P.barrier()
    cv.reset(mark)
    qf = cv.f32(2048)
    sq = cv.f32(2048)
    qnb = cv.bf16(2048)
    qT = cv.bf16(2048)
    zf = cv.f32(2048)
    gtf = cv.f32(48)
    cb = cv.bf16(2048)
    ss16 = cv.f32(16)
    rs16 = cv.f32(16)
    Ef = cv.f32(512)
    Ec = cv.bf16(512)
    rdb = cv.f32(512)
    impT = cv.f32(128)
    sc = cv.f32(32)
    cmp3 = cv.f32(1024)
    cnt = cv.f32(32)
    sneg = cv.bf16(32)
    sn4 = cv.bf16(512)
    E = [cv.bf16(512) for _ in range(2)]
    acc = cv.f32(2048)
    rd2 = cv.f32(2)
    cf2 = cv.f32(2)
    rsq = 128 ** -0.5

    def finalize(br, g, first):
        for h4 in range(4):
            h = 4 * g + h4
            c0 = br * 16 + h
            V(lambda e, h4=h4: e.tensor_scalar(out=rd2[:, 0:1], in0=bO4[h4][:, 128:129], scalar1=1e-30, scalar2=None, op0=ALU.add),
              [bO4k[h4]], ["rd2"])
            V(lambda e: e.reciprocal(out=rd2[:, 0:1], in_=rd2[:, 0:1]), ["rd2"], ["rd2"])
            V(lambda e, c0=c0: e.tensor_tensor(out=cf2[:, 0:1], in0=rd2[:, 0:1], in1=gtf[:, c0:c0 + 1], op=ALU.mult), ["rd2", "gtf"], ["cf2"])
            if first:
                V(lambda e, h4=h4, h=h: e.tensor_scalar(out=acc[:, h * 128:(h + 1) * 128], in0=bO4[h4][:, 0:128],
                                                        scalar1=cf2[:, 0:1], scalar2=None, op0=ALU.mult),
                  [bO4k[h4], "cf2"], ["acc"])
            else:
                V(lambda e, h4=h4, h=h: e.scalar_tensor_tensor(out=acc[:, h * 128:(h + 1) * 128], in0=bO4[h4][:, 0:128],
                                                               scalar=cf2[:, 0:1], in1=acc[:, h * 128:(h + 1) * 128],
                                                               op0=ALU.mult, op1=ALU.add),
                  [bO4k[h4], "cf2", "acc"], ["acc"])

    ecnt = 0
    for i in range(NTq):
        t0 = i * 128
        P.dma("sync", qf, U[t0:t0 + 128, C_QB:C_QB + 2048], reads=["U"], writes=["qf"])
        P.dma("sync", zf, U[t0:t0 + 128, C_ZB:C_ZB + 2048], reads=["U"], writes=["zf"])
        P.dma("sync", gtf, U[t0:t0 + 128, C_GB:C_GB + 48], reads=["U"], writes=["gtf"])
        for hh in range(2):
            P.dma("sync", cb[:NCB_, hh * 1024:(hh + 1) * 1024].rearrange("p (h q) -> p h q", h=8),
                  AP(Gd.tensor, hh * 8 * 128 * L + (128 * i - 31 + R0), [[L - 16, NCB_], [128 * L, 8], [1, 128]]),
                  reads=["G"], writes=["cb"])
        S(lambda e: e.activation(out=sq, in_=qf, func=AF.Square), ["qf"], ["sq"])
        V(lambda e: e.tensor_reduce(out=ss16, in_=sq.rearrange("p (h d) -> p h d", h=16), axis=AX.X, op=ALU.add), ["sq"], ["ss16"])
        S(lambda e: e.activation(out=rs16, in_=ss16, func=AF.Ln, scale=1.0 / 128, bias=epsb), ["ss16", "epsb"], ["rs16"])
        S(lambda e: e.activation(out=rs16, in_=rs16, func=AF.Exp, scale=-0.5), ["rs16"], ["rs16"])
        V(lambda e: e.tensor_tensor(out=sq.rearrange("p (h d) -> p h d", h=16), in0=qf.rearrange("p (h d) -> p h d", h=16),
                                    in1=rs16.unsqueeze(2).broadcast_to([128, 16, 128]), op=ALU.mult), ["qf", "rs16", "sq"], ["sq"])
        V(lambda e: e.scalar_tensor_tensor(out=qnb.rearrange("p (h d) -> p h d", h=16), in0=sq.rearrange("p (h d) -> p h d", h=16),
                                           scalar=rsq, in1=qgain.unsqueeze(1).broadcast_to([128, 16, 128]),
                                           op0=ALU.mult, op1=ALU.mult), ["sq", "qgain"], ["qnb"])
        for hb, (bbf, bkey) in enumerate(((b0bf, "bank0"), (b1bf, "bank1"))):
            for j in range(8):
                T(lambda e, hb=hb, j=j, bbf=bbf: e.transpose(out=bbf[:, j * 128:(j + 1) * 128],
                                                             in_=qnb[:, (hb * 8 + j) * 128:(hb * 8 + j + 1) * 128], identity=idb),
                  ["qnb", "idb"], [bkey])
            V(lambda e, hb=hb, bbf=bbf: e.tensor_copy(out=qT[:, hb * 1024:(hb + 1) * 1024], in_=bbf[:, 0:1024]), [bkey], ["qT"])
        S(lambda e: e.activation(out=zf, in_=zf, func=AF.Silu), ["zf"], ["zf"])
        S(lambda e: e.activation(out=gtf, in_=gtf, func=AF.Sigmoid), ["gtf"], ["gtf"])

        for g in range(4):
            q4 = qT[:, g * 512:(g + 1) * 512]
            T(lambda e, g=g, q4=q4: e.matmul(bS0[:NCB_, 0:512], lhsT=kcT[:, g, 0:NCB_], rhs=q4, start=True, stop=False),
              ["kcT", "qT"], ["bank2"])
            T(lambda e, g=g: e.matmul(bS0[:NCB_, 0:512], lhsT=idb[:NCB_, :NCB_], rhs=cb[:NCB_, g * 512:(g + 1) * 512], start=False, stop=True),
              ["idb", "cb"], ["bank2"])
            S(lambda e: e.activation(out=Ef[:NCB_, :], in_=bS0[:NCB_, 0:512], func=AF.Exp), ["bank2"], ["Ef"])
            G(lambda e: e.tensor_copy(out=Ec[:NCB_, :], in_=Ef[:NCB_, :]), ["Ef"], ["Ec"])
            for h in range(4):
                T(lambda e, g=g, h=h: e.matmul(bO4[h][:, 0:129], lhsT=Ec[:NCB_, h * 128:(h + 1) * 128],
                                               rhs=vce[:NCB_, g, 0:129], start=True, stop=True), ["Ec", "vce"], [bO4k[h]])
            T(lambda e: e.matmul(b1[:NCB_, 0:512], lhsT=onesf[:NCB_, :NCB_], rhs=Ef[:NCB_, :], start=True, stop=True),
              ["onesf", "Ef"], ["bank1"])
            V(lambda e: e.tensor_scalar(out=rdb[:NCB_, :], in0=b1[:NCB_, 0:512], scalar1=1e-30, scalar2=None, op0=ALU.add), ["bank1"], ["rdb"])
            V(lambda e: e.reciprocal(out=rdb[:NCB_, :], in_=rdb[:NCB_, :]), ["rdb"], ["rdb"])
            V(lambda e: e.tensor_tensor(out=Ef[:NCB_, :], in0=Ef[:NCB_, :], in1=rdb[:NCB_, :], op=ALU.mult), ["Ef", "rdb"], ["Ef"])
            V(lambda e: e.tensor_reduce(out=impT[:NCB_, :], in_=Ef[:NCB_, :].rearrange("p (h q) -> p q h", h=4), axis=AX.X, op=ALU.add),
              ["Ef"], ["impT"])
            T(lambda e: e.matmul(b0[:, 0:32], lhsT=impT[:NCB_, :], rhs=Mc[:NCB_, :], start=True, stop=True), ["impT", "nsac"], ["bank0"])
            V(lambda e, i=i: e.tensor_tensor(out=sc, in0=b0[:, 0:32], in1=KM[:, i, :], op=ALU.mult), ["bank0", "nsac"], ["sc"])
            V(lambda e, i=i: e.tensor_tensor(out=sc, in0=sc, in1=FM[:, i, :], op=ALU.add), ["sc", "nsac"], ["sc"])
            V(lambda e: e.tensor_tensor(out=cmp3.rearrange("p (a b) -> p a b", a=32), in0=sc.unsqueeze(1).broadcast_to([128, 32, 32]),
                                        in1=sc.unsqueeze(2).broadcast_to([128, 32, 32]), op=ALU.is_gt), ["sc"], ["cmp3"])
            V(lambda e: e.tensor_reduce(out=cnt, in_=cmp3.rearrange("p (a b) -> p a b", a=32), axis=AX.X, op=ALU.add), ["cmp3"], ["cnt"])
            V(lambda e: e.tensor_scalar(out=sneg, in0=cnt, scalar1=15.5, scalar2=NEGM, op0=ALU.is_gt, op1=ALU.mult), ["cnt"], ["sneg"])
            T(lambda e: e.transpose(out=b0bf[:32, 0:128], in_=sneg[:, 0:32], identity=idb), ["sneg", "idb"], ["bank0"])
            V(lambda e: e.tensor_copy(out=sn4[:32, :].rearrange("p (h q) -> p h q", h=4),
                                      in_=b0bf[:32, 0:128].unsqueeze(1).broadcast_to([32, 4, 128])), ["bank0"], ["sn4"])
            finalize(0, g, True)
            for br, kT_, ve, kts in ((1, ksT, vse, list(range(0, i + 1))), (2, kwT, vwe, list(range(max(0, i - 4), i + 1)))):
                for kt in kts:
                    d = ecnt % 2
                    ecnt += 1
                    bSx, bSxk = bS[d], bSk[d]
                    T(lambda e, g=g, kt=kt, kT_=kT_, bSx=bSx, q4=q4: e.matmul(bSx[:, 0:512], lhsT=kT_[:, g, kt * 128:(kt + 1) * 128], rhs=q4,
                                                                            start=True, stop=False),
                      ["ksT", "kwT", "qT"], [bSxk])
                    if br == 1:
                        T(lambda e, kt=kt, bSx=bSx: e.matmul(bSx[:, 0:512], lhsT=Ek[:32, kt * 128:(kt + 1) * 128], rhs=sn4[:32, :],
                                                             start=False, stop=False), ["Ek", "sn4"], [bSxk])
                    if kt == i:
                        lh, rh, rk = idb, Bd[:, g * 512:(g + 1) * 512], "Bd"
                    elif kt == i - 1:
                        lh, rh, rk = idb, Bo[:, g * 512:(g + 1) * 512], "Bo"
                    elif br == 2 and kt == i - 4:
                        lh, rh, rk = idb, Bw[:, g * 512:(g + 1) * 512], "Bw"
                    else:
                        lh, rh, rk = ones1[0:1, :], tbrow[0:1, g * 512:(g + 1) * 512], "tbrow"
                    T(lambda e, bSx=bSx, lh=lh, rh=rh: e.matmul(bSx[:, 0:512], lhsT=lh, rhs=rh, start=False, stop=True),
                      ["idb", "ones1", rk], [bSxk])
                    S(lambda e, d=d, bSx=bSx: e.activation(out=E[d], in_=bSx[:, 0:512], func=AF.Exp), [bSxk], ["E%d" % d])
                    for h in range(4):
                        T(lambda e, g=g, kt=kt, h=h, d=d, ve=ve, kts=kts: e.matmul(
                            bO4[h][:, 0:129], lhsT=E[d][:, h * 128:(h + 1) * 128],
                            rhs=ve[:, kt, g, 0:129], start=(kt == kts[0]), stop=(kt == kts[-1])),
                          ["E%d" % d, "vse", "vwe"], [bO4k[h]])
                finalize(br, g, False)
        V(lambda e: e.tensor_tensor(out=acc, in0=acc, in1=zf, op=ALU.mult), ["acc", "zf"], ["acc"])
        P.dma("scalar", OB[t0:t0 + 128, :], acc, reads=["acc"], writes=["OB"], group="acc")


NSAS_W = 2712


def make_nsa_sample_consts():
    ncs, nb = 1023, 257
    c = np.zeros((128, NSAS_W), np.float32)
    j = np.arange(nb)
    lo = np.clip((64 * j - 32) // 16 + 1, 0, ncs)
    hi = np.clip(-(-(64 * (j + 1)) // 16), 0, ncs)
    n = np.arange(1024)[:, None]
    m = ((n >= lo[None]) & (n < hi[None]) & (n < ncs)).astype(np.float32)
    c[:, 0:2056] = m.reshape(8, 128, nb).transpose(1, 0, 2).reshape(128, 2056)
    forced = (j == 0) | (j == 256) | (j == 255)
    c[0:8, 2056:2313] = (~forced).astype(np.float32)
    c[0:8, 2313:2570] = np.where(forced, 1e9, 0.0)
    kp = np.arange(128)[:, None]
    t = np.arange(8)[None, :]
    c[:, 2570:2578] = np.where(kp <= t, NEGM, 0.0)
    c[0, 2578:2578 + 64] = 1.0
    c[1, 2578 + 64:2578 + 128] = 1.0
    c[:, 2706] = np.arange(128)
    return c


def stage_nsa_sample(P, nc, cv, banks, U, KSNs, ksn_key, KWN, OB, nsac_d, nsas_d, TVd, Gd, q_norm_g, k_norm_g,
                     pe_k, w_k, proj_k, pe_v, w_v, proj_v, pool_kc, pool_vc, pool_ks, pool_vs, ckw, cvw, ptab,
                     idf, idb, epsb):
    from concourse.ap import AP
    L, R0 = NSA_L, NSA_R0
    T0 = SEQ
    V = lambda fn, r, w: P.op("vector", fn, r, w)
    S = lambda fn, r, w: P.op("scalar", fn, r, w)
    G = lambda fn, r, w: P.op("gpsimd", fn, r, w)
    T = lambda fn, r, w: P.op("tensor", fn, r, w, skip_same=True)
    b0, b1, bS0, bS1 = banks[0:4]
    bS = [bS0, bS1]
    bSk = ["bank2", "bank3"]
    bO4 = [banks[4], banks[5], banks[6], banks[7]]
    bO4k = ["bank4", "bank5", "bank6", "bank7"]
    b0bf = b0[:].bitcast(BF16)
    b1bf = b1[:].bitcast(BF16)
    rsq = 128 ** -0.5

    nsas = cv.f32(NSAS_W)
    Ms = nsas[:, 0:2056].rearrange("p (a j) -> p a j", a=8)
    KMs = nsas[:, 2056:2313]
    FMs = nsas[:, 2313:2570]
    LTs = nsas[:, 2570:2578]
    E2f = nsas[:, 2578:2706]
    iot = nsas[:, 2706:2707]
    nsm = cv.f32(512)
    P.dma("sync", nsas, nsas_d, writes=["nsas"])
    P.dma("sync", nsm[:, 0:256], nsac_d[:, 3248:3504], writes=["nsm"])
    P.dma("sync", nsm[:, 256:272], nsac_d[:, 1184:1200], writes=["nsm"])
    SelA, SelB, MAB = nsm[:, 0:128], nsm[:, 128:256], nsm[:, 256:272]
    E2 = cv.bf16(128)
    V(lambda e: e.tensor_copy(out=E2[:2, :], in_=E2f[:2, :]), ["nsas"], ["E2"])
    ones1 = cv.bf16(128)
    onesf = cv.f32(128)
    V(lambda e: e.memset(ones1, 1.0), [], ["ones1"])
    V(lambda e: e.memset(onesf, 1.0), [], ["onesf"])
    qgain = cv.f32(128)
    kgain0 = cv.f32(128)
    P.dma("sync", qgain[:8, :], q_norm_g.broadcast_to([8, 128]), writes=["qgain"])
    P.dma("sync", kgain0, k_norm_g[0:1, :].broadcast_to([128, 128]), writes=["kgain0"])

    pti = cv.f32(128).bitcast(I32)
    ptf = cv.f32(128)
    idx = cv.f32(128).bitcast(I32)
    P.dma("sync", pti, ptab.broadcast_to([128, 128]), writes=["pti"])
    V(lambda e: e.tensor_copy(out=ptf, in_=pti), ["pti"], ["ptf"])
    V(lambda e: e.tensor_scalar(out=ptf, in0=ptf, scalar1=128.0, scalar2=iot, op0=ALU.mult, op1=ALU.add), ["ptf", "nsas"], ["ptf"])
    V(lambda e: e.tensor_copy(out=idx, in_=ptf), ["ptf"], ["idx"])

    def gather(dst, dkey, pool, pg):
        fn = lambda e, dst=dst, pool=pool, pg=pg: e.indirect_dma_start(
            out=dst, out_offset=None, in_=pool, in_offset=bass.IndirectOffsetOnAxis(ap=idx[:, pg:pg + 1], axis=0))
        P.dma_fn("gpsimd", fn, reads=["idx"], writes=[dkey])

    tb31b = cv.bf16(16)
    tb31f = cv.f32(16)
    tbrow = cv.bf16(128)
    Bc7 = cv.bf16(128)
    B127 = cv.bf16(128)
    Bnew = cv.bf16(128)
    Bw0 = cv.bf16(128)
    P.dma("sync", tb31b, AP(TVd.tensor, R0 + 200, [[0, 128], [L, 16]]), reads=["TV"], writes=["tb31b"], allow_slow_non_contiguous=True)
    V(lambda e: e.tensor_copy(out=tb31f, in_=tb31b), ["tb31b"], ["tb31f"])
    V(lambda e: e.tensor_copy(out=tbrow[0:1, :].rearrange("p (h q) -> p h q", h=16),
                              in_=tb31b[0:1, :].unsqueeze(2).broadcast_to([1, 16, 8])), ["tb31b"], ["tbrow"])
    for hh in range(2):
        P.dma("sync", Bc7[:, hh * 64:(hh + 1) * 64].rearrange("p (h q) -> p h q", h=8),
              AP(Gd.tensor, hh * 8 * 128 * L + R0 + 2017, [[L - 16, 128], [128 * L, 8], [1, 8]]), reads=["G"], writes=["Bc7"])
        P.dma("sync", B127[:, hh * 64:(hh + 1) * 64].rearrange("p (h q) -> p h q", h=8),
              AP(Gd.tensor, hh * 8 * 128 * L + R0 + 128, [[L - 1, 128], [128 * L, 8], [1, 8]]), reads=["G"], writes=["B127"])
    P.dma("sync", Bnew[:8, :].rearrange("p (h q) -> p h q", h=16),
          AP(Gd.tensor, R0, [[L - 1, 8], [128 * L, 16], [1, 8]]), reads=["G"], writes=["Bnew"])
    for h in range(16):
        V(lambda e, h=h: e.tensor_scalar(out=Bw0[:, h * 8:(h + 1) * 8], in0=LTs, scalar1=tb31f[:, h:h + 1], scalar2=None, op0=ALU.add),
          ["nsas", "tb31f"], ["Bw0"])

    qf = cv.f32(2048)
    sq = cv.f32(2048)
    qnb = cv.bf16(2048)
    ss16 = cv.f32(16)
    rs16 = cv.f32(16)
    qTs = cv.bf16(128)
    gT = cv.f32(12)
    zT = cv.f32(512)
    acc = cv.f32(512)
    rd = cv.f32(1)
    cf = cv.f32(1)
    P.dma("sync", qf[:8, :], U[T0:T0 + 8, C_QB:C_QB + 2048], reads=["U"], writes=["qf"])
    for h4 in range(4):
        P.dma("sync", gT[h4 * 8:(h4 + 1) * 8, :].rearrange("p (b g) -> p b g", b=3),
              AP(U.tensor, T0 * IN_W + C_GB + h4, [[IN_W, 8], [16, 3], [4, 4]]), reads=["U"], writes=["gT"], allow_slow_non_contiguous=True)
        P.dma("sync", zT[h4 * 8:(h4 + 1) * 8, :].rearrange("p (g d) -> p g d", g=4),
              AP(U.tensor, T0 * IN_W + C_ZB + h4 * 128, [[IN_W, 8], [512, 4], [1, 128]]), reads=["U"], writes=["zT"])
    S(lambda e: e.activation(out=sq[:8, :], in_=qf[:8, :], func=AF.Square), ["qf"], ["sq"])
    V(lambda e: e.tensor_reduce(out=ss16[:8, :], in_=sq[:8, :].rearrange("p (h d) -> p h d", h=16), axis=AX.X, op=ALU.add), ["sq"], ["ss16"])
    S(lambda e: e.activation(out=rs16[:8, :], in_=ss16[:8, :], func=AF.Ln, scale=1.0 / 128, bias=epsb[:8, :]), ["ss16", "epsb"], ["rs16"])
    S(lambda e: e.activation(out=rs16[:8, :], in_=rs16[:8, :], func=AF.Exp, scale=-0.5), ["rs16"], ["rs16"])
    V(lambda e: e.tensor_tensor(out=sq[:8, :].rearrange("p (h d) -> p h d", h=16), in0=qf[:8, :].rearrange("p (h d) -> p h d", h=16),
                                in1=rs16[:8, :].unsqueeze(2).broadcast_to([8, 16, 128]), op=ALU.mult), ["qf", "rs16", "sq"], ["sq"])
    V(lambda e: e.scalar_tensor_tensor(out=qnb[:8, :].rearrange("p (h d) -> p h d", h=16), in0=sq[:8, :].rearrange("p (h d) -> p h d", h=16),
                                       scalar=rsq, in1=qgain[:8, :].unsqueeze(1).broadcast_to([8, 16, 128]),
                                       op0=ALU.mult, op1=ALU.mult), ["sq", "qgain"], ["qnb"])
    for h in range(16):
        T(lambda e, h=h: e.transpose(out=b0bf[:, h * 8:(h + 1) * 8], in_=qnb[:8, h * 128:(h + 1) * 128], identity=idb[:8, :8]),
          ["qnb", "idb"], ["bank0"])
    V(lambda e: e.tensor_copy(out=qTs, in_=b0bf[:, 0:128]), ["bank0"], ["qTs"])
    S(lambda e: e.activation(out=zT[:32, :], in_=zT[:32, :], func=AF.Silu), ["zT"], ["zT"])
    S(lambda e: e.activation(out=gT[:32, :], in_=gT[:32, :], func=AF.Sigmoid), ["gT"], ["gT"])

    def finalize(br, first):
        for g in range(4):
            V(lambda e, g=g: e.tensor_scalar(out=rd[:32, :], in0=bO4[g][:32, 128:129], scalar1=1e-30, scalar2=None, op0=ALU.add),
              [bO4k[g]], ["rd"])
            V(lambda e: e.reciprocal(out=rd[:32, :], in_=rd[:32, :]), ["rd"], ["rd"])
            V(lambda e, g=g: e.tensor_tensor(out=cf[:32, :], in0=rd[:32, :], in1=gT[:32, br * 4 + g:br * 4 + g + 1], op=ALU.mult),
              ["rd", "gT"], ["cf"])
            if first:
                V(lambda e, g=g: e.tensor_scalar(out=acc[:32, g * 128:(g + 1) * 128], in0=bO4[g][:32, 0:128], scalar1=cf[:32, 0:1],
                                                 scalar2=None, op0=ALU.mult), [bO4k[g], "cf"], ["acc"])
            else:
                V(lambda e, g=g: e.scalar_tensor_tensor(out=acc[:32, g * 128:(g + 1) * 128], in0=bO4[g][:32, 0:128], scalar=cf[:32, 0:1],
                                                        in1=acc[:32, g * 128:(g + 1) * 128], op0=ALU.mult, op1=ALU.add),
                  [bO4k[g], "cf", "acc"], ["acc"])

    wk32 = cv.f32(1)
    wv32 = cv.f32(1)
    pek = cv.f32(128)
    pev = cv.f32(128)
    wrep = cv.f32(4)
    pec = cv.f32(2)
    WAB = cv.bf16(32)
    pjf = cv.f32(256)
    pjb = cv.bf16(256)
    P.dma("sync", wk32[:32, :], w_k, writes=["wk32"])
    P.dma("sync", wv32[:32, :], w_v, writes=["wv32"])
    P.dma("sync", pek[:32, :], pe_k, writes=["pek"])
    P.dma("sync", pev[:32, :], pe_v, writes=["pev"])
    for s_, (w32, wkey) in enumerate(((wk32, "wk32"), (wv32, "wv32"))):
        T(lambda e, s_=s_, w32=w32: e.matmul(b1[:, 2 * s_:2 * s_ + 1], lhsT=SelA[:32, :], rhs=w32[:32, :], start=True, stop=True),
          ["nsm", wkey], ["bank1"])
        T(lambda e, s_=s_, w32=w32: e.matmul(b1[:, 2 * s_ + 1:2 * s_ + 2], lhsT=SelB[:32, :], rhs=w32[:32, :], start=True, stop=True),
          ["nsm", wkey], ["bank1"])
    T(lambda e: e.matmul(b1[:, 8:9], lhsT=pek[:32, :], rhs=wk32[:32, :], start=True, stop=True), ["pek", "wk32"], ["bank1"])
    T(lambda e: e.matmul(b1[:, 9:10], lhsT=pev[:32, :], rhs=wv32[:32, :], start=True, stop=True), ["pev", "wv32"], ["bank1"])
    V(lambda e: e.tensor_copy(out=wrep, in_=b1[:, 0:4]), ["bank1"], ["wrep"])
    V(lambda e: e.tensor_copy(out=pec, in_=b1[:, 8:10]), ["bank1"], ["pec"])
    for s_ in range(2):
        for ab in range(2):
            V(lambda e, s_=s_, ab=ab: e.tensor_scalar(out=WAB[:, s_ * 16 + ab * 8:s_ * 16 + ab * 8 + 8], in0=MAB[:, ab * 8:ab * 8 + 8],
                                                      scalar1=wrep[:, 2 * s_ + ab:2 * s_ + ab + 1], scalar2=None, op0=ALU.mult),
              ["nsm", "wrep"], ["WAB"])
    P.dma("sync", pjf[:, 0:128], proj_k, writes=["pjf"])
    P.dma("sync", pjf[:, 128:256], proj_v, writes=["pjf"])
    V(lambda e: e.tensor_copy(out=pjb, in_=pjf), ["pjf"], ["pjb"])

    ATs = cv.f32(4096).rearrange("p (g n) -> p g n", g=4)
    BTs = cv.f32(4096).rearrange("p (g n) -> p g n", g=4)
    pooled = cv.bf16(4096).rearrange("p (g n) -> p g n", g=4)
    kcTs = cv.bf16(4096).rearrange("p (g n) -> p g n", g=4)
    vces = cv.bf16(8 * 4 * 132).rearrange("p (a g e) -> p a g e", a=8, g=4)
    stg = [cv.f32(512) for _ in range(4)]
    sbf = [cv.bf16(512) for _ in range(4)]
    ksq = cv.f32(512)
    kss = cv.f32(4)
    krs = cv.f32(4)
    kcn = cv.f32(512)
    kcnb = cv.bf16(512)
    V(lambda e: e.memset(vces, 1.0), [], ["vces"])
    for s_, pool in enumerate((pool_kc, pool_vc)):
        V(lambda e: e.memset(pooled, 0.0), [], ["pooled"])
        for pg in range(128):
            d = pg % 2
            sk, bk_ = "sg%d" % d, "sb%d" % d
            gather(stg[d], sk, pool, pg)
            V(lambda e, d=d: e.tensor_copy(out=sbf[d], in_=stg[d]), [sk], [bk_])
            for g in range(4):
                c0 = (g % 2) * 256 + (pg % 16) * 16
                T(lambda e, s_=s_, g=g, d=d, c0=c0: e.matmul(bS[g // 2][:, c0:c0 + 16], lhsT=sbf[d][:, g * 128:(g + 1) * 128],
                                                             rhs=WAB[:, s_ * 16:s_ * 16 + 16], start=True, stop=True),
                  [bk_, "WAB"], [bSk[g // 2]])
            if pg % 16 == 15:
                blk = pg // 16
                for gg in range(2):
                    vw_ = bS[gg][:, 0:512].rearrange("p (g i ab c) -> p g i ab c", g=2, i=16, ab=2)
                    V(lambda e, gg=gg, vw_=vw_, blk=blk: e.tensor_copy(
                        out=ATs[:, 2 * gg:2 * gg + 2, blk * 128:(blk + 1) * 128].rearrange("p g (i c) -> p g i c", c=8),
                        in_=vw_[:, :, :, 0, :]), [bSk[gg]], ["ATs"])
                    V(lambda e, gg=gg, vw_=vw_, blk=blk: e.tensor_copy(
                        out=BTs[:, 2 * gg:2 * gg + 2, blk * 128:(blk + 1) * 128].rearrange("p g (i c) -> p g i c", c=8),
                        in_=vw_[:, :, :, 1, :]), [bSk[gg]], ["BTs"])
        V(lambda e, s_=s_: e.scalar_tensor_tensor(out=pooled[:, :, 0:1023], in0=ATs[:, :, 0:1023], scalar=pec[:, s_:s_ + 1],
                                                  in1=BTs[:, :, 1:1024], op0=ALU.add, op1=ALU.add), ["ATs", "BTs", "pec"], ["pooled"])
        for nt in range(8):
            for g in range(4):
                T(lambda e, s_=s_, g=g, nt=nt: e.matmul(b1[:, g * 128:(g + 1) * 128], lhsT=pooled[:, g, nt * 128:(nt + 1) * 128],
                                                        rhs=pjb[:, s_ * 128:(s_ + 1) * 128], start=True, stop=True),
                  ["pooled", "pjb"], ["bank1"])
            if s_ == 0:
                S(lambda e: e.activation(out=ksq, in_=b1[:, 0:512], func=AF.Square), ["bank1"], ["ksq"])
                V(lambda e: e.tensor_reduce(out=kss, in_=ksq.rearrange("p (g d) -> p g d", g=4), axis=AX.X, op=ALU.add), ["ksq"], ["kss"])
                S(lambda e: e.activation(out=krs, in_=kss, func=AF.Ln, scale=1.0 / 128, bias=epsb), ["kss", "epsb"], ["krs"])
                S(lambda e: e.activation(out=krs, in_=krs, func=AF.Exp, scale=-0.5), ["krs"], ["krs"])
                V(lambda e: e.tensor_tensor(out=kcn.rearrange("p (g d) -> p g d", g=4), in0=b1[:, 0:512].rearrange("p (g d) -> p g d", g=4),
                                            in1=krs.unsqueeze(2).broadcast_to([128, 4, 128]), op=ALU.mult), ["bank1", "krs"], ["kcn"])
                V(lambda e: e.tensor_tensor(out=kcnb.rearrange("p (g d) -> p g d", g=4), in0=kcn.rearrange("p (g d) -> p g d", g=4),
                                            in1=kgain0.unsqueeze(1).broadcast_to([128, 4, 128]), op=ALU.mult), ["kcn", "kgain0"], ["kcnb"])
                for g in range(4):
                    T(lambda e, g=g: e.transpose(out=b0bf[:, g * 128:(g + 1) * 128], in_=kcnb[:, g * 128:(g + 1) * 128], identity=idb),
                      ["kcnb", "idb"], ["bank0"])
                V(lambda e, nt=nt: e.tensor_copy(out=kcTs[:, :, nt * 128:(nt + 1) * 128], in_=b0bf[:, 0:512].rearrange("p (g n) -> p g n", g=4)),
                  ["bank0"], ["kcTs"])
            else:
                V(lambda e, nt=nt: e.tensor_copy(out=vces[:, nt, :, 0:128], in_=b1[:, 0:512].rearrange("p (g e) -> p g e", g=4)),
                  ["bank1"], ["vces"])

    Ef = cv.f32(256)
    Ec = cv.bf16(256)
    rdb = cv.f32(32)
    impT = cv.f32(64)
    sc = cv.f32(264)
    cmpc = cv.f32(16 * 257)
    cnt = cv.f32(264)
    sneg = cv.bf16(264)
    snT = cv.f32(16)
    snP = cv.bf16(128 * 8)
    snP4 = [cv.bf16(128 * 32) for _ in range(4)]
    SNd = nc.dram_tensor("SNd", [4, 264, 8], BF16, kind="Internal").ap()
    for g in range(4):
        q4 = qTs[:, g * 32:(g + 1) * 32]
        for nt in range(8):
            T(lambda e, g=g, nt=nt, q4=q4: e.matmul(bS0[:, nt * 32:(nt + 1) * 32], lhsT=kcTs[:, g, nt * 128:(nt + 1) * 128], rhs=q4,
                                                    start=True, stop=False), ["kcTs", "qTs"], ["bank2"])
            if nt == 7:
                T(lambda e, g=g, nt=nt: e.matmul(bS0[:, nt * 32:(nt + 1) * 32], lhsT=idb, rhs=Bc7[:, g * 32:(g + 1) * 32], start=False, stop=True),
                  ["idb", "Bc7"], ["bank2"])
            else:
                T(lambda e, g=g, nt=nt: e.matmul(bS0[:, nt * 32:(nt + 1) * 32], lhsT=ones1[0:1, :], rhs=tbrow[0:1, g * 32:(g + 1) * 32],
                                                 start=False, stop=True), ["ones1", "tbrow"], ["bank2"])
        S(lambda e: e.activation(out=Ef, in_=bS0[:, 0:256], func=AF.Exp), ["bank2"], ["Ef"])
        G(lambda e: e.tensor_copy(out=Ec, in_=Ef), ["Ef"], ["Ec"])
        for nt in range(8):
            T(lambda e, g=g, nt=nt: e.matmul(bO4[g][:32, 0:129], lhsT=Ec[:, nt * 32:(nt + 1) * 32], rhs=vces[:, nt, g, 0:129],
                                             start=(nt == 0), stop=(nt == 7)), ["Ec", "vces"], [bO4k[g]])
        for nt in range(8):
            T(lambda e, nt=nt: e.matmul(b1[:, 0:32], lhsT=onesf, rhs=Ef[:, nt * 32:(nt + 1) * 32], start=(nt == 0), stop=(nt == 7)),
              ["onesf", "Ef"], ["bank1"])
        V(lambda e: e.tensor_scalar(out=rdb, in0=b1[:, 0:32], scalar1=1e-30, scalar2=None, op0=ALU.add), ["bank1"], ["rdb"])
        V(lambda e: e.reciprocal(out=rdb, in_=rdb), ["rdb"], ["rdb"])
        V(lambda e: e.tensor_tensor(out=Ef.rearrange("p (a c) -> p a c", a=8), in0=Ef.rearrange("p (a c) -> p a c", a=8),
                                    in1=rdb.unsqueeze(1).broadcast_to([128, 8, 32]), op=ALU.mult), ["Ef", "rdb"], ["Ef"])
        V(lambda e: e.tensor_reduce(out=impT.rearrange("p (a t) -> p a t", a=8), in_=Ef.rearrange("p (a h t) -> p a t h", a=8, h=4),
                                    axis=AX.X, op=ALU.add), ["Ef"], ["impT"])
        for nt in range(8):
            T(lambda e, nt=nt: e.matmul(b0[:8, 0:257], lhsT=impT[:, nt * 8:(nt + 1) * 8], rhs=Ms[:, nt, :], start=(nt == 0), stop=(nt == 7)),
              ["impT", "nsas"], ["bank0"])
        V(lambda e: e.tensor_tensor(out=sc[:8, 0:257], in0=b0[:8, 0:257], in1=KMs[:8, :], op=ALU.mult), ["bank0", "nsas"], ["sc"])
        V(lambda e: e.tensor_tensor(out=sc[:8, 0:257], in0=sc[:8, 0:257], in1=FMs[:8, :], op=ALU.add), ["sc", "nsas"], ["sc"])
        for j0 in range(0, 257, 16):
            jw = min(16, 257 - j0)
            V(lambda e, j0=j0, jw=jw: e.tensor_tensor(out=cmpc[:8, 0:jw * 257].rearrange("p (a b) -> p a b", a=jw),
                                                      in0=sc[:8, 0:257].unsqueeze(1).broadcast_to([8, jw, 257]),
                                                      in1=sc[:8, j0:j0 + jw].unsqueeze(2).broadcast_to([8, jw, 257]), op=ALU.is_gt),
              ["sc"], ["cmpc"])
            V(lambda e, j0=j0, jw=jw: e.tensor_reduce(out=cnt[:8, j0:j0 + jw], in_=cmpc[:8, 0:jw * 257].rearrange("p (a b) -> p a b", a=jw),
                                                      axis=AX.X, op=ALU.add), ["cmpc"], ["cnt"])
        V(lambda e: e.tensor_scalar(out=sneg[:8, 0:257], in0=cnt[:8, 0:257], scalar1=15.5, scalar2=NEGM, op0=ALU.is_gt, op1=ALU.mult),
          ["cnt"], ["sneg"])
        for jt, (j0, jw) in enumerate(((0, 128), (128, 128), (256, 1))):
            T(lambda e, j0=j0, jw=jw: e.transpose(out=b0bf[:jw, 512:520], in_=sneg[:8, j0:j0 + jw], identity=idb[:8, :8]),
              ["sneg", "idb"], ["bank0"])
            V(lambda e, jw=jw: e.tensor_copy(out=snT.bitcast(BF16)[:jw, 0:8], in_=b0bf[:jw, 512:520]), ["bank0"], ["snT"])
            P.dma("sync", SNd[g, j0:j0 + jw, :], snT.bitcast(BF16)[:jw, 0:8], reads=["snT"], writes=["SNd"])
        P.dma("sync", snP[:2, :].rearrange("p (a t) -> p a t", a=128), AP(SNd.tensor, g * 264 * 8, [[8, 2], [16, 128], [1, 8]]),
              reads=["SNd"], writes=["snP"])
        V(lambda e, g=g: e.tensor_copy(out=snP4[g][:2, :].rearrange("p (a h t) -> p a h t", a=128, h=4),
                                       in_=snP[:2, :].rearrange("p (a t) -> p a t", a=128).unsqueeze(2).broadcast_to([2, 128, 4, 8])),
          ["snP"], ["snP4%d" % g])
    finalize(0, True)

    kTp = [cv.bf16(512) for _ in range(2)]
    vpe = [cv.bf16(4 * 132).rearrange("p (g e) -> p g e", g=4) for _ in range(2)]
    E4 = [cv.bf16(128) for _ in range(2)]
    for d in range(2):
        V(lambda e, d=d: e.memset(vpe[d], 1.0), [], ["vpe%d" % d])

    def attend_tile(br, it, first, last, kload, vload, rows, bias_of):
        d = it % 2
        r = rows
        sk, sv = "sg%d" % (2 + d), "sg%d" % d
        kload(stg[2 + d], sk)
        vload(stg[d], sv)
        G(lambda e, d=d, r=r: e.tensor_copy(out=sbf[d][:r, :], in_=stg[2 + d][:r, :]), [sk], ["sb%d" % d])
        V(lambda e, d=d, r=r: e.tensor_copy(out=vpe[d][:r, :, 0:128], in_=stg[d][:r, :].rearrange("p (g e) -> p g e", g=4)), [sv], ["vpe%d" % d])
        bT, bTk = (b0bf, "bank0") if d == 0 else (b1bf, "bank1")
        for g in range(4):
            T(lambda e, d=d, g=g, r=r, bT=bT: e.transpose(out=bT[:, g * 128:g * 128 + r], in_=sbf[d][:r, g * 128:(g + 1) * 128],
                                                         identity=idb[:r, :r]), ["sb%d" % d, "idb"], [bTk])
        V(lambda e, d=d, r=r, bT=bT: e.tensor_copy(out=kTp[d].rearrange("p (g k) -> p g k", g=4)[:, :, 0:r],
                                                   in_=bT[:, 0:512].rearrange("p (g k) -> p g k", g=4)[:, :, 0:r]), [bTk], ["kTp%d" % d])
        for g in range(4):
            T(lambda e, d=d, g=g, r=r: e.matmul(bS[d][:r, g * 32:(g + 1) * 32], lhsT=kTp[d][:, g * 128:g * 128 + r],
                                                rhs=qTs[:, g * 32:(g + 1) * 32], start=True, stop=False), ["kTp%d" % d, "qTs"], [bSk[d]])
            if br == 1 and it < 128:
                T(lambda e, d=d, g=g, it=it: e.matmul(bS[d][:, g * 32:(g + 1) * 32], lhsT=E2[:2, :], rhs=snP4[g][:2, it * 32:(it + 1) * 32],
                                                      start=False, stop=False), ["E2", "snP4%d" % g], [bSk[d]])
            lh, rh, rk = bias_of(g)
            T(lambda e, d=d, g=g, r=r, lh=lh, rh=rh: e.matmul(bS[d][:r, g * 32:(g + 1) * 32], lhsT=lh, rhs=rh, start=False, stop=True),
              ["idb", "ones1", rk], [bSk[d]])
        S(lambda e, d=d, r=r: e.activation(out=E4[d][:r, :], in_=bS[d][:r, 0:128], func=AF.Exp), [bSk[d]], ["E4%d" % d])
        for g in range(4):
            T(lambda e, d=d, g=g, r=r: e.matmul(bO4[g][:32, 0:129], lhsT=E4[d][:r, g * 32:(g + 1) * 32], rhs=vpe[d][:r, g, 0:129],
                                                start=first, stop=last), ["E4%d" % d, "vpe%d" % d], [bO4k[g]])

    cbias = lambda g: (ones1[0:1, :], tbrow[0:1, g * 32:(g + 1) * 32], "tbrow")
    for pg in range(128):
        bias_of = (lambda g: (idb, B127[:, g * 32:(g + 1) * 32], "B127")) if pg == 127 else cbias
        attend_tile(1, pg, pg == 0, False,
                    lambda dst, key, pg=pg: gather(dst, key, pool_ks, pg),
                    lambda dst, key, pg=pg: gather(dst, key, pool_vs, pg), 128, bias_of)
    newbias = lambda g: (idb[:8, :8], Bnew[:8, g * 32:(g + 1) * 32], "Bnew")
    attend_tile(1, 128, False, True,
                lambda dst, key: P.dma("sync", dst[:8, :], KSNs, reads=[ksn_key], writes=[key]),
                lambda dst, key: P.dma("sync", dst[:8, :], U[T0:T0 + 8, C_VS:C_VS + 512], reads=["U"], writes=[key]), 8, newbias)
    finalize(1, False)

    for kt in range(4):
        if kt == 0:
            bias_of = lambda g: (idb, Bw0[:, g * 32:(g + 1) * 32], "Bw0")
        elif kt == 3:
            bias_of = lambda g: (idb, B127[:, g * 32:(g + 1) * 32], "B127")
        else:
            bias_of = cbias
        attend_tile(2, 200 + kt, kt == 0, False,
                    lambda dst, key, kt=kt: P.dma("sync", dst, ckw[kt * 128:(kt + 1) * 128, :], writes=[key]),
                    lambda dst, key, kt=kt: P.dma("sync", dst, cvw[kt * 128:(kt + 1) * 128, :], writes=[key]), 128, bias_of)
    attend_tile(2, 204, False, True,
                lambda dst, key: P.dma("sync", dst[:8, :], KWN[T0:T0 + 8, :], reads=["KWN"], writes=[key]),
                lambda dst, key: P.dma("sync", dst[:8, :], U[T0:T0 + 8, C_VW:C_VW + 512], reads=["U"], writes=[key]), 8, newbias)
    finalize(2, False)

    V(lambda e: e.tensor_tensor(out=acc[:32, :], in0=acc[:32, :], in1=zT[:32, :], op=ALU.mult), ["acc", "zT"], ["acc"])
    for h4 in range(4):
        P.dma("scalar", AP(OB.tensor, T0 * 2048 + h4 * 128, [[2048, 8], [512, 4], [1, 128]]),
              acc[h4 * 8:(h4 + 1) * 8, :].rearrange("p (g d) -> p g d", g=4), reads=["acc"], writes=["OB"], group="acc")


def build_program():
    nc = bass.Bass("TRN2", target_bir_lowering=False)
    P = Prog(nc)

    def din(name, shape, dt=F32):
        return nc.dram_tensor(name, list(shape), dt, kind="ExternalInput").ap()

    def dout(name, shape, dt=F32):
        return nc.dram_tensor(name, list(shape), dt, kind="ExternalOutput").ap()

    def dscr(name, shape, dt=F32):
        return nc.dram_tensor(name, list(shape), dt, kind="Internal").ap()

    xp = din("xp", [SEQ, D_MODEL])
    xs = din("xs", [DEC_SEQ, D_MODEL])
    w_in = din("w_in", [D_MODEL, IN_W])
    norm_g = din("norm_g", [1, D_MODEL])
    k_norm_g = din("k_norm_g", [3, 128])
    ident = din("ident", [128, 128])
    ckw = din("ckw", [512, 512])
    cvw = din("cvw", [512, 512])
    consts_d = din("consts", [128, 6, 128])
    conv_w = din("conv_w", [4, 8192])
    a_log = din("a_log", [1, 32])
    dt_bias = din("dt_bias", [1, 32])
    gnorm_g = din("gnorm_g", [1, 128])
    sconv = din("sconv", [3, 8192])
    sgdn = din("sgdn", [32, 128, 128])
    nsac_d = din("nsac", [128, NSAC_W])
    nsas_d = din("nsas", [128, NSAS_W])
    pool_kc = din("pool_kc", [1280 * 128, 512])
    pool_vc = din("pool_vc", [1280 * 128, 512])
    pool_ks = din("pool_ks", [1280 * 128, 512])
    pool_vs = din("pool_vs", [1280 * 128, 512])
    ptab = din("ptab", [1, 128], I32)
    oh_d = din("oh", [33, NSA_L])
    rel_bias = din("rel_bias", [32, 16])
    q_norm_g = din("q_norm_g", [1, 128])
    pe_k = din("pe_k", [32, 128])
    w_k = din("w_k", [32, 1])
    proj_k = din("proj_k", [128, 128])
    pe_v = din("pe_v", [32, 128])
    w_v = din("w_v", [32, 1])
    proj_v = din("proj_v", [128, 128])
    w_a = din("w_a", [4096, D_MODEL])
    w_b = din("w_b", [D_MODEL, D_MODEL])
    w_o = din("w_o", [D_MODEL, D_MODEL])

    o_y_p = dout("y_p", [SEQ, D_MODEL])
    o_y_s = dout("y_s", [DEC_SEQ, D_MODEL])
    o_p_kc = dout("p_kc", [SEQ, 512])
    o_p_vc = dout("p_vc", [SEQ, 512])
    o_p_ks = dout("p_ks", [SEQ, 512])
    o_p_vs = dout("p_vs", [SEQ, 512])
    o_p_kw = dout("p_kw", [512, 512])
    o_p_vw = dout("p_vw", [512, 512])
    o_p_conv = dout("p_conv", [3, 8192])
    o_p_gdn = dout("p_gdn", [32 * 128, 128])
    o_s_kc = dout("s_kc", [DEC_SEQ, 512])
    o_s_vc = dout("s_vc", [DEC_SEQ, 512])
    o_s_ks = dout("s_ks", [DEC_SEQ, 512])
    o_s_vs = dout("s_vs", [DEC_SEQ, 512])
    o_s_kw = dout("s_kw", [512, 512])
    o_s_vw = dout("s_vw", [512, 512])
    o_s_conv = dout("s_conv", [3, 8192])
    o_s_gdn = dout("s_gdn", [32 * 128, 128])

    U = dscr("U", [NTOK, IN_W])
    OA = dscr("OA", [NTOK, 4096])
    OB = dscr("OB", [NTOK, 2048])
    KWN = dscr("KWN", [NTOK, 512])
    TVd = nc.dram_tensor("TV", [16, NSA_L], BF16, kind="Internal").ap()
    Gd = nc.dram_tensor("G", [16, 128, NSA_L], BF16, kind="Internal").ap()
    M1 = dscr("M1", [NTOK, D_MODEL])
    M2 = dscr("M2", [NTOK, D_MODEL])

    NW = 48640
    big = P.stack.enter_context(nc.sbuf_tensor("big", [128, NW], F32))
    cv = Carver(big, NW)
    banks = [P.stack.enter_context(nc.psum_tensor("bank%d" % i, [128, 512], F32)) for i in range(8)]

    idf = cv.f32(128)
    idb = cv.bf16(128)
    epsb = cv.f32(1)
    base0 = cv.off

    P.dma("sync", idf, ident, writes=["idf"])
    P.op("vector", lambda e: e.tensor_copy(out=idb, in_=idf), reads=["idf"], writes=["idb"])
    P.op("vector", lambda e: e.memset(epsb, EPS), writes=["epsb"])

    NT = 17
    KC = 16

    def rows(i):
        return 128 if i < 16 else DEC_SEQ

    cv.reset(base0)
    hT = cv.bf16(KC * NTOK).rearrange("p (k t) -> p k t", k=KC)
    gb = cv.f32(D_MODEL)
    wf = [cv.f32(KC * 512).rearrange("p (k n) -> p k n", k=KC) for _ in range(2)]
    wb = [cv.bf16(KC * 512).rearrange("p (k n) -> p k n", k=KC) for _ in range(2)]
    hb = cv.bf16(D_MODEL)
    ss = cv.f32(1)
    rstd = cv.f32(1)
    ost = [cv.f32(512) for _ in range(4)]
    xt = [wf[i].rearrange("p k n -> p (k n)")[:, 0:D_MODEL] for i in range(2)]
    sq = cv.f32(D_MODEL)

    P.dma("sync", gb, norm_g.broadcast_to([128, D_MODEL]), writes=["gb"])

    for i in range(NT):
        r = rows(i)
        xi = xt[i % 2]
        xk = "wf%d" % (i % 2)
        src = xp[i * 128:(i + 1) * 128, :] if i < 16 else xs
        P.dma("sync", xi[:r, :], src, writes=[xk])
        P.op("scalar", lambda e, xi=xi, r=r: e.activation(out=sq[:r, :], in_=xi[:r, :], func=AF.Square,
                                                            accum_out=ss[:r, :]),
             reads=[xk], writes=["sq", "ss"])
        P.op("scalar", lambda e, r=r: e.activation(out=rstd[:r, :], in_=ss[:r, :], func=AF.Ln,
                                                    scale=1.0 / D_MODEL, bias=epsb[:r, :]),
             reads=["ss", "epsb"], writes=["rstd"])
        P.op("scalar", lambda e, r=r: e.activation(out=rstd[:r, :], in_=rstd[:r, :], func=AF.Exp, scale=-0.5),
             reads=["rstd"], writes=["rstd"])
        P.op("vector", lambda e, xi=xi, r=r: e.scalar_tensor_tensor(out=hb[:r, :], in0=xi[:r, :], scalar=rstd[:r, 0:1],
                                                                     in1=gb[:r, :], op0=ALU.mult, op1=ALU.mult),
             reads=[xk, "rstd", "gb"], writes=["hb"])
        for k4 in range(4):
            bk = banks[k4 % 2]
            bkey = "bank%d" % (k4 % 2)
            pT = bk[:].bitcast(BF16)
            for j in range(4):
                k = k4 * 4 + j
                P.op("tensor", lambda e, k=k, j=j, r=r, pT=pT: e.transpose(
                    out=pT[:, j * 128:j * 128 + r], in_=hb[:r, k * 128:(k + 1) * 128], identity=idb[:r, :r]),
                    reads=["hb", "idb"], writes=[bkey], skip_same=True)
            eng = "vector" if k4 % 2 == 0 else "gpsimd"
            eng = "vector"
            P.op(eng, lambda e, k4=k4, r=r, pT=pT, i=i: e.tensor_copy(
                out=hT[:, k4 * 4:(k4 + 1) * 4, i * 128:i * 128 + r],
                in_=pT[:, 0:512].rearrange("p (j t) -> p j t", j=4)[:, :, 0:r]),
                reads=[bkey], writes=["hT"])

    NCB = (IN_W + 511) // 512
    ev = 0
    for cb in range(NCB):
        c0 = cb * 512
        cw = min(512, IN_W - c0)
        wfi, wbi = wf[cb % 2], wb[cb % 2]
        wfk, wbk = "wf%d" % (cb % 2), "wb%d" % (cb % 2)
        wsrc = w_in[:, c0:c0 + cw].rearrange("(k p) n -> p k n", p=128)
        P.dma("sync", wfi[:, 0:8, 0:cw], wsrc[:, 0:8, :], writes=[wfk])
        P.dma("sync", wfi[:, 8:16, 0:cw], wsrc[:, 8:16, :], writes=[wfk])
        P.op("gpsimd", lambda e, wfi=wfi, wbi=wbi, cw=cw: e.tensor_copy(out=wbi[:, :, 0:cw], in_=wfi[:, :, 0:cw]),
             reads=[wfk], writes=[wbk])
        for i in range(NT):
            r = rows(i)
            bi = 2 + (ev % 4)
            bk, bkey = banks[bi], "bank%d" % bi
            for k in range(KC):
                P.op("tensor", lambda e, k=k, i=i, r=r, bk=bk, wbi=wbi, cw=cw: e.matmul(
                    bk[:r, 0:cw], lhsT=hT[:, k, i * 128:i * 128 + r], rhs=wbi[:, k, 0:cw],
                    start=(k == 0), stop=(k == KC - 1)),
                    reads=["hT", wbk], writes=[bkey], skip_same=True)
            o = ost[ev % 4]
            okey = "ost%d" % (ev % 4)
            if ev % 2 == 0:
                P.op("scalar", lambda e, o=o, bk=bk, r=r, cw=cw: e.copy(out=o[:r, 0:cw], in_=bk[:r, 0:cw]),
                     reads=[bkey], writes=[okey])
            else:
                P.op("vector", lambda e, o=o, bk=bk, r=r, cw=cw: e.tensor_copy(out=o[:r, 0:cw], in_=bk[:r, 0:cw]),
                     reads=[bkey], writes=[okey])
            P.dma("scalar" if ev % 2 == 0 else "gpsimd", U[i * 128:i * 128 + r, c0:c0 + cw], o[:r, 0:cw],
                  reads=[okey], writes=["U"], group=okey)
            ev += 1

    P.barrier()

    def cp(dst, src, key):
        P.dma("sync", dst, src, reads=["U"], writes=[key])

    cp(o_p_kc, U[0:SEQ, C_KC:C_KC + 512], "o_p_kc")
    cp(o_p_vc, U[0:SEQ, C_VC:C_VC + 512], "o_p_vc")
    cp(o_p_vs, U[0:SEQ, C_VS:C_VS + 512], "o_p_vs")
    cp(o_p_vw, U[SEQ - 512:SEQ, C_VW:C_VW + 512], "o_p_vw")
    cp(o_p_conv, U[SEQ - 3:SEQ, 0:8192], "o_p_conv")
    cp(o_s_kc, U[SEQ:NTOK, C_KC:C_KC + 512], "o_s_kc")
    cp(o_s_vc, U[SEQ:NTOK, C_VC:C_VC + 512], "o_s_vc")
    cp(o_s_vs, U[SEQ:NTOK, C_VS:C_VS + 512], "o_s_vs")
    cp(o_s_conv, U[NTOK - 3:NTOK, 0:8192], "o_s_conv")
    cp(o_s_vw[0:504, :], cvw[8:512, :], "o_s_vw")
    cp(o_s_vw[504:512, :], U[SEQ:NTOK, C_VW:C_VW + 512], "o_s_vw")
    cp(o_s_kw[0:504, :], ckw[8:512, :], "o_s_kw")

    cv.reset(base0)
    kg = cv.f32(2 * 512)
    kt = [cv.f32(512) for _ in range(2)]
    ksq = cv.f32(512)
    kss = cv.f32(4)
    krs = cv.f32(4)
    kn = [cv.f32(512) for _ in range(2)]
    for wi, row in enumerate((1, 2)):
        for g in range(4):
            P.dma("sync", kg[:, wi * 512 + g * 128: wi * 512 + (g + 1) * 128],
                  k_norm_g[row:row + 1, :].broadcast_to([128, 128]), writes=["kg"])
    cnt = 0
    for wi, col in enumerate((C_KS, C_KW)):
        for i in range(NT):
            r = rows(i)
            t = kt[cnt % 2]
            tk = "kt%d" % (cnt % 2)
            o = kn[cnt % 2]
            ok = "kn%d" % (cnt % 2)
            P.dma("sync", t[:r, :], U[i * 128:i * 128 + r, col:col + 512], reads=["U"], writes=[tk])
            P.op("vector", lambda e, t=t, r=r: e.tensor_tensor(out=ksq[:r, :], in0=t[:r, :], in1=t[:r, :], op=ALU.mult),
                 reads=[tk], writes=["ksq"])
            P.op("vector", lambda e, r=r: e.tensor_reduce(out=kss[:r, :], in_=ksq[:r, :].rearrange("p (g d) -> p g d", g=4),
                                                           axis=AX.X, op=ALU.add),
                 reads=["ksq"], writes=["kss"])
            P.op("scalar", lambda e, r=r: e.activation(out=krs[:r, :], in_=kss[:r, :], func=AF.Ln, scale=1.0 / 128,
                                                        bias=epsb[:r, :]),
                 reads=["kss", "epsb"], writes=["krs"])
            P.op("scalar", lambda e, r=r: e.activation(out=krs[:r, :], in_=krs[:r, :], func=AF.Exp, scale=-0.5),
                 reads=["krs"], writes=["krs"])
            for g in range(4):
                P.op("vector", lambda e, t=t, o=o, r=r, g=g, wi=wi: e.scalar_tensor_tensor(
                    out=o[:r, g * 128:(g + 1) * 128], in0=t[:r, g * 128:(g + 1) * 128], scalar=krs[:r, g:g + 1],
                    in1=kg[:r, wi * 512 + g * 128: wi * 512 + (g + 1) * 128], op0=ALU.mult, op1=ALU.mult),
                    reads=[tk, "krs", "kg"], writes=[ok])
            if wi == 0:
                dst = o_p_ks[i * 128:(i + 1) * 128, :] if i < 16 else o_s_ks
                dk = "o_p_ks" if i < 16 else "o_s_ks"
            else:
                P.dma("scalar", KWN[i * 128:i * 128 + r, :], o[:r, :], reads=[ok], writes=["KWN"], group=ok)
                if i < 12:
                    cnt += 1
                    continue
                dst = o_p_kw[(i - 12) * 128:(i - 11) * 128, :] if i < 16 else o_s_kw[504:512, :]
                dk = "o_p_kw" if i < 16 else "o_s_kw"
            P.dma("scalar", dst, o[:r, :], reads=[ok], writes=[dk], group=ok)
            cnt += 1

    P.barrier()
    cv.reset(base0)
    stage_gdn(P, nc, cv, banks, U, OA, consts_d, conv_w, a_log, dt_bias, gnorm_g, sconv, sgdn,
              o_p_gdn, o_s_gdn, idb, epsb, list(range(16)), 16)

    P.barrier()
    cv.reset(base0)
    stage_nsa_prompt(P, nc, cv, banks, U, o_p_ks, "o_p_ks", KWN, OB, nsac_d, oh_d, TVd, Gd, rel_bias, q_norm_g, k_norm_g,
                     pe_k, w_k, proj_k, pe_v, w_v, proj_v, idf, idb, epsb, 16)

    P.barrier()
    cv.reset(base0)
    stage_nsa_sample(P, nc, cv, banks, U, o_s_ks, "o_s_ks", KWN, OB, nsac_d, nsas_d, TVd, Gd, q_norm_g, k_norm_g,
                     pe_k, w_k, proj_k, pe_v, w_v, proj_v, pool_kc, pool_vc, pool_ks, pool_vs, ckw, cvw, ptab,
                     idf, idb, epsb)

    def xsrc(i, r, c0, cw):
        return (xp[i * 128:i * 128 + r, c0:c0 + cw], "xp") if i < 16 else (xs[0:r, c0:c0 + cw], "xs")

    def ydst(i, r, c0, cw):
        return (o_y_p[i * 128:i * 128 + r, c0:c0 + cw], "o_y_p") if i < 16 else (o_y_s[0:r, c0:c0 + cw], "o_y_s")

    stage_merge(P, cv, banks, idb, base0, U, OA, OB, M1, M2, w_a, w_b, w_o, xsrc, ydst, NTOK)

    P.barrier()
    P.emit()
    return nc


_NC_CACHE = {}


def kernel(**inputs):
    f = lambda a: np.ascontiguousarray(np.asarray(a, dtype=np.float32))
    x_prompt = f(inputs["x_prompt"])
    x_sample = f(inputs["x_sample"])
    w_in = f(inputs["w_in"])
    norm_g = f(inputs["norm_g"]).reshape(1, D_MODEL)
    k_norm_g = f(inputs["k_norm_g"])
    ckw = f(inputs["cache_k_win"])
    cvw = f(inputs["cache_v_win"])
    ident = np.eye(128, dtype=np.float32)
    consts = make_consts()
    conv_w = f(inputs["gdn_conv_w"])
    a_log = f(inputs["gdn_a_log"]).reshape(1, 32)
    dt_bias = f(inputs["gdn_dt_bias"]).reshape(1, 32)
    gnorm_g = f(inputs["gdn_norm_g"]).reshape(1, 128)
    sconv = f(inputs["state_conv"])
    sgdn = f(inputs["state_gdn"])
    oh, nsac = make_nsa_consts(16)
    nsas = make_nsa_sample_consts()
    pool_kc = f(inputs["cache_k_cmp"]).reshape(1280 * 128, 512)
    pool_vc = f(inputs["cache_v_cmp"]).reshape(1280 * 128, 512)
    pool_ks = f(inputs["cache_k_sel"]).reshape(1280 * 128, 512)
    pool_vs = f(inputs["cache_v_sel"]).reshape(1280 * 128, 512)
    ptab = np.ascontiguousarray(np.asarray(inputs["page_table"], dtype=np.int32))
    rel_bias = f(inputs["rel_bias"])
    q_norm_g = f(inputs["q_norm_g"]).reshape(1, 128)
    pe_k = f(inputs["cmp_pe_k"])
    w_k = f(inputs["cmp_w_k"]).reshape(32, 1)
    proj_k = f(inputs["cmp_proj_k"])
    pe_v = f(inputs["cmp_pe_v"])
    w_v = f(inputs["cmp_w_v"]).reshape(32, 1)
    proj_v = f(inputs["cmp_proj_v"])
    w_a = f(inputs["w_branch_a"])
    w_b = f(inputs["w_branch_b"])
    w_o = f(inputs["w_out"])

    if "nc" not in _NC_CACHE:
        _NC_CACHE["nc"] = build_program()
    nc = _NC_CACHE["nc"]

    in_maps = []
    for c in range(8):
        b = c // 2
        in_maps.append({
            "xp": x_prompt[b],
            "xs": x_sample[c],
            "w_in": w_in,
            "norm_g": norm_g,
            "k_norm_g": k_norm_g,
            "ident": ident,
            "ckw": ckw[c].reshape(512, 512),
            "cvw": cvw[c].reshape(512, 512),
            "consts": consts,
            "conv_w": conv_w,
            "a_log": a_log,
            "dt_bias": dt_bias,
            "gnorm_g": gnorm_g,
            "sconv": sconv[c],
            "sgdn": sgdn[c],
            "nsac": nsac,
            "nsas": nsas,
            "pool_kc": pool_kc, "pool_vc": pool_vc, "pool_ks": pool_ks, "pool_vs": pool_vs,
            "ptab": ptab[c:c + 1],
            "oh": oh,
            "rel_bias": rel_bias,
            "q_norm_g": q_norm_g,
            "pe_k": pe_k, "w_k": w_k, "proj_k": proj_k,
            "pe_v": pe_v, "w_v": w_v, "proj_v": proj_v,
            "w_a": w_a,
            "w_b": w_b,
            "w_o": w_o,
        })
    res = run_bass_kernel_spmd(nc, in_maps, core_ids=list(range(8)))
    R = res.results

    def pst(name, shape):
        return np.stack([np.asarray(R[2 * b][name], dtype=np.float32).reshape(shape) for b in range(4)])

    def sst(name, shape):
        return np.stack([np.asarray(R[c][name], dtype=np.float32).reshape(shape) for c in range(8)])

    outs = (
        pst("y_p", (SEQ, D_MODEL)), sst("y_s", (DEC_SEQ, D_MODEL)),
        pst("p_kc", (SEQ, 4, 128)), pst("p_vc", (SEQ, 4, 128)), pst("p_ks", (SEQ, 4, 128)), pst("p_vs", (SEQ, 4, 128)),
        pst("p_kw", (512, 4, 128)), pst("p_vw", (512, 4, 128)),
        pst("p_conv", (3, 8192)), pst("p_gdn", (32, 128, 128)),
        sst("s_kc", (DEC_SEQ, 4, 128)), sst("s_vc", (DEC_SEQ, 4, 128)), sst("s_ks", (DEC_SEQ, 4, 128)),
        sst("s_vs", (DEC_SEQ, 4, 128)), sst("s_kw", (512, 4, 128)), sst("s_vw", (512, 4, 128)),
        sst("s_conv", (3, 8192)), sst("s_gdn", (32, 128, 128)),
    )
    return outs
```

```python
import os
import math
from contextlib import ExitStack
import numpy as np
import concourse.bass as bass
import concourse.mybir as mybir
from concourse.bass_utils import run_bass_kernel_spmd

F32 = mybir.dt.float32
BF16 = mybir.dt.bfloat16
I32 = mybir.dt.int32
ALU = mybir.AluOpType
AF = mybir.ActivationFunctionType
AX = mybir.AxisListType

EPOCH = 20000

D_MODEL = 2048
SEQ = 2048
DEC_SEQ = 8
NTOK = SEQ + DEC_SEQ
IN_W = 23664
C_Q, C_K, C_V, C_Z = 0, 2048, 4096, 8192
C_B, C_A = 12288, 12320
C_QB = 12352
C_KC, C_VC, C_KS, C_VS, C_KW, C_VW = 14400, 14912, 15424, 15936, 16448, 16960
C_GB = 17472
C_ZB = 17520
C_MA, C_MB = 19568, 21616
EPS = 1e-6


class Prog:
    ENGS = ("sync", "scalar", "vector", "gpsimd", "tensor")

    def __init__(self, nc):
        self.nc = nc
        self.stack = ExitStack()
        self.streams = {e: [] for e in self.ENGS}
        self.count = {e: 0 for e in self.ENGS}
        self.waited = {e: {} for e in self.ENGS}
        self.res = {}
        self.dma_sem = {}
        self.semkeys = set()
        self.n_ops = 0
        self.limit = int(os.environ.get("PROG_LIMIT", "1000000000"))
        self.alias = {}
        self.free_phys = []
        self.n_phys = 0
        self.log = []

    def _r(self, key):
        if key not in self.res:
            self.res[key] = [[], []]
        return self.res[key]

    def _deps(self, eng, reads, writes, skip_same=False):
        ev = {}

        def add(e):
            k, v = e
            if skip_same and k[0] == "eng" and k[1] == eng:
                return
            if ev.get(k, 0) < v:
                ev[k] = v
        for r in reads:
            for e in self._r(r)[0]:
                add(e)
        for w in writes:
            st = self._r(w)
            for e in st[0]:
                add(e)
            for e in st[1]:
                add(e)
        out = []
        wd = self.waited[eng]
        for k, v in ev.items():
            if wd.get(k, 0) < v:
                wd[k] = v
                out.append((k, v))
        return out

    def _commit(self, event, reads, writes):
        for r in reads:
            if r in writes:
                continue
            st = self._r(r)
            st[1] = [e for e in st[1] if e[0] != event[0]] + [event]
        for w in writes:
            st = self._r(w)
            st[0] = [e for e in st[0] if e[0] != event[0]] + [event]
            st[1] = []

    def op(self, eng, fn, reads=(), writes=(), skip_same=False):
        if self.n_ops >= self.limit:
            return
        self.log.append((eng, "op", tuple(reads), tuple(writes)))
        waits = self._deps(eng, reads, writes, skip_same)
        self.count[eng] += 1
        n = self.count[eng]
        ep = (n - 1) // EPOCH
        key = ("eng", eng, ep)
        val = n - ep * EPOCH
        self.semkeys.add(key)
        for k, _ in waits:
            self.semkeys.add(k)
        self.streams[eng].append((waits, fn, key, 1))
        self._commit((key, val), reads, writes)
        self.n_ops += 1

    def _phys(self, name):
        if name not in self.alias:
            if self.free_phys:
                self.free_phys.sort(key=lambda p: self.dma_sem.get(p, 0))
                self.alias[name] = self.free_phys.pop(0)
            else:
                self.alias[name] = "phys%d" % self.n_phys
                self.n_phys += 1
        return self.alias[name]

    def dma(self, q, out, in_, reads=(), writes=(), group=None, **kw):
        if self.n_ops >= self.limit:
            return
        self.log.append((q, "dma", tuple(reads), tuple(writes)))
        waits = self._deps(q, reads, writes)
        g = self._phys(group if group is not None else writes[0])
        key = ("dma", g)
        self.dma_sem[g] = self.dma_sem.get(g, 0) + 16
        val = self.dma_sem[g]
        self.semkeys.add(key)
        for k, _ in waits:
            self.semkeys.add(k)
        fn = (lambda e, out=out, in_=in_, kw=kw: e.dma_start(out=out, in_=in_, **kw))
        self.streams[q].append((waits, fn, key, 16))
        self._commit((key, val), reads, writes)
        self.n_ops += 1

    def dma_fn(self, q, fn, reads=(), writes=(), group=None):
        if self.n_ops >= self.limit:
            return
        self.log.append((q, "dmafn", tuple(reads), tuple(writes)))
        waits = self._deps(q, reads, writes)
        g = self._phys(group if group is not None else writes[0])
        key = ("dma", g)
        self.dma_sem[g] = self.dma_sem.get(g, 0) + 16
        val = self.dma_sem[g]
        self.semkeys.add(key)
        for k, _ in waits:
            self.semkeys.add(k)
        self.streams[q].append((waits, fn, key, 16))
        self._commit((key, val), reads, writes)
        self.n_ops += 1

    def barrier(self):
        if os.environ.get("PROG_VERBOSE"):
            print("barrier at op", self.n_ops, flush=True)
        evs = []
        for e in self.ENGS:
            n = self.count[e]
            if n > 0:
                ep = (n - 1) // EPOCH
                evs.append((("eng", e, ep), n - ep * EPOCH))
        for g, v in self.dma_sem.items():
            evs.append((("dma", g), v))
        for e in self.ENGS:
            waits = []
            for k, v in evs:
                if k[0] == "eng" and k[1] == e:
                    continue
                if self.waited[e].get(k, 0) < v:
                    self.waited[e][k] = v
                    waits.append((k, v))
                    self.semkeys.add(k)
            self.streams[e].append((waits, None, None, 0))
        self.free_phys = sorted(set(self.free_phys) | set(self.alias.values()))
        self.alias = {}

    def emit(self):
        nc = self.nc
        sems = {}
        for i, k in enumerate(sorted(self.semkeys, key=str)):
            sems[k] = self.stack.enter_context(nc.semaphore("s%d" % i))
        self.nsems = len(sems)
        streams = self.streams

        def run(e, lst):
            for waits, fn, key, inc in lst:
                for k, v in waits:
                    e.wait_ge(sems[k], v)
                if fn is not None:
                    fn(e).then_inc(sems[key], inc)

        with nc.Block() as block:
            @block.sync
            def _(e):
                run(e, streams["sync"])

            @block.scalar
            def _(e):
                run(e, streams["scalar"])

            @block.vector
            def _(e):
                run(e, streams["vector"])

            @block.gpsimd
            def _(e):
                run(e, streams["gpsimd"])

            @block.tensor
            def _(e):
                run(e, streams["tensor"])
        self.stack.close()


class Carver:
    def __init__(self, big, nwords):
        self.big = big
        self.n = nwords
        self.off = 0

    def reset(self, off=0):
        self.off = off

    def f32(self, nwords, shape=None):
        a = self.big[:, self.off:self.off + nwords]
        self.off += (nwords + 7) // 8 * 8
        assert self.off <= self.n, ("SBUF overflow", self.off, self.n)
        return a

    def bf16(self, nelem):
        assert nelem % 2 == 0
        return self.f32(nelem // 2).bitcast(BF16)


def make_consts():
    c = np.zeros((128, 6, 128), np.float32)
    p = np.arange(128)[:, None]
    f = np.arange(128)[None, :]
    c[:, 0, :] = (p <= f)
    c[:, 1, :] = np.where(p > f, 0.0, -1e30)
    c[:, 2, :] = np.where(f > p, 0.0, -1e30)
    c[:, 3, :] = np.where(f >= p, 0.0, -1e30)
    c[:, 4, :] = 1.0
    c[:, 5, :] = (p == f)
    return c


def stage_gdn(P, nc, cv, banks, U, OA, consts_d, conv_w, a_log, dt_bias, gnorm_g, sconv, sgdn,
              o_p_gdn, o_s_gdn, idb, epsb, hq_list, n_ptiles):
    from concourse.ap import AP
    SEQ_ = n_ptiles * 128
    NT = n_ptiles + 1
    IW = U.shape[1]
    V = lambda fn, r, w: P.op("vector", fn, r, w)
    S = lambda fn, r, w: P.op("scalar", fn, r, w)
    G = lambda fn, r, w: P.op("gpsimd", fn, r, w)
    T = lambda fn, r, w: P.op("tensor", fn, r, w, skip_same=True)

    cst = cv.f32(6 * 128).rearrange("p (a b) -> p a b", a=6)
    tri, maskL, maskU, maskUE, ones_f, idf = (cst[:, k, :] for k in range(6))
    oneb = cv.f32(1)
    ba = cv.f32(NT * 64).rearrange("p (i c) -> p i c", i=NT)
    def gate_buf():
        return cv.f32(NT * 32).rearrange("p (i c) -> p i c", i=NT)
    lnb, beta, g_all, gc_all, a_all, ea_all, ngc_all, eg_all = (gate_buf() for _ in range(8))
    nA = cv.f32(32)
    dtb = cv.f32(32)
    gng = cv.f32(128)
    Sp = cv.f32(32 * 128).rearrange("p (h e) -> p h e", h=32)
    Ss = cv.f32(32 * 128).rearrange("p (h e) -> p h e", h=32)
    Sp_bf = cv.bf16(32 * 128).rearrange("p (h e) -> p h e", h=32)
    Ss_bf = cv.bf16(32 * 128).rearrange("p (h e) -> p h e", h=32)
    bT, bK, bM, bX, bR, bV, bS, bO = banks
    bTb = bT[:].bitcast(BF16)

    P.dma("sync", cst, consts_d, writes=["cst"])
    V(lambda e: e.memset(oneb, 1.0), [], ["oneb"])
    G(lambda e: e.memset(ba, 0.0), [], ["ba"])
    P.dma("sync", ba[:, 0:n_ptiles, :], U[0:SEQ_, C_B:C_B + 64].rearrange("(i p) c -> p i c", p=128),
          reads=["U"], writes=["ba"])
    P.dma("sync", ba[:DEC_SEQ, n_ptiles, :], U[SEQ_:SEQ_ + DEC_SEQ, C_B:C_B + 64], reads=["U"], writes=["ba"])
    P.dma("sync", nA, a_log.broadcast_to([128, 32]), writes=["nA"])
    P.dma("sync", dtb, dt_bias.broadcast_to([128, 32]), writes=["dtb"])
    P.dma("sync", gng, gnorm_g.broadcast_to([128, 128]), writes=["gng"])
    for q4 in range(4):
        P.dma("sync", Ss[:, q4 * 8:(q4 + 1) * 8, :], sgdn[q4 * 8:(q4 + 1) * 8].rearrange("h d e -> d h e"), writes=["Ss"])
    V(lambda e: e.memset(Sp, 0.0), [], ["Sp"])
    V(lambda e: e.memset(Sp_bf, 0.0), [], ["Sp_bf"])
    S(lambda e: e.copy(out=Ss_bf, in_=Ss), ["Ss"], ["Ss_bf"])
    S(lambda e: e.activation(out=nA, in_=nA, func=AF.Exp), ["nA"], ["nA"])
    V(lambda e: e.tensor_scalar(out=nA, in0=nA, scalar1=-1.0, scalar2=None, op0=ALU.mult), ["nA"], ["nA"])
    S(lambda e: e.activation(out=lnb, in_=ba[:, :, 0:32], func=AF.Exp, scale=-1.0), ["ba"], ["lnb"])
    S(lambda e: e.activation(out=lnb, in_=lnb, func=AF.Ln, bias=oneb), ["lnb", "oneb"], ["lnb"])
    V(lambda e: e.tensor_scalar(out=lnb, in0=lnb, scalar1=-1.0, scalar2=None, op0=ALU.mult), ["lnb"], ["lnb"])
    S(lambda e: e.activation(out=beta, in_=lnb, func=AF.Exp), ["lnb"], ["beta"])
    for i in range(NT):
        V(lambda e, i=i: e.tensor_tensor(out=g_all[:, i, :], in0=ba[:, i, 32:64], in1=dtb, op=ALU.add),
          ["ba", "dtb"], ["g_all"])
    t1, t2, t3, t4 = a_all, ea_all, ngc_all, eg_all
    V(lambda e: e.tensor_scalar(out=t2, in0=g_all, scalar1=-1.0, scalar2=None, op0=ALU.mult), ["g_all"], ["ea_all"])
    V(lambda e: e.tensor_tensor(out=t1, in0=g_all, in1=t2, op=ALU.max), ["g_all", "ea_all"], ["a_all"])
    S(lambda e: e.activation(out=t1, in_=t1, func=AF.Exp, scale=-1.0), ["a_all"], ["a_all"])
    V(lambda e: e.tensor_scalar(out=t2, in0=t1, scalar1=2.0, scalar2=None, op0=ALU.add), ["a_all"], ["ea_all"])
    V(lambda e: e.reciprocal(out=t2, in_=t2), ["ea_all"], ["ea_all"])
    V(lambda e: e.tensor_tensor(out=t2, in0=t2, in1=t1, op=ALU.mult), ["ea_all", "a_all"], ["ea_all"])
    V(lambda e: e.tensor_tensor(out=t3, in0=t2, in1=t2, op=ALU.mult), ["ea_all"], ["ngc_all"])
    V(lambda e: e.tensor_scalar(out=t4, in0=t3, scalar1=1.0 / 13, scalar2=None, op0=ALU.mult), ["ngc_all"], ["eg_all"])
    for cc in (1.0 / 11, 1.0 / 9, 1.0 / 7, 1.0 / 5, 1.0 / 3):
        V(lambda e, cc=cc: e.scalar_tensor_tensor(out=t4, in0=t4, scalar=cc, in1=t3, op0=ALU.add, op1=ALU.mult),
          ["eg_all", "ngc_all"], ["eg_all"])
    V(lambda e: e.scalar_tensor_tensor(out=t4, in0=t4, scalar=1.0, in1=t2, op0=ALU.add, op1=ALU.mult),
      ["eg_all", "ea_all"], ["eg_all"])
    V(lambda e: e.tensor_scalar(out=g_all, in0=g_all, scalar1=0.0, scalar2=None, op0=ALU.max), ["g_all"], ["g_all"])
    V(lambda e: e.scalar_tensor_tensor(out=g_all, in0=t4, scalar=2.0, in1=g_all, op0=ALU.mult, op1=ALU.add),
      ["eg_all", "g_all"], ["g_all"])
    for i in range(NT):
        V(lambda e, i=i: e.tensor_tensor(out=g_all[:, i, :], in0=g_all[:, i, :], in1=nA, op=ALU.mult),
          ["g_all", "nA"], ["g_all"])
    for i in range(NT):
        r = 128 if i < n_ptiles else DEC_SEQ
        T(lambda e, i=i, r=r: e.matmul(bM[:r, 0:32], lhsT=tri[:r, :r], rhs=g_all[:r, i, :], start=True, stop=True),
          ["cst", "g_all"], ["bM"])
        V(lambda e, i=i, r=r: e.tensor_copy(out=gc_all[:r, i, :], in_=bM[:r, 0:32]), ["bM"], ["gc_all"])
    for i in range(NT):
        r = 128 if i < n_ptiles else DEC_SEQ
        V(lambda e, i=i, r=r: e.tensor_tensor(out=a_all[:r, i, :], in0=lnb[:r, i, :], in1=gc_all[:r, i, :], op=ALU.add),
          ["lnb", "gc_all"], ["a_all"])
        V(lambda e, i=i, r=r: e.tensor_scalar(out=ngc_all[:r, i, :], in0=gc_all[:r, i, :], scalar1=-1.0, scalar2=None,
                                                op0=ALU.mult), ["gc_all"], ["ngc_all"])
        S(lambda e, i=i, r=r: e.activation(out=ea_all[:r, i, :], in_=a_all[:r, i, :], func=AF.Exp), ["a_all"], ["ea_all"])
        S(lambda e, i=i, r=r: e.activation(out=eg_all[:r, i, :], in_=gc_all[:r, i, :], func=AF.Exp), ["gc_all"], ["eg_all"])

    P.barrier()
    SHARED = {"cst", "U", "idb", "epsb", "gc_all", "a_all", "ea_all", "ngc_all", "beta", "eg_all", "gng", "oneb"}

    def build_chain(c, hqs):
        rec = []
        K = lambda names: [n if n in SHARED else "%s_%d" % (n, c) for n in names]
        V = lambda fn, r, w: rec.append(("op", "vector", fn, K(r), K(w), False))
        S = lambda fn, r, w: rec.append(("op", "scalar", fn, K(r), K(w), False))
        G = lambda fn, r, w: rec.append(("op", "gpsimd", fn, K(r), K(w), False))
        T = lambda fn, r, w: rec.append(("op", "tensor", fn, K(r), K(w), True))

        def D(q, out, in_, reads=(), writes=(), group=None):
            rec.append(("dma", q, out, in_, K(reads), K(writes), K([group])[0] if group else None))
        bA, bB, bC, bD = banks[4 * c:4 * c + 4]
        bTb = bA[:].bitcast(BF16)
        bK = bA[:, 256:512]
        bM = bB[:, 0:256]
        bX = bB[:, 256:512]
        bR = bC[:, 0:256]
        bV = bC[:, 256:384]
        bS = bC[:, 384:512]
        bO = bD[:, 0:128]
        Wc = cv.f32(4 * 512).rearrange("p (j c) -> p j c", j=4)
        X = [cv.f32(4 * 512).rearrange("p (j c) -> p j c", j=4) for _ in range(2)]
        Z = [cv.f32(256) for _ in range(2)]
        prod = cv.f32(4 * 512).rearrange("p (j c) -> p j c", j=4)
        conv = cv.f32(512)
        A = cv.f32(512)
        zs = cv.f32(256)
        sq2 = cv.f32(256)
        ss2 = cv.f32(2)
        rq = cv.f32(2)
        qn = cv.f32(128)
        kn = cv.f32(128)
        qn_bf = cv.bf16(128)
        kn_bf = cv.bf16(128)
        kT = cv.bf16(128)
        qT = cv.bf16(128)
        Dg = cv.f32(128)
        Da = cv.f32(128)
        arg = cv.f32(128)
        argT = cv.f32(128)
        argG = cv.f32(128)
        E1 = cv.f32(128)
        E1T = cv.f32(128)
        GT = cv.f32(128)
        XX = [cv.f32(256).rearrange("p (a b) -> p a b", a=2) for _ in range(2)]
        R = [cv.f32(256) for _ in range(2)]
        Rf = cv.f32(256)
        w_bf = cv.bf16(128)
        wT = cv.bf16(128)
        vnew_bf = cv.bf16(128)
        ekd = cv.f32(1)
        gl = cv.f32(1)
        kdec_bf = cv.bf16(128)
        attnT = cv.bf16(128)
        qg_bf = cv.bf16(128)
        qgT = cv.bf16(128)
        o_sb = cv.f32(128)
        osq = cv.f32(128)
        oss = cv.f32(1)
        ors = cv.f32(1)
        ot = cv.f32(128)
        oa = [cv.f32(256) for _ in range(2)]

        cnt = 0
        for hq in hqs:
            segs = ((C_Q + hq * 128, 0, 128), (C_K + hq * 128, 128, 128), (C_V + hq * 256, 256, 256))
            for col, s0, w in segs:
                D("sync", Wc[:, :, s0:s0 + w], AP(conv_w.tensor, col, [[0, 128], [8192, 4], [1, w]]), writes=["Wc"])
            for i in range(NT):
                samp = i == n_ptiles
                r = DEC_SEQ if samp else 128
                t0 = i * 128
                Xi = X[cnt % 2]
                Xk = "X%d" % (cnt % 2)
                Zi = Z[cnt % 2]
                Zk = "Z%d" % (cnt % 2)
                oai = oa[cnt % 2]
                oak = "oa%d" % (cnt % 2)
                cnt += 1
                Sx, Sx_bf, Sk, Sbk = (Ss, Ss_bf, "Ss", "Ss_bf") if samp else (Sp, Sp_bf, "Sp", "Sp_bf")
                if i == 0:
                    G(lambda e, Xi=Xi: e.memset(Xi, 0.0), [], [Xk])
                    for col, s0, w in segs:
                        for j in range(4):
                            sh = 3 - j
                            D("sync", Xi[sh:128, j, s0:s0 + w], U[0:128 - sh, col:col + w], reads=["U"], writes=[Xk])
                elif samp:
                    for col, s0, w in segs:
                        for j in range(4):
                            sh = 3 - j
                            if sh > 0:
                                D("sync", Xi[0:sh, j, s0:s0 + w], sconv[j:3, col:col + w], writes=[Xk])
                            D("sync", Xi[sh:DEC_SEQ, j, s0:s0 + w], U[t0:t0 + DEC_SEQ - sh, col:col + w],
                                  reads=["U"], writes=[Xk])
                else:
                    for col, s0, w in segs:
                        D("sync", Xi[:, :, s0:s0 + w],
                              AP(U.tensor, (t0 - 3) * IW + col, [[IW, 128], [IW, 4], [1, w]]), reads=["U"], writes=[Xk])
                D("sync", Zi[:r, :], U[t0:t0 + r, C_Z + hq * 256:C_Z + hq * 256 + 256], reads=["U"], writes=[Zk])
                G(lambda e, Xi=Xi, r=r: e.tensor_tensor(out=prod[:r], in0=Xi[:r], in1=Wc[:r], op=ALU.mult), [Xk, "Wc"], ["prod"])
                V(lambda e, r=r: e.tensor_reduce(out=conv[:r, :], in_=prod[:r].rearrange("p j c -> p c j"), axis=AX.X, op=ALU.add),
                  ["prod"], ["conv"])
                S(lambda e, r=r: e.activation(out=A[:r, :], in_=conv[:r, :], func=AF.Silu), ["conv"], ["A"])
                S(lambda e, r=r, Zi=Zi: e.activation(out=zs[:r, :], in_=Zi[:r, :], func=AF.Silu), [Zk], ["zs"])
                V(lambda e, r=r: e.tensor_tensor(out=sq2[:r, :], in0=A[:r, 0:256], in1=A[:r, 0:256], op=ALU.mult), ["A"], ["sq2"])
                V(lambda e, r=r: e.tensor_reduce(out=ss2[:r, :], in_=sq2[:r, :].rearrange("p (a b) -> p a b", a=2), axis=AX.X,
                                                  op=ALU.add), ["sq2"], ["ss2"])
                S(lambda e, r=r: e.activation(out=rq[:r, :], in_=ss2[:r, :], func=AF.Ln, bias=epsb[:r, :]), ["ss2", "epsb"], ["rq"])
                S(lambda e, r=r: e.activation(out=rq[:r, :], in_=rq[:r, :], func=AF.Exp, scale=-0.5), ["rq"], ["rq"])
                G(lambda e, r=r: e.tensor_scalar(out=qn[:r, :], in0=A[:r, 0:128], scalar1=rq[:r, 0:1], scalar2=128 ** -0.5,
                                                  op0=ALU.mult, op1=ALU.mult), ["A", "rq"], ["qn"])
                G(lambda e, r=r: e.tensor_scalar(out=kn[:r, :], in0=A[:r, 128:256], scalar1=rq[:r, 1:2], scalar2=None,
                                                  op0=ALU.mult), ["A", "rq"], ["kn"])
                G(lambda e, r=r: e.tensor_copy(out=qn_bf[:r, :], in_=qn[:r, :]), ["qn"], ["qn_bf"])
                G(lambda e, r=r: e.tensor_copy(out=kn_bf[:r, :], in_=kn[:r, :]), ["kn"], ["kn_bf"])
                T(lambda e, r=r: e.transpose(out=bTb[:, 0:r], in_=kn_bf[:r, :], identity=idb[:r, :r]), ["kn_bf", "idb"], ["bT"])
                T(lambda e, r=r: e.transpose(out=bTb[:, 128:128 + r], in_=qn_bf[:r, :], identity=idb[:r, :r]), ["qn_bf", "idb"], ["bT"])
                V(lambda e, r=r: e.tensor_copy(out=kT[:, :r], in_=bTb[:, 0:r]), ["bT"], ["kT"])
                V(lambda e, r=r: e.tensor_copy(out=qT[:, :r], in_=bTb[:, 128:128 + r]), ["bT"], ["qT"])
                T(lambda e, r=r: e.matmul(bK[:r, 0:r], lhsT=kT[:, :r], rhs=kT[:, :r], start=True, stop=True), ["kT"], ["bK"])
                T(lambda e, r=r: e.matmul(bK[:r, 128:128 + r], lhsT=kT[:, :r], rhs=qT[:, :r], start=True, stop=True),
                  ["kT", "qT"], ["bK"])
                for hv in range(2):
                    h = 2 * hq + hv
                    gcc, ac, eac, ngc, bec, egc = (t[:r, i, h:h + 1] for t in (gc_all, a_all, ea_all, ngc_all, beta, eg_all))
                    gk = ["gc_all", "a_all", "ea_all", "ngc_all", "beta", "eg_all"]
                    G(lambda e, r=r, gcc=gcc: e.tensor_scalar(out=Dg[:r, :r], in0=idf[:r, :r], scalar1=gcc, scalar2=None,
                                                               op0=ALU.mult), ["cst"] + gk, ["Dg"])
                    G(lambda e, r=r, ac=ac: e.tensor_scalar(out=Da[:r, :r], in0=idf[:r, :r], scalar1=ac, scalar2=None,
                                                             op0=ALU.mult), ["cst"] + gk, ["Da"])
                    T(lambda e, r=r: e.matmul(bM[:, 0:r], lhsT=ones_f[:r, :], rhs=Dg[:r, :r], start=True, stop=True),
                      ["cst", "Dg"], ["bM"])
                    T(lambda e, r=r: e.matmul(bM[:, 128:128 + r], lhsT=ones_f[:r, :], rhs=Da[:r, :r], start=True, stop=True),
                      ["cst", "Da"], ["bM"])
                    V(lambda e, r=r: e.scalar_tensor_tensor(out=arg[:r, :r], in0=bM[:r, 0:r], scalar=-1.0, in1=maskL[:r, :r],
                                                             op0=ALU.mult, op1=ALU.add), ["bM", "cst"], ["arg"])
                    S(lambda e, r=r, ac=ac: e.activation(out=E1[:r, :r], in_=arg[:r, :r], func=AF.Exp, bias=ac),
                      ["arg"] + gk, ["E1"])
                    V(lambda e, r=r: e.scalar_tensor_tensor(out=XX[0][:r, 0, :r], in0=E1[:r, :r], scalar=-1.0, in1=bK[:r, 0:r],
                                                             op0=ALU.mult, op1=ALU.mult), ["E1", "bK"], ["XX0"])
                    V(lambda e, r=r: e.tensor_tensor(out=argT[:r, :r], in0=bM[:r, 128:128 + r], in1=maskU[:r, :r], op=ALU.add),
                      ["bM", "cst"], ["argT"])
                    S(lambda e, r=r, ngc=ngc: e.activation(out=E1T[:r, :r], in_=argT[:r, :r], func=AF.Exp, bias=ngc),
                      ["argT"] + gk, ["E1T"])
                    V(lambda e, r=r: e.scalar_tensor_tensor(out=XX[0][:r, 1, :r], in0=E1T[:r, :r], scalar=-1.0, in1=bK[:r, 0:r],
                                                             op0=ALU.mult, op1=ALU.mult), ["E1T", "bK"], ["XX0"])
                    V(lambda e, r=r: e.tensor_tensor(out=argG[:r, :r], in0=bM[:r, 0:r], in1=maskUE[:r, :r], op=ALU.add),
                      ["bM", "cst"], ["argG"])
                    S(lambda e, r=r, ngc=ngc: e.activation(out=GT[:r, :r], in_=argG[:r, :r], func=AF.Exp, bias=ngc),
                      ["argG"] + gk, ["GT"])
                    V(lambda e, r=r: e.tensor_tensor(out=attnT[:r, :r], in0=GT[:r, :r], in1=bK[:r, 128:128 + r], op=ALU.mult),
                      ["GT", "bK"], ["attnT"])
                    G(lambda e, r=r, hv=hv, bec=bec: e.tensor_scalar(out=R[0][:r, 0:128], in0=A[:r, 256 + hv * 128:384 + hv * 128],
                                                                      scalar1=bec, scalar2=None, op0=ALU.mult),
                      ["A"] + gk, ["R0"])
                    G(lambda e, r=r, eac=eac: e.tensor_scalar(out=R[0][:r, 128:256], in0=kn[:r, :], scalar1=eac, scalar2=None,
                                                               op0=ALU.mult), ["kn"] + gk, ["R0"])
                    S(lambda e, r=r, ngc=ngc: e.activation(out=ekd[:r, :], in_=bM[:r, r - 1:r], func=AF.Exp, bias=ngc),
                      ["bM"] + gk, ["ekd"])
                    S(lambda e, r=r: e.activation(out=gl, in_=bM[:, r - 1:r], func=AF.Exp), ["bM"], ["gl"])
                    G(lambda e, r=r: e.tensor_scalar(out=kdec_bf[:r, :], in0=kn[:r, :], scalar1=ekd[:r, 0:1], scalar2=None,
                                                      op0=ALU.mult), ["kn", "ekd"], ["kdec_bf"])
                    G(lambda e, r=r, egc=egc: e.tensor_scalar(out=qg_bf[:r, :], in0=qn[:r, :], scalar1=egc, scalar2=None,
                                                               op0=ALU.mult), ["qn"] + gk, ["qg_bf"])
                    T(lambda e, r=r: e.transpose(out=bTb[:, 256:256 + r], in_=qg_bf[:r, :], identity=idb[:r, :r]),
                      ["qg_bf", "idb"], ["bT"])
                    V(lambda e, r=r: e.tensor_copy(out=qgT[:, :r], in_=bTb[:, 256:256 + r]), ["bT"], ["qgT"])
                    for k in range(7):
                        cur, nxt = k % 2, (k + 1) % 2
                        T(lambda e, r=r, cur=cur: e.matmul(bR[:r, 0:256], lhsT=idf[:r, :r], rhs=R[cur][:r, :], start=True, stop=False),
                          ["cst", "R%d" % cur], ["bR"])
                        T(lambda e, r=r, cur=cur: e.matmul(bR[:r, 0:256], lhsT=XX[cur][:r, 1, :r], rhs=R[cur][:r, :], start=False, stop=True),
                          ["XX%d" % cur, "R%d" % cur], ["bR"])
                        if k < 6:
                            S(lambda e, r=r, nxt=nxt: e.copy(out=R[nxt][:r, :], in_=bR[:r, 0:256]), ["bR"], ["R%d" % nxt])
                            T(lambda e, r=r, cur=cur: e.matmul(bX[:r, 0:r], lhsT=XX[cur][:r, 1, :r], rhs=XX[cur][:r, 0, :r], start=True, stop=True),
                              ["XX%d" % cur], ["bX"])
                            T(lambda e, r=r, cur=cur: e.matmul(bX[:r, 128:128 + r], lhsT=XX[cur][:r, 0, :r], rhs=XX[cur][:r, 1, :r], start=True, stop=True),
                              ["XX%d" % cur], ["bX"])
                            V(lambda e, r=r, nxt=nxt: e.tensor_copy(out=XX[nxt][:r, :, :r],
                                                                     in_=bX[:r, 0:256].rearrange("p (a b) -> p a b", a=2)[:, :, 0:r]),
                              ["bX"], ["XX%d" % nxt])
                        else:
                            V(lambda e, r=r: e.tensor_copy(out=Rf[:r, :], in_=bR[:r, 0:256]), ["bR"], ["Rf"])
                            V(lambda e, r=r: e.tensor_copy(out=w_bf[:r, :], in_=bR[:r, 128:256]), ["bR"], ["w_bf"])
                    T(lambda e, r=r: e.transpose(out=bTb[:, 384:384 + r], in_=w_bf[:r, :], identity=idb[:r, :r]), ["w_bf", "idb"], ["bT"])
                    V(lambda e, r=r: e.tensor_copy(out=wT[:, :r], in_=bTb[:, 384:384 + r]), ["bT"], ["wT"])
                    T(lambda e, r=r, h=h, Sx_bf=Sx_bf: e.matmul(bV[:r, 0:128], lhsT=wT[:, :r], rhs=Sx_bf[:, h, :], start=True, stop=True),
                      ["wT", Sbk], ["bV"])
                    V(lambda e, r=r: e.tensor_tensor(out=vnew_bf[:r, :], in0=Rf[:r, 0:128], in1=bV[:r, 0:128], op=ALU.subtract),
                      ["Rf", "bV"], ["vnew_bf"])
                    T(lambda e, r=r, h=h, Sx_bf=Sx_bf: e.matmul(bO[:r, 0:128], lhsT=qgT[:, :r], rhs=Sx_bf[:, h, :], start=True, stop=False),
                      ["qgT", Sbk], ["bO"])
                    T(lambda e, r=r: e.matmul(bO[:r, 0:128], lhsT=attnT[:r, :r], rhs=vnew_bf[:r, :], start=False, stop=True),
                      ["attnT", "vnew_bf"], ["bO"])
                    T(lambda e, r=r: e.matmul(bS[:, 0:128], lhsT=kdec_bf[:r, :], rhs=vnew_bf[:r, :], start=True, stop=True),
                      ["kdec_bf", "vnew_bf"], ["bS"])
                    V(lambda e, h=h, Sx=Sx: e.scalar_tensor_tensor(out=Sx[:, h, :], in0=Sx[:, h, :], scalar=gl[:, 0:1], in1=bS[:, 0:128],
                                                                   op0=ALU.mult, op1=ALU.add), [Sk, "gl", "bS"], [Sk])
                    S(lambda e, h=h, Sx=Sx, Sx_bf=Sx_bf: e.copy(out=Sx_bf[:, h, :], in_=Sx[:, h, :]), [Sk], [Sbk])
                    S(lambda e, r=r: e.copy(out=o_sb[:r, :], in_=bO[:r, 0:128]), ["bO"], ["o_sb"])
                    V(lambda e, r=r: e.tensor_tensor(out=osq[:r, :], in0=o_sb[:r, :], in1=o_sb[:r, :], op=ALU.mult), ["o_sb"], ["osq"])
                    V(lambda e, r=r: e.tensor_reduce(out=oss[:r, :], in_=osq[:r, :], axis=AX.X, op=ALU.add), ["osq"], ["oss"])
                    S(lambda e, r=r: e.activation(out=ors[:r, :], in_=oss[:r, :], func=AF.Ln, scale=1.0 / 128, bias=epsb[:r, :]),
                      ["oss", "epsb"], ["ors"])
                    S(lambda e, r=r: e.activation(out=ors[:r, :], in_=ors[:r, :], func=AF.Exp, scale=-0.5), ["ors"], ["ors"])
                    V(lambda e, r=r: e.scalar_tensor_tensor(out=ot[:r, :], in0=o_sb[:r, :], scalar=ors[:r, 0:1], in1=gng[:r, :],
                                                             op0=ALU.mult, op1=ALU.mult), ["o_sb", "ors", "gng"], ["ot"])
                    G(lambda e, r=r, hv=hv, oai=oai: e.tensor_tensor(out=oai[:r, hv * 128:(hv + 1) * 128], in0=ot[:r, :],
                                                                      in1=zs[:r, hv * 128:(hv + 1) * 128], op=ALU.mult),
                      ["ot", "zs"], [oak])
                D("scalar", OA[t0:t0 + r, hq * 256:(hq + 1) * 256], oai[:r, :], reads=[oak], writes=["OA"], group=oak)
        return rec

    recs = [build_chain(0, hq_list[0::2]), build_chain(1, hq_list[1::2])]
    for k in range(max(len(r_) for r_ in recs)):
        for r_ in recs:
            if k < len(r_):
                it = r_[k]
                if it[0] == "op":
                    P.op(it[1], it[2], it[3], it[4], skip_same=it[5])
                else:
                    P.dma(it[1], it[2], it[3], reads=it[4], writes=it[5], group=it[6])
    for q4 in range(4):
        P.dma("scalar", o_p_gdn[q4 * 1024:(q4 + 1) * 1024, :].rearrange("(h d) e -> d h e", h=8), Sp[:, q4 * 8:(q4 + 1) * 8, :],
              reads=["Sp_0", "Sp_1"], writes=["o_p_gdn"])
        P.dma("scalar", o_s_gdn[q4 * 1024:(q4 + 1) * 1024, :].rearrange("(h d) e -> d h e", h=8), Ss[:, q4 * 8:(q4 + 1) * 8, :],
              reads=["Ss_0", "Ss_1"], writes=["o_s_gdn"])


def stage_linear(P, cv, banks, idb, src, srckey, K, W, N, ntok, groups, CW, epi, nm):
    KC = K // 128
    rows = lambda i: min(128, ntok - i * 128)
    gmax = max((len(g) - 1) * 128 + (rows(g[-1]) + 7) // 8 * 8 for g in groups)
    hT = cv.bf16(KC * gmax).rearrange("p (k t) -> p k t", k=KC)
    wf = [cv.f32(KC * CW).rearrange("p (k n) -> p k n", k=KC) for _ in range(2)]
    wb = [cv.bf16(KC * CW).rearrange("p (k n) -> p k n", k=KC) for _ in range(2)]
    hb = cv.bf16(K)
    xt = [w.rearrange("p k n -> p (k n)")[:, 0:K] for w in wf]
    k_hT, k_hb = nm + "hT", nm + "hb"
    k_wf = [nm + "wf0", nm + "wf1"]
    k_wb = [nm + "wb0", nm + "wb1"]
    NCB = N // CW
    ev = 0
    ld = 0
    for g in groups:
        for li, i in enumerate(g):
            r = rows(i)
            xi, xk = xt[ld % 2], k_wf[ld % 2]
            ld += 1
            P.dma("sync", xi[:r, :], src[i * 128:i * 128 + r, 0:K], reads=[srckey], writes=[xk])
            P.op("gpsimd", lambda e, xi=xi, r=r: e.tensor_copy(out=hb[:r, :], in_=xi[:r, :]), reads=[xk], writes=[k_hb])
            for k4 in range(KC // 4):
                bk = banks[k4 % 2]
                bkey = "bank%d" % (k4 % 2)
                pT = bk[:].bitcast(BF16)
                for j in range(4):
                    k = k4 * 4 + j
                    P.op("tensor", lambda e, k=k, j=j, r=r, pT=pT: e.transpose(
                        out=pT[:, j * 128:j * 128 + r], in_=hb[:r, k * 128:(k + 1) * 128], identity=idb[:r, :r]),
                        reads=[k_hb, "idb"], writes=[bkey], skip_same=True)
                P.op("vector", lambda e, k4=k4, r=r, pT=pT, li=li: e.tensor_copy(
                    out=hT[:, k4 * 4:(k4 + 1) * 4, li * 128:li * 128 + r],
                    in_=pT[:, 0:512].rearrange("p (j t) -> p j t", j=4)[:, :, 0:r]),
                    reads=[bkey], writes=[k_hT])
        for cb in range(NCB):
            c0 = cb * CW
            wfi, wbi = wf[ld % 2], wb[ld % 2]
            wfk, wbk = k_wf[ld % 2], k_wb[ld % 2]
            ld += 1
            wsrc = W[:, c0:c0 + CW].rearrange("(k p) n -> p k n", p=128)
            for k8 in range(KC // 8):
                P.dma("sync", wfi[:, k8 * 8:(k8 + 1) * 8, :], wsrc[:, k8 * 8:(k8 + 1) * 8, :], writes=[wfk])
            P.op("gpsimd", lambda e, wfi=wfi, wbi=wbi: e.tensor_copy(out=wbi, in_=wfi), reads=[wfk], writes=[wbk])
            for li, i in enumerate(g):
                r = rows(i)
                bi = 2 + (ev % 4)
                bk, bkey = banks[bi], "bank%d" % bi
                for k in range(KC):
                    P.op("tensor", lambda e, k=k, li=li, r=r, bk=bk, wbi=wbi: e.matmul(
                        bk[:r, 0:CW], lhsT=hT[:, k, li * 128:li * 128 + r], rhs=wbi[:, k, :],
                        start=(k == 0), stop=(k == KC - 1)),
                        reads=[k_hT, wbk], writes=[bkey], skip_same=True)
                epi(i, r, c0, CW, bk, bkey, ev)
                ev += 1


def make_epi(P, cv, nm, CW, dstf, gatef=None, addf=None):
    ost = [cv.f32(CW) for _ in range(4)]
    gt = [cv.f32(CW) for _ in range(4)] if gatef else None
    at = [cv.f32(CW) for _ in range(4)] if addf else None

    def epi(i, r, c0, cw, bk, bkey, ev):
        s = ev % 4
        ok = "%sost%d" % (nm, s)
        if gatef:
            gap, gkey = gatef(i, r, c0, cw)
            gk = "%sgt%d" % (nm, s)
            P.dma("sync", gt[s][:r, :cw], gap, reads=[gkey], writes=[gk])
            P.op("scalar", lambda e, s=s, r=r, cw=cw: e.activation(out=gt[s][:r, :cw], in_=gt[s][:r, :cw], func=AF.Sigmoid),
                 reads=[gk], writes=[gk])
        if addf:
            aap, akey = addf(i, r, c0, cw)
            ak = "%sat%d" % (nm, s)
            P.dma("sync", at[s][:r, :cw], aap, reads=[akey], writes=[ak])
        if gatef:
            P.op("vector", lambda e, s=s, r=r, cw=cw, bk=bk: e.tensor_tensor(out=ost[s][:r, :cw], in0=bk[:r, :cw], in1=gt[s][:r, :cw],
                                                                             op=ALU.mult), reads=[bkey, gk], writes=[ok])
            if addf:
                P.op("gpsimd", lambda e, s=s, r=r, cw=cw: e.tensor_tensor(out=ost[s][:r, :cw], in0=ost[s][:r, :cw], in1=at[s][:r, :cw],
                                                                          op=ALU.add), reads=[ok, ak], writes=[ok])
        else:
            P.op("vector", lambda e, s=s, r=r, cw=cw, bk=bk: e.tensor_tensor(out=ost[s][:r, :cw], in0=bk[:r, :cw], in1=at[s][:r, :cw],
                                                                             op=ALU.add), reads=[bkey, ak], writes=[ok])
        dap, dkey = dstf(i, r, c0, cw)
        P.dma("scalar" if ev % 2 == 0 else "gpsimd", dap, ost[s][:r, :cw], reads=[ok], writes=[dkey], group=ok)
    return epi


def stage_merge(P, cv, banks, idb, base0, U, OA, OB, M1, M2, w_a, w_b, w_o, xsrc, ydst, ntok):
    nt = (ntok + 127) // 128
    alltiles = list(range(nt))
    half = (nt + 1) // 2
    sl = lambda A, key: (lambda i, r, c0, cw: (A[i * 128:i * 128 + r, c0:c0 + cw], key))
    P.barrier()
    cv.reset(base0)
    epi = make_epi(P, cv, "m", 256, sl(M1, "M1"), gatef=lambda i, r, c0, cw: (U[i * 128:i * 128 + r, C_MA + c0:C_MA + c0 + cw], "U"))
    stage_linear(P, cv, banks, idb, OA, "OA", 4096, w_a, D_MODEL, ntok, [alltiles[:half], alltiles[half:]], 256, epi, "m")
    P.barrier()
    cv.reset(base0)
    epi = make_epi(P, cv, "m", 512, sl(M2, "M2"), gatef=lambda i, r, c0, cw: (U[i * 128:i * 128 + r, C_MB + c0:C_MB + c0 + cw], "U"),
                   addf=sl(M1, "M1"))
    stage_linear(P, cv, banks, idb, OB, "OB", 2048, w_b, D_MODEL, ntok, [alltiles], 512, epi, "m")
    P.barrier()
    cv.reset(base0)
    epi = make_epi(P, cv, "m", 512, ydst, addf=xsrc)
    stage_linear(P, cv, banks, idb, M2, "M2", 2048, w_o, D_MODEL, ntok, [alltiles], 512, epi, "m")


NSA_L = 4352
NSA_R0 = 2176
NSAC_W = 3504
NEGM = -30000.0


def t5_bucket_np(rel):
    rel = np.asarray(rel, np.int64)
    n = np.maximum(rel, 0)
    nf = np.maximum(n, 1).astype(np.float32)
    large = 16 + (np.log(nf / np.float32(16)) / np.float32(math.log(8.0)) * np.float32(16)).astype(np.int32)
    return np.where(n < 16, n, np.minimum(large, 31))


def make_nsa_consts(n_ptiles):
    L, R0 = NSA_L, NSA_R0
    seq = n_ptiles * 128
    ncb = seq // 16 - 1
    oh = np.zeros((33, L), np.float32)
    rel = np.arange(L) - R0
    b = np.where(rel < 0, 32, t5_bucket_np(rel))
    oh[b, np.arange(L)] = 1.0
    c = np.zeros((128, NSAC_W), np.float32)
    j = np.arange(32)
    lo = np.clip((64 * j - 32) // 16 + 1, 0, ncb)
    hi = np.clip(-(-(64 * (j + 1)) // 16), 0, ncb)
    n = np.arange(128)[:, None]
    c[:, 0:32] = ((n >= lo[None, :]) & (n < hi[None, :]) & (n < ncb)).astype(np.float32)
    p = np.arange(128)[:, None, None]
    ii = np.arange(16)[None, :, None]
    jj = np.arange(32)[None, None, :]
    qblk = (128 * ii + p) // 64
    valid = jj <= qblk
    forced = (jj == 0) | (jj == qblk) | (jj == qblk - 1)
    km = (valid & ~forced).astype(np.float32)
    fm = np.where(~valid, -1e9, np.where(forced, 1e9, 0.0)).astype(np.float32)
    c[:, 32:544] = km.reshape(128, 512)
    c[:, 544:1056] = fm.reshape(128, 512)
    k = np.arange(128)[:, None]
    q = np.arange(128)[None, :]
    c[:, 1056:1184] = np.where(q < k, 0.0, NEGM)
    t = np.arange(128)[:, None]
    cc = np.arange(8)[None, :]
    mab = (t // 16 == cc).astype(np.float32)
    c[:, 1184:1192] = mab
    c[:, 1192:1200] = mab
    kk = np.arange(2048)[None, :]
    c[0:32, 1200:3248] = (np.arange(32)[:, None] == kk // 64).astype(np.float32)
    pp = np.arange(32)[:, None]
    tt = np.arange(128)[None, :]
    c[0:32, 3248:3376] = (pp == tt % 16).astype(np.float32)
    c[0:32, 3376:3504] = (pp == 16 + tt % 16).astype(np.float32)
    return oh, c


def stage_nsa_prompt(P, nc, cv, banks, U, KSN, ksn_key, KWN, OB, nsac_d, oh_d, TVd, Gd, rel_bias, q_norm_g, k_norm_g,
                     pe_k, w_k, proj_k, pe_v, w_v, proj_v, idf, idb, epsb, n_ptiles):
    from concourse.ap import AP
    NTq = n_ptiles
    SEQ_ = NTq * 128
    NCB_ = SEQ_ // 16 - 1
    L, R0 = NSA_L, NSA_R0
    V = lambda fn, r, w: P.op("vector", fn, r, w)
    S = lambda fn, r, w: P.op("scalar", fn, r, w)
    G = lambda fn, r, w: P.op("gpsimd", fn, r, w)
    T = lambda fn, r, w: P.op("tensor", fn, r, w, skip_same=True)
    b0, b1, bS0, bS1, bOa, bOb, bX, bY = banks
    bS = [bS0, bS1]
    bO = [bOa, bOb]
    bSk = ["bank2", "bank3"]
    bOk = ["bank4", "bank5"]
    bO4 = [banks[4], banks[5], banks[6], banks[7]]
    bO4k = ["bank4", "bank5", "bank6", "bank7"]
    b0bf = b0[:].bitcast(BF16)
    b1bf = b1[:].bitcast(BF16)

    nsac = cv.f32(NSAC_W)
    Mc = nsac[:, 0:32]
    KM = nsac[:, 32:544].rearrange("p (i j) -> p i j", i=16)
    FM = nsac[:, 544:1056].rearrange("p (i j) -> p i j", i=16)
    LT = nsac[:, 1056:1184]
    MAB = nsac[:, 1184:1200]
    Ekf = nsac[:, 1200:3248]
    SelA = nsac[:, 3248:3376]
    SelB = nsac[:, 3376:3504]
    Ek = cv.bf16(2048)
    Bd = cv.bf16(2048)
    Bo = cv.bf16(2048)
    Bw = cv.bf16(2048)
    tb31b = cv.bf16(16)
    tb31f = cv.f32(16)
    tbrow = cv.bf16(2048)
    ones1 = cv.bf16(128)
    onesf = cv.f32(128)
    qgain = cv.f32(128)
    kgain0 = cv.f32(128)
    ksT = cv.bf16(4 * SEQ_).rearrange("p (g t) -> p g t", g=4)
    kwT = cv.bf16(4 * SEQ_).rearrange("p (g t) -> p g t", g=4)
    vse = cv.bf16(NTq * 4 * 132).rearrange("p (i g e) -> p i g e", i=NTq, g=4)
    vwe = cv.bf16(NTq * 4 * 132).rearrange("p (i g e) -> p i g e", i=NTq, g=4)
    kcT = cv.bf16(512).rearrange("p (g n) -> p g n", g=4)
    vce = cv.bf16(4 * 132).rearrange("p (g e) -> p g e", g=4)
    mark = cv.off

    tab = cv.f32(16)
    oh = cv.f32(L)
    tvb = cv.bf16(L)
    stg = [[cv.f32(512) for _ in range(2)] for _ in range(6)]
    sbf = [[cv.bf16(512) for _ in range(2)] for _ in range(4)]
    wk32 = cv.f32(1)
    wv32 = cv.f32(1)
    pek = cv.f32(128)
    pev = cv.f32(128)
    wrep = cv.f32(4)
    pec = cv.f32(2)
    WAB = cv.bf16(32)
    pjf = cv.f32(256)
    pjb = cv.bf16(256)
    ATs = cv.f32(2 * 512).rearrange("p (s g n) -> p s g n", s=2, g=4)
    BTs = cv.f32(2 * 512).rearrange("p (s g n) -> p s g n", s=2, g=4)
    pooled = cv.bf16(2 * 512).rearrange("p (s g n) -> p s g n", s=2, g=4)
    ksq = cv.f32(512)
    kss = cv.f32(4)
    krs = cv.f32(4)
    kcn = cv.f32(512)
    kcnb = cv.bf16(512)

    P.dma("sync", nsac, nsac_d, writes=["nsac"])
    G(lambda e: e.tensor_copy(out=Ek[:32, :], in_=Ekf[:32, :]), ["nsac"], ["Ek"])
    V(lambda e: e.memset(ones1, 1.0), [], ["ones1"])
    V(lambda e: e.memset(onesf, 1.0), [], ["onesf"])
    P.dma("sync", qgain, q_norm_g.broadcast_to([128, 128]), writes=["qgain"])
    P.dma("sync", kgain0, k_norm_g[0:1, :].broadcast_to([128, 128]), writes=["kgain0"])

    V(lambda e: e.memset(tab[:33, :], NEGM), [], ["tab"])
    P.dma("sync", tab[:32, :], rel_bias, writes=["tab"])
    P.dma("sync", oh[:33, :], oh_d, writes=["oh"])
    nch = (L + 511) // 512
    for c in range(nch):
        w = min(512, L - c * 512)
        T(lambda e, c=c, w=w: e.matmul(bX[:16, 0:w], lhsT=tab[:33, 0:16], rhs=oh[:33, c * 512:c * 512 + w], start=True, stop=True),
          ["tab", "oh"], ["bank6"])
        V(lambda e, c=c, w=w: e.tensor_copy(out=tvb[:16, c * 512:c * 512 + w], in_=bX[:16, 0:w]), ["bank6"], ["tvb"])
    P.dma("sync", TVd, tvb[:16, :], reads=["tvb"], writes=["TV"])
    for h in range(16):
        P.dma("sync", Gd[h], TVd[h:h + 1, :].broadcast_to([128, L]), reads=["TV"], writes=["G"])
    for hh in range(2):
        P.dma("sync", Bd[:, hh * 1024:(hh + 1) * 1024].rearrange("p (h q) -> p h q", h=8),
              AP(Gd.tensor, hh * 8 * 128 * L + R0, [[L - 1, 128], [128 * L, 8], [1, 128]]), reads=["G"], writes=["Bd"])
        P.dma("sync", Bo[:, hh * 1024:(hh + 1) * 1024].rearrange("p (h q) -> p h q", h=8),
              AP(Gd.tensor, hh * 8 * 128 * L + R0 + 128, [[L - 1, 128], [128 * L, 8], [1, 128]]), reads=["G"], writes=["Bo"])
    P.dma("sync", tb31b, AP(TVd.tensor, R0 + 200, [[0, 128], [L, 16]]), reads=["TV"], writes=["tb31b"],
          allow_slow_non_contiguous=True)
    V(lambda e: e.tensor_copy(out=tb31f, in_=tb31b), ["tb31b"], ["tb31f"])
    for h in range(16):
        V(lambda e, h=h: e.tensor_scalar(out=Bw[:, h * 128:(h + 1) * 128], in0=LT, scalar1=tb31f[:, h:h + 1], scalar2=None,
                                         op0=ALU.add), ["nsac", "tb31f"], ["Bw"])
    V(lambda e: e.tensor_copy(out=tbrow[0:1, :].rearrange("p (h q) -> p h q", h=16),
                              in_=tb31b[0:1, :].unsqueeze(2).broadcast_to([1, 16, 128])), ["tb31b"], ["tbrow"])

    P.dma("sync", wk32[:32, :], w_k, writes=["wk32"])
    P.dma("sync", wv32[:32, :], w_v, writes=["wv32"])
    P.dma("sync", pek[:32, :], pe_k, writes=["pek"])
    P.dma("sync", pev[:32, :], pe_v, writes=["pev"])
    for s_, (w32, wkey) in enumerate(((wk32, "wk32"), (wv32, "wv32"))):
        T(lambda e, s_=s_, w32=w32: e.matmul(bX[:, 2 * s_:2 * s_ + 1], lhsT=SelA[:32, :], rhs=w32[:32, :], start=True, stop=True),
          ["nsac", wkey], ["bank6"])
        T(lambda e, s_=s_, w32=w32: e.matmul(bX[:, 2 * s_ + 1:2 * s_ + 2], lhsT=SelB[:32, :], rhs=w32[:32, :], start=True, stop=True),
          ["nsac", wkey], ["bank6"])
    T(lambda e: e.matmul(bX[:, 8:9], lhsT=pek[:32, :], rhs=wk32[:32, :], start=True, stop=True), ["pek", "wk32"], ["bank6"])
    T(lambda e: e.matmul(bX[:, 9:10], lhsT=pev[:32, :], rhs=wv32[:32, :], start=True, stop=True), ["pev", "wv32"], ["bank6"])
    V(lambda e: e.tensor_copy(out=wrep, in_=bX[:, 0:4]), ["bank6"], ["wrep"])
    V(lambda e: e.tensor_copy(out=pec, in_=bX[:, 8:10]), ["bank6"], ["pec"])
    for s_ in range(2):
        for ab in range(2):
            V(lambda e, s_=s_, ab=ab: e.tensor_scalar(out=WAB[:, s_ * 16 + ab * 8:s_ * 16 + ab * 8 + 8], in0=MAB[:, ab * 8:ab * 8 + 8],
                                                      scalar1=wrep[:, 2 * s_ + ab:2 * s_ + ab + 1], scalar2=None, op0=ALU.mult),
              ["nsac", "wrep"], ["WAB"])
    P.dma("sync", pjf[:, 0:128], proj_k, writes=["pjf"])
    P.dma("sync", pjf[:, 128:256], proj_v, writes=["pjf"])
    V(lambda e: e.tensor_copy(out=pjb, in_=pjf), ["pjf"], ["pjb"])
    V(lambda e: e.memset(vse, 1.0), [], ["vse"])
    V(lambda e: e.memset(vwe, 1.0), [], ["vwe"])
    V(lambda e: e.memset(vce, 1.0), [], ["vce"])

    srcs = ((KSN, 0, ksn_key), (KWN, 0, "KWN"), (U, C_VS, "U"), (U, C_VW, "U"), (U, C_KC, "U"), (U, C_VC, "U"))
    for i in range(NTq):
        t0 = i * 128
        d = i % 2
        for s_, (src, col, key) in enumerate(srcs):
            P.dma("sync", stg[s_][d], src[t0:t0 + 128, col:col + 512], reads=[key], writes=["stg%d%d" % (s_, d)])
        for s_, (dstT, bbf, bkey, dk) in enumerate(((ksT, b0bf, "bank0", "ksT"), (kwT, b1bf, "bank1", "kwT"))):
            G(lambda e, s_=s_, d=d: e.tensor_copy(out=sbf[s_][d], in_=stg[s_][d]), ["stg%d%d" % (s_, d)], ["sbf%d%d" % (s_, d)])
            for g in range(4):
                T(lambda e, s_=s_, d=d, g=g, bbf=bbf: e.transpose(out=bbf[:, g * 128:(g + 1) * 128], in_=sbf[s_][d][:, g * 128:(g + 1) * 128],
                                                                 identity=idb), ["sbf%d%d" % (s_, d), "idb"], [bkey])
            V(lambda e, dstT=dstT, bbf=bbf, t0=t0: e.tensor_copy(out=dstT[:, :, t0:t0 + 128],
                                                                 in_=bbf[:, 0:512].rearrange("p (g t) -> p g t", g=4)), [bkey], [dk])
        for s_, (dstV, dk) in ((2, (vse, "vse")), (3, (vwe, "vwe"))):
            G(lambda e, s_=s_, d=d, dstV=dstV, i=i: e.tensor_copy(out=dstV[:, i, :, 0:128],
                                                                  in_=stg[s_][d].rearrange("p (g e) -> p g e", g=4)),
              ["stg%d%d" % (s_, d)], [dk])
        for s_ in range(2):
            sb = sbf[2 + s_][d]
            sk = "sbf%d%d" % (2 + s_, d)
            V(lambda e, s_=s_, d=d, sb=sb: e.tensor_copy(out=sb, in_=stg[4 + s_][d]), ["stg%d%d" % (4 + s_, d)], [sk])
            for g in range(4):
                bk = (bS if s_ == 0 else bO)[g // 2]
                bkey = (bSk if s_ == 0 else bOk)[g // 2]
                c0 = (g % 2) * 256 + i * 16
                T(lambda e, s_=s_, g=g, sb=sb, bk=bk, c0=c0: e.matmul(bk[:, c0:c0 + 16], lhsT=sb[:, g * 128:(g + 1) * 128],
                                                                      rhs=WAB[:, s_ * 16:s_ * 16 + 16], start=True, stop=True),
                  [sk, "WAB"], [bkey])
    for s_ in range(2):
        for gg in range(2):
            bk = (bS if s_ == 0 else bO)[gg]
            bkey = (bSk if s_ == 0 else bOk)[gg]
            vw_ = bk[:, 0:512].rearrange("p (g i ab c) -> p g i ab c", g=2, i=16, ab=2)
            V(lambda e, s_=s_, gg=gg, vw_=vw_: e.tensor_copy(
                out=ATs[:, s_, 2 * gg:2 * gg + 2, 0:NTq * 8].rearrange("p g (i c) -> p g i c", c=8), in_=vw_[:, :, 0:NTq, 0, :]),
              [bkey], ["ATs"])
            V(lambda e, s_=s_, gg=gg, vw_=vw_: e.tensor_copy(
                out=BTs[:, s_, 2 * gg:2 * gg + 2, 0:NTq * 8].rearrange("p g (i c) -> p g i c", c=8), in_=vw_[:, :, 0:NTq, 1, :]),
              [bkey], ["BTs"])
        V(lambda e, s_=s_: e.scalar_tensor_tensor(out=pooled[:, s_, :, 0:NCB_], in0=ATs[:, s_, :, 0:NCB_], scalar=pec[:, s_:s_ + 1],
                                                  in1=BTs[:, s_, :, 1:NCB_ + 1], op0=ALU.add, op1=ALU.add),
          ["ATs", "BTs", "pec"], ["pooled"])
        bk, bkey = (bX, "bank6") if s_ == 0 else (bY, "bank7")
        for g in range(4):
            T(lambda e, s_=s_, g=g, bk=bk: e.matmul(bk[:NCB_, g * 128:(g + 1) * 128], lhsT=pooled[:, s_, g, 0:NCB_],
                                                    rhs=pjb[:, s_ * 128:(s_ + 1) * 128], start=True, stop=True),
              ["pooled", "pjb"], [bkey])
    S(lambda e: e.activation(out=ksq[:NCB_, :], in_=bX[:NCB_, 0:512], func=AF.Square), ["bank6"], ["ksq"])
    V(lambda e: e.tensor_reduce(out=kss[:NCB_, :], in_=ksq[:NCB_, :].rearrange("p (g d) -> p g d", g=4), axis=AX.X, op=ALU.add),
      ["ksq"], ["kss"])
    S(lambda e: e.activation(out=krs[:NCB_, :], in_=kss[:NCB_, :], func=AF.Ln, scale=1.0 / 128, bias=epsb[:NCB_, :]),
      ["kss", "epsb"], ["krs"])
    S(lambda e: e.activation(out=krs[:NCB_, :], in_=krs[:NCB_, :], func=AF.Exp, scale=-0.5), ["krs"], ["krs"])
    V(lambda e: e.tensor_tensor(out=kcn[:NCB_, :].rearrange("p (g d) -> p g d", g=4), in0=bX[:NCB_, 0:512].rearrange("p (g d) -> p g d", g=4),
                                in1=krs[:NCB_, :].unsqueeze(2).broadcast_to([NCB_, 4, 128]), op=ALU.mult), ["bank6", "krs"], ["kcn"])
    V(lambda e: e.tensor_tensor(out=kcnb[:NCB_, :].rearrange("p (g d) -> p g d", g=4), in0=kcn[:NCB_, :].rearrange("p (g d) -> p g d", g=4),
                                in1=kgain0[:NCB_, :].unsqueeze(1).broadcast_to([NCB_, 4, 128]), op=ALU.mult), ["kcn", "kgain0"], ["kcnb"])
    for g in range(4):
        T(lambda e, g=g: e.transpose(out=b0bf[:, g * 128:g * 128 + NCB_], in_=kcnb[:NCB_, g * 128:(g + 1) * 128],
                                     identity=idb[:NCB_, :NCB_]), ["kcnb", "idb"], ["bank0"])
    V(lambda e: e.tensor_copy(out=kcT[:, :, 0:NCB_], in_=b0bf[:, 0:512].rearrange("p (g n) -> p g n", g=4)[:, :, 0:NCB_]),
      ["bank0"], ["kcT"])
    V(lambda e: e.tensor_copy(out=vce[:NCB_, :, 0:128], in_=bY[:NCB_, 0:512].rearrange("p (g e) -> p g e", g=4)), ["bank7"], ["vce"])

    P.barrier()
    cv.reset(mark)
    qf = cv.f32(2048)
    sq = cv.f32(2048)
    qnb = cv.bf16(2048)
    qT = cv.bf16(2048)
    zf = cv.f32(2048)
    gtf = cv.f32(48)
    cb = cv.bf16(2048)
    ss16 = cv.f32(16)
    rs16 = cv.f32(16)
    Ef4 = [cv.f32(512) for _ in range(4)]
    Pnb4 = [cv.bf16(512) for _ in range(4)]
    rdb4 = [cv.f32(512) for _ in range(4)]
    impT4 = [cv.f32(128) for _ in range(4)]
    sc4 = [cv.f32(32) for _ in range(4)]
    cmp34 = [cv.bf16(1024) for _ in range(4)]
    cnt4 = [cv.f32(32) for _ in range(4)]
    sneg4 = [cv.bf16(32) for _ in range(4)]
    sn44 = [cv.bf16(512) for _ in range(4)]
    cbank = [b0, b1, bS0, bS1]
    cbankk = ["bank0", "bank1", "bank2", "bank3"]
    E = [cv.bf16(512) for _ in range(2)]
    acc = cv.f32(2048)
    rd2 = cv.f32(2)
    cf2 = cv.f32(2)
    rsq = 128 ** -0.5

    def finalize(br, g, first):
        for h4 in range(4):
            h = 4 * g + h4
            c0 = br * 16 + h
            V(lambda e, h4=h4: e.tensor_scalar(out=rd2[:, 0:1], in0=bO4[h4][:, 128:129], scalar1=1e-30, scalar2=None, op0=ALU.add),
              [bO4k[h4]], ["rd2"])
            V(lambda e: e.reciprocal(out=rd2[:, 0:1], in_=rd2[:, 0:1]), ["rd2"], ["rd2"])
            V(lambda e, c0=c0: e.tensor_tensor(out=cf2[:, 0:1], in0=rd2[:, 0:1], in1=gtf[:, c0:c0 + 1], op=ALU.mult), ["rd2", "gtf"], ["cf2"])
            if first:
                V(lambda e, h4=h4, h=h: e.tensor_scalar(out=acc[:, h * 128:(h + 1) * 128], in0=bO4[h4][:, 0:128],
                                                        scalar1=cf2[:, 0:1], scalar2=None, op0=ALU.mult),
                  [bO4k[h4], "cf2"], ["acc"])
            else:
                V(lambda e, h4=h4, h=h: e.scalar_tensor_tensor(out=acc[:, h * 128:(h + 1) * 128], in0=bO4[h4][:, 0:128],
                                                               scalar=cf2[:, 0:1], in1=acc[:, h * 128:(h + 1) * 128],
                                                               op0=ALU.mult, op1=ALU.add),
                  [bO4k[h4], "cf2", "acc"], ["acc"])

    ecnt = 0
    for i in range(NTq):
        t0 = i * 128
        P.dma("sync", qf, U[t0:t0 + 128, C_QB:C_QB + 2048], reads=["U"], writes=["qf"])
        P.dma("sync", zf, U[t0:t0 + 128, C_ZB:C_ZB + 2048], reads=["U"], writes=["zf"])
        P.dma("sync", gtf, U[t0:t0 + 128, C_GB:C_GB + 48], reads=["U"], writes=["gtf"])
        for hh in range(2):
            P.dma("sync", cb[:NCB_, hh * 1024:(hh + 1) * 1024].rearrange("p (h q) -> p h q", h=8),
                  AP(Gd.tensor, hh * 8 * 128 * L + (128 * i - 31 + R0), [[L - 16, NCB_], [128 * L, 8], [1, 128]]),
                  reads=["G"], writes=["cb"])
        S(lambda e: e.activation(out=sq, in_=qf, func=AF.Square), ["qf"], ["sq"])
        V(lambda e: e.tensor_reduce(out=ss16, in_=sq.rearrange("p (h d) -> p h d", h=16), axis=AX.X, op=ALU.add), ["sq"], ["ss16"])
        S(lambda e: e.activation(out=rs16, in_=ss16, func=AF.Ln, scale=1.0 / 128, bias=epsb), ["ss16", "epsb"], ["rs16"])
        S(lambda e: e.activation(out=rs16, in_=rs16, func=AF.Exp, scale=-0.5), ["rs16"], ["rs16"])
        V(lambda e: e.tensor_tensor(out=sq.rearrange("p (h d) -> p h d", h=16), in0=qf.rearrange("p (h d) -> p h d", h=16),
                                    in1=rs16.unsqueeze(2).broadcast_to([128, 16, 128]), op=ALU.mult), ["qf", "rs16", "sq"], ["sq"])
        V(lambda e: e.scalar_tensor_tensor(out=qnb.rearrange("p (h d) -> p h d", h=16), in0=sq.rearrange("p (h d) -> p h d", h=16),
                                           scalar=rsq, in1=qgain.unsqueeze(1).broadcast_to([128, 16, 128]),
                                           op0=ALU.mult, op1=ALU.mult), ["sq", "qgain"], ["qnb"])
        for hb, (bbf, bkey) in enumerate(((b0bf, "bank0"), (b1bf, "bank1"))):
            for j in range(8):
                T(lambda e, hb=hb, j=j, bbf=bbf: e.transpose(out=bbf[:, j * 128:(j + 1) * 128],
                                                             in_=qnb[:, (hb * 8 + j) * 128:(hb * 8 + j + 1) * 128], identity=idb),
                  ["qnb", "idb"], [bkey])
            V(lambda e, hb=hb, bbf=bbf: e.tensor_copy(out=qT[:, hb * 1024:(hb + 1) * 1024], in_=bbf[:, 0:1024]), [bkey], ["qT"])
        S(lambda e: e.activation(out=zf, in_=zf, func=AF.Silu), ["zf"], ["zf"])
        S(lambda e: e.activation(out=gtf, in_=gtf, func=AF.Sigmoid), ["gtf"], ["gtf"])

        recs = []
        for g in range(4):
            rec = []
            Vr = lambda fn_, r, w, rec=rec: rec.append(("vector", fn_, r, w, False))
            Sr = lambda fn_, r, w, rec=rec: rec.append(("scalar", fn_, r, w, False))
            Gr = lambda fn_, r, w, rec=rec: rec.append(("gpsimd", fn_, r, w, False))
            Tr = lambda fn_, r, w, rec=rec: rec.append(("tensor", fn_, r, w, True))
            q4 = qT[:, g * 512:(g + 1) * 512]
            bk, bkk = cbank[g], cbankk[g]
            bkbf = bk[:].bitcast(BF16)
            Ef, Pnb, rdb, impT, sc, cmp3, cnt, sneg, sn4 = Ef4[g], Pnb4[g], rdb4[g], impT4[g], sc4[g], cmp34[g], cnt4[g], sneg4[g], sn44[g]
            kE, kP, kr, ki, ks, kc3, kct, ksn, ks4 = ("%s_%d" % (n_, g) for n_ in ("Ef", "Pnb", "rdb", "impT", "sc", "cmp3", "cnt", "sneg", "sn4"))
            Tr(lambda e, g=g, q4=q4, bk=bk: e.matmul(bk[:NCB_, 0:512], lhsT=kcT[:, g, 0:NCB_], rhs=q4, start=True, stop=False),
               ["kcT", "qT"], [bkk])
            Tr(lambda e, g=g, bk=bk: e.matmul(bk[:NCB_, 0:512], lhsT=idb[:NCB_, :NCB_], rhs=cb[:NCB_, g * 512:(g + 1) * 512], start=False, stop=True),
               ["idb", "cb"], [bkk])
            Sr(lambda e, bk=bk, Ef=Ef: e.activation(out=Ef[:NCB_, :], in_=bk[:NCB_, 0:512], func=AF.Exp), [bkk], [kE])
            Tr(lambda e, bk=bk, Ef=Ef: e.matmul(bk[:NCB_, 0:512], lhsT=onesf[:NCB_, :NCB_], rhs=Ef[:NCB_, :], start=True, stop=True),
               ["onesf", kE], [bkk])
            Vr(lambda e, bk=bk, rdb=rdb: e.tensor_scalar(out=rdb[:NCB_, :], in0=bk[:NCB_, 0:512], scalar1=1e-30, scalar2=None, op0=ALU.add),
               [bkk], [kr])
            Vr(lambda e, rdb=rdb: e.reciprocal(out=rdb[:NCB_, :], in_=rdb[:NCB_, :]), [kr], [kr])
            Vr(lambda e, Ef=Ef, rdb=rdb: e.tensor_tensor(out=Ef[:NCB_, :], in0=Ef[:NCB_, :], in1=rdb[:NCB_, :], op=ALU.mult), [kE, kr], [kE])
            Gr(lambda e, Ef=Ef, Pnb=Pnb: e.tensor_copy(out=Pnb[:NCB_, :], in_=Ef[:NCB_, :]), [kE], [kP])
            for h in range(4):
                Tr(lambda e, g=g, h=h, Pnb=Pnb: e.matmul(bO4[g][:, h * 128:(h + 1) * 128], lhsT=Pnb[:NCB_, h * 128:(h + 1) * 128],
                                                        rhs=vce[:NCB_, g, 0:128], start=True, stop=True), [kP, "vce"], [bO4k[g]])
            Vr(lambda e, Ef=Ef, impT=impT: e.tensor_reduce(out=impT[:NCB_, :], in_=Ef[:NCB_, :].rearrange("p (h q) -> p q h", h=4),
                                                           axis=AX.X, op=ALU.add), [kE], [ki])
            Tr(lambda e, bk=bk, impT=impT: e.matmul(bk[:, 0:32], lhsT=impT[:NCB_, :], rhs=Mc[:NCB_, :], start=True, stop=True),
               [ki, "nsac"], [bkk])
            Vr(lambda e, i=i, bk=bk, sc=sc: e.tensor_tensor(out=sc, in0=bk[:, 0:32], in1=KM[:, i, :], op=ALU.mult), [bkk, "nsac"], [ks])
            Vr(lambda e, i=i, sc=sc: e.tensor_tensor(out=sc, in0=sc, in1=FM[:, i, :], op=ALU.add), [ks, "nsac"], [ks])
            Vr(lambda e, sc=sc, cmp3=cmp3: e.tensor_tensor(out=cmp3.rearrange("p (a b) -> p a b", a=32), in0=sc.unsqueeze(1).broadcast_to([128, 32, 32]),
                                                           in1=sc.unsqueeze(2).broadcast_to([128, 32, 32]), op=ALU.is_gt), [ks], [kc3])
            Vr(lambda e, cmp3=cmp3, cnt=cnt: e.tensor_reduce(out=cnt, in_=cmp3.rearrange("p (a b) -> p a b", a=32), axis=AX.X, op=ALU.add),
               [kc3], [kct])
            Vr(lambda e, cnt=cnt, sneg=sneg: e.tensor_scalar(out=sneg, in0=cnt, scalar1=15.5, scalar2=NEGM, op0=ALU.is_gt, op1=ALU.mult),
               [kct], [ksn])
            Tr(lambda e, bkbf=bkbf, sneg=sneg: e.transpose(out=bkbf[:32, 0:128], in_=sneg[:, 0:32], identity=idb), [ksn, "idb"], [bkk])
            Vr(lambda e, bkbf=bkbf, sn4=sn4: e.tensor_copy(out=sn4[:32, :].rearrange("p (h q) -> p h q", h=4),
                                                           in_=bkbf[:32, 0:128].unsqueeze(1).broadcast_to([32, 4, 128])), [bkk], [ks4])
            for h in range(4):
                hh = 4 * g + h
                Vr(lambda e, g=g, h=h, hh=hh: e.tensor_scalar(out=acc[:, hh * 128:(hh + 1) * 128], in0=bO4[g][:, h * 128:(h + 1) * 128],
                                                             scalar1=gtf[:, hh:hh + 1], scalar2=None, op0=ALU.mult),
                   [bO4k[g], "gtf"], ["acc"])
            recs.append(rec)
        for k in range(max(len(r_) for r_ in recs)):
            for r_ in recs:
                if k < len(r_):
                    it = r_[k]
                    P.op(it[0], it[1], it[2], it[3], skip_same=it[4])

        for g in range(4):
            q4 = qT[:, g * 512:(g + 1) * 512]
            for br, kT_, ve, kts in ((1, ksT, vse, list(range(0, i + 1))), (2, kwT, vwe, list(range(max(0, i - 4), i + 1)))):
                for kt in kts:
                    d = ecnt % 2
                    ecnt += 1
                    bSx, bSxk = bS[d], bSk[d]
                    T(lambda e, g=g, kt=kt, kT_=kT_, bSx=bSx, q4=q4: e.matmul(bSx[:, 0:512], lhsT=kT_[:, g, kt * 128:(kt + 1) * 128], rhs=q4,
                                                                            start=True, stop=False),
                      ["ksT", "kwT", "qT"], [bSxk])
                    if br == 1:
                        T(lambda e, kt=kt, bSx=bSx, g=g: e.matmul(bSx[:, 0:512], lhsT=Ek[:32, kt * 128:(kt + 1) * 128], rhs=sn44[g][:32, :],
                                                             start=False, stop=False), ["Ek", "sn4_%d" % g], [bSxk])
                    if kt == i:
                        lh, rh, rk = idb, Bd[:, g * 512:(g + 1) * 512], "Bd"
                    elif kt == i - 1:
                        lh, rh, rk = idb, Bo[:, g * 512:(g + 1) * 512], "Bo"
                    elif br == 2 and kt == i - 4:
                        lh, rh, rk = idb, Bw[:, g * 512:(g + 1) * 512], "Bw"
                    else:
                        lh, rh, rk = ones1[0:1, :], tbrow[0:1, g * 512:(g + 1) * 512], "tbrow"
                    T(lambda e, bSx=bSx, lh=lh, rh=rh: e.matmul(bSx[:, 0:512], lhsT=lh, rhs=rh, start=False, stop=True),
                      ["idb", "ones1", rk], [bSxk])
                    S(lambda e, d=d, bSx=bSx: e.activation(out=E[d], in_=bSx[:, 0:512], func=AF.Exp), [bSxk], ["E%d" % d])
                    for h in range(4):
                        T(lambda e, g=g, kt=kt, h=h, d=d, ve=ve, kts=kts: e.matmul(
                            bO4[h][:, 0:129], lhsT=E[d][:, h * 128:(h + 1) * 128],
                            rhs=ve[:, kt, g, 0:129], start=(kt == kts[0]), stop=(kt == kts[-1])),
                          ["E%d" % d, "vse", "vwe"], [bO4k[h]])
                finalize(br, g, False)
        V(lambda e: e.tensor_tensor(out=acc, in0=acc, in1=zf, op=ALU.mult), ["acc", "zf"], ["acc"])
        P.dma("scalar", OB[t0:t0 + 128, :], acc, reads=["acc"], writes=["OB"], group="acc")


NSAS_W = 2712


def make_nsa_sample_consts():
    ncs, nb = 1023, 257
    c = np.zeros((128, NSAS_W), np.float32)
    j = np.arange(nb)
    lo = np.clip((64 * j - 32) // 16 + 1, 0, ncs)
    hi = np.clip(-(-(64 * (j + 1)) // 16), 0, ncs)
    n = np.arange(1024)[:, None]
    m = ((n >= lo[None]) & (n < hi[None]) & (n < ncs)).astype(np.float32)
    c[:, 0:2056] = m.reshape(8, 128, nb).transpose(1, 0, 2).reshape(128, 2056)
    forced = (j == 0) | (j == 256) | (j == 255)
    c[0:8, 2056:2313] = (~forced).astype(np.float32)
    c[0:8, 2313:2570] = np.where(forced, 1e9, 0.0)
    kp = np.arange(128)[:, None]
    t = np.arange(8)[None, :]
    c[:, 2570:2578] = np.where(kp <= t, NEGM, 0.0)
    c[0, 2578:2578 + 64] = 1.0
    c[1, 2578 + 64:2578 + 128] = 1.0
    c[:, 2706] = np.arange(128)
    return c


def stage_nsa_sample(P, nc, cv, banks, U, KSNs, ksn_key, KWN, OB, nsac_d, nsas_d, TVd, Gd, q_norm_g, k_norm_g,
                     pe_k, w_k, proj_k, pe_v, w_v, proj_v, pool_kc, pool_vc, pool_ks, pool_vs, ckw, cvw, ptab,
                     idf, idb, epsb):
    from concourse.ap import AP
    L, R0 = NSA_L, NSA_R0
    T0 = SEQ
    V = lambda fn, r, w: P.op("vector", fn, r, w)
    S = lambda fn, r, w: P.op("scalar", fn, r, w)
    G = lambda fn, r, w: P.op("gpsimd", fn, r, w)
    T = lambda fn, r, w: P.op("tensor", fn, r, w, skip_same=True)
    b0, b1, bS0, bS1 = banks[0:4]
    bS = [bS0, bS1]
    bSk = ["bank2", "bank3"]
    bO4 = [banks[4], banks[5], banks[6], banks[7]]
    bO4k = ["bank4", "bank5", "bank6", "bank7"]
    b0bf = b0[:].bitcast(BF16)
    b1bf = b1[:].bitcast(BF16)
    rsq = 128 ** -0.5

    nsas = cv.f32(NSAS_W)
    Ms = nsas[:, 0:2056].rearrange("p (a j) -> p a j", a=8)
    KMs = nsas[:, 2056:2313]
    FMs = nsas[:, 2313:2570]
    LTs = nsas[:, 2570:2578]
    E2f = nsas[:, 2578:2706]
    iot = nsas[:, 2706:2707]
    nsm = cv.f32(512)
    P.dma("sync", nsas, nsas_d, writes=["nsas"])
    P.dma("sync", nsm[:, 0:256], nsac_d[:, 3248:3504], writes=["nsm"])
    P.dma("sync", nsm[:, 256:272], nsac_d[:, 1184:1200], writes=["nsm"])
    SelA, SelB, MAB = nsm[:, 0:128], nsm[:, 128:256], nsm[:, 256:272]
    E2 = cv.bf16(128)
    V(lambda e: e.tensor_copy(out=E2[:2, :], in_=E2f[:2, :]), ["nsas"], ["E2"])
    ones1 = cv.bf16(128)
    onesf = cv.f32(128)
    V(lambda e: e.memset(ones1, 1.0), [], ["ones1"])
    V(lambda e: e.memset(onesf, 1.0), [], ["onesf"])
    qgain = cv.f32(128)
    kgain0 = cv.f32(128)
    P.dma("sync", qgain[:8, :], q_norm_g.broadcast_to([8, 128]), writes=["qgain"])
    P.dma("sync", kgain0, k_norm_g[0:1, :].broadcast_to([128, 128]), writes=["kgain0"])

    pti = cv.f32(128).bitcast(I32)
    ptf = cv.f32(128)
    idx = cv.f32(128).bitcast(I32)
    P.dma("sync", pti, ptab.broadcast_to([128, 128]), writes=["pti"])
    V(lambda e: e.tensor_copy(out=ptf, in_=pti), ["pti"], ["ptf"])
    V(lambda e: e.tensor_scalar(out=ptf, in0=ptf, scalar1=128.0, scalar2=iot, op0=ALU.mult, op1=ALU.add), ["ptf", "nsas"], ["ptf"])
    V(lambda e: e.tensor_copy(out=idx, in_=ptf), ["ptf"], ["idx"])

    def gather(dst, dkey, pool, pg):
        fn = lambda e, dst=dst, pool=pool, pg=pg: e.indirect_dma_start(
            out=dst, out_offset=None, in_=pool, in_offset=bass.IndirectOffsetOnAxis(ap=idx[:, pg:pg + 1], axis=0))
        P.dma_fn("gpsimd", fn, reads=["idx"], writes=[dkey])

    tb31b = cv.bf16(16)
    tb31f = cv.f32(16)
    tbrow = cv.bf16(128)
    Bc7 = cv.bf16(128)
    B127 = cv.bf16(128)
    Bnew = cv.bf16(128)
    Bw0 = cv.bf16(128)
    P.dma("sync", tb31b, AP(TVd.tensor, R0 + 200, [[0, 128], [L, 16]]), reads=["TV"], writes=["tb31b"], allow_slow_non_contiguous=True)
    V(lambda e: e.tensor_copy(out=tb31f, in_=tb31b), ["tb31b"], ["tb31f"])
    V(lambda e: e.tensor_copy(out=tbrow[0:1, :].rearrange("p (h q) -> p h q", h=16),
                              in_=tb31b[0:1, :].unsqueeze(2).broadcast_to([1, 16, 8])), ["tb31b"], ["tbrow"])
    for hh in range(2):
        P.dma("sync", Bc7[:, hh * 64:(hh + 1) * 64].rearrange("p (h q) -> p h q", h=8),
              AP(Gd.tensor, hh * 8 * 128 * L + R0 + 2017, [[L - 16, 128], [128 * L, 8], [1, 8]]), reads=["G"], writes=["Bc7"])
        P.dma("sync", B127[:, hh * 64:(hh + 1) * 64].rearrange("p (h q) -> p h q", h=8),
              AP(Gd.tensor, hh * 8 * 128 * L + R0 + 128, [[L - 1, 128], [128 * L, 8], [1, 8]]), reads=["G"], writes=["B127"])
    P.dma("sync", Bnew[:8, :].rearrange("p (h q) -> p h q", h=16),
          AP(Gd.tensor, R0, [[L - 1, 8], [128 * L, 16], [1, 8]]), reads=["G"], writes=["Bnew"])
    for h in range(16):
        V(lambda e, h=h: e.tensor_scalar(out=Bw0[:, h * 8:(h + 1) * 8], in0=LTs, scalar1=tb31f[:, h:h + 1], scalar2=None, op0=ALU.add),
          ["nsas", "tb31f"], ["Bw0"])

    qf = cv.f32(2048)
    sq = cv.f32(2048)
    qnb = cv.bf16(2048)
    ss16 = cv.f32(16)
    rs16 = cv.f32(16)
    qTs = cv.bf16(128)
    gT = cv.f32(12)
    zT = cv.f32(512)
    acc = cv.f32(512)
    rd = cv.f32(1)
    cf = cv.f32(1)
    P.dma("sync", qf[:8, :], U[T0:T0 + 8, C_QB:C_QB + 2048], reads=["U"], writes=["qf"])
    for h4 in range(4):
        P.dma("sync", gT[h4 * 8:(h4 + 1) * 8, :].rearrange("p (b g) -> p b g", b=3),
              AP(U.tensor, T0 * IN_W + C_GB + h4, [[IN_W, 8], [16, 3], [4, 4]]), reads=["U"], writes=["gT"], allow_slow_non_contiguous=True)
        P.dma("sync", zT[h4 * 8:(h4 + 1) * 8, :].rearrange("p (g d) -> p g d", g=4),
              AP(U.tensor, T0 * IN_W + C_ZB + h4 * 128, [[IN_W, 8], [512, 4], [1, 128]]), reads=["U"], writes=["zT"])
    S(lambda e: e.activation(out=sq[:8, :], in_=qf[:8, :], func=AF.Square), ["qf"], ["sq"])
    V(lambda e: e.tensor_reduce(out=ss16[:8, :], in_=sq[:8, :].rearrange("p (h d) -> p h d", h=16), axis=AX.X, op=ALU.add), ["sq"], ["ss16"])
    S(lambda e: e.activation(out=rs16[:8, :], in_=ss16[:8, :], func=AF.Ln, scale=1.0 / 128, bias=epsb[:8, :]), ["ss16", "epsb"], ["rs16"])
    S(lambda e: e.activation(out=rs16[:8, :], in_=rs16[:8, :], func=AF.Exp, scale=-0.5), ["rs16"], ["rs16"])
    V(lambda e: e.tensor_tensor(out=sq[:8, :].rearrange("p (h d) -> p h d", h=16), in0=qf[:8, :].rearrange("p (h d) -> p h d", h=16),
                                in1=rs16[:8, :].unsqueeze(2).broadcast_to([8, 16, 128]), op=ALU.mult), ["qf", "rs16", "sq"], ["sq"])
    V(lambda e: e.scalar_tensor_tensor(out=qnb[:8, :].rearrange("p (h d) -> p h d", h=16), in0=sq[:8, :].rearrange("p (h d) -> p h d", h=16),
                                       scalar=rsq, in1=qgain[:8, :].unsqueeze(1).broadcast_to([8, 16, 128]),
                                       op0=ALU.mult, op1=ALU.mult), ["sq", "qgain"], ["qnb"])
    for h in range(16):
        T(lambda e, h=h: e.transpose(out=b0bf[:, h * 8:(h + 1) * 8], in_=qnb[:8, h * 128:(h + 1) * 128], identity=idb[:8, :8]),
          ["qnb", "idb"], ["bank0"])
    V(lambda e: e.tensor_copy(out=qTs, in_=b0bf[:, 0:128]), ["bank0"], ["qTs"])
    S(lambda e: e.activation(out=zT[:32, :], in_=zT[:32, :], func=AF.Silu), ["zT"], ["zT"])
    S(lambda e: e.activation(out=gT[:32, :], in_=gT[:32, :], func=AF.Sigmoid), ["gT"], ["gT"])

    def finalize(br, first):
        for g in range(4):
            V(lambda e, g=g: e.tensor_scalar(out=rd[:32, :], in0=bO4[g][:32, 128:129], scalar1=1e-30, scalar2=None, op0=ALU.add),
              [bO4k[g]], ["rd"])
            V(lambda e: e.reciprocal(out=rd[:32, :], in_=rd[:32, :]), ["rd"], ["rd"])
            V(lambda e, g=g: e.tensor_tensor(out=cf[:32, :], in0=rd[:32, :], in1=gT[:32, br * 4 + g:br * 4 + g + 1], op=ALU.mult),
              ["rd", "gT"], ["cf"])
            if first:
                V(lambda e, g=g: e.tensor_scalar(out=acc[:32, g * 128:(g + 1) * 128], in0=bO4[g][:32, 0:128], scalar1=cf[:32, 0:1],
                                                 scalar2=None, op0=ALU.mult), [bO4k[g], "cf"], ["acc"])
            else:
                V(lambda e, g=g: e.scalar_tensor_tensor(out=acc[:32, g * 128:(g + 1) * 128], in0=bO4[g][:32, 0:128], scalar=cf[:32, 0:1],
                                                        in1=acc[:32, g * 128:(g + 1) * 128], op0=ALU.mult, op1=ALU.add),
                  [bO4k[g], "cf", "acc"], ["acc"])

    wk32 = cv.f32(1)
    wv32 = cv.f32(1)
    pek = cv.f32(128)
    pev = cv.f32(128)
    wrep = cv.f32(4)
    pec = cv.f32(2)
    WAB = cv.bf16(32)
    pjf = cv.f32(256)
    pjb = cv.bf16(256)
    P.dma("sync", wk32[:32, :], w_k, writes=["wk32"])
    P.dma("sync", wv32[:32, :], w_v, writes=["wv32"])
    P.dma("sync", pek[:32, :], pe_k, writes=["pek"])
    P.dma("sync", pev[:32, :], pe_v, writes=["pev"])
    for s_, (w32, wkey) in enumerate(((wk32, "wk32"), (wv32, "wv32"))):
        T(lambda e, s_=s_, w32=w32: e.matmul(b1[:, 2 * s_:2 * s_ + 1], lhsT=SelA[:32, :], rhs=w32[:32, :], start=True, stop=True),
          ["nsm", wkey], ["bank1"])
        T(lambda e, s_=s_, w32=w32: e.matmul(b1[:, 2 * s_ + 1:2 * s_ + 2], lhsT=SelB[:32, :], rhs=w32[:32, :], start=True, stop=True),
          ["nsm", wkey], ["bank1"])
    T(lambda e: e.matmul(b1[:, 8:9], lhsT=pek[:32, :], rhs=wk32[:32, :], start=True, stop=True), ["pek", "wk32"], ["bank1"])
    T(lambda e: e.matmul(b1[:, 9:10], lhsT=pev[:32, :], rhs=wv32[:32, :], start=True, stop=True), ["pev", "wv32"], ["bank1"])
    V(lambda e: e.tensor_copy(out=wrep, in_=b1[:, 0:4]), ["bank1"], ["wrep"])
    V(lambda e: e.tensor_copy(out=pec, in_=b1[:, 8:10]), ["bank1"], ["pec"])
    for s_ in range(2):
        for ab in range(2):
            V(lambda e, s_=s_, ab=ab: e.tensor_scalar(out=WAB[:, s_ * 16 + ab * 8:s_ * 16 + ab * 8 + 8], in0=MAB[:, ab * 8:ab * 8 + 8],
                                                      scalar1=wrep[:, 2 * s_ + ab:2 * s_ + ab + 1], scalar2=None, op0=ALU.mult),
              ["nsm", "wrep"], ["WAB"])
    P.dma("sync", pjf[:, 0:128], proj_k, writes=["pjf"])
    P.dma("sync", pjf[:, 128:256], proj_v, writes=["pjf"])
    V(lambda e: e.tensor_copy(out=pjb, in_=pjf), ["pjf"], ["pjb"])

    ATs = cv.f32(4096).rearrange("p (g n) -> p g n", g=4)
    BTs = cv.f32(4096).rearrange("p (g n) -> p g n", g=4)
    pooled = cv.bf16(4096).rearrange("p (g n) -> p g n", g=4)
    kcTs = cv.bf16(4096).rearrange("p (g n) -> p g n", g=4)
    vces = cv.bf16(8 * 4 * 132).rearrange("p (a g e) -> p a g e", a=8, g=4)
    stg = [cv.f32(512) for _ in range(4)]
    sbf = [cv.bf16(512) for _ in range(4)]
    ksq = cv.f32(512)
    kss = cv.f32(4)
    krs = cv.f32(4)
    kcn = cv.f32(512)
    kcnb = cv.bf16(512)
    V(lambda e: e.memset(vces, 1.0), [], ["vces"])
    for s_, pool in enumerate((pool_kc, pool_vc)):
        V(lambda e: e.memset(pooled, 0.0), [], ["pooled"])
        for pg in range(128):
            d = pg % 2
            sk, bk_ = "sg%d" % d, "sb%d" % d
            gather(stg[d], sk, pool, pg)
            V(lambda e, d=d: e.tensor_copy(out=sbf[d], in_=stg[d]), [sk], [bk_])
            for g in range(4):
                c0 = (g % 2) * 256 + (pg % 16) * 16
                T(lambda e, s_=s_, g=g, d=d, c0=c0: e.matmul(bS[g // 2][:, c0:c0 + 16], lhsT=sbf[d][:, g * 128:(g + 1) * 128],
                                                             rhs=WAB[:, s_ * 16:s_ * 16 + 16], start=True, stop=True),
                  [bk_, "WAB"], [bSk[g // 2]])
            if pg % 16 == 15:
                blk = pg // 16
                for gg in range(2):
                    vw_ = bS[gg][:, 0:512].rearrange("p (g i ab c) -> p g i ab c", g=2, i=16, ab=2)
                    V(lambda e, gg=gg, vw_=vw_, blk=blk: e.tensor_copy(
                        out=ATs[:, 2 * gg:2 * gg + 2, blk * 128:(blk + 1) * 128].rearrange("p g (i c) -> p g i c", c=8),
                        in_=vw_[:, :, :, 0, :]), [bSk[gg]], ["ATs"])
                    V(lambda e, gg=gg, vw_=vw_, blk=blk: e.tensor_copy(
                        out=BTs[:, 2 * gg:2 * gg + 2, blk * 128:(blk + 1) * 128].rearrange("p g (i c) -> p g i c", c=8),
                        in_=vw_[:, :, :, 1, :]), [bSk[gg]], ["BTs"])
        V(lambda e, s_=s_: e.scalar_tensor_tensor(out=pooled[:, :, 0:1023], in0=ATs[:, :, 0:1023], scalar=pec[:, s_:s_ + 1],
                                                  in1=BTs[:, :, 1:1024], op0=ALU.add, op1=ALU.add), ["ATs", "BTs", "pec"], ["pooled"])
        for nt in range(8):
            for g in range(4):
                T(lambda e, s_=s_, g=g, nt=nt: e.matmul(b1[:, g * 128:(g + 1) * 128], lhsT=pooled[:, g, nt * 128:(nt + 1) * 128],
                                                        rhs=pjb[:, s_ * 128:(s_ + 1) * 128], start=True, stop=True),
                  ["pooled", "pjb"], ["bank1"])
            if s_ == 0:
                S(lambda e: e.activation(out=ksq, in_=b1[:, 0:512], func=AF.Square), ["bank1"], ["ksq"])
                V(lambda e: e.tensor_reduce(out=kss, in_=ksq.rearrange("p (g d) -> p g d", g=4), axis=AX.X, op=ALU.add), ["ksq"], ["kss"])
                S(lambda e: e.activation(out=krs, in_=kss, func=AF.Ln, scale=1.0 / 128, bias=epsb), ["kss", "epsb"], ["krs"])
                S(lambda e: e.activation(out=krs, in_=krs, func=AF.Exp, scale=-0.5), ["krs"], ["krs"])
                V(lambda e: e.tensor_tensor(out=kcn.rearrange("p (g d) -> p g d", g=4), in0=b1[:, 0:512].rearrange("p (g d) -> p g d", g=4),
                                            in1=krs.unsqueeze(2).broadcast_to([128, 4, 128]), op=ALU.mult), ["bank1", "krs"], ["kcn"])
                V(lambda e: e.tensor_tensor(out=kcnb.rearrange("p (g d) -> p g d", g=4), in0=kcn.rearrange("p (g d) -> p g d", g=4),
                                            in1=kgain0.unsqueeze(1).broadcast_to([128, 4, 128]), op=ALU.mult), ["kcn", "kgain0"], ["kcnb"])
                for g in range(4):
                    T(lambda e, g=g: e.transpose(out=b0bf[:, g * 128:(g + 1) * 128], in_=kcnb[:, g * 128:(g + 1) * 128], identity=idb),
                      ["kcnb", "idb"], ["bank0"])
                V(lambda e, nt=nt: e.tensor_copy(out=kcTs[:, :, nt * 128:(nt + 1) * 128], in_=b0bf[:, 0:512].rearrange("p (g n) -> p g n", g=4)),
                  ["bank0"], ["kcTs"])
            else:
                V(lambda e, nt=nt: e.tensor_copy(out=vces[:, nt, :, 0:128], in_=b1[:, 0:512].rearrange("p (g e) -> p g e", g=4)),
                  ["bank1"], ["vces"])

    Ef = cv.f32(256)
    Ec = cv.bf16(256)
    rdb = cv.f32(32)
    impT = cv.f32(64)
    sc = cv.f32(264)
    cmpc = cv.f32(16 * 257)
    cnt = cv.f32(264)
    sneg = cv.bf16(264)
    snT = cv.f32(16)
    snP = cv.bf16(128 * 8)
    snP4 = [cv.bf16(128 * 32) for _ in range(4)]
    SNd = nc.dram_tensor("SNd", [4, 264, 8], BF16, kind="Internal").ap()
    for g in range(4):
        q4 = qTs[:, g * 32:(g + 1) * 32]
        for nt in range(8):
            T(lambda e, g=g, nt=nt, q4=q4: e.matmul(bS0[:, nt * 32:(nt + 1) * 32], lhsT=kcTs[:, g, nt * 128:(nt + 1) * 128], rhs=q4,
                                                    start=True, stop=False), ["kcTs", "qTs"], ["bank2"])
            if nt == 7:
                T(lambda e, g=g, nt=nt: e.matmul(bS0[:, nt * 32:(nt + 1) * 32], lhsT=idb, rhs=Bc7[:, g * 32:(g + 1) * 32], start=False, stop=True),
                  ["idb", "Bc7"], ["bank2"])
            else:
                T(lambda e, g=g, nt=nt: e.matmul(bS0[:, nt * 32:(nt + 1) * 32], lhsT=ones1[0:1, :], rhs=tbrow[0:1, g * 32:(g + 1) * 32],
                                                 start=False, stop=True), ["ones1", "tbrow"], ["bank2"])
        S(lambda e: e.activation(out=Ef, in_=bS0[:, 0:256], func=AF.Exp), ["bank2"], ["Ef"])
        G(lambda e: e.tensor_copy(out=Ec, in_=Ef), ["Ef"], ["Ec"])
        for nt in range(8):
            T(lambda e, g=g, nt=nt: e.matmul(bO4[g][:32, 0:129], lhsT=Ec[:, nt * 32:(nt + 1) * 32], rhs=vces[:, nt, g, 0:129],
                                             start=(nt == 0), stop=(nt == 7)), ["Ec", "vces"], [bO4k[g]])
        for nt in range(8):
            T(lambda e, nt=nt: e.matmul(b1[:, 0:32], lhsT=onesf, rhs=Ef[:, nt * 32:(nt + 1) * 32], start=(nt == 0), stop=(nt == 7)),
              ["onesf", "Ef"], ["bank1"])
        V(lambda e: e.tensor_scalar(out=rdb, in0=b1[:, 0:32], scalar1=1e-30, scalar2=None, op0=ALU.add), ["bank1"], ["rdb"])
        V(lambda e: e.reciprocal(out=rdb, in_=rdb), ["rdb"], ["rdb"])
        V(lambda e: e.tensor_tensor(out=Ef.rearrange("p (a c) -> p a c", a=8), in0=Ef.rearrange("p (a c) -> p a c", a=8),
                                    in1=rdb.unsqueeze(1).broadcast_to([128, 8, 32]), op=ALU.mult), ["Ef", "rdb"], ["Ef"])
        V(lambda e: e.tensor_reduce(out=impT.rearrange("p (a t) -> p a t", a=8), in_=Ef.rearrange("p (a h t) -> p a t h", a=8, h=4),
                                    axis=AX.X, op=ALU.add), ["Ef"], ["impT"])
        for nt in range(8):
            T(lambda e, nt=nt: e.matmul(b0[:8, 0:257], lhsT=impT[:, nt * 8:(nt + 1) * 8], rhs=Ms[:, nt, :], start=(nt == 0), stop=(nt == 7)),
              ["impT", "nsas"], ["bank0"])
        V(lambda e: e.tensor_tensor(out=sc[:8, 0:257], in0=b0[:8, 0:257], in1=KMs[:8, :], op=ALU.mult), ["bank0", "nsas"], ["sc"])
        V(lambda e: e.tensor_tensor(out=sc[:8, 0:257], in0=sc[:8, 0:257], in1=FMs[:8, :], op=ALU.add), ["sc", "nsas"], ["sc"])
        for j0 in range(0, 257, 16):
            jw = min(16, 257 - j0)
            V(lambda e, j0=j0, jw=jw: e.tensor_tensor(out=cmpc[:8, 0:jw * 257].rearrange("p (a b) -> p a b", a=jw),
                                                      in0=sc[:8, 0:257].unsqueeze(1).broadcast_to([8, jw, 257]),
                                                      in1=sc[:8, j0:j0 + jw].unsqueeze(2).broadcast_to([8, jw, 257]), op=ALU.is_gt),
              ["sc"], ["cmpc"])
            V(lambda e, j0=j0, jw=jw: e.tensor_reduce(out=cnt[:8, j0:j0 + jw], in_=cmpc[:8, 0:jw * 257].rearrange("p (a b) -> p a b", a=jw),
                                                      axis=AX.X, op=ALU.add), ["cmpc"], ["cnt"])
        V(lambda e: e.tensor_scalar(out=sneg[:8, 0:257], in0=cnt[:8, 0:257], scalar1=15.5, scalar2=NEGM, op0=ALU.is_gt, op1=ALU.mult),
          ["cnt"], ["sneg"])
        for jt, (j0, jw) in enumerate(((0, 128), (128, 128), (256, 1))):
            T(lambda e, j0=j0, jw=jw: e.transpose(out=b0bf[:jw, 512:520], in_=sneg[:8, j0:j0 + jw], identity=idb[:8, :8]),
              ["sneg", "idb"], ["bank0"])
            V(lambda e, jw=jw: e.tensor_copy(out=snT.bitcast(BF16)[:jw, 0:8], in_=b0bf[:jw, 512:520]), ["bank0"], ["snT"])
            P.dma("sync", SNd[g, j0:j0 + jw, :], snT.bitcast(BF16)[:jw, 0:8], reads=["snT"], writes=["SNd"])
        P.dma("sync", snP[:2, :].rearrange("p (a t) -> p a t", a=128), AP(SNd.tensor, g * 264 * 8, [[8, 2], [16, 128], [1, 8]]),
              reads=["SNd"], writes=["snP"])
        V(lambda e, g=g: e.tensor_copy(out=snP4[g][:2, :].rearrange("p (a h t) -> p a h t", a=128, h=4),
                                       in_=snP[:2, :].rearrange("p (a t) -> p a t", a=128).unsqueeze(2).broadcast_to([2, 128, 4, 8])),
          ["snP"], ["snP4%d" % g])
    finalize(0, True)

    kTp = [cv.bf16(512) for _ in range(2)]
    vpe = [cv.bf16(4 * 132).rearrange("p (g e) -> p g e", g=4) for _ in range(2)]
    E4 = [cv.bf16(128) for _ in range(2)]
    for d in range(2):
        V(lambda e, d=d: e.memset(vpe[d], 1.0), [], ["vpe%d" % d])

    def attend_tile(br, it, first, last, kload, vload, rows, bias_of):
        d = it % 2
        r = rows
        sk, sv = "sg%d" % (2 + d), "sg%d" % d
        kload(stg[2 + d], sk)
        vload(stg[d], sv)
        G(lambda e, d=d, r=r: e.tensor_copy(out=sbf[d][:r, :], in_=stg[2 + d][:r, :]), [sk], ["sb%d" % d])
        V(lambda e, d=d, r=r: e.tensor_copy(out=vpe[d][:r, :, 0:128], in_=stg[d][:r, :].rearrange("p (g e) -> p g e", g=4)), [sv], ["vpe%d" % d])
        bT, bTk = (b0bf, "bank0") if d == 0 else (b1bf, "bank1")
        for g in range(4):
            T(lambda e, d=d, g=g, r=r, bT=bT: e.transpose(out=bT[:, g * 128:g * 128 + r], in_=sbf[d][:r, g * 128:(g + 1) * 128],
                                                         identity=idb[:r, :r]), ["sb%d" % d, "idb"], [bTk])
        V(lambda e, d=d, r=r, bT=bT: e.tensor_copy(out=kTp[d].rearrange("p (g k) -> p g k", g=4)[:, :, 0:r],
                                                   in_=bT[:, 0:512].rearrange("p (g k) -> p g k", g=4)[:, :, 0:r]), [bTk], ["kTp%d" % d])
        for g in range(4):
            T(lambda e, d=d, g=g, r=r: e.matmul(bS[d][:r, g * 32:(g + 1) * 32], lhsT=kTp[d][:, g * 128:g * 128 + r],
                                                rhs=qTs[:, g * 32:(g + 1) * 32], start=True, stop=False), ["kTp%d" % d, "qTs"], [bSk[d]])
            if br == 1 and it < 128:
                T(lambda e, d=d, g=g, it=it: e.matmul(bS[d][:, g * 32:(g + 1) * 32], lhsT=E2[:2, :], rhs=snP4[g][:2, it * 32:(it + 1) * 32],
                                                      start=False, stop=False), ["E2", "snP4%d" % g], [bSk[d]])
            lh, rh, rk = bias_of(g)
            T(lambda e, d=d, g=g, r=r, lh=lh, rh=rh: e.matmul(bS[d][:r, g * 32:(g + 1) * 32], lhsT=lh, rhs=rh, start=False, stop=True),
              ["idb", "ones1", rk], [bSk[d]])
        S(lambda e, d=d, r=r: e.activation(out=E4[d][:r, :], in_=bS[d][:r, 0:128], func=AF.Exp), [bSk[d]], ["E4%d" % d])
        for g in range(4):
            T(lambda e, d=d, g=g, r=r: e.matmul(bO4[g][:32, 0:129], lhsT=E4[d][:r, g * 32:(g + 1) * 32], rhs=vpe[d][:r, g, 0:129],
                                                start=first, stop=last), ["E4%d" % d, "vpe%d" % d], [bO4k[g]])

    cbias = lambda g: (ones1[0:1, :], tbrow[0:1, g * 32:(g + 1) * 32], "tbrow")
    for pg in range(128):
        bias_of = (lambda g: (idb, B127[:, g * 32:(g + 1) * 32], "B127")) if pg == 127 else cbias
        attend_tile(1, pg, pg == 0, False,
                    lambda dst, key, pg=pg: gather(dst, key, pool_ks, pg),
                    lambda dst, key, pg=pg: gather(dst, key, pool_vs, pg), 128, bias_of)
    newbias = lambda g: (idb[:8, :8], Bnew[:8, g * 32:(g + 1) * 32], "Bnew")
    attend_tile(1, 128, False, True,
                lambda dst, key: P.dma("sync", dst[:8, :], KSNs, reads=[ksn_key], writes=[key]),
                lambda dst, key: P.dma("sync", dst[:8, :], U[T0:T0 + 8, C_VS:C_VS + 512], reads=["U"], writes=[key]), 8, newbias)
    finalize(1, False)

    for kt in range(4):
        if kt == 0:
            bias_of = lambda g: (idb, Bw0[:, g * 32:(g + 1) * 32], "Bw0")
        elif kt == 3:
            bias_of = lambda g: (idb, B127[:, g * 32:(g + 1) * 32], "B127")
        else:
            bias_of = cbias
        attend_tile(2, 200 + kt, kt == 0, False,
                    lambda dst, key, kt=kt: P.dma("sync", dst, ckw[kt * 128:(kt + 1) * 128, :], writes=[key]),
                    lambda dst, key, kt=kt: P.dma("sync", dst, cvw[kt * 128:(kt + 1) * 128, :], writes=[key]), 128, bias_of)
    attend_tile(2, 204, False, True,
                lambda dst, key: P.dma("sync", dst[:8, :], KWN[T0:T0 + 8, :], reads=["KWN"], writes=[key]),
                lambda dst, key: P.dma("sync", dst[:8, :], U[T0:T0 + 8, C_VW:C_VW + 512], reads=["U"], writes=[key]), 8, newbias)
    finalize(2, False)

    V(lambda e: e.tensor_tensor(out=acc[:32, :], in0=acc[:32, :], in1=zT[:32, :], op=ALU.mult), ["acc", "zT"], ["acc"])
    for h4 in range(4):
        P.dma("scalar", AP(OB.tensor, T0 * 2048 + h4 * 128, [[2048, 8], [512, 4], [1, 128]]),
              acc[h4 * 8:(h4 + 1) * 8, :].rearrange("p (g d) -> p g d", g=4), reads=["acc"], writes=["OB"], group="acc")


def build_program():
    nc = bass.Bass("TRN2", target_bir_lowering=False)
    P = Prog(nc)

    def din(name, shape, dt=F32):
        return nc.dram_tensor(name, list(shape), dt, kind="ExternalInput").ap()

    def dout(name, shape, dt=F32):
        return nc.dram_tensor(name, list(shape), dt, kind="ExternalOutput").ap()

    def dscr(name, shape, dt=F32):
        return nc.dram_tensor(name, list(shape), dt, kind="Internal").ap()

    xp = din("xp", [SEQ, D_MODEL])
    xs = din("xs", [DEC_SEQ, D_MODEL])
    w_in = din("w_in", [D_MODEL, IN_W])
    norm_g = din("norm_g", [1, D_MODEL])
    k_norm_g = din("k_norm_g", [3, 128])
    ident = din("ident", [128, 128])
    ckw = din("ckw", [512, 512])
    cvw = din("cvw", [512, 512])
    consts_d = din("consts", [128, 6, 128])
    conv_w = din("conv_w", [4, 8192])
    a_log = din("a_log", [1, 32])
    dt_bias = din("dt_bias", [1, 32])
    gnorm_g = din("gnorm_g", [1, 128])
    sconv = din("sconv", [3, 8192])
    sgdn = din("sgdn", [32, 128, 128])
    nsac_d = din("nsac", [128, NSAC_W])
    nsas_d = din("nsas", [128, NSAS_W])
    pool_kc = din("pool_kc", [1280 * 128, 512])
    pool_vc = din("pool_vc", [1280 * 128, 512])
    pool_ks = din("pool_ks", [1280 * 128, 512])
    pool_vs = din("pool_vs", [1280 * 128, 512])
    ptab = din("ptab", [1, 128], I32)
    oh_d = din("oh", [33, NSA_L])
    rel_bias = din("rel_bias", [32, 16])
    q_norm_g = din("q_norm_g", [1, 128])
    pe_k = din("pe_k", [32, 128])
    w_k = din("w_k", [32, 1])
    proj_k = din("proj_k", [128, 128])
    pe_v = din("pe_v", [32, 128])
    w_v = din("w_v", [32, 1])
    proj_v = din("proj_v", [128, 128])
    w_a = din("w_a", [4096, D_MODEL])
    w_b = din("w_b", [D_MODEL, D_MODEL])
    w_o = din("w_o", [D_MODEL, D_MODEL])

    o_y_p = dout("y_p", [SEQ, D_MODEL])
    o_y_s = dout("y_s", [DEC_SEQ, D_MODEL])
    o_p_kc = dout("p_kc", [SEQ, 512])
    o_p_vc = dout("p_vc", [SEQ, 512])
    o_p_ks = dout("p_ks", [SEQ, 512])
    o_p_vs = dout("p_vs", [SEQ, 512])
    o_p_kw = dout("p_kw", [512, 512])
    o_p_vw = dout("p_vw", [512, 512])
    o_p_conv = dout("p_conv", [3, 8192])
    o_p_gdn = dout("p_gdn", [32 * 128, 128])
    o_s_kc = dout("s_kc", [DEC_SEQ, 512])
    o_s_vc = dout("s_vc", [DEC_SEQ, 512])
    o_s_ks = dout("s_ks", [DEC_SEQ, 512])
    o_s_vs = dout("s_vs", [DEC_SEQ, 512])
    o_s_kw = dout("s_kw", [512, 512])
    o_s_vw = dout("s_vw", [512, 512])
    o_s_conv = dout("s_conv", [3, 8192])
    o_s_gdn = dout("s_gdn", [32 * 128, 128])

    U = dscr("U", [NTOK, IN_W])
    OA = dscr("OA", [NTOK, 4096])
    OB = dscr("OB", [NTOK, 2048])
    KWN = dscr("KWN", [NTOK, 512])
    TVd = nc.dram_tensor("TV", [16, NSA_L], BF16, kind="Internal").ap()
    Gd = nc.dram_tensor("G", [16, 128, NSA_L], BF16, kind="Internal").ap()
    M1 = dscr("M1", [NTOK, D_MODEL])
    M2 = dscr("M2", [NTOK, D_MODEL])

    NW = 48640
    big = P.stack.enter_context(nc.sbuf_tensor("big", [128, NW], F32))
    cv = Carver(big, NW)
    banks = [P.stack.enter_context(nc.psum_tensor("bank%d" % i, [128, 512], F32)) for i in range(8)]

    idf = cv.f32(128)
    idb = cv.bf16(128)
    epsb = cv.f32(1)
    base0 = cv.off

    P.dma("sync", idf, ident, writes=["idf"])
    P.op("vector", lambda e: e.tensor_copy(out=idb, in_=idf), reads=["idf"], writes=["idb"])
    P.op("vector", lambda e: e.memset(epsb, EPS), writes=["epsb"])

    NT = 17
    KC = 16

    def rows(i):
        return 128 if i < 16 else DEC_SEQ

    cv.reset(base0)
    hT = cv.bf16(KC * NTOK).rearrange("p (k t) -> p k t", k=KC)
    gb = cv.f32(D_MODEL)
    wf = [cv.f32(KC * 512).rearrange("p (k n) -> p k n", k=KC) for _ in range(2)]
    wb = [cv.bf16(KC * 512).rearrange("p (k n) -> p k n", k=KC) for _ in range(2)]
    hb = cv.bf16(D_MODEL)
    ss = cv.f32(1)
    rstd = cv.f32(1)
    ost = [cv.f32(512) for _ in range(4)]
    xt = [wf[i].rearrange("p k n -> p (k n)")[:, 0:D_MODEL] for i in range(2)]
    sq = cv.f32(D_MODEL)

    P.dma("sync", gb, norm_g.broadcast_to([128, D_MODEL]), writes=["gb"])

    for i in range(NT):
        r = rows(i)
        xi = xt[i % 2]
        xk = "wf%d" % (i % 2)
        src = xp[i * 128:(i + 1) * 128, :] if i < 16 else xs
        P.dma("sync", xi[:r, :], src, writes=[xk])
        P.op("scalar", lambda e, xi=xi, r=r: e.activation(out=sq[:r, :], in_=xi[:r, :], func=AF.Square,
                                                            accum_out=ss[:r, :]),
             reads=[xk], writes=["sq", "ss"])
        P.op("scalar", lambda e, r=r: e.activation(out=rstd[:r, :], in_=ss[:r, :], func=AF.Ln,
                                                    scale=1.0 / D_MODEL, bias=epsb[:r, :]),
             reads=["ss", "epsb"], writes=["rstd"])
        P.op("scalar", lambda e, r=r: e.activation(out=rstd[:r, :], in_=rstd[:r, :], func=AF.Exp, scale=-0.5),
             reads=["rstd"], writes=["rstd"])
        P.op("vector", lambda e, xi=xi, r=r: e.scalar_tensor_tensor(out=hb[:r, :], in0=xi[:r, :], scalar=rstd[:r, 0:1],
                                                                     in1=gb[:r, :], op0=ALU.mult, op1=ALU.mult),
             reads=[xk, "rstd", "gb"], writes=["hb"])
        for k4 in range(4):
            bk = banks[k4 % 2]
            bkey = "bank%d" % (k4 % 2)
            pT = bk[:].bitcast(BF16)
            for j in range(4):
                k = k4 * 4 + j
                P.op("tensor", lambda e, k=k, j=j, r=r, pT=pT: e.transpose(
                    out=pT[:, j * 128:j * 128 + r], in_=hb[:r, k * 128:(k + 1) * 128], identity=idb[:r, :r]),
                    reads=["hb", "idb"], writes=[bkey], skip_same=True)
            eng = "vector" if k4 % 2 == 0 else "gpsimd"
            eng = "vector"
            P.op(eng, lambda e, k4=k4, r=r, pT=pT, i=i: e.tensor_copy(
                out=hT[:, k4 * 4:(k4 + 1) * 4, i * 128:i * 128 + r],
                in_=pT[:, 0:512].rearrange("p (j t) -> p j t", j=4)[:, :, 0:r]),
                reads=[bkey], writes=["hT"])

    NCB = (IN_W + 511) // 512
    ev = 0
    for cb in range(NCB):
        c0 = cb * 512
        cw = min(512, IN_W - c0)
        wfi, wbi = wf[cb % 2], wb[cb % 2]
        wfk, wbk = "wf%d" % (cb % 2), "wb%d" % (cb % 2)
        wsrc = w_in[:, c0:c0 + cw].rearrange("(k p) n -> p k n", p=128)
        P.dma("sync", wfi[:, 0:8, 0:cw], wsrc[:, 0:8, :], writes=[wfk])
        P.dma("sync", wfi[:, 8:16, 0:cw], wsrc[:, 8:16, :], writes=[wfk])
        P.op("gpsimd", lambda e, wfi=wfi, wbi=wbi, cw=cw: e.tensor_copy(out=wbi[:, :, 0:cw], in_=wfi[:, :, 0:cw]),
             reads=[wfk], writes=[wbk])
        for i in range(NT):
            r = rows(i)
            bi = 2 + (ev % 4)
            bk, bkey = banks[bi], "bank%d" % bi
            for k in range(KC):
                P.op("tensor", lambda e, k=k, i=i, r=r, bk=bk, wbi=wbi, cw=cw: e.matmul(
                    bk[:r, 0:cw], lhsT=hT[:, k, i * 128:i * 128 + r], rhs=wbi[:, k, 0:cw],
                    start=(k == 0), stop=(k == KC - 1)),
                    reads=["hT", wbk], writes=[bkey], skip_same=True)
            o = ost[ev % 4]
            okey = "ost%d" % (ev % 4)
            if ev % 2 == 0:
                P.op("scalar", lambda e, o=o, bk=bk, r=r, cw=cw: e.copy(out=o[:r, 0:cw], in_=bk[:r, 0:cw]),
                     reads=[bkey], writes=[okey])
            else:
                P.op("vector", lambda e, o=o, bk=bk, r=r, cw=cw: e.tensor_copy(out=o[:r, 0:cw], in_=bk[:r, 0:cw]),
                     reads=[bkey], writes=[okey])
            P.dma("scalar" if ev % 2 == 0 else "gpsimd", U[i * 128:i * 128 + r, c0:c0 + cw], o[:r, 0:cw],
                  reads=[okey], writes=["U"], group=okey)
            ev += 1

    P.barrier()

    def cp(dst, src, key):
        P.dma("sync", dst, src, reads=["U"], writes=[key])

    cp(o_p_kc, U[0:SEQ, C_KC:C_KC + 512], "o_p_kc")
    cp(o_p_vc, U[0:SEQ, C_VC:C_VC + 512], "o_p_vc")
    cp(o_p_vs, U[0:SEQ, C_VS:C_VS + 512], "o_p_vs")
    cp(o_p_vw, U[SEQ - 512:SEQ, C_VW:C_VW + 512], "o_p_vw")
    cp(o_p_conv, U[SEQ - 3:SEQ, 0:8192], "o_p_conv")
    cp(o_s_kc, U[SEQ:NTOK, C_KC:C_KC + 512], "o_s_kc")
    cp(o_s_vc, U[SEQ:NTOK, C_VC:C_VC + 512], "o_s_vc")
    cp(o_s_vs, U[SEQ:NTOK, C_VS:C_VS + 512], "o_s_vs")
    cp(o_s_conv, U[NTOK - 3:NTOK, 0:8192], "o_s_conv")
    cp(o_s_vw[0:504, :], cvw[8:512, :], "o_s_vw")
    cp(o_s_vw[504:512, :], U[SEQ:NTOK, C_VW:C_VW + 512], "o_s_vw")
    cp(o_s_kw[0:504, :], ckw[8:512, :], "o_s_kw")

    cv.reset(base0)
    kg = cv.f32(2 * 512)
    kt = [cv.f32(512) for _ in range(2)]
    ksq = cv.f32(512)
    kss = cv.f32(4)
    krs = cv.f32(4)
    kn = [cv.f32(512) for _ in range(2)]
    for wi, row in enumerate((1, 2)):
        for g in range(4):
            P.dma("sync", kg[:, wi * 512 + g * 128: wi * 512 + (g + 1) * 128],
                  k_norm_g[row:row + 1, :].broadcast_to([128, 128]), writes=["kg"])
    cnt = 0
    for wi, col in enumerate((C_KS, C_KW)):
        for i in range(NT):
            r = rows(i)
            t = kt[cnt % 2]
            tk = "kt%d" % (cnt % 2)
            o = kn[cnt % 2]
            ok = "kn%d" % (cnt % 2)
            P.dma("sync", t[:r, :], U[i * 128:i * 128 + r, col:col + 512], reads=["U"], writes=[tk])
            P.op("vector", lambda e, t=t, r=r: e.tensor_tensor(out=ksq[:r, :], in0=t[:r, :], in1=t[:r, :], op=ALU.mult),
                 reads=[tk], writes=["ksq"])
            P.op("vector", lambda e, r=r: e.tensor_reduce(out=kss[:r, :], in_=ksq[:r, :].rearrange("p (g d) -> p g d", g=4),
                                                           axis=AX.X, op=ALU.add),
                 reads=["ksq"], writes=["kss"])
            P.op("scalar", lambda e, r=r: e.activation(out=krs[:r, :], in_=kss[:r, :], func=AF.Ln, scale=1.0 / 128,
                                                        bias=epsb[:r, :]),
                 reads=["kss", "epsb"], writes=["krs"])
            P.op("scalar", lambda e, r=r: e.activation(out=krs[:r, :], in_=krs[:r, :], func=AF.Exp, scale=-0.5),
                 reads=["krs"], writes=["krs"])
            for g in range(4):
                P.op("vector", lambda e, t=t, o=o, r=r, g=g, wi=wi: e.scalar_tensor_tensor(
                    out=o[:r, g * 128:(g + 1) * 128], in0=t[:r, g * 128:(g + 1) * 128], scalar=krs[:r, g:g + 1],
                    in1=kg[:r, wi * 512 + g * 128: wi * 512 + (g + 1) * 128], op0=ALU.mult, op1=ALU.mult),
                    reads=[tk, "krs", "kg"], writes=[ok])
            if wi == 0:
                dst = o_p_ks[i * 128:(i + 1) * 128, :] if i < 16 else o_s_ks
                dk = "o_p_ks" if i < 16 else "o_s_ks"
            else:
                P.dma("scalar", KWN[i * 128:i * 128 + r, :], o[:r, :], reads=[ok], writes=["KWN"], group=ok)
                if i < 12:
                    cnt += 1
                    continue
                dst = o_p_kw[(i - 12) * 128:(i - 11) * 128, :] if i < 16 else o_s_kw[504:512, :]
                dk = "o_p_kw" if i < 16 else "o_s_kw"
            P.dma("scalar", dst, o[:r, :], reads=[ok], writes=[dk], group=ok)
            cnt += 1

    P.barrier()
    cv.reset(base0)
    stage_gdn(P, nc, cv, banks, U, OA, consts_d, conv_w, a_log, dt_bias, gnorm_g, sconv, sgdn,
              o_p_gdn, o_s_gdn, idb, epsb, list(range(16)), 16)

    P.barrier()
    cv.reset(base0)
    stage_nsa_prompt(P, nc, cv, banks, U, o_p_ks, "o_p_ks", KWN, OB, nsac_d, oh_d, TVd, Gd, rel_bias, q_norm_g, k_norm_g,
                     pe_k, w_k, proj_k, pe_v, w_v, proj_v, idf, idb, epsb, 16)

    P.barrier()
    cv.reset(base0)
    stage_nsa_sample(P, nc, cv, banks, U, o_s_ks, "o_s_ks", KWN, OB, nsac_d, nsas_d, TVd, Gd, q_norm_g, k_norm_g,
                     pe_k, w_k, proj_k, pe_v, w_v, proj_v, pool_kc, pool_vc, pool_ks, pool_vs, ckw, cvw, ptab,
                     idf, idb, epsb)

    def xsrc(i, r, c0, cw):
        return (xp[i * 128:i * 128 + r, c0:c0 + cw], "xp") if i < 16 else (xs[0:r, c0:c0 + cw], "xs")

    def ydst(i, r, c0, cw):
        return (o_y_p[i * 128:i * 128 + r, c0:c0 + cw], "o_y_p") if i < 16 else (o_y_s[0:r, c0:c0 + cw], "o_y_s")

    stage_merge(P, cv, banks, idb, base0, U, OA, OB, M1, M2, w_a, w_b, w_o, xsrc, ydst, NTOK)

    P.barrier()
    P.emit()
    return nc


_NC_CACHE = {}


def kernel(**inputs):
    f = lambda a: np.ascontiguousarray(np.asarray(a, dtype=np.float32))
    x_prompt = f(inputs["x_prompt"])
    x_sample = f(inputs["x_sample"])
    w_in = f(inputs["w_in"])
    norm_g = f(inputs["norm_g"]).reshape(1, D_MODEL)
    k_norm_g = f(inputs["k_norm_g"])
    ckw = f(inputs["cache_k_win"])
    cvw = f(inputs["cache_v_win"])
    ident = np.eye(128, dtype=np.float32)
    consts = make_consts()
    conv_w = f(inputs["gdn_conv_w"])
    a_log = f(inputs["gdn_a_log"]).reshape(1, 32)
    dt_bias = f(inputs["gdn_dt_bias"]).reshape(1, 32)
    gnorm_g = f(inputs["gdn_norm_g"]).reshape(1, 128)
    sconv = f(inputs["state_conv"])
    sgdn = f(inputs["state_gdn"])
    oh, nsac = make_nsa_consts(16)
    nsas = make_nsa_sample_consts()
    pool_kc = f(inputs["cache_k_cmp"]).reshape(1280 * 128, 512)
    pool_vc = f(inputs["cache_v_cmp"]).reshape(1280 * 128, 512)
    pool_ks = f(inputs["cache_k_sel"]).reshape(1280 * 128, 512)
    pool_vs = f(inputs["cache_v_sel"]).reshape(1280 * 128, 512)
    ptab = np.ascontiguousarray(np.asarray(inputs["page_table"], dtype=np.int32))
    rel_bias = f(inputs["rel_bias"])
    q_norm_g = f(inputs["q_norm_g"]).reshape(1, 128)
    pe_k = f(inputs["cmp_pe_k"])
    w_k = f(inputs["cmp_w_k"]).reshape(32, 1)
    proj_k = f(inputs["cmp_proj_k"])
    pe_v = f(inputs["cmp_pe_v"])
    w_v = f(inputs["cmp_w_v"]).reshape(32, 1)
    proj_v = f(inputs["cmp_proj_v"])
    w_a = f(inputs["w_branch_a"])
    w_b = f(inputs["w_branch_b"])
    w_o = f(inputs["w_out"])

    if "nc" not in _NC_CACHE:
        _NC_CACHE["nc"] = build_program()
    nc = _NC_CACHE["nc"]

    in_maps = []
    for c in range(8):
        b = c // 2
        in_maps.append({
            "xp": x_prompt[b],
            "xs": x_sample[c],
            "w_in": w_in,
            "norm_g": norm_g,
            "k_norm_g": k_norm_g,
            "ident": ident,
            "ckw": ckw[c].reshape(512, 512),
            "cvw": cvw[c].reshape(512, 512),
            "consts": consts,
            "conv_w": conv_w,
            "a_log": a_log,
            "dt_bias": dt_bias,
            "gnorm_g": gnorm_g,
            "sconv": sconv[c],
            "sgdn": sgdn[c],
            "nsac": nsac,
            "nsas": nsas,
            "pool_kc": pool_kc, "pool_vc": pool_vc, "pool_ks": pool_ks, "pool_vs": pool_vs,
            "ptab": ptab[c:c + 1],
            "oh": oh,
            "rel_bias": rel_bias,
            "q_norm_g": q_norm_g,
            "pe_k": pe_k, "w_k": w_k, "proj_k": proj_k,
            "pe_v": pe_v, "w_v": w_v, "proj_v": proj_v,
            "w_a": w_a,
            "w_b": w_b,
            "w_o": w_o,
        })
    res = run_bass_kernel_spmd(nc, in_maps, core_ids=list(range(8)))
    R = res.results

    def pst(name, shape):
        return np.stack([np.asarray(R[2 * b][name], dtype=np.float32).reshape(shape) for b in range(4)])

    def sst(name, shape):
        return np.stack([np.asarray(R[c][name], dtype=np.float32).reshape(shape) for c in range(8)])

    outs = (
        pst("y_p", (SEQ, D_MODEL)), sst("y_s", (DEC_SEQ, D_MODEL)),
        pst("p_kc", (SEQ, 4, 128)), pst("p_vc", (SEQ, 4, 128)), pst("p_ks", (SEQ, 4, 128)), pst("p_vs", (SEQ, 4, 128)),
        pst("p_kw", (512, 4, 128)), pst("p_vw", (512, 4, 128)),
        pst("p_conv", (3, 8192)), pst("p_gdn", (32, 128, 128)),
        sst("s_kc", (DEC_SEQ, 4, 128)), sst("s_vc", (DEC_SEQ, 4, 128)), sst("s_ks", (DEC_SEQ, 4, 128)),
        sst("s_vs", (DEC_SEQ, 4, 128)), sst("s_kw", (512, 4, 128)), sst("s_vw", (512, 4, 128)),
        sst("s_conv", (3, 8192)), sst("s_gdn", (32, 128, 128)),
    )
    return outs
```
